# Optimizing a Trainium2 kernel written in Bass

```python
import math
import jax, jax.numpy as jnp
from jax import lax
import numpy as np

D_MODEL = 2048
BATCH = 4
SEQ = 2048
DEPTH = 1
DEC_BATCH = 128
DEC_SEQ = 8
PAST_LEN = 16384
PAGE_SIZE = 128

ML_HEADS = 4
ML_DK = 256
ML_DV = 512
ML_QK = ML_HEADS * ML_DK
ML_V = ML_HEADS * ML_DV
HG_HEADS = 8
HG_DK = 128
HG_DV = 256
HG_K = HG_HEADS * HG_DK
HG_V = HG_HEADS * HG_DV
D_FF = 4 * D_MODEL
CHUNK = 64
LN_EPS = 1e-5
DEEPNORM_ALPHA = (2.0 * DEPTH) ** 0.25
DEEPNORM_BETA = (8.0 * DEPTH) ** -0.25
SPLITS = (ML_QK, ML_QK, ML_V, ML_HEADS, ML_HEADS, ML_V, HG_K, HG_K, HG_V, HG_V, D_MODEL, D_MODEL)
D_IN = sum(SPLITS)

kernel_name = "hybrid_mlstm_hgrn2_deepnorm_step"


def _layernorm(x, g, b):
    xf = x.astype(jnp.float32)
    mu = jnp.mean(xf, axis=-1, keepdims=True)
    var = jnp.mean(jnp.square(xf - mu), axis=-1, keepdims=True)
    return ((xf - mu) * lax.rsqrt(var + LN_EPS) * g.astype(jnp.float32) + b.astype(jnp.float32)).astype(x.dtype)


def _split_chunks(a, nc, L):
    a = a.reshape(a.shape[:2] + (nc, L) + a.shape[3:])
    return jnp.moveaxis(a, 2, 0)


def _merge_chunks(h):
    h = jnp.moveaxis(h, 0, 2)
    return h.reshape(h.shape[:2] + (h.shape[2] * h.shape[3],) + h.shape[4:])


def _mlstm(q, k, v, ig, lf, C0, n0, m0):
    T = q.shape[2]
    L = math.gcd(T, CHUNK)
    nc = T // L
    causal = jnp.tril(jnp.ones((L, L), dtype=bool))

    def step(carry, xs):
        C, n, m = carry
        qc, kc, vc, igc, lfc = xs
        b = jnp.cumsum(lfc, axis=-1)
        logd = b[..., :, None] - b[..., None, :] + igc[..., None, :]
        logd = jnp.where(causal, logd, -jnp.inf)
        inter = b + m[..., None]
        m_t = jnp.maximum(inter, jnp.max(logd, axis=-1))
        d = jnp.exp(logd - m_t[..., None])
        w_inter = jnp.exp(inter - m_t)
        s = jnp.einsum('bhtk,bhsk->bhts', qc, kc) * d
        num = jnp.einsum('bhts,bhsv->bhtv', s, vc) + w_inter[..., None] * jnp.einsum('bhtk,bhkv->bhtv', qc, C)
        den = jnp.sum(s, axis=-1) + w_inter * jnp.einsum('bhtk,bhk->bht', qc, n)
        h = num / jnp.maximum(jnp.abs(den), jnp.exp(-m_t))[..., None]
        m_new = m_t[..., -1]
        w_end = jnp.exp(b[..., -1:] - b + igc - m_new[..., None])
        decay = jnp.exp(b[..., -1] + m - m_new)
        C_new = decay[..., None, None] * C + jnp.einsum('bhs,bhsk,bhsv->bhkv', w_end, kc, vc)
        n_new = decay[..., None] * n + jnp.einsum('bhs,bhsk->bhk', w_end, kc)
        return (C_new, n_new, m_new), h

    xs = (_split_chunks(q, nc, L), _split_chunks(k, nc, L), _split_chunks(v, nc, L),
          _split_chunks(ig, nc, L), _split_chunks(lf, nc, L))
    (C, n, m), h = lax.scan(step, (C0, n0, m0), xs)
    return _merge_chunks(h), C, n, m


def _hgrn2(q, lf, kin, i, S0):
    T = q.shape[2]
    L = math.gcd(T, CHUNK)
    nc = T // L
    causal = jnp.tril(jnp.ones((L, L), dtype=bool))

    def step(S, xs):
        qc, lfc, kc, ic = xs
        G = jnp.cumsum(lfc, axis=2)
        rel = G[:, :, :, None, :] - G[:, :, None, :, :]
        rel = jnp.where(causal[:, :, None], rel, -jnp.inf)
        a = jnp.sum(qc[:, :, :, None, :] * jnp.exp(rel) * kc[:, :, None, :, :], axis=-1)
        o = jnp.einsum('bhts,bhsv->bhtv', a, ic) + jnp.einsum('bhtk,bhkv->bhtv', qc * jnp.exp(G), S)
        G_end = G[:, :, -1]
        S_new = jnp.exp(G_end)[..., None] * S + jnp.einsum('bhsk,bhsv->bhkv', kc * jnp.exp(G_end[:, :, None] - G), ic)
        return S_new, o

    xs = (_split_chunks(q, nc, L), _split_chunks(lf, nc, L), _split_chunks(kin, nc, L), _split_chunks(i, nc, L))
    S, o = lax.scan(step, S0, xs)
    return _merge_chunks(o), S


def _heads(a, n_heads):
    B, T, _ = a.shape
    return a.reshape(B, T, n_heads, -1).transpose(0, 2, 1, 3).astype(jnp.float32)


def _unheads(h):
    B, H, T, D = h.shape
    return h.transpose(0, 2, 1, 3).reshape(B, T, H * D)


def _layer(x, C0, n0, m0, S0, lb, w_in, b_ig, b_fg, ml_norm_g, hg_norm_g,
           w_branch_a, w_branch_b, w_out, ln1_g, ln1_b, w_up, w_down, ln2_g, ln2_b):
    f32 = jnp.float32
    offs = []
    acc = 0
    for s in SPLITS[:-1]:
        acc += s
        offs.append(acc)
    proj = jnp.einsum('btd,de->bte', x, w_in)
    (ml_q, ml_k, ml_v, ml_i, ml_f, ml_o, hg_q, hg_f, hg_i, hg_g, gate_a, gate_b) = jnp.split(proj, offs, axis=-1)

    q = _heads(ml_q, ML_HEADS)
    k = _heads(ml_k, ML_HEADS) * (ML_DK ** -0.5)
    v = _heads(ml_v, ML_HEADS)
    ig = (ml_i.astype(f32) + b_ig.astype(f32)).transpose(0, 2, 1)
    lf = jax.nn.log_sigmoid(ml_f.astype(f32) + b_fg.astype(f32)).transpose(0, 2, 1)
    h, C, n, m = _mlstm(q, k, v, ig, lf, C0.astype(f32), n0.astype(f32), m0.astype(f32))
    mu = jnp.mean(h, axis=-1, keepdims=True)
    var = jnp.mean(jnp.square(h - mu), axis=-1, keepdims=True)
    h = _unheads((h - mu) * lax.rsqrt(var + LN_EPS)) * ml_norm_g.astype(f32)
    branch_a = (h * jax.nn.sigmoid(ml_o.astype(f32))).astype(x.dtype)

    hq = _heads(hg_q, HG_HEADS)
    lbh = lb.reshape(HG_HEADS, 1, HG_DK)
    f = lbh + (1.0 - lbh) * jax.nn.sigmoid(_heads(hg_f, HG_HEADS))
    o, S = _hgrn2(hq, jnp.log(f), 1.0 - f, _heads(hg_i, HG_HEADS), S0.astype(f32))
    o = o * lax.rsqrt(jnp.mean(jnp.square(o), axis=-1, keepdims=True) + LN_EPS)
    o = _unheads(o) * hg_norm_g.astype(f32)
    branch_b = (o * jax.nn.silu(hg_g.astype(f32))).astype(x.dtype)

    ya = jnp.einsum('btv,vd->btd', branch_a, w_branch_a)
    yb = jnp.einsum('btv,vd->btd', branch_b, w_branch_b)
    merged = jax.nn.sigmoid(gate_a) * ya + jax.nn.sigmoid(gate_b) * yb
    mix = jnp.einsum('btd,de->bte', merged, w_out)
    x1 = _layernorm(DEEPNORM_ALPHA * x + mix, ln1_g, ln1_b)

    hid = jnp.square(jax.nn.relu(jnp.einsum('btd,df->btf', x1, w_up)))
    ff = jnp.einsum('btf,fd->btd', hid, w_down)
    y = _layernorm(DEEPNORM_ALPHA * x1 + ff, ln2_g, ln2_b)
    dt = x.dtype
    return y, C.astype(dt), n.astype(dt), m.astype(dt), S.astype(dt)


def setup_inputs(seed: int = 0) -> dict:
    key = jax.random.key(seed)
    ks = jax.random.split(key, 24)
    f32 = jnp.float32

    def nrm(k, shape, s):
        return jax.random.normal(k, shape, f32) * s

    return {
        "x_prompt": nrm(ks[0], (BATCH, SEQ, D_MODEL), 1.0),
        "x_sample": nrm(ks[1], (DEC_BATCH, DEC_SEQ, D_MODEL), 1.0),
        "state_mlstm_C": nrm(ks[2], (DEPTH, DEC_BATCH, ML_HEADS, ML_DK, ML_DV), 1.0),
        "state_mlstm_n": nrm(ks[3], (DEPTH, DEC_BATCH, ML_HEADS, ML_DK), 1.0),
        "state_mlstm_m": jax.random.uniform(ks[4], (DEPTH, DEC_BATCH, ML_HEADS), f32, 0.0, 3.0),
        "state_hgrn_S": nrm(ks[5], (DEPTH, DEC_BATCH, HG_HEADS, HG_DK, HG_DV), 1.0),
        "hg_lb_logits": nrm(ks[6], (DEPTH + 1, HG_K), 0.1),
        "w_in": nrm(ks[7], (DEPTH, D_MODEL, D_IN), D_MODEL ** -0.5),
        "b_ig": nrm(ks[8], (DEPTH, ML_HEADS), 0.1),
        "b_fg": jnp.linspace(3.0, 6.0, ML_HEADS, dtype=f32)[None, :] + nrm(ks[9], (DEPTH, ML_HEADS), 0.1),
        "ml_norm_g": 1.0 + nrm(ks[10], (DEPTH, ML_V), 0.05),
        "hg_norm_g": 1.0 + nrm(ks[11], (DEPTH, HG_V), 0.05),
        "w_branch_a": nrm(ks[12], (DEPTH, ML_V, D_MODEL), ML_V ** -0.5),
        "w_branch_b": nrm(ks[13], (DEPTH, HG_V, D_MODEL), HG_V ** -0.5),
        "w_out": nrm(ks[14], (DEPTH, D_MODEL, D_MODEL), DEEPNORM_BETA * D_MODEL ** -0.5),
        "ln1_g": 1.0 + nrm(ks[15], (DEPTH, D_MODEL), 0.05),
        "ln1_b": nrm(ks[16], (DEPTH, D_MODEL), 0.02),
        "w_up": nrm(ks[17], (DEPTH, D_MODEL, D_FF), D_MODEL ** -0.5),
        "w_down": nrm(ks[18], (DEPTH, D_FF, D_MODEL), DEEPNORM_BETA * D_FF ** -0.5),
        "ln2_g": 1.0 + nrm(ks[19], (DEPTH, D_MODEL), 0.05),
        "ln2_b": nrm(ks[20], (DEPTH, D_MODEL), 0.02),
    }


def reference(x_prompt, x_sample, state_mlstm_C, state_mlstm_n, state_mlstm_m, state_hgrn_S,
              hg_lb_logits, w_in, b_ig, b_fg, ml_norm_g, hg_norm_g, w_branch_a, w_branch_b, w_out,
              ln1_g, ln1_b, w_up, w_down, ln2_g, ln2_b):
    dt = x_prompt.dtype
    lb_all = jnp.cumsum(jax.nn.softmax(hg_lb_logits.astype(jnp.float32), axis=0), axis=0)
    xp, xs = x_prompt, x_sample
    Cp_l, np_l, mp_l, Sp_l = [], [], [], []
    Cs_l, ns_l, ms_l, Ss_l = [], [], [], []
    for l in range(DEPTH):
        params = (lb_all[l], w_in[l], b_ig[l], b_fg[l], ml_norm_g[l], hg_norm_g[l],
                  w_branch_a[l], w_branch_b[l], w_out[l], ln1_g[l], ln1_b[l], w_up[l], w_down[l], ln2_g[l], ln2_b[l])
        C0 = jnp.zeros((BATCH, ML_HEADS, ML_DK, ML_DV), dt)
        n0 = jnp.zeros((BATCH, ML_HEADS, ML_DK), dt)
        m0 = jnp.zeros((BATCH, ML_HEADS), dt)
        S0 = jnp.zeros((BATCH, HG_HEADS, HG_DK, HG_DV), dt)
        xp, Cp, n_p, mp, Sp = _layer(xp, C0, n0, m0, S0, *params)
        xs, Cs, n_s, ms, Ss = _layer(xs, state_mlstm_C[l], state_mlstm_n[l], state_mlstm_m[l], state_hgrn_S[l], *params)
        Cp_l.append(Cp); np_l.append(n_p); mp_l.append(mp); Sp_l.append(Sp)
        Cs_l.append(Cs); ns_l.append(n_s); ms_l.append(ms); Ss_l.append(Ss)
    return (xp, xs,
            jnp.stack(Cp_l), jnp.stack(np_l), jnp.stack(mp_l), jnp.stack(Sp_l),
            jnp.stack(Cs_l), jnp.stack(ns_l), jnp.stack(ms_l), jnp.stack(Ss_l))
```

```python
import contextlib
import numpy as np
import concourse.bass as bass
import concourse.mybir as mybir
from concourse.alu_op_type import AluOpType as ALU
from concourse.bass_utils import run_bass_kernel_spmd

F32 = mybir.dt.float32
BF16 = mybir.dt.bfloat16
AF = mybir.ActivationFunctionType
AX = mybir.AxisListType

ML_H, ML_DK, ML_DV = 4, 256, 512
HG_H, HG_DK, HG_DV = 8, 128, 256
LN_EPS = 1e-5
ALPHA = 2.0 ** 0.25
NSQ = 16
SQL = 8
BW = 256
NEG = -60000.0


class _Proxy:
    def __init__(self):
        self.call = None

    def __getattr__(self, name):
        def f(*a, **k):
            self.call = (name, a, k)
            return self
        return f


class Sch:
    def __init__(self, nc, ndma=6):
        self.nc = nc
        self.E = {'pe': nc.tensor, 'act': nc.scalar, 'dve': nc.vector, 'pool': nc.gpsimd, 'sp': nc.sync}
        self.semh, self.cnt = {}, {}
        for k in self.E:
            self.semh[k] = nc.alloc_semaphore("c_" + k)
            self.cnt[k] = 0
        self.waited = {k: {} for k in self.E}
        self.dq = {}
        for q in ('sp', 'pool'):
            sems = []
            for j in range(ndma):
                nm = "d_%s%d" % (q, j)
                self.semh[nm] = nc.alloc_semaphore(nm)
                self.cnt[nm] = 0
                sems.append(nm)
            self.dq[q] = [sems, 0]
        self.track = {}
        self.n_inst = 0
        self.dead = False
        self.rec = None

    def _need(self, eng, reads, writes):
        need = {}

        def add(tok):
            if tok is None:
                return
            s, v = tok
            if eng == 'pe' and s == 'pe':
                return
            if self.waited[eng].get(s, 0) < v:
                need[s] = max(need.get(s, 0), v)
        for k in reads:
            t = self.track.get(k)
            if t:
                add(t[0])
        for k in writes:
            t = self.track.get(k)
            if t:
                add(t[0])
                for r in t[1]:
                    add(r)
        for s, v in need.items():
            self.E[eng].wait_ge(self.semh[s], v)
            self.waited[eng][s] = v

    def _upd(self, tok, reads, writes):
        for k in reads:
            t = self.track.setdefault(k, [None, []])
            t[1].append(tok)
            if len(t[1]) > 64:
                t[1] = t[1][-64:] if False else self._compact(t[1])
        for k in writes:
            self.track[k] = [tok, []]

    @staticmethod
    def _compact(lst):
        best = {}
        for s, v in lst:
            best[s] = max(best.get(s, 0), v)
        return list(best.items())

    def start_record(self):
        assert self.rec is None
        self.rec = []

    def stop_record(self):
        r, self.rec = self.rec, None
        return r

    def replay_interleaved(self, lists):
        its = [iter(l) for l in lists]
        live = list(range(len(its)))
        while live:
            for i in list(live):
                item = next(its[i], None)
                if item is None:
                    live.remove(i)
                    continue
                if item[0] == 'op':
                    _, eng, (name, a, k), reads, writes = item
                    self.op(eng, lambda e, name=name, a=a, k=k: getattr(e, name)(*a, **k), reads, writes)
                else:
                    _, q, out, in_, reads, writes, kw = item
                    self.dma(q, out, in_, reads, writes, **kw)

    def op(self, eng, fn, reads=(), writes=()):
        if self.dead:
            return
        if self.rec is not None:
            p = _Proxy()
            fn(p)
            assert p.call is not None
            self.rec.append(('op', eng, p.call, tuple(reads), tuple(writes)))
            return
        pr = [k for k in reads if k.startswith('ps') and k not in writes]
        if pr:
            writes = list(writes) + pr
        self._need(eng, reads, writes)
        inst = fn(self.E[eng])
        self.cnt[eng] += 1
        inst.then_inc(self.semh[eng], 1)
        self._upd((eng, self.cnt[eng]), reads, writes)
        self.n_inst += 1

    def dma(self, q, out, in_, reads=(), writes=(), **kw):
        if self.dead:
            return
        if self.rec is not None:
            self.rec.append(('dma', q, out, in_, tuple(reads), tuple(writes), kw))
            return
        sems, idx = self.dq[q]
        s = sems[idx % len(sems)]
        self.dq[q][1] = idx + 1
        if self.cnt[s] > 0 and self.waited[q].get(s, 0) < self.cnt[s]:
            self.E[q].wait_ge(self.semh[s], self.cnt[s])
            self.waited[q][s] = self.cnt[s]
        self._need(q, reads, writes)
        inst = self.E[q].dma_start(out=out, in_=in_, **kw)
        self.cnt[s] += 16
        inst.then_inc(self.semh[s], 16)
        self._upd((s, self.cnt[s]), reads, writes)
        self.n_inst += 1

    def barrier(self):
        if self.dead:
            return
        assert self.rec is None
        for eng in self.E:
            for s, v in self.cnt.items():
                if v > 0 and s != eng and self.waited[eng].get(s, 0) < v:
                    self.E[eng].wait_ge(self.semh[s], v)
                    self.waited[eng][s] = v
        self.track = {}

    def finish(self):
        self.dead = False
        self.barrier()


class Cfg:
    def __init__(self, D, DFF, TH):
        self.D, self.DFF, self.TH = D, DFF, TH
        self.KC = D // 128
        self.NTP = TH // 128
        self.NT = self.NTP + 1
        self.M = self.NT * 128
        self.GT = max(1, self.NTP // 2)
        self.MG = (self.GT + 1) * 128
        self.RSZ = max(self.KC * self.MG, 2 * (4 * self.MG + 1280 * (self.GT + 1)))
        self.FG = min(16, DFF // 128)
        self.NFG = DFF // (128 * self.FG)
        self.MLQK, self.MLV = ML_H * ML_DK, ML_H * ML_DV
        self.HGK, self.HGV = HG_H * HG_DK, HG_H * HG_DV
        o = 0
        self.off = {}
        for nm, sz in (('mq', self.MLQK), ('mk', self.MLQK), ('mv', self.MLV), ('mi', ML_H), ('mf', ML_H),
                       ('mo', self.MLV), ('hq', self.HGK), ('hf', self.HGK), ('hi', self.HGV), ('hg', self.HGV),
                       ('ga', D), ('gb', D)):
            self.off[nm] = o
            o += sz
        self.DIN = o


DBG = {'stop': None}


class _Stop(Exception):
    pass


def build(cfg, wplan=None, record=None):
    D, KC, TH, NTP, NT, M = cfg.D, cfg.KC, cfg.TH, cfg.NTP, cfg.NT, cfg.M
    off = cfg.off
    nc = bass.Bass("TRN2", target_bir_lowering=False)

    def din(name, shape):
        return nc.dram_tensor(name, list(shape), F32, kind="ExternalInput").ap()

    def dout(name, shape):
        return nc.dram_tensor(name, list(shape), F32, kind="ExternalOutput").ap()

    x_pre = din("x_pre", [TH, D])
    x_main = din("x_main", [M, D])
    st_C = din("st_C", [NSQ, ML_H, ML_DK, ML_DV])
    st_n = din("st_n", [NSQ * ML_H * 2, 128])
    st_m = din("st_m", [NSQ, ML_H])
    st_S = din("st_S", [NSQ, HG_H, HG_DK, HG_DV])
    lb_logits = din("lb_logits", [2 * HG_H, 128])
    w_in = din("w_in", [D, cfg.DIN])
    b_ig = din("b_ig", [1, ML_H])
    b_fg = din("b_fg", [1, ML_H])
    ml_g = din("ml_g", [cfg.MLV // 128, 128])
    hg_g = din("hg_g", [cfg.HGV // 128, 128])
    w_ba = din("w_ba", [cfg.MLV, D])
    w_bb = din("w_bb", [cfg.HGV, D])
    w_out = din("w_out", [D, D])
    ln1_g = din("ln1_g", [1, D])
    ln1_b = din("ln1_b", [1, D])
    w_up = din("w_up", [D, cfg.DFF])
    w_dn = din("w_dn", [cfg.DFF, D])
    ln2_g = din("ln2_g", [1, D])
    ln2_b = din("ln2_b", [1, D])

    y_main = dout("y_main", [M, D])
    o_Cp = dout("o_Cp", [ML_H, ML_DK, ML_DV])
    o_np = dout("o_np", [ML_H * 2, 128])
    o_mp = dout("o_mp", [1, ML_H])
    o_Sp = dout("o_Sp", [HG_H, HG_DK, HG_DV])
    o_Cs = dout("o_Cs", [NSQ, ML_H, ML_DK, ML_DV])
    o_ns = dout("o_ns", [NSQ * ML_H * 2, 128])
    o_ms = dout("o_ms", [NSQ, ML_H])
    o_Ss = dout("o_Ss", [NSQ, HG_H, HG_DK, HG_DV])

    S = Sch(nc)
    es = contextlib.ExitStack()

    uid = [0]

    def sb(stack, name, shape, dt=F32):
        uid[0] += 1
        return stack.enter_context(nc.sbuf_tensor("%s_%d" % (name, uid[0]), list(shape), dt))

    with es:
        ps = [es.enter_context(nc.psum_tensor("ps%d" % i, [128, 512], F32)) for i in range(8)]
        PK = ["ps%d" % i for i in range(8)]

        ident = sb(es, "ident", [128, 128])
        identb = sb(es, "identb", [128, 128], BF16)
        ones = sb(es, "ones", [128, 128])
        onesb = sb(es, "onesb", [128, 2], BF16)
        mst_c = sb(es, "mst_c", [128, 128])
        mst_b = sb(es, "mst_b", [128, 128])
        mts_b = sb(es, "mts_b", [128, 128])
        neg_c = sb(es, "neg_c", [128, 128])
        neg_b = sb(es, "neg_b", [128, 128])
        sel_c = sb(es, "sel_c", [128, 128])
        sel_b = sb(es, "sel_b", [128, 128])
        ind = sb(es, "ind", [128, NSQ])
        lastind = sb(es, "lastind", [128, NSQ])
        indT = sb(es, "indT", [NSQ, 128])
        blkb = sb(es, "blkb", [128, NSQ, 128], BF16)
        GT, MG = cfg.GT, cfg.MG
        rst = sb(es, "rst", [128, MG])
        gcol_a = sb(es, "gcol_a", [128, cfg.MLV // 128])
        gcol_b = sb(es, "gcol_b", [128, cfg.HGV // 128])
        lbc = sb(es, "lbc", [128, 2 * HG_H])
        bigb = sb(es, "bigb", [128, ML_H])
        nbfg = sb(es, "nbfg", [128, ML_H])
        ctmp = sb(es, "ctmp", [128, 128])

        def P(fn, w, r=()):
            S.op('pool', fn, reads=r, writes=w)

        P(lambda e: e.memset(ident[:], 1.0), ['ident'])
        P(lambda e: e.affine_select(out=ident[:], in_=ident[:], pattern=[[-1, 128]], compare_op=ALU.is_equal,
                                    fill=0.0, base=0, channel_multiplier=1), ['ident'], ['ident'])
        P(lambda e: e.tensor_copy(out=identb[:], in_=ident[:]), ['identb'], ['ident'])
        P(lambda e: e.memset(ones[:], 1.0), ['ones'])
        P(lambda e: e.memset(onesb[:], 1.0), ['onesb'])
        P(lambda e: e.memset(mst_c[:], 1.0), ['mst_c'])
        P(lambda e: e.affine_select(out=mst_c[:], in_=mst_c[:], pattern=[[1, 128]], compare_op=ALU.is_ge,
                                    fill=0.0, base=0, channel_multiplier=-1), ['mst_c'], ['mst_c'])
        P(lambda e: e.memset(ctmp[:], 1.0), ['ctmp'])
        P(lambda e: e.affine_select(out=ctmp[:], in_=ctmp[:], pattern=[[-1, 128]], compare_op=ALU.is_ge,
                                    fill=0.0, base=0, channel_multiplier=1), ['ctmp'], ['ctmp'])
        P(lambda e: e.tensor_scalar(out=neg_c[:], in0=ctmp[:], scalar1=-1.0, scalar2=-NEG, op0=ALU.add, op1=ALU.mult),
          ['neg_c'], ['ctmp'])
        P(lambda e: e.memset(sel_c[:], 1.0), ['sel_c'])
        P(lambda e: e.affine_select(out=sel_c[:], in_=sel_c[:], pattern=[[0, 128]], compare_op=ALU.is_equal,
                                    fill=0.0, base=-127, channel_multiplier=1), ['sel_c'], ['sel_c'])
        P(lambda e: e.memset(ind[:], 1.0), ['ind'])
        P(lambda e: e.affine_select(out=ind[:], in_=ind[:], pattern=[[-SQL, NSQ]], compare_op=ALU.is_ge,
                                    fill=0.0, base=0, channel_multiplier=1), ['ind'], ['ind'])
        P(lambda e: e.affine_select(out=ind[:], in_=ind[:], pattern=[[SQL, NSQ]], compare_op=ALU.is_ge,
                                    fill=0.0, base=SQL - 1, channel_multiplier=-1), ['ind'], ['ind'])
        P(lambda e: e.memset(lastind[:], 1.0), ['lastind'])
        P(lambda e: e.affine_select(out=lastind[:], in_=lastind[:], pattern=[[-SQL, NSQ]], compare_op=ALU.is_equal,
                                    fill=0.0, base=-(SQL - 1), channel_multiplier=1), ['lastind'], ['lastind'])
        P(lambda e: e.memset(indT[:], 1.0), ['indT'])
        P(lambda e: e.affine_select(out=indT[:], in_=indT[:], pattern=[[1, 128]], compare_op=ALU.is_ge,
                                    fill=0.0, base=0, channel_multiplier=-SQL), ['indT'], ['indT'])
        P(lambda e: e.affine_select(out=indT[:], in_=indT[:], pattern=[[-1, 128]], compare_op=ALU.is_ge,
                                    fill=0.0, base=SQL - 1, channel_multiplier=SQL), ['indT'], ['indT'])
        P(lambda e: e.memset(blkb[:], 1.0), ['blkb'])
        P(lambda e: e.affine_select(out=blkb[:], in_=blkb[:], pattern=[[-SQL, NSQ], [1, 128]], compare_op=ALU.is_ge,
                                    fill=0.0, base=0, channel_multiplier=0), ['blkb'], ['blkb'])
        P(lambda e: e.affine_select(out=blkb[:], in_=blkb[:], pattern=[[SQL, NSQ], [-1, 128]], compare_op=ALU.is_ge,
                                    fill=0.0, base=SQL - 1, channel_multiplier=0), ['blkb'], ['blkb'])
        S.op('pe', lambda e: e.matmul(ps[0][:, 0:128], lhsT=indT[:], rhs=indT[:], start=True, stop=True),
             reads=['indT'], writes=[PK[0]])
        S.op('dve', lambda e: e.tensor_tensor(out=mst_b[:], in0=ps[0][:, 0:128], in1=mst_c[:], op=ALU.mult),
             reads=[PK[0], 'mst_c'], writes=['mst_b'])
        S.op('dve', lambda e: e.tensor_tensor(out=mts_b[:], in0=ps[0][:, 0:128], in1=ctmp[:], op=ALU.mult),
             reads=[PK[0], 'ctmp'], writes=['mts_b'])
        S.op('dve', lambda e: e.tensor_scalar(out=neg_b[:], in0=mts_b[:], scalar1=-1.0, scalar2=-NEG, op0=ALU.add,
                                              op1=ALU.mult), reads=['mts_b'], writes=['neg_b'])
        S.op('dve', lambda e: e.tensor_copy(out=sel_b[:].rearrange("p (q j) -> p q j", j=SQL),
                                            in_=lastind[:].unsqueeze(2).to_broadcast([128, NSQ, SQL])),
             reads=['lastind'], writes=['sel_b'])
        P(lambda e: e.memset(rst[:], 1.0), ['rst'])
        P(lambda e: e.memset(rst[:, 0:GT * 128].rearrange("p (a b) -> p a b", b=128)[:, :, 0:1], 0.0), ['rst'], ['rst'])
        P(lambda e: e.memset(rst[:, GT * 128:MG].rearrange("p (a b) -> p a b", b=SQL)[:, :, 0:1], 0.0), ['rst'], ['rst'])

        rowt = sb(es, "rowt", [128, 128])

        def load_cols(src_rows, nrows, dst, dkey):
            S.dma('sp', rowt[0:nrows, :], src_rows, writes=['rowt'])
            S.op('pe', lambda e: e.matmul(ps[1][:, 0:nrows], lhsT=rowt[0:nrows, :], rhs=ident[0:nrows, 0:nrows],
                                          start=True, stop=True), reads=['rowt', 'ident'], writes=[PK[1]])
            S.op('dve', lambda e: e.tensor_copy(out=dst, in_=ps[1][:, 0:nrows]), reads=[PK[1]], writes=[dkey])

        load_cols(ml_g, cfg.MLV // 128, gcol_a[:], 'gcol_a')
        load_cols(hg_g, cfg.HGV // 128, gcol_b[:], 'gcol_b')
        load_cols(lb_logits, 2 * HG_H, lbc[:], 'lbc')
        S.op('dve', lambda e: e.tensor_tensor(out=lbc[:, 0:HG_H], in0=lbc[:, 0:HG_H], in1=lbc[:, HG_H:2 * HG_H],
                                              op=ALU.subtract), reads=['lbc'], writes=['lbc'])
        S.op('act', lambda e: e.activation(out=lbc[:, 0:HG_H], in_=lbc[:, 0:HG_H], func=AF.Exp, scale=-1.0),
             reads=['lbc'], writes=['lbc'])
        S.op('act', lambda e: e.activation(out=lbc[:, 0:HG_H], in_=lbc[:, 0:HG_H], func=AF.Ln, bias=1.0),
             reads=['lbc'], writes=['lbc'])
        S.op('act', lambda e: e.activation(out=lbc[:, 0:HG_H], in_=lbc[:, 0:HG_H], func=AF.Exp, scale=-1.0),
             reads=['lbc'], writes=['lbc'])
        S.op('dve', lambda e: e.tensor_scalar(out=lbc[:, HG_H:2 * HG_H], in0=lbc[:, 0:HG_H], scalar1=-1.0, scalar2=1.0,
                                              op0=ALU.mult, op1=ALU.add), reads=['lbc'], writes=['lbc'])
        S.dma('sp', bigb[:], b_ig.partition_broadcast(128), writes=['bigb'])
        S.dma('sp', nbfg[:], b_fg.partition_broadcast(128), writes=['nbfg'])
        S.op('dve', lambda e: e.tensor_scalar(out=nbfg[:], in0=nbfg[:], scalar1=-1.0, scalar2=None, op0=ALU.mult),
             reads=['nbfg'], writes=['nbfg'])

        NWB = 4
        wbuf = [sb(es, "wbuf%d" % i, [128, 16, BW], BF16) for i in range(NWB)]
        wstate = {'issued': 0, 'req': 0}

        def _issue(spec):
            wap, r0, kc, c0, ncol = spec
            i = wstate['issued']
            wstate['issued'] += 1
            buf = wbuf[i % NWB]
            key = 'wbuf%d' % (i % NWB)
            src = wap[r0:r0 + kc * 128, c0:c0 + ncol].rearrange("(c p) n -> p c n", p=128)
            S.dma('pool', buf[:, 0:kc, 0:ncol], src, writes=[key])

        def wget(wname, wap, r0, kc, c0, ncol):
            if record is not None:
                record.append((wname, r0, kc, c0, ncol))
            idx = wstate['req']
            wstate['req'] += 1
            if wstate['issued'] <= idx:
                assert wstate['issued'] == idx
                _issue((wap, r0, kc, c0, ncol))
            if wplan is not None:
                assert wplan[idx] == (wname, r0, kc, c0, ncol), (idx, wplan[idx], (wname, r0, kc, c0, ncol))
                while wstate['issued'] < min(len(wplan), idx + NWB):
                    nm, a, b, c, d = wplan[wstate['issued']]
                    _issue((WMAP[nm], a, b, c, d))
            return wbuf[idx % NWB], 'wbuf%d' % (idx % NWB)

        WMAP = {'w_in': w_in, 'w_ba': w_ba, 'w_bb': w_bb, 'w_out': w_out, 'w_up': w_up, 'w_dn': w_dn}
        psrot = {'i': 0}

        def nextps(lo=0, hi=8):
            i = lo + psrot['i'] % (hi - lo)
            psrot['i'] += 1
            return ps[i], PK[i]

        evrot = {'i': 0}

        def evac_eng():
            evrot['i'] += 1
            return 'dve' if evrot['i'] % 2 else 'act'

        def copy_op(eng, out, in_, reads, writes, scale=None):
            if scale is not None:
                if eng == 'act':
                    S.op('act', lambda e: e.activation(out=out, in_=in_, func=AF.Copy, scale=scale), reads=reads, writes=writes)
                else:
                    S.op(eng, lambda e: e.tensor_scalar(out=out, in0=in_, scalar1=scale, scalar2=None, op0=ALU.mult),
                         reads=reads, writes=writes)
            elif eng == 'act':
                S.op('act', lambda e: e.activation(out=out, in_=in_, func=AF.Copy), reads=reads, writes=writes)
            else:
                S.op(eng, lambda e: e.tensor_copy(out=out, in_=in_), reads=reads, writes=writes)

        def proj_T(wb, wkey, ncol, actT, akey, kc, tok0, pst, pkey):
            for c in range(kc):
                S.op('pe', lambda e, c=c: e.matmul(pst[:, 0:ncol], lhsT=actT[:, c, tok0:tok0 + 128], rhs=wb[:, c, 0:ncol],
                                                   start=(c == 0), stop=(c == kc - 1)),
                     reads=[wkey, akey], writes=[pkey])

        def proj_F(wb, wkey, m0, mcol, actT, akey, kc, tok0, ntok, pst, pkey):
            for c in range(kc):
                S.op('pe', lambda e, c=c: e.matmul(pst[0:mcol, 0:ntok], lhsT=wb[:, c, m0:m0 + mcol],
                                                   rhs=actT[:, c, tok0:tok0 + ntok], start=(c == 0), stop=(c == kc - 1)),
                     reads=[wkey, akey], writes=[pkey])

        def ntiles(total):
            out, t = [], 0
            while t < total:
                n = min(512, total - t)
                out.append((t, n))
                t += n
            return out

        def sigmoid_from(eng_in, out, in_, reads, writes):
            S.op('act', lambda e: e.activation(out=out, in_=in_, func=AF.Exp, scale=-1.0), reads=reads, writes=writes)
            S.op('act', lambda e: e.activation(out=out, in_=out, func=AF.Ln, bias=1.0), reads=writes, writes=writes)
            S.op('act', lambda e: e.activation(out=out, in_=out, func=AF.Exp, scale=-1.0), reads=writes, writes=writes)

        def load_xT(x_ap, rows, xT, xkey):
            with contextlib.ExitStack() as st2:
                xt = [sb(st2, "xt%d" % i, [128, D], BF16) for i in range(3)]
                for t, r0 in enumerate(rows):
                    b = xt[t % 3]
                    bk = "xt%d" % (t % 3)
                    S.dma('pool', b[:], x_ap[r0:r0 + 128, :], writes=[bk])
                    for g in range(0, KC, 4):
                        n = min(4, KC - g)
                        pt, pk = nextps()
                        for j in range(n):
                            S.op('pe', lambda e, j=j: e.matmul(pt[:, j * 128:(j + 1) * 128],
                                                               lhsT=b[:, (g + j) * 128:(g + j + 1) * 128], rhs=identb[:],
                                                               start=True, stop=True), reads=[bk, 'identb'], writes=[pk])
                        copy_op(evac_eng(), xT[:, g:g + n, t * 128:(t + 1) * 128],
                                pt[:, 0:n * 128].rearrange("p (a b) -> p a b", b=128), [pk], [xkey])
                S.barrier()

        Cst = [sb(es, "Cst%d" % h, [128, 2, ML_DV]) for h in range(ML_H)]
        nst = sb(es, "nst", [128, ML_H, 2])
        mst = sb(es, "mst", [128, ML_H])
        Sst = [sb(es, "Sst%d" % h, [128, HG_DV]) for h in range(HG_H)]
        for h in range(ML_H):
            P(lambda e, h=h: e.memset(Cst[h][:], 0.0), ['Cst%d' % h])
        P(lambda e: e.memset(nst[:], 0.0), ['nst'])
        P(lambda e: e.memset(mst[:], 0.0), ['mst'])
        for h in range(HG_H):
            P(lambda e, h=h: e.memset(Sst[h][:], 0.0), ['Sst%d' % h])

        chk_holder = [lambda tag: None]
        def mixer_pass(xT, xkey, ntile, full, brA, brB, samp, R, final, rmask=None):
            rmask = rst if rmask is None else rmask
            tiles = list(range(ntile)) + ([ntile] if samp else [])
            ntl = len(tiles)
            Mtok = ntl * 128
            with contextlib.ExitStack() as st:
                ift = sb(st, "ift", [128, ntl, 2 * ML_H])
                wb, wk = wget('w_in', w_in, 0, KC, off['mi'], 2 * ML_H)
                for ti in range(ntl):
                    pt, pk = nextps()
                    proj_T(wb, wk, 2 * ML_H, xT, xkey, KC, ti * 128, pt, pk)
                    copy_op('dve', ift[:, ti, :], pt[:, 0:2 * ML_H], [pk], ['ift'])
                chk_holder[0]('ift')
                ro = [0]

                def rview(n, inner):
                    v = R[:, ro[0]:ro[0] + n].rearrange("p (a b) -> p a b", b=inner)
                    ro[0] += n
                    return v
                PBS = []
                for i in range(2):
                    PB = {'k_tok': rview(ntl * ML_DK, ML_DK), 'v_tok': rview(ntl * ML_DV, ML_DV)}
                    if full:
                        PB['qT'] = rview(2 * Mtok, Mtok)
                        PB['kT'] = rview(2 * Mtok, Mtok)
                        PB['og'] = rview(ntl * ML_DV, ML_DV)
                    PBS.append(PB)
                if samp:
                    NCB = 3
                    Cf = [sb(st, "Cf%d" % i, [128, 2, ML_DV]) for i in range(NCB)]
                    Cfb = [sb(st, "Cfb%d" % i, [128, 2, ML_DV], BF16) for i in range(NCB)]
                    qTq = [sb(st, "qTq%d" % i, [128, 2, 128], BF16) for i in range(2)]
                    kwq = [sb(st, "kwq%d" % i, [128, ML_DK], BF16) for i in range(2)]
                    nall = sb(st, "nall", [128, NSQ * ML_H * 2])
                    nallb = sb(st, "nallb", [128, NSQ * ML_H * 2], BF16)
                    nrow = sb(st, "nrow", [128, 128])
                    Dbc = sb(st, "Dbc", [128, NSQ])
                    dsel = sb(st, "dsel", [128, NSQ])
                    msamp = sb(st, "msamp", [NSQ, ML_H])
                    mtok = sb(st, "mtok", [128, ML_H])
                    mnew_all = sb(st, "mnew_all", [128, ML_H])
                    mout = sb(st, "mout", [NSQ, ML_H])
                    S.dma('sp', nrow[:], st_n, writes=['nrow'])
                    chk_holder[0]('sA0')
                    S.op('pe', lambda e: e.matmul(ps[4][:, 0:128], lhsT=nrow[:], rhs=ident[:], start=True, stop=True),
                         reads=['nrow', 'ident'], writes=[PK[4]])
                    copy_op('dve', nall[:], ps[4][:, 0:128], [PK[4]], ['nall'])
                    chk_holder[0]('sA05')
                    copy_op('act', nallb[:], ps[4][:, 0:128], [PK[4]], ['nallb'])
                    chk_holder[0]('sA1')
                    S.dma('sp', msamp[:], st_m, writes=['msamp'])
                    S.op('pe', lambda e: e.matmul(ps[4][:, 0:ML_H], lhsT=indT[:], rhs=msamp[:], start=True, stop=True),
                         reads=['indT', 'msamp'], writes=[PK[4]])
                    copy_op('dve', mtok[:], ps[4][:, 0:ML_H], [PK[4]], ['mtok'])
                    chk_holder[0]('sA')

                def ml_rec(h, X, tiles):
                    sfx = X['sfx']
                    K = lambda nm: nm + sfx
                    Cbf, nbf, sc, bm, mprev = X['Cbf'], X['nbf'], X['sc'], X['bm'], X['mprev']
                    diagc, logd, dmat, sdm, sdT, kw = X['diagc'], X['logd'], X['dmat'], X['sdm'], X['sdT'], X['kw']
                    hbuf, h2, bst, bmv = X['hbuf'], X['h2'], X['bst'], X['bmv']
                    PB = X['PB']
                    k_tok, v_tok = PB['k_tok'], PB['v_tok']
                    qT, kT, og = PB.get('qT'), PB.get('kT'), PB.get('og')
                    bG, bQ, bA, bB = X['bG'], X['bQ'], X['bA'], X['bB']
                    kG, kQ, kA, kB = PK[bG], PK[bQ], PK[bA], PK[bB]
                    cB0, cC0, cS0, cQ0, cT0, cN0 = X['cols']
                    urot = [0]

                    def ubank():
                        bnk = X['ub'][urot[0] % len(X['ub'])]
                        urot[0] += 1
                        return ps[bnk], PK[bnk]
                    Ck, nk, mk_ = 'Cst%d' % h, 'nst%d' % h, 'mst%d' % h
                    S.op('act', lambda e: e.activation(out=Cbf[:], in_=Cst[h][:], func=AF.Copy), reads=[Ck], writes=[K('Cbf')])
                    S.op('dve', lambda e: e.tensor_copy(out=nbf[:], in_=nst[:, h, :]), reads=[nk], writes=[K('nbf')])
                    S.op('dve', lambda e: e.tensor_copy(out=mprev[:], in_=mst[:, h:h + 1]), reads=[mk_], writes=[K('mprev')])
                    for ti in tiles:
                        is_s = samp and ti == ntl - 1
                        mstm = mst_b if is_s else mst_c
                        mstk = 'mst_b' if is_s else 'mst_c'
                        negm_ = neg_b if is_s else neg_c
                        negk = 'neg_b' if is_s else 'neg_c'
                        selm = sel_b if is_s else sel_c
                        selk = 'sel_b' if is_s else 'sel_c'
                        tok = slice(ti * 128, (ti + 1) * 128)
                        col = lambda j: sc[:, j:j + 1]
                        if is_s:
                            S.op('dve', lambda e: e.tensor_copy(out=mprev[:], in_=mtok[:, h:h + 1]), reads=['mtok'],
                                 writes=[K('mprev')])
                        S.op('dve', lambda e: e.tensor_scalar(out=col(0), in0=ift[:, ti, h:h + 1], scalar1=bigb[:, h:h + 1],
                                                              scalar2=None, op0=ALU.add), reads=['ift', 'bigb'], writes=[K('sc0')])
                        S.op('act', lambda e: e.activation(out=col(1), in_=ift[:, ti, ML_H + h:ML_H + h + 1], func=AF.Exp,
                                                           scale=-1.0, bias=nbfg[:, h:h + 1]), reads=['ift', 'nbfg'],
                             writes=[K('sc1')])
                        S.op('act', lambda e: e.activation(out=col(1), in_=col(1), func=AF.Ln, bias=1.0), reads=[K('sc1')],
                             writes=[K('sc1')])
                        S.op('pe', lambda e: e.matmul(ps[bG][:, cB0:cB0 + 1], lhsT=mstm[:], rhs=col(1), start=True, stop=True),
                             reads=[mstk, K('sc1')], writes=[kG])
                        S.op('dve', lambda e: e.tensor_copy(out=bm[:, 0:1], in_=ps[bG][:, cB0:cB0 + 1]), reads=[kG], writes=[K('bm0')])
                        S.op('dve', lambda e: e.tensor_tensor(out=col(2), in0=col(0), in1=bm[:, 0:1], op=ALU.add),
                             reads=[K('sc0'), K('bm0')], writes=[K('sc2')])
                        S.op('dve', lambda e: e.tensor_scalar(out=diagc[:], in0=ident[:], scalar1=col(2), scalar2=None,
                                                              op0=ALU.mult), reads=['ident', K('sc2')], writes=[K('diagc')])
                        S.op('pe', lambda e: e.matmul(ps[bG][:, cC0:cC0 + 128], lhsT=ones[:], rhs=diagc[:], start=True, stop=True),
                             reads=['ones', K('diagc')], writes=[kG])
                        S.op('dve', lambda e: e.scalar_tensor_tensor(out=logd[:], in0=ps[bG][:, cC0:cC0 + 128], scalar=bm[:, 0:1],
                                                                     in1=negm_[:], op0=ALU.subtract, op1=ALU.add),
                             reads=[kG, K('bm0'), negk], writes=[K('logd')])
                        S.op('dve', lambda e: e.tensor_reduce(out=col(3), in_=logd[:], axis=AX.X, op=ALU.max),
                             reads=[K('logd')], writes=[K('sc3')])
                        S.op('dve', lambda e: e.tensor_tensor(out=col(4), in0=mprev[:], in1=bm[:, 0:1], op=ALU.subtract),
                             reads=[K('mprev'), K('bm0')], writes=[K('sc4')])
                        S.op('dve', lambda e: e.tensor_tensor(out=bm[:, 1:2], in0=col(4), in1=col(3), op=ALU.max),
                             reads=[K('sc4'), K('sc3')], writes=[K('bm1')])
                        S.op('dve', lambda e: e.tensor_scalar(out=col(5), in0=bm[:, 1:2], scalar1=-1.0, scalar2=None,
                                                              op0=ALU.mult), reads=[K('bm1')], writes=[K('sc5')])
                        S.op('pe', lambda e: e.matmul(ps[bG][:, cS0:cS0 + 2], lhsT=selm[:], rhs=bm[:], start=True, stop=True),
                             reads=[selk, K('bm0'), K('bm1')], writes=[kG])
                        S.op('dve', lambda e: e.tensor_tensor(out=col(8), in0=bm[:, 0:1], in1=ps[bG][:, cS0:cS0 + 1],
                                                              op=ALU.subtract), reads=[K('bm0'), kG], writes=[K('sc8')])
                        S.op('dve', lambda e: e.tensor_tensor(out=col(8), in0=col(8), in1=col(0), op=ALU.add),
                             reads=[K('sc8'), K('sc0')], writes=[K('sc8')])
                        S.op('dve', lambda e: e.tensor_scalar(out=col(9), in0=ps[bG][:, cS0 + 1:cS0 + 2], scalar1=-1.0, scalar2=None,
                                                              op0=ALU.mult), reads=[kG], writes=[K('sc9')])
                        S.op('act', lambda e: e.activation(out=col(10), in_=col(8), func=AF.Exp, bias=col(9)),
                             reads=[K('sc8'), K('sc9')], writes=[K('sc10')])
                        S.op('dve', lambda e: e.tensor_tensor(out=col(11), in0=mprev[:], in1=ps[bG][:, cS0:cS0 + 1],
                                                              op=ALU.subtract), reads=[K('mprev'), kG], writes=[K('sc11')])
                        S.op('act', lambda e: e.activation(out=col(12), in_=col(11), func=AF.Exp, bias=col(9)),
                             reads=[K('sc11'), K('sc9')], writes=[K('sc12')])
                        if full:
                            S.op('act', lambda e: e.activation(out=dmat[:], in_=logd[:], func=AF.Exp, bias=col(5)),
                                 reads=[K('logd'), K('sc5')], writes=[K('dmat')])
                            S.op('act', lambda e: e.activation(out=col(6), in_=col(4), func=AF.Exp, bias=col(5)),
                                 reads=[K('sc4'), K('sc5')], writes=[K('sc6')])
                            S.op('act', lambda e: e.activation(out=col(7), in_=col(5), func=AF.Exp), reads=[K('sc5')],
                                 writes=[K('sc7')])
                            for c in range(2):
                                S.op('pe', lambda e, c=c: e.matmul(ps[bQ][:, cQ0:cQ0 + 128], lhsT=qT[:, c, tok], rhs=kT[:, c, tok],
                                                                   start=(c == 0), stop=(c == 1)), reads=[K('qT'), K('kT')],
                                     writes=[kQ])
                            S.op('dve', lambda e: e.scalar_tensor_tensor(out=sdm[:], in0=ps[bQ][:, cQ0:cQ0 + 128], scalar=1.0,
                                                                         in1=dmat[:], op0=ALU.mult, op1=ALU.mult,
                                                                         accum_out=col(13)),
                                 reads=[kQ, K('dmat')], writes=[K('sdm'), K('sc13')])
                            S.op('pe', lambda e: e.matmul(ps[bQ][:, cT0:cT0 + 128], lhsT=sdm[:], rhs=ident[:], start=True, stop=True),
                                 reads=[K('sdm'), 'ident'], writes=[kQ])
                            copy_op('act', sdT[:], ps[bQ][:, cT0:cT0 + 128], [kQ], [K('sdT')])
                            S.op('pe', lambda e: e.matmul(ps[bA][:, :], lhsT=sdT[:], rhs=v_tok[:, ti, :], start=True, stop=True),
                                 reads=[K('sdT'), K('v_tok')], writes=[kA])
                        if not is_s:
                            if full:
                                for c in range(2):
                                    S.op('pe', lambda e, c=c: e.matmul(ps[bB][:, :], lhsT=qT[:, c, tok], rhs=Cbf[:, c, :],
                                                                       start=(c == 0), stop=(c == 1)), reads=[K('qT'), K('Cbf')],
                                         writes=[kB])
                                for c in range(2):
                                    S.op('pe', lambda e, c=c: e.matmul(ps[bQ][:, cN0:cN0 + 1], lhsT=qT[:, c, tok], rhs=nbf[:, c:c + 1],
                                                                       start=(c == 0), stop=(c == 1)), reads=[K('qT'), K('nbf')],
                                         writes=[kQ])
                            S.op('dve', lambda e: e.tensor_scalar(out=kw[:], in0=k_tok[:, ti, :], scalar1=col(10), scalar2=None,
                                                                  op0=ALU.mult), reads=[K('k_tok'), K('sc10')], writes=[K('kw')])
                            for c in range(2):
                                pt, pk = ubank()
                                S.op('pe', lambda e, c=c, pt=pt: e.matmul(pt[:, :], lhsT=kw[:, c * 128:(c + 1) * 128],
                                                                          rhs=v_tok[:, ti, :], start=True, stop=True),
                                     reads=[K('kw'), K('v_tok')], writes=[pk])
                                S.op('dve', lambda e, c=c, pt=pt: e.scalar_tensor_tensor(
                                    out=Cst[h][:, c, :], in0=Cst[h][:, c, :], scalar=col(12), in1=pt[:, :],
                                    op0=ALU.mult, op1=ALU.add), reads=[Ck, K('sc12'), pk, K('Cbf')], writes=[Ck])
                            pt, pk = ubank()
                            for c in range(2):
                                S.op('pe', lambda e, c=c, pt=pt: e.matmul(pt[:, c:c + 1], lhsT=kw[:, c * 128:(c + 1) * 128],
                                                                          rhs=onesb[:, 0:1], start=True, stop=True),
                                     reads=[K('kw'), 'onesb'], writes=[pk])
                            S.op('dve', lambda e, pt=pt: e.scalar_tensor_tensor(
                                out=nst[:, h, :], in0=nst[:, h, :], scalar=col(12), in1=pt[:, 0:2],
                                op0=ALU.mult, op1=ALU.add), reads=[nk, K('sc12'), pk, K('nbf')], writes=[nk])
                            S.op('act', lambda e: e.activation(out=Cbf[:], in_=Cst[h][:], func=AF.Copy), reads=[Ck],
                                 writes=[K('Cbf')])
                            S.op('dve', lambda e: e.tensor_copy(out=nbf[:], in_=nst[:, h, :]), reads=[nk], writes=[K('nbf')])
                            S.op('dve', lambda e: e.tensor_copy(out=mprev[:], in_=ps[bG][:, cS0 + 1:cS0 + 2]), reads=[kG],
                                 writes=[K('mprev')])
                            S.op('dve', lambda e: e.tensor_copy(out=mst[:, h:h + 1], in_=ps[bG][:, cS0 + 1:cS0 + 2]), reads=[kG],
                                 writes=[mk_])
                        else:
                            S.op('dve', lambda e: e.tensor_copy(out=mnew_all[:, h:h + 1], in_=ps[bG][:, cS0 + 1:cS0 + 2]),
                                 reads=[kG], writes=['mnew_all'])
                            S.op('dve', lambda e: e.tensor_scalar(out=dsel[:], in0=lastind[:], scalar1=col(12), scalar2=None,
                                                                  op0=ALU.mult), reads=['lastind', K('sc12')], writes=['dsel'])
                            pt, pk = ubank()
                            S.op('pe', lambda e, pt=pt: e.matmul(pt[:, 0:NSQ], lhsT=ones[:], rhs=dsel[:], start=True, stop=True),
                                 reads=['ones', 'dsel'], writes=[pk])
                            copy_op('dve', Dbc[:], pt[:, 0:NSQ], [pk], ['Dbc'])
                            S.op('dve', lambda e: e.tensor_scalar(out=kw[:], in0=k_tok[:, ti, :], scalar1=col(10), scalar2=None,
                                                                  op0=ALU.mult), reads=[K('k_tok'), K('sc10')], writes=[K('kw')])
                            def cload(q):
                                S.dma('sp', Cf[q % NCB][:], st_C[q, h].rearrange("(c p) v -> p c v", p=128),
                                      writes=['Cf%d' % (q % NCB)])
                            for q in range(min(NCB - 1, NSQ)):
                                cload(q)
                            for q in range(NSQ):
                                if q + NCB - 1 < NSQ:
                                    cload(q + NCB - 1)
                                cf, cfk = Cf[q % NCB], 'Cf%d' % (q % NCB)
                                cb, cbk = Cfb[q % NCB], 'Cfb%d' % (q % NCB)
                                qq, qqk = qTq[q % 2], 'qTq%d' % (q % 2)
                                kq, kqk = kwq[q % 2], 'kwq%d' % (q % 2)
                                S.op('dve', lambda e, q=q, qq=qq: e.tensor_tensor(
                                    out=qq[:], in0=qT[:, :, tok], in1=blkb[:, q:q + 1, :].to_broadcast([128, 2, 128]),
                                    op=ALU.mult), reads=[K('qT'), 'blkb'], writes=[qqk])
                                S.op('dve', lambda e, q=q, kq=kq: e.tensor_scalar(
                                    out=kq[:], in0=kw[:], scalar1=ind[:, q:q + 1], scalar2=None, op0=ALU.mult),
                                    reads=[K('kw'), 'ind'], writes=[kqk])
                                copy_op('act', cb[:], cf[:], [cfk], [cbk])
                                for c in range(2):
                                    S.op('pe', lambda e, c=c, q=q, cb=cb: e.matmul(
                                        ps[bB][:, :], lhsT=qq[:, c, :], rhs=cb[:, c, :],
                                        start=(q == 0 and c == 0), stop=(q == NSQ - 1 and c == 1)),
                                        reads=[qqk, cbk], writes=[kB])
                                for c in range(2):
                                    j = (q * ML_H + h) * 2 + c
                                    S.op('pe', lambda e, c=c, q=q, j=j: e.matmul(
                                        ps[bQ][:, cN0:cN0 + 1], lhsT=qq[:, c, :], rhs=nallb[:, j:j + 1],
                                        start=(q == 0 and c == 0), stop=(q == NSQ - 1 and c == 1)),
                                        reads=[qqk, 'nallb'], writes=[kQ])
                                for c in range(2):
                                    pt, pk = ubank()
                                    S.op('pe', lambda e, c=c, q=q, pt=pt: e.matmul(
                                        pt[:, :], lhsT=kq[:, c * 128:(c + 1) * 128], rhs=v_tok[:, ti, :],
                                        start=True, stop=True), reads=[kqk, K('v_tok')], writes=[pk])
                                    S.op('dve', lambda e, c=c, q=q, pt=pt, cf=cf: e.scalar_tensor_tensor(
                                        out=cf[:, c, :], in0=cf[:, c, :], scalar=Dbc[:, q:q + 1], in1=pt[:, :],
                                        op0=ALU.mult, op1=ALU.add), reads=[cfk, 'Dbc', pk, cbk], writes=[cfk])
                                S.dma('sp', o_Cs[q, h].rearrange("(c p) v -> p c v", p=128), cf[:], reads=[cfk])
                                pt, pk = ubank()
                                for c in range(2):
                                    S.op('pe', lambda e, c=c, q=q, pt=pt: e.matmul(
                                        pt[:, c:c + 1], lhsT=kq[:, c * 128:(c + 1) * 128], rhs=onesb[:, 0:1],
                                        start=True, stop=True), reads=[kqk, 'onesb'], writes=[pk])
                                j0 = (q * ML_H + h) * 2
                                S.op('dve', lambda e, q=q, pt=pt, j0=j0: e.scalar_tensor_tensor(
                                    out=nall[:, j0:j0 + 2], in0=nall[:, j0:j0 + 2], scalar=Dbc[:, q:q + 1], in1=pt[:, 0:2],
                                    op0=ALU.mult, op1=ALU.add), reads=['nall', 'Dbc', pk], writes=['nall'])
                        if is_s:
                            chk_holder[0]('sC')
                        if full:
                            S.op('dve', lambda e: e.scalar_tensor_tensor(out=col(14), in0=ps[bQ][:, cN0:cN0 + 1], scalar=col(6),
                                                                         in1=col(13), op0=ALU.mult, op1=ALU.add),
                                 reads=[kQ, K('sc6'), K('sc13')], writes=[K('sc14')])
                            S.op('act', lambda e: e.activation(out=col(14), in_=col(14), func=AF.Abs), reads=[K('sc14')],
                                 writes=[K('sc14')])
                            S.op('dve', lambda e: e.tensor_tensor(out=col(14), in0=col(14), in1=col(7), op=ALU.max),
                                 reads=[K('sc14'), K('sc7')], writes=[K('sc14')])
                            S.op('dve', lambda e: e.reciprocal(out=col(15), in_=col(14)), reads=[K('sc14')], writes=[K('sc15')])
                            S.op('dve', lambda e: e.tensor_tensor(out=col(16), in0=col(15), in1=col(6), op=ALU.mult),
                                 reads=[K('sc15'), K('sc6')], writes=[K('sc16')])
                            S.op('act', lambda e: e.activation(out=h2[:], in_=ps[bB][:, :], func=AF.Identity, scale=col(16)),
                                 reads=[kB, K('sc16')], writes=[K('h2')])
                            S.op('dve', lambda e: e.scalar_tensor_tensor(out=hbuf[:], in0=ps[bA][:, :], scalar=col(15),
                                                                         in1=h2[:], op0=ALU.mult, op1=ALU.add),
                                 reads=[kA, K('sc15'), K('h2')], writes=[K('hbuf')])
                            S.op('dve', lambda e: e.bn_stats(out=bst[:], in_=hbuf[:]), reads=[K('hbuf')], writes=[K('bst')])
                            S.op('dve', lambda e: e.bn_aggr(out=bmv[:], in_=bst[:]), reads=[K('bst')], writes=[K('bmv')])
                            S.op('act', lambda e: e.activation(out=col(17), in_=bmv[:, 1:2], func=AF.Ln, bias=LN_EPS),
                                 reads=[K('bmv')], writes=[K('sc17')])
                            S.op('act', lambda e: e.activation(out=col(17), in_=col(17), func=AF.Exp, scale=-0.5),
                                 reads=[K('sc17')], writes=[K('sc17')])
                            S.op('dve', lambda e: e.tensor_scalar(out=col(18), in0=bmv[:, 0:1], scalar1=-1.0, scalar2=col(17),
                                                                  op0=ALU.mult, op1=ALU.mult), reads=[K('bmv'), K('sc17')],
                                 writes=[K('sc18')])
                            S.op('act', lambda e: e.activation(out=h2[:], in_=hbuf[:], func=AF.Identity, scale=col(17),
                                                               bias=col(18)), reads=[K('hbuf'), K('sc17'), K('sc18')], writes=[K('h2')])
                            S.op('dve', lambda e: e.tensor_tensor(out=hbuf[:], in0=h2[:], in1=og[:, ti, :], op=ALU.mult),
                                 reads=[K('h2'), K('og')], writes=[K('hbuf')])
                            for j in range(4):
                                S.op('pe', lambda e, j=j: e.matmul(ps[bA][:, j * 128:(j + 1) * 128],
                                                                   lhsT=hbuf[:, j * 128:(j + 1) * 128], rhs=ident[:],
                                                                   start=True, stop=True), reads=[K('hbuf'), 'ident'],
                                     writes=[kA])
                            for j in range(4):
                                ch = 4 * h + j
                                if j % 2 == 0:
                                    S.op('act', lambda e, j=j, ch=ch: e.activation(
                                        out=brA[:, ch, tok], in_=ps[bA][:, j * 128:(j + 1) * 128], func=AF.Identity,
                                        scale=gcol_a[:, ch:ch + 1]), reads=[kA, 'gcol_a'], writes=[K('brA')])
                                else:
                                    S.op('dve', lambda e, j=j, ch=ch: e.tensor_scalar(
                                        out=brA[:, ch, tok], in0=ps[bA][:, j * 128:(j + 1) * 128], scalar1=gcol_a[:, ch:ch + 1],
                                        scalar2=None, op0=ALU.mult), reads=[kA, 'gcol_a'], writes=[K('brA')])
                def ml_proj(h, PB, psfx):
                    k_tok, v_tok = PB['k_tok'], PB['v_tok']
                    qT, kT, og = PB.get('qT'), PB.get('kT'), PB.get('og')
                    h2 = MX[0]['h2']
                    def tmode(colname, width, dst, dkey, post=None, scale=None):
                        for c0 in range(0, width, BW):
                            wb, wk = wget('w_in', w_in, 0, KC, off[colname] + h * width + c0, BW)
                            for ti in range(ntl):
                                pt, pk = nextps()
                                proj_T(wb, wk, BW, xT, xkey, KC, ti * 128, pt, pk)
                                if post is None:
                                    copy_op(evac_eng(), dst[:, ti, c0:c0 + BW], pt[:, 0:BW], [pk], [dkey], scale=scale)
                                else:
                                    post(dst[:, ti, c0:c0 + BW], pt[:, 0:BW], pk, dkey)

                    def fmode(colname, dst, dkey, scale=None):
                        wb, wk = wget('w_in', w_in, 0, KC, off[colname] + h * ML_DK, BW)
                        for cc in range(2):
                            for (t0, n) in ntiles(Mtok):
                                pt, pk = nextps()
                                proj_F(wb, wk, cc * 128, 128, xT, xkey, KC, t0, n, pt, pk)
                                copy_op(evac_eng(), dst[:, cc, t0:t0 + n], pt[:, 0:n], [pk], [dkey], scale=scale)

                    if full:
                        fmode('mq', qT, 'qT' + psfx)
                        fmode('mk', kT, 'kT' + psfx, scale=ML_DK ** -0.5)
                    tmode('mk', ML_DK, k_tok, 'k_tok' + psfx, scale=ML_DK ** -0.5)
                    tmode('mv', ML_DV, v_tok, 'v_tok' + psfx)
                    if full:
                        osig = sb(st, "osig%d" % h, [128, BW]) if False else None

                        def opost(dst, src, pk, dkey):
                            sigmoid_from('act', h2[:, 0:BW], src, [pk], ['h2_0'])
                            S.op('dve', lambda e: e.tensor_copy(out=dst, in_=h2[:, 0:BW]), reads=['h2_0'], writes=[dkey])
                        tmode('mo', ML_DV, og, 'og' + psfx, post=opost)


                def ml_scratch(i):
                    return {'sfx': '_%d' % i,
                            'Cbf': sb(st, "Cbf", [128, 2, ML_DV], BF16), 'nbf': sb(st, "nbf", [128, 2], BF16),
                            'sc': sb(st, "sc", [128, 24]), 'bm': sb(st, "bm", [128, 2]), 'mprev': sb(st, "mprev", [128, 1]),
                            'diagc': sb(st, "diagc", [128, 128]), 'logd': sb(st, "logd", [128, 128]),
                            'dmat': sb(st, "dmat", [128, 128]), 'sdm': sb(st, "sdm", [128, 128]),
                            'sdT': sb(st, "sdT", [128, 128], BF16), 'kw': sb(st, "kw", [128, ML_DK], BF16),
                            'hbuf': sb(st, "hbuf", [128, ML_DV]), 'h2': sb(st, "h2", [128, ML_DV]),
                            'bst': sb(st, "bst", [128, 6]), 'bmv': sb(st, "bmv", [128, 2])}
                MX = [ml_scratch(0), ml_scratch(1)]
                hbuf = MX[0]['hbuf']
                MERGED = (384, 128, 386, 0, 256, 390)
                MX[0].update(bG=4, bQ=4, bA=5, bB=6, cols=MERGED)
                MX[1].update(bG=0, bQ=0, bA=1, bB=2, cols=MERGED)
                for h0 in range(0, ML_H, 2):
                    for i in range(2):
                        ml_proj(h0 + i, PBS[i], '_%d' % i)
                        MX[i]['PB'] = PBS[i]
                    ptiles = list(range(ntile))
                    MX[0]['ub'] = [7]
                    MX[1]['ub'] = [3]
                    lists = []
                    for i in range(2):
                        S.start_record()
                        ml_rec(h0 + i, MX[i], ptiles)
                        lists.append(S.stop_record())
                    S.replay_interleaved(lists)
                    if samp:
                        MX[0]['ub'] = [7, 0, 1, 2, 3]
                        ml_rec(h0, MX[0], [ntile])
                        MX[1]['ub'] = [3, 4, 5, 6, 7]
                        ml_rec(h0 + 1, MX[1], [ntile])
                if final:
                    for h in range(ML_H):
                        S.dma('sp', o_Cp[h].rearrange("(c p) v -> p c v", p=128), Cst[h][:], reads=['Cst%d' % h])
                    S.op('pe', lambda e: e.matmul(ps[4][0:ML_H * 2, 0:128], lhsT=nst[:].rearrange("p h c -> p (h c)"),
                                                  rhs=ident[:], start=True, stop=True), reads=['nst%d' % hh for hh in range(ML_H)] + ['ident'], writes=[PK[4]])
                    copy_op('dve', hbuf[0:ML_H * 2, 0:128], ps[4][0:ML_H * 2, 0:128], [PK[4]], ['hbuf_0'])
                    S.dma('sp', o_np, hbuf[0:ML_H * 2, 0:128], reads=['hbuf_0'])
                    S.dma('sp', o_mp, mst[0:1, :], reads=['mst%d' % hh for hh in range(ML_H)])
                    if samp:
                        S.op('pe', lambda e: e.matmul(ps[4][:, 0:128], lhsT=nall[:], rhs=ident[:], start=True, stop=True),
                             reads=['nall', 'ident'], writes=[PK[4]])
                        copy_op('dve', nrow[:], ps[4][:, 0:128], [PK[4]], ['nrow'])
                        S.dma('sp', o_ns, nrow[:], reads=['nrow'])
                        S.op('pe', lambda e: e.matmul(ps[4][0:NSQ, 256:256 + ML_H], lhsT=lastind[:], rhs=mnew_all[:],
                                                      start=True, stop=True), reads=['lastind', 'mnew_all'], writes=[PK[4]])
                        copy_op('dve', mout[:], ps[4][0:NSQ, 256:256 + ML_H], [PK[4]], ['mout'])
                        S.dma('sp', o_ms, mout[:], reads=['mout'])
                S.barrier()
            chk_holder[0]('ml')

            with contextlib.ExitStack() as st:
                ro = [0]

                def rview2(n, inner=None):
                    v = R[:, ro[0]:ro[0] + n]
                    if inner is not None:
                        v = v.rearrange("p (a b) -> p a b", b=inner)
                    ro[0] += n
                    return v
                qg = [rview2(Mtok) for i in range(2)]
                kg = [rview2(Mtok) for i in range(2)]
                eG = [sb(st, "eG%d" % i, [128, Mtok]) for i in range(2)]
                i_tok = [rview2(ntl * HG_DV, HG_DV) for i in range(2)]
                if full:
                    sg = [rview2(ntl * HG_DV, HG_DV) for i in range(2)]
                fa = sb(st, "fa", [128, Mtok])
                fb = sb(st, "fb", [128, Mtok])
                qf = sb(st, "qf", [128, Mtok])
                NSB = 4
                hres = []
                for i in range(2):
                    X = {'Sbf': sb(st, "Sbf", [128, HG_DV], BF16), 'aTm': sb(st, "aTm", [128, 128], BF16),
                         'kgt': sb(st, "kgt", [128, 128], BF16), 'obuf': sb(st, "obuf", [128, HG_DV]),
                         'tmpS': sb(st, "tmpS", [128, HG_DV]), 'hc': sb(st, "hc", [128, 4]),
                         'banks': (5, 6, 7, 0) if i == 0 else (1, 2, 3, 4)}
                    if samp:
                        X['Sf'] = [sb(st, "Sf%d" % j, [128, HG_DV]) for j in range(NSB)]
                        X['Sfb'] = [sb(st, "Sfb%d" % j, [128, HG_DV], BF16) for j in range(NSB)]
                        X['qgq'] = [sb(st, "qgq%d" % j, [128, 128], BF16) for j in range(2)]
                        X['kgq'] = [sb(st, "kgq%d" % j, [128, 128], BF16) for j in range(2)]
                    hres.append(X)
                obuf = hres[0]['obuf']
                for pr in range(HG_H // 2):
                    if full:
                        wb, wk = wget('w_in', w_in, 0, KC, off['hq'] + pr * BW, BW)
                        for i in range(2):
                            for (t0, n) in ntiles(Mtok):
                                pt, pk = nextps()
                                proj_F(wb, wk, i * 128, 128, xT, xkey, KC, t0, n, pt, pk)
                                copy_op(evac_eng(), (qf if i == 0 else fb)[:, t0:t0 + n], pt[:, 0:n], [pk],
                                        ['qf' if i == 0 else 'fb'])
                    wb, wk = wget('w_in', w_in, 0, KC, off['hf'] + pr * BW, BW)
                    for i in range(2):
                        hd = 2 * pr + i
                        for (t0, n) in ntiles(Mtok):
                            pt, pk = nextps()
                            proj_F(wb, wk, i * 128, 128, xT, xkey, KC, t0, n, pt, pk)
                            sigmoid_from('act', fa[:, t0:t0 + n], pt[:, 0:n], [pk], ['fa'])
                        S.op('dve', lambda e, hd=hd: e.tensor_scalar(out=fa[:], in0=fa[:], scalar1=lbc[:, HG_H + hd:HG_H + hd + 1],
                                                                     scalar2=lbc[:, hd:hd + 1], op0=ALU.mult, op1=ALU.add),
                             reads=['fa', 'lbc'], writes=['fa'])
                        S.op('act', lambda e, i=i: e.activation(out=eG[i][:], in_=fa[:], func=AF.Ln), reads=['fa'],
                             writes=['eG%d' % i])
                        rs = rmask[:, 0:Mtok]
                        S.op('dve', lambda e, i=i: e.tensor_tensor_scan(out=eG[i][:], data0=rs, data1=eG[i][:], initial=0.0,
                                                                        op0=ALU.mult, op1=ALU.add),
                             reads=['rst', 'eG%d' % i], writes=['eG%d' % i])
                        S.op('dve', lambda e: e.tensor_scalar(out=fa[:], in0=fa[:], scalar1=-1.0, scalar2=1.0, op0=ALU.mult,
                                                              op1=ALU.add), reads=['fa'], writes=['fa'])
                        tq = sb(st, "tq%d_%d" % (pr, i), [1, 1]) if False else None
                        S.op('act', lambda e, i=i: e.activation(out=kg[i][:], in_=eG[i][:], func=AF.Exp, scale=-1.0),
                             reads=['eG%d' % i], writes=['kg%d' % i])
                        S.op('dve', lambda e, i=i: e.tensor_tensor(out=kg[i][:], in0=kg[i][:], in1=fa[:], op=ALU.mult),
                             reads=['kg%d' % i, 'fa'], writes=['kg%d' % i])
                        S.op('act', lambda e, i=i: e.activation(out=eG[i][:], in_=eG[i][:], func=AF.Exp),
                             reads=['eG%d' % i], writes=['eG%d' % i])
                        if full:
                            qsrc, qk_ = (qf, 'qf') if i == 0 else (fb, 'fb')
                            S.op('dve', lambda e, i=i, qsrc=qsrc: e.tensor_tensor(out=qg[i][:], in0=qsrc[:], in1=eG[i][:],
                                                                                  op=ALU.mult),
                                 reads=[qk_, 'eG%d' % i], writes=['qg%d' % i])
                    for i in range(2):
                        hd = 2 * pr + i
                        wb, wk = wget('w_in', w_in, 0, KC, off['hi'] + hd * HG_DV, BW)
                        for ti in range(ntl):
                            pt, pk = nextps()
                            proj_T(wb, wk, BW, xT, xkey, KC, ti * 128, pt, pk)
                            copy_op(evac_eng(), i_tok[i][:, ti, :], pt[:, 0:BW], [pk], ['i_tok%d' % i])
                    if full:
                        for i in range(2):
                            hd = 2 * pr + i
                            wb, wk = wget('w_in', w_in, 0, KC, off['hg'] + hd * HG_DV, BW)
                            for ti in range(ntl):
                                pt, pk = nextps()
                                proj_T(wb, wk, BW, xT, xkey, KC, ti * 128, pt, pk)
                                sigmoid_from('act', obuf[:], pt[:, 0:BW], [pk], ['obuf'])
                                S.op('dve', lambda e, i=i, ti=ti, pt=pt: e.tensor_tensor(out=sg[i][:, ti, :], in0=pt[:, 0:BW],
                                                                                         in1=obuf[:], op=ALU.mult),
                                     reads=[pk, 'obuf'], writes=['sg%d' % i])
                    def hg_chain(i, hd):
                        X = hres[i]
                        Sbf, aTm, kgt, obuf, tmpS, hc = X['Sbf'], X['aTm'], X['kgt'], X['obuf'], X['tmpS'], X['hc']
                        bA, bO, bT, bS = X['banks']
                        kA, kO, kT, kS = PK[bA], PK[bO], PK[bT], PK[bS]
                        sfx = '_%d' % i
                        Sk = 'Sst%d' % hd
                        qgk, kgk, eGk, itk = 'qg%d' % i, 'kg%d' % i, 'eG%d' % i, 'i_tok%d' % i
                        copy_op('act', Sbf[:], Sst[hd][:], [Sk], ['Sbf' + sfx])
                        for ti in range(ntl):
                            is_s = samp and ti == ntl - 1
                            tok = slice(ti * 128, (ti + 1) * 128)
                            mm = mst_b if is_s else mst_c
                            mmk = 'mst_b' if is_s else 'mst_c'
                            if full:
                                S.op('pe', lambda e: e.matmul(ps[bA][:, 0:128], lhsT=kg[i][:, tok], rhs=qg[i][:, tok],
                                                              start=True, stop=True), reads=[kgk, qgk], writes=[kA])
                                S.op('dve', lambda e: e.tensor_tensor(out=aTm[:], in0=ps[bA][:, 0:128], in1=mm[:], op=ALU.mult),
                                     reads=[kA, mmk], writes=['aTm' + sfx])
                                S.op('pe', lambda e: e.matmul(ps[bO][:, 0:HG_DV], lhsT=aTm[:], rhs=i_tok[i][:, ti, :],
                                                              start=True, stop=False), reads=['aTm' + sfx, itk], writes=[kO])
                            S.op('pe', lambda e: e.matmul(ps[bA][:, 128:256], lhsT=kg[i][:, tok], rhs=identb[:],
                                                          start=True, stop=True), reads=[kgk, 'identb'], writes=[kA])
                            copy_op('act', kgt[:], ps[bA][:, 128:256], [kA], ['kgt' + sfx])
                            if not is_s:
                                if full:
                                    S.op('pe', lambda e: e.matmul(ps[bO][:, 0:HG_DV], lhsT=qg[i][:, tok], rhs=Sbf[:],
                                                                  start=False, stop=True), reads=[qgk, 'Sbf' + sfx], writes=[kO])
                                S.op('pe', lambda e: e.matmul(ps[bS][:, 0:HG_DV], lhsT=kgt[:], rhs=i_tok[i][:, ti, :],
                                                              start=True, stop=True), reads=['kgt' + sfx, itk], writes=[kS])
                                S.op('dve', lambda e: e.tensor_tensor(out=tmpS[:], in0=Sst[hd][:], in1=ps[bS][:, 0:HG_DV],
                                                                      op=ALU.add), reads=[Sk, kS], writes=['tmpS' + sfx])
                                ecol = eG[i][:, ti * 128 + 127:ti * 128 + 128]
                                S.op('act', lambda e: e.activation(out=Sst[hd][:], in_=tmpS[:], func=AF.Identity, scale=ecol),
                                     reads=['tmpS' + sfx, eGk, 'Sbf' + sfx], writes=[Sk])
                                copy_op('dve', Sbf[:], Sst[hd][:], [Sk], ['Sbf' + sfx])
                            else:
                                Sf, Sfb, qgq, kgq = X['Sf'], X['Sfb'], X['qgq'], X['kgq']

                                def sload(q):
                                    S.dma('sp', Sf[q % NSB][:], st_S[q, hd], writes=['Sf%d%s' % (q % NSB, sfx)])
                                for q in range(min(NSB - 1, NSQ)):
                                    sload(q)
                                for q in range(NSQ):
                                    if q + NSB - 1 < NSQ:
                                        sload(q + NSB - 1)
                                    sf, sfk = Sf[q % NSB], 'Sf%d%s' % (q % NSB, sfx)
                                    sfb, sfbk = Sfb[q % NSB], 'Sfb%d%s' % (q % NSB, sfx)
                                    qq, qqk = qgq[q % 2], 'qgq%d%s' % (q % 2, sfx)
                                    kq, kqk = kgq[q % 2], 'kgq%d%s' % (q % 2, sfx)
                                    if full:
                                        S.op('dve', lambda e: e.tensor_tensor(out=qq[:], in0=qg[i][:, tok], in1=blkb[:, q, :],
                                                                              op=ALU.mult), reads=[qgk, 'blkb'], writes=[qqk])
                                    S.op('dve', lambda e: e.tensor_scalar(out=kq[:], in0=kgt[:], scalar1=ind[:, q:q + 1],
                                                                          scalar2=None, op0=ALU.mult),
                                         reads=['kgt' + sfx, 'ind'], writes=[kqk])
                                    if full:
                                        copy_op('act', sfb[:], sf[:], [sfk], [sfbk])
                                        S.op('pe', lambda e: e.matmul(ps[bO][:, 0:HG_DV], lhsT=qq[:], rhs=sfb[:],
                                                                      start=False, stop=(q == NSQ - 1)),
                                             reads=[qqk, sfbk], writes=[kO])
                                    S.op('pe', lambda e: e.matmul(ps[bS][:, 0:HG_DV], lhsT=kq[:], rhs=i_tok[i][:, ti, :],
                                                                  start=True, stop=True), reads=[kqk, itk], writes=[kS])
                                    S.op('dve', lambda e: e.tensor_tensor(out=tmpS[:], in0=sf[:], in1=ps[bS][:, 0:HG_DV],
                                                                          op=ALU.add), reads=[sfk, kS], writes=['tmpS' + sfx])
                                    ecol = eG[i][:, ti * 128 + q * SQL + SQL - 1:ti * 128 + q * SQL + SQL]
                                    S.op('act', lambda e: e.activation(out=sf[:], in_=tmpS[:], func=AF.Identity, scale=ecol),
                                         reads=['tmpS' + sfx, eGk, sfbk], writes=[sfk])
                                    S.dma('sp', o_Ss[q, hd], sf[:], reads=[sfk])
                            if full:
                                S.op('act', lambda e: e.activation(out=obuf[:], in_=ps[bO][:, 0:HG_DV], func=AF.Square,
                                                                   accum_out=hc[:, 0:1]), reads=[kO],
                                     writes=['obuf' + sfx, 'hc0' + sfx])
                                S.op('act', lambda e: e.activation(out=hc[:, 1:2], in_=hc[:, 0:1], func=AF.Ln, scale=1.0 / HG_DV,
                                                                   bias=LN_EPS), reads=['hc0' + sfx], writes=['hc1' + sfx])
                                S.op('act', lambda e: e.activation(out=hc[:, 1:2], in_=hc[:, 1:2], func=AF.Exp, scale=-0.5),
                                     reads=['hc1' + sfx], writes=['hc1' + sfx])
                                S.op('dve', lambda e: e.scalar_tensor_tensor(
                                    out=obuf[:], in0=ps[bO][:, 0:HG_DV], scalar=hc[:, 1:2], in1=sg[i][:, ti, :],
                                    op0=ALU.mult, op1=ALU.mult), reads=[kO, 'hc1' + sfx, 'sg%d' % i, 'obuf' + sfx],
                                    writes=['obuf' + sfx])
                                for j in range(2):
                                    S.op('pe', lambda e: e.matmul(ps[bT][:, j * 128:(j + 1) * 128],
                                                                  lhsT=obuf[:, j * 128:(j + 1) * 128], rhs=ident[:],
                                                                  start=True, stop=True), reads=['obuf' + sfx, 'ident'],
                                         writes=[kT])
                                for j in range(2):
                                    ch = 2 * hd + j
                                    if j == 0:
                                        S.op('act', lambda e: e.activation(
                                            out=brB[:, ch, tok], in_=ps[bT][:, j * 128:(j + 1) * 128], func=AF.Identity,
                                            scale=gcol_b[:, ch:ch + 1]), reads=[kT, 'gcol_b'], writes=['brB' + sfx])
                                    else:
                                        S.op('dve', lambda e: e.tensor_scalar(
                                            out=brB[:, ch, tok], in0=ps[bT][:, j * 128:(j + 1) * 128],
                                            scalar1=gcol_b[:, ch:ch + 1], scalar2=None, op0=ALU.mult),
                                            reads=[kT, 'gcol_b'], writes=['brB' + sfx])

                    lists = []
                    for i in range(2):
                        S.start_record()
                        hg_chain(i, 2 * pr + i)
                        lists.append(S.stop_record())
                    S.replay_interleaved(lists)
                if final:
                    for hd in range(HG_H):
                        S.dma('sp', o_Sp[hd], Sst[hd][:], reads=['Sst%d' % hd])
                S.barrier()


        R = sb(es, "R", [128, cfg.RSZ], BF16)
        NMV, NHV = cfg.MLV // 128, cfg.HGV // 128
        DFF, FG, NFG = cfg.DFF, cfg.FG, cfg.NFG

        def layernorm(zt, zk, gt, bt, st, lst, lmv, lc):
            nchk = (D + 511) // 512
            for j in range(nchk):
                a, b_ = j * 512, min(D, (j + 1) * 512)
                S.op('dve', lambda e, j=j, a=a, b_=b_: e.bn_stats(out=lst[:, j, :], in_=zt[:, a:b_]), reads=[zk], writes=['lst'])
            S.op('dve', lambda e: e.bn_aggr(out=lmv[:], in_=lst[:].rearrange("p a b -> p (a b)")), reads=['lst'], writes=['lmv'])
            S.op('act', lambda e: e.activation(out=lc[:, 0:1], in_=lmv[:, 1:2], func=AF.Ln, bias=LN_EPS), reads=['lmv'],
                 writes=['lc0'])
            S.op('act', lambda e: e.activation(out=lc[:, 0:1], in_=lc[:, 0:1], func=AF.Exp, scale=-0.5), reads=['lc0'],
                 writes=['lc0'])
            S.op('dve', lambda e: e.tensor_scalar(out=lc[:, 1:2], in0=lmv[:, 0:1], scalar1=-1.0, scalar2=lc[:, 0:1],
                                                  op0=ALU.mult, op1=ALU.mult), reads=['lmv', 'lc0'], writes=['lc1'])
            S.op('act', lambda e: e.activation(out=zt, in_=zt, func=AF.Identity, scale=lc[:, 0:1], bias=lc[:, 1:2]),
                 reads=[zk, 'lc0', 'lc1'], writes=[zk])
            S.op('dve', lambda e: e.tensor_tensor(out=zt, in0=zt, in1=gt[:], op=ALU.mult), reads=[zk, 'gb'], writes=[zk])
            S.op('pool', lambda e: e.tensor_tensor(out=zt, in0=zt, in1=bt[:], op=ALU.add), reads=[zk, 'bb'], writes=[zk])

        def run_group(x_ap, rows, nprompt, full, samp, final, rmask=None):
            ntl = len(rows)
            Mt = ntl * 128
            with contextlib.ExitStack() as stG:
                xT = sb(stG, "xT", [128, KC, Mt], BF16)
                brA = brB = None
                if full:
                    brA = sb(stG, "brA", [128, NMV, Mt], BF16)
                    brB = sb(stG, "brB", [128, NHV, Mt], BF16)
                load_xT(x_ap, rows, xT, 'xT')
                chk('ldx')
                mixer_pass(xT, 'xT', nprompt, full, brA, brB, samp, R, final, rmask)
                if full:
                    chk('mix')
                if not full:
                    S.barrier()
                    return
                mrg = R[:, 0:KC * Mt].rearrange("p (a b) -> p a b", b=Mt)
                with contextlib.ExitStack() as st:
                    sga = sb(st, "sga", [128, 2, Mt])
                    sgb = sb(st, "sgb", [128, 2, Mt])
                    m1 = sb(st, "m1", [128, 2, Mt])
                    for d0 in range(0, D, BW):
                        nsub = min(BW, D - d0) // 128
                        for (gname, dst, dk) in (('ga', sga, 'sga'), ('gb', sgb, 'sgb')):
                            wb, wk = wget('w_in', w_in, 0, KC, off[gname] + d0, nsub * 128)
                            for i in range(nsub):
                                for (t0, n) in ntiles(Mt):
                                    pt, pk = nextps()
                                    proj_F(wb, wk, i * 128, 128, xT, 'xT', KC, t0, n, pt, pk)
                                    sigmoid_from('act', dst[:, i, t0:t0 + n], pt[:, 0:n], [pk], [dk])
                        wb, wk = wget('w_ba', w_ba, 0, NMV, d0, nsub * 128)
                        for i in range(nsub):
                            for (t0, n) in ntiles(Mt):
                                pt, pk = nextps()
                                proj_F(wb, wk, i * 128, 128, brA, 'brA', NMV, t0, n, pt, pk)
                                S.op('dve', lambda e, i=i, t0=t0, n=n, pt=pt: e.tensor_tensor(
                                    out=m1[:, i, t0:t0 + n], in0=pt[:, 0:n], in1=sga[:, i, t0:t0 + n], op=ALU.mult),
                                    reads=[pk, 'sga'], writes=['m1'])
                        wb, wk = wget('w_bb', w_bb, 0, NHV, d0, nsub * 128)
                        for i in range(nsub):
                            for (t0, n) in ntiles(Mt):
                                pt, pk = nextps()
                                proj_F(wb, wk, i * 128, 128, brB, 'brB', NHV, t0, n, pt, pk)
                                S.op('dve', lambda e, i=i, t0=t0, n=n, pt=pt: e.tensor_tensor(
                                    out=sgb[:, i, t0:t0 + n], in0=pt[:, 0:n], in1=sgb[:, i, t0:t0 + n], op=ALU.mult),
                                    reads=[pk, 'sgb'], writes=['sgb'])
                                S.op('pool', lambda e, i=i, t0=t0, n=n, d0=d0: e.tensor_tensor(
                                    out=mrg[:, d0 // 128 + i, t0:t0 + n], in0=sgb[:, i, t0:t0 + n], in1=m1[:, i, t0:t0 + n],
                                    op=ALU.add), reads=['sgb', 'm1'], writes=['mrg'])
                    S.barrier()
            S.barrier()
            chk('merge')
            with contextlib.ExitStack() as st:
                z = sb(st, "z", [128, ntl, D])
                gt = sb(st, "gt", [128, D])
                bt = sb(st, "bt", [128, D])
                hid = sb(st, "hid", [128, FG, Mt], BF16)
                rtmp = sb(st, "rtmp", [128, 512])
                lst = sb(st, "lst", [128, (D + 511) // 512, 6])
                lmv = sb(st, "lmv", [128, 2])
                lc = sb(st, "lc", [128, 2])
                zk = lambda ti: 'z%d' % ti
                for ti in range(ntl):
                    S.dma('sp', z[:, ti, :], x_ap[rows[ti]:rows[ti] + 128, :], writes=[zk(ti)])
                S.dma('sp', gt[:], ln1_g.partition_broadcast(128), writes=['gb'])
                S.dma('sp', bt[:], ln1_b.partition_broadcast(128), writes=['bb'])
                for d0 in range(0, D, BW):
                    wb, wk = wget('w_out', w_out, 0, KC, d0, BW)
                    for ti in range(ntl):
                        pt, pk = nextps()
                        proj_T(wb, wk, BW, mrg, 'mrg', KC, ti * 128, pt, pk)
                        S.op('dve', lambda e, ti=ti, d0=d0, pt=pt: e.scalar_tensor_tensor(
                            out=z[:, ti, d0:d0 + BW], in0=z[:, ti, d0:d0 + BW], scalar=ALPHA, in1=pt[:, 0:BW],
                            op0=ALU.mult, op1=ALU.add), reads=[zk(ti), pk], writes=[zk(ti)])
                for ti in range(ntl):
                    layernorm(z[:, ti, :], zk(ti), gt, bt, st, lst, lmv, lc)
                S.barrier()
                x1T = mrg
                zb = [sb(st, "zb%d" % i, [128, D], BF16) for i in range(2)]
                for ti in range(ntl):
                    zbt, zbk = zb[ti % 2], 'zb%d' % (ti % 2)
                    copy_op('act' if ti % 2 else 'dve', zbt[:], z[:, ti, :], [zk(ti)], [zbk])
                    for g in range(0, KC, 4):
                        n = min(4, KC - g)
                        pt, pk = nextps()
                        for j in range(n):
                            S.op('pe', lambda e, j=j, g=g, ti=ti, pt=pt: e.matmul(
                                pt[:, j * 128:(j + 1) * 128], lhsT=zbt[:, (g + j) * 128:(g + j + 1) * 128], rhs=identb[:],
                                start=True, stop=True), reads=[zbk, 'identb'], writes=[pk])
                        copy_op(evac_eng(), x1T[:, g:g + n, ti * 128:(ti + 1) * 128],
                                pt[:, 0:n * 128].rearrange("p (a b) -> p a b", b=128), [pk], ['x1T'])
                S.dma('sp', gt[:], ln2_g.partition_broadcast(128), writes=['gb'])
                S.dma('sp', bt[:], ln2_b.partition_broadcast(128), writes=['bb'])
                for fg in range(NFG):
                    for f0 in range(0, FG * 128, BW):
                        wb, wk = wget('w_up', w_up, 0, KC, fg * FG * 128 + f0, BW)
                        for i in range(BW // 128):
                            for (t0, n) in ntiles(Mt):
                                pt, pk = nextps()
                                proj_F(wb, wk, i * 128, 128, x1T, 'x1T', KC, t0, n, pt, pk)
                                S.op('act', lambda e, n=n, pt=pt: e.activation(out=rtmp[:, 0:n], in_=pt[:, 0:n], func=AF.Relu),
                                     reads=[pk], writes=['rtmp'])
                                S.op('dve', lambda e, i=i, f0=f0, t0=t0, n=n: e.tensor_tensor(
                                    out=hid[:, f0 // 128 + i, t0:t0 + n], in0=rtmp[:, 0:n], in1=rtmp[:, 0:n], op=ALU.mult),
                                    reads=['rtmp'], writes=['hid'])
                    for d0 in range(0, D, BW):
                        wb, wk = wget('w_dn', w_dn, fg * FG * 128, FG, d0, BW)
                        for ti in range(ntl):
                            pt, pk = nextps()
                            proj_T(wb, wk, BW, hid, 'hid', FG, ti * 128, pt, pk)
                            if fg == 0:
                                S.op('dve', lambda e, ti=ti, d0=d0, pt=pt: e.scalar_tensor_tensor(
                                    out=z[:, ti, d0:d0 + BW], in0=z[:, ti, d0:d0 + BW], scalar=ALPHA, in1=pt[:, 0:BW],
                                    op0=ALU.mult, op1=ALU.add), reads=[zk(ti), pk], writes=[zk(ti)])
                            else:
                                S.op('dve', lambda e, ti=ti, d0=d0, pt=pt: e.tensor_tensor(
                                    out=z[:, ti, d0:d0 + BW], in0=z[:, ti, d0:d0 + BW], in1=pt[:, 0:BW], op=ALU.add),
                                    reads=[zk(ti), pk], writes=[zk(ti)])
                for ti in range(ntl):
                    layernorm(z[:, ti, :], zk(ti), gt, bt, st, lst, lmv, lc)
                    S.dma('sp', y_main[rows[ti]:rows[ti] + 128, :], z[:, ti, :], reads=[zk(ti)])
                S.barrier()

        chkcnt = {}

        def chk(tag):
            chkcnt[tag] = chkcnt.get(tag, 0) + 1
            if DBG['stop'] == tag or DBG['stop'] == '%s#%d' % (tag, chkcnt[tag]):
                S.dead = True

        chk_holder[0] = chk

        def _drive():
            chk('consts')
            with contextlib.ExitStack() as stp:
                rstp = sb(stp, "rstp", [128, NTP * 128])
                P(lambda e: e.memset(rstp[:], 1.0), ['rstp'])
                P(lambda e: e.memset(rstp[:].rearrange("p (a b) -> p a b", b=128)[:, :, 0:1], 0.0), ['rstp'], ['rstp'])
                S.barrier()
                run_group(x_pre, [t * 128 for t in range(NTP)], NTP, False, False, False, rmask=rstp)
            chk('pre')
            groups = list(range(0, NTP, GT))
            for gi, g0 in enumerate(groups):
                last = gi == len(groups) - 1
                rows = [t * 128 for t in range(g0, min(NTP, g0 + GT))]
                npr = len(rows)
                if last:
                    rows = rows + [NTP * 128]
                run_group(x_main, rows, npr, True, last, last)
                chk('g%d' % gi)
        try:
            _drive()
        except _Stop:
            pass
        S.finish()
    return nc, S


_CACHE = {}


def _get_program(cfg_key):
    if cfg_key not in _CACHE:
        cfg = Cfg(*cfg_key)
        rec = []
        build(cfg, wplan=None, record=rec)
        nc, S = build(cfg, wplan=rec, record=None)
        _CACHE[cfg_key] = (cfg, nc)
    return _CACHE[cfg_key]


def run_module(inputs, D, DFF, SEQ, BATCH, DEC_BATCH, core_ids=None):
    TH = SEQ // 2
    cfg, nc = _get_program((D, DFF, TH))
    ncores = 2 * BATCH
    assert DEC_BATCH == ncores * NSQ
    f = lambda a: np.ascontiguousarray(np.asarray(a, dtype=np.float32))
    xp, xs = f(inputs["x_prompt"]), f(inputs["x_sample"])
    stC, stn = f(inputs["state_mlstm_C"])[0], f(inputs["state_mlstm_n"])[0]
    stm, stS = f(inputs["state_mlstm_m"])[0], f(inputs["state_hgrn_S"])[0]
    shared = {
        "lb_logits": f(inputs["hg_lb_logits"]).reshape(2 * HG_H, 128),
        "w_in": f(inputs["w_in"])[0], "b_ig": f(inputs["b_ig"]).reshape(1, ML_H), "b_fg": f(inputs["b_fg"]).reshape(1, ML_H),
        "ml_g": f(inputs["ml_norm_g"]).reshape(-1, 128), "hg_g": f(inputs["hg_norm_g"]).reshape(-1, 128),
        "w_ba": f(inputs["w_branch_a"])[0], "w_bb": f(inputs["w_branch_b"])[0], "w_out": f(inputs["w_out"])[0],
        "ln1_g": f(inputs["ln1_g"]).reshape(1, D), "ln1_b": f(inputs["ln1_b"]).reshape(1, D),
        "w_up": f(inputs["w_up"])[0], "w_dn": f(inputs["w_down"])[0],
        "ln2_g": f(inputs["ln2_g"]).reshape(1, D), "ln2_b": f(inputs["ln2_b"]).reshape(1, D),
    }
    in_maps = []
    for c in range(ncores):
        b, half = c // 2, c % 2
        sl = slice(c * NSQ, (c + 1) * NSQ)
        m = dict(shared)
        m["x_pre"] = np.ascontiguousarray(xp[b, 0:TH]) if half == 1 else np.zeros((TH, D), np.float32)
        m["x_main"] = np.ascontiguousarray(np.concatenate([xp[b, half * TH:(half + 1) * TH], xs[sl].reshape(NSQ * SQL, D)], 0))
        m["st_C"] = np.ascontiguousarray(stC[sl])
        m["st_n"] = np.ascontiguousarray(stn[sl].reshape(NSQ * ML_H * 2, 128))
        m["st_m"] = np.ascontiguousarray(stm[sl])
        m["st_S"] = np.ascontiguousarray(stS[sl])
        in_maps.append(m)
    res = run_bass_kernel_spmd(nc, in_maps, core_ids=list(range(ncores)) if core_ids is None else core_ids)
    rs = res.results
    y_p = np.zeros((BATCH, SEQ, D), np.float32)
    y_s = np.zeros((DEC_BATCH, SQL, D), np.float32)
    Cp = np.zeros((1, BATCH, ML_H, ML_DK, ML_DV), np.float32)
    n_p = np.zeros((1, BATCH, ML_H, ML_DK), np.float32)
    mp = np.zeros((1, BATCH, ML_H), np.float32)
    Sp = np.zeros((1, BATCH, HG_H, HG_DK, HG_DV), np.float32)
    Cs = np.zeros((1, DEC_BATCH, ML_H, ML_DK, ML_DV), np.float32)
    ns = np.zeros((1, DEC_BATCH, ML_H, ML_DK), np.float32)
    ms = np.zeros((1, DEC_BATCH, ML_H), np.float32)
    Ss = np.zeros((1, DEC_BATCH, HG_H, HG_DK, HG_DV), np.float32)
    for c in range(ncores):
        b, half = c // 2, c % 2
        r = rs[c]
        sl = slice(c * NSQ, (c + 1) * NSQ)
        y_p[b, half * TH:(half + 1) * TH] = r["y_main"][0:TH]
        y_s[sl] = r["y_main"][TH:].reshape(NSQ, SQL, D)
        if half == 1:
            Cp[0, b] = r["o_Cp"]
            n_p[0, b] = r["o_np"].reshape(ML_H, ML_DK)
            mp[0, b] = r["o_mp"].reshape(ML_H)
            Sp[0, b] = r["o_Sp"]
        Cs[0, sl] = r["o_Cs"]
        ns[0, sl] = r["o_ns"].reshape(NSQ, ML_H, ML_DK)
        ms[0, sl] = r["o_ms"]
        Ss[0, sl] = r["o_Ss"]
    return (y_p, y_s, Cp, n_p, mp, Sp, Cs, ns, ms, Ss)


def kernel(**inputs):
    return run_module(inputs, D=2048, DFF=8192, SEQ=2048, BATCH=4, DEC_BATCH=128)
```

```python
import contextlib
import numpy as np
import concourse.bass as bass
import concourse.mybir as mybir
from concourse.alu_op_type import AluOpType as ALU
from concourse.bass_utils import run_bass_kernel_spmd

F32 = mybir.dt.float32
BF16 = mybir.dt.bfloat16
AF = mybir.ActivationFunctionType
AX = mybir.AxisListType

ML_H, ML_DK, ML_DV = 4, 256, 512
HG_H, HG_DK, HG_DV = 8, 128, 256
LN_EPS = 1e-5
ALPHA = 2.0 ** 0.25
NSQ = 16
SQL = 8
BW = 256
NEG = -60000.0


class _Proxy:
    def __init__(self):
        self.call = None

    def __getattr__(self, name):
        def f(*a, **k):
            self.call = (name, a, k)
            return self
        return f


class Sch:
    def __init__(self, nc, ndma=6):
        self.nc = nc
        self.E = {'pe': nc.tensor, 'act': nc.scalar, 'dve': nc.vector, 'pool': nc.gpsimd, 'sp': nc.sync}
        self.semh, self.cnt = {}, {}
        for k in self.E:
            self.semh[k] = nc.alloc_semaphore("c_" + k)
            self.cnt[k] = 0
        self.waited = {k: {} for k in self.E}
        self.dq = {}
        for q in ('sp', 'pool'):
            sems = []
            for j in range(ndma):
                nm = "d_%s%d" % (q, j)
                self.semh[nm] = nc.alloc_semaphore(nm)
                self.cnt[nm] = 0
                sems.append(nm)
            self.dq[q] = [sems, 0]
        self.track = {}
        self.n_inst = 0
        self.dead = False
        self.rec = None

    def _need(self, eng, reads, writes):
        need = {}

        def add(tok):
            if tok is None:
                return
            s, v = tok
            if eng == 'pe' and s == 'pe':
                return
            if self.waited[eng].get(s, 0) < v:
                need[s] = max(need.get(s, 0), v)
        for k in reads:
            t = self.track.get(k)
            if t:
                add(t[0])
        for k in writes:
            t = self.track.get(k)
            if t:
                add(t[0])
                for r in t[1]:
                    add(r)
        for s, v in need.items():
            self.E[eng].wait_ge(self.semh[s], v)
            self.waited[eng][s] = v

    def _upd(self, tok, reads, writes):
        for k in reads:
            t = self.track.setdefault(k, [None, []])
            t[1].append(tok)
            if len(t[1]) > 64:
                t[1] = t[1][-64:] if False else self._compact(t[1])
        for k in writes:
            self.track[k] = [tok, []]

    @staticmethod
    def _compact(lst):
        best = {}
        for s, v in lst:
            best[s] = max(best.get(s, 0), v)
        return list(best.items())

    def start_record(self):
        assert self.rec is None
        self.rec = []

    def stop_record(self):
        r, self.rec = self.rec, None
        return r

    def replay_interleaved(self, lists):
        its = [iter(l) for l in lists]
        live = list(range(len(its)))
        while live:
            for i in list(live):
                item = next(its[i], None)
                if item is None:
                    live.remove(i)
                    continue
                if item[0] == 'op':
                    _, eng, (name, a, k), reads, writes = item
                    self.op(eng, lambda e, name=name, a=a, k=k: getattr(e, name)(*a, **k), reads, writes)
                else:
                    _, q, out, in_, reads, writes, kw = item
                    self.dma(q, out, in_, reads, writes, **kw)

    def op(self, eng, fn, reads=(), writes=()):
        if self.dead:
            return
        if self.rec is not None:
            p = _Proxy()
            fn(p)
            assert p.call is not None
            self.rec.append(('op', eng, p.call, tuple(reads), tuple(writes)))
            return
        pr = [k for k in reads if k.startswith('ps') and k not in writes]
        if pr:
            writes = list(writes) + pr
        self._need(eng, reads, writes)
        inst = fn(self.E[eng])
        self.cnt[eng] += 1
        inst.then_inc(self.semh[eng], 1)
        self._upd((eng, self.cnt[eng]), reads, writes)
        self.n_inst += 1

    def dma(self, q, out, in_, reads=(), writes=(), **kw):
        if self.dead:
            return
        if self.rec is not None:
            self.rec.append(('dma', q, out, in_, tuple(reads), tuple(writes), kw))
            return
        sems, idx = self.dq[q]
        s = sems[idx % len(sems)]
        self.dq[q][1] = idx + 1
        if self.cnt[s] > 0 and self.waited[q].get(s, 0) < self.cnt[s]:
            self.E[q].wait_ge(self.semh[s], self.cnt[s])
            self.waited[q][s] = self.cnt[s]
        self._need(q, reads, writes)
        inst = self.E[q].dma_start(out=out, in_=in_, **kw)
        self.cnt[s] += 16
        inst.then_inc(self.semh[s], 16)
        self._upd((s, self.cnt[s]), reads, writes)
        self.n_inst += 1

    def barrier(self):
        if self.dead:
            return
        assert self.rec is None
        for eng in self.E:
            for s, v in self.cnt.items():
                if v > 0 and s != eng and self.waited[eng].get(s, 0) < v:
                    self.E[eng].wait_ge(self.semh[s], v)
                    self.waited[eng][s] = v
        self.track = {}

    def finish(self):
        self.dead = False
        self.barrier()


class Cfg:
    def __init__(self, D, DFF, TH):
        self.D, self.DFF, self.TH = D, DFF, TH
        self.KC = D // 128
        self.NTP = TH // 128
        self.NT = self.NTP + 1
        self.M = self.NT * 128
        self.GT = max(1, self.NTP // 2)
        self.MG = (self.GT + 1) * 128
        self.RSZ = max(self.KC * self.MG, 2 * (4 * self.MG + 1280 * (self.GT + 1)))
        self.FG = min(16, DFF // 128)
        self.NFG = DFF // (128 * self.FG)
        self.MLQK, self.MLV = ML_H * ML_DK, ML_H * ML_DV
        self.HGK, self.HGV = HG_H * HG_DK, HG_H * HG_DV
        o = 0
        self.off = {}
        for nm, sz in (('mq', self.MLQK), ('mk', self.MLQK), ('mv', self.MLV), ('mi', ML_H), ('mf', ML_H),
                       ('mo', self.MLV), ('hq', self.HGK), ('hf', self.HGK), ('hi', self.HGV), ('hg', self.HGV),
                       ('ga', D), ('gb', D)):
            self.off[nm] = o
            o += sz
        self.DIN = o


DBG = {'stop': None}


class _Stop(Exception):
    pass


def build(cfg, wplan=None, record=None):
    D, KC, TH, NTP, NT, M = cfg.D, cfg.KC, cfg.TH, cfg.NTP, cfg.NT, cfg.M
    off = cfg.off
    nc = bass.Bass("TRN2", target_bir_lowering=False)

    def din(name, shape):
        return nc.dram_tensor(name, list(shape), F32, kind="ExternalInput").ap()

    def dout(name, shape):
        return nc.dram_tensor(name, list(shape), F32, kind="ExternalOutput").ap()

    x_pre = din("x_pre", [TH, D])
    x_main = din("x_main", [M, D])
    st_C = din("st_C", [NSQ, ML_H, ML_DK, ML_DV])
    st_n = din("st_n", [NSQ * ML_H * 2, 128])
    st_m = din("st_m", [NSQ, ML_H])
    st_S = din("st_S", [NSQ, HG_H, HG_DK, HG_DV])
    lb_logits = din("lb_logits", [2 * HG_H, 128])
    w_in = din("w_in", [D, cfg.DIN])
    b_ig = din("b_ig", [1, ML_H])
    b_fg = din("b_fg", [1, ML_H])
    ml_g = din("ml_g", [cfg.MLV // 128, 128])
    hg_g = din("hg_g", [cfg.HGV // 128, 128])
    w_ba = din("w_ba", [cfg.MLV, D])
    w_bb = din("w_bb", [cfg.HGV, D])
    w_out = din("w_out", [D, D])
    ln1_g = din("ln1_g", [1, D])
    ln1_b = din("ln1_b", [1, D])
    w_up = din("w_up", [D, cfg.DFF])
    w_dn = din("w_dn", [cfg.DFF, D])
    ln2_g = din("ln2_g", [1, D])
    ln2_b = din("ln2_b", [1, D])

    y_main = dout("y_main", [M, D])
    o_Cp = dout("o_Cp", [ML_H, ML_DK, ML_DV])
    o_np = dout("o_np", [ML_H * 2, 128])
    o_mp = dout("o_mp", [1, ML_H])
    o_Sp = dout("o_Sp", [HG_H, HG_DK, HG_DV])
    o_Cs = dout("o_Cs", [NSQ, ML_H, ML_DK, ML_DV])
    o_ns = dout("o_ns", [NSQ * ML_H * 2, 128])
    o_ms = dout("o_ms", [NSQ, ML_H])
    o_Ss = dout("o_Ss", [NSQ, HG_H, HG_DK, HG_DV])

    S = Sch(nc)
    es = contextlib.ExitStack()

    uid = [0]

    def sb(stack, name, shape, dt=F32):
        uid[0] += 1
        return stack.enter_context(nc.sbuf_tensor("%s_%d" % (name, uid[0]), list(shape), dt))

    with es:
        ps = [es.enter_context(nc.psum_tensor("ps%d" % i, [128, 512], F32)) for i in range(8)]
        PK = ["ps%d" % i for i in range(8)]

        ident = sb(es, "ident", [128, 128])
        identb = sb(es, "identb", [128, 128], BF16)
        ones = sb(es, "ones", [128, 128])
        onesb = sb(es, "onesb", [128, 2], BF16)
        mst_c = sb(es, "mst_c", [128, 128])
        mst_b = sb(es, "mst_b", [128, 128])
        mts_b = sb(es, "mts_b", [128, 128])
        neg_c = sb(es, "neg_c", [128, 128])
        neg_b = sb(es, "neg_b", [128, 128])
        sel_c = sb(es, "sel_c", [128, 128])
        sel_b = sb(es, "sel_b", [128, 128])
        ind = sb(es, "ind", [128, NSQ])
        lastind = sb(es, "lastind", [128, NSQ])
        indT = sb(es, "indT", [NSQ, 128])
        blkb = sb(es, "blkb", [128, NSQ, 128], BF16)
        GT, MG = cfg.GT, cfg.MG
        rst = sb(es, "rst", [128, MG])
        gcol_a = sb(es, "gcol_a", [128, cfg.MLV // 128])
        gcol_b = sb(es, "gcol_b", [128, cfg.HGV // 128])
        lbc = sb(es, "lbc", [128, 2 * HG_H])
        bigb = sb(es, "bigb", [128, ML_H])
        nbfg = sb(es, "nbfg", [128, ML_H])
        ctmp = sb(es, "ctmp", [128, 128])

        def P(fn, w, r=()):
            S.op('pool', fn, reads=r, writes=w)

        P(lambda e: e.memset(ident[:], 1.0), ['ident'])
        P(lambda e: e.affine_select(out=ident[:], in_=ident[:], pattern=[[-1, 128]], compare_op=ALU.is_equal,
                                    fill=0.0, base=0, channel_multiplier=1), ['ident'], ['ident'])
        P(lambda e: e.tensor_copy(out=identb[:], in_=ident[:]), ['identb'], ['ident'])
        P(lambda e: e.memset(ones[:], 1.0), ['ones'])
        P(lambda e: e.memset(onesb[:], 1.0), ['onesb'])
        P(lambda e: e.memset(mst_c[:], 1.0), ['mst_c'])
        P(lambda e: e.affine_select(out=mst_c[:], in_=mst_c[:], pattern=[[1, 128]], compare_op=ALU.is_ge,
                                    fill=0.0, base=0, channel_multiplier=-1), ['mst_c'], ['mst_c'])
        P(lambda e: e.memset(ctmp[:], 1.0), ['ctmp'])
        P(lambda e: e.affine_select(out=ctmp[:], in_=ctmp[:], pattern=[[-1, 128]], compare_op=ALU.is_ge,
                                    fill=0.0, base=0, channel_multiplier=1), ['ctmp'], ['ctmp'])
        P(lambda e: e.tensor_scalar(out=neg_c[:], in0=ctmp[:], scalar1=-1.0, scalar2=-NEG, op0=ALU.add, op1=ALU.mult),
          ['neg_c'], ['ctmp'])
        P(lambda e: e.memset(sel_c[:], 1.0), ['sel_c'])
        P(lambda e: e.affine_select(out=sel_c[:], in_=sel_c[:], pattern=[[0, 128]], compare_op=ALU.is_equal,
                                    fill=0.0, base=-127, channel_multiplier=1), ['sel_c'], ['sel_c'])
        P(lambda e: e.memset(ind[:], 1.0), ['ind'])
        P(lambda e: e.affine_select(out=ind[:], in_=ind[:], pattern=[[-SQL, NSQ]], compare_op=ALU.is_ge,
                                    fill=0.0, base=0, channel_multiplier=1), ['ind'], ['ind'])
        P(lambda e: e.affine_select(out=ind[:], in_=ind[:], pattern=[[SQL, NSQ]], compare_op=ALU.is_ge,
                                    fill=0.0, base=SQL - 1, channel_multiplier=-1), ['ind'], ['ind'])
        P(lambda e: e.memset(lastind[:], 1.0), ['lastind'])
        P(lambda e: e.affine_select(out=lastind[:], in_=lastind[:], pattern=[[-SQL, NSQ]], compare_op=ALU.is_equal,
                                    fill=0.0, base=-(SQL - 1), channel_multiplier=1), ['lastind'], ['lastind'])
        P(lambda e: e.memset(indT[:], 1.0), ['indT'])
        P(lambda e: e.affine_select(out=indT[:], in_=indT[:], pattern=[[1, 128]], compare_op=ALU.is_ge,
                                    fill=0.0, base=0, channel_multiplier=-SQL), ['indT'], ['indT'])
        P(lambda e: e.affine_select(out=indT[:], in_=indT[:], pattern=[[-1, 128]], compare_op=ALU.is_ge,
                                    fill=0.0, base=SQL - 1, channel_multiplier=SQL), ['indT'], ['indT'])
        P(lambda e: e.memset(blkb[:], 1.0), ['blkb'])
        P(lambda e: e.affine_select(out=blkb[:], in_=blkb[:], pattern=[[-SQL, NSQ], [1, 128]], compare_op=ALU.is_ge,
                                    fill=0.0, base=0, channel_multiplier=0), ['blkb'], ['blkb'])
        P(lambda e: e.affine_select(out=blkb[:], in_=blkb[:], pattern=[[SQL, NSQ], [-1, 128]], compare_op=ALU.is_ge,
                                    fill=0.0, base=SQL - 1, channel_multiplier=0), ['blkb'], ['blkb'])
        S.op('pe', lambda e: e.matmul(ps[0][:, 0:128], lhsT=indT[:], rhs=indT[:], start=True, stop=True),
             reads=['indT'], writes=[PK[0]])
        S.op('dve', lambda e: e.tensor_tensor(out=mst_b[:], in0=ps[0][:, 0:128], in1=mst_c[:], op=ALU.mult),
             reads=[PK[0], 'mst_c'], writes=['mst_b'])
        S.op('dve', lambda e: e.tensor_tensor(out=mts_b[:], in0=ps[0][:, 0:128], in1=ctmp[:], op=ALU.mult),
             reads=[PK[0], 'ctmp'], writes=['mts_b'])
        S.op('dve', lambda e: e.tensor_scalar(out=neg_b[:], in0=mts_b[:], scalar1=-1.0, scalar2=-NEG, op0=ALU.add,
                                              op1=ALU.mult), reads=['mts_b'], writes=['neg_b'])
        S.op('dve', lambda e: e.tensor_copy(out=sel_b[:].rearrange("p (q j) -> p q j", j=SQL),
                                            in_=lastind[:].unsqueeze(2).to_broadcast([128, NSQ, SQL])),
             reads=['lastind'], writes=['sel_b'])
        P(lambda e: e.memset(rst[:], 1.0), ['rst'])
        P(lambda e: e.memset(rst[:, 0:GT * 128].rearrange("p (a b) -> p a b", b=128)[:, :, 0:1], 0.0), ['rst'], ['rst'])
        P(lambda e: e.memset(rst[:, GT * 128:MG].rearrange("p (a b) -> p a b", b=SQL)[:, :, 0:1], 0.0), ['rst'], ['rst'])

        rowt = sb(es, "rowt", [128, 128])

        def load_cols(src_rows, nrows, dst, dkey):
            S.dma('sp', rowt[0:nrows, :], src_rows, writes=['rowt'])
            S.op('pe', lambda e: e.matmul(ps[1][:, 0:nrows], lhsT=rowt[0:nrows, :], rhs=ident[0:nrows, 0:nrows],
                                          start=True, stop=True), reads=['rowt', 'ident'], writes=[PK[1]])
            S.op('dve', lambda e: e.tensor_copy(out=dst, in_=ps[1][:, 0:nrows]), reads=[PK[1]], writes=[dkey])

        load_cols(ml_g, cfg.MLV // 128, gcol_a[:], 'gcol_a')
        load_cols(hg_g, cfg.HGV // 128, gcol_b[:], 'gcol_b')
        load_cols(lb_logits, 2 * HG_H, lbc[:], 'lbc')
        S.op('dve', lambda e: e.tensor_tensor(out=lbc[:, 0:HG_H], in0=lbc[:, 0:HG_H], in1=lbc[:, HG_H:2 * HG_H],
                                              op=ALU.subtract), reads=['lbc'], writes=['lbc'])
        S.op('act', lambda e: e.activation(out=lbc[:, 0:HG_H], in_=lbc[:, 0:HG_H], func=AF.Exp, scale=-1.0),
             reads=['lbc'], writes=['lbc'])
        S.op('act', lambda e: e.activation(out=lbc[:, 0:HG_H], in_=lbc[:, 0:HG_H], func=AF.Ln, bias=1.0),
             reads=['lbc'], writes=['lbc'])
        S.op('act', lambda e: e.activation(out=lbc[:, 0:HG_H], in_=lbc[:, 0:HG_H], func=AF.Exp, scale=-1.0),
             reads=['lbc'], writes=['lbc'])
        S.op('dve', lambda e: e.tensor_scalar(out=lbc[:, HG_H:2 * HG_H], in0=lbc[:, 0:HG_H], scalar1=-1.0, scalar2=1.0,
                                              op0=ALU.mult, op1=ALU.add), reads=['lbc'], writes=['lbc'])
        S.dma('sp', bigb[:], b_ig.partition_broadcast(128), writes=['bigb'])
        S.dma('sp', nbfg[:], b_fg.partition_broadcast(128), writes=['nbfg'])
        S.op('dve', lambda e: e.tensor_scalar(out=nbfg[:], in0=nbfg[:], scalar1=-1.0, scalar2=None, op0=ALU.mult),
             reads=['nbfg'], writes=['nbfg'])

        NWB = 4
        wbuf = [sb(es, "wbuf%d" % i, [128, 16, BW], BF16) for i in range(NWB)]
        wstate = {'issued': 0, 'req': 0}

        def _issue(spec):
            wap, r0, kc, c0, ncol = spec
            i = wstate['issued']
            wstate['issued'] += 1
            buf = wbuf[i % NWB]
            key = 'wbuf%d' % (i % NWB)
            src = wap[r0:r0 + kc * 128, c0:c0 + ncol].rearrange("(c p) n -> p c n", p=128)
            S.dma('pool', buf[:, 0:kc, 0:ncol], src, writes=[key])

        def wget(wname, wap, r0, kc, c0, ncol):
            if record is not None:
                record.append((wname, r0, kc, c0, ncol))
            idx = wstate['req']
            wstate['req'] += 1
            if wstate['issued'] <= idx:
                assert wstate['issued'] == idx
                _issue((wap, r0, kc, c0, ncol))
            if wplan is not None:
                assert wplan[idx] == (wname, r0, kc, c0, ncol), (idx, wplan[idx], (wname, r0, kc, c0, ncol))
                while wstate['issued'] < min(len(wplan), idx + NWB):
                    nm, a, b, c, d = wplan[wstate['issued']]
                    _issue((WMAP[nm], a, b, c, d))
            return wbuf[idx % NWB], 'wbuf%d' % (idx % NWB)

        WMAP = {'w_in': w_in, 'w_ba': w_ba, 'w_bb': w_bb, 'w_out': w_out, 'w_up': w_up, 'w_dn': w_dn}
        psrot = {'i': 0}

        def nextps(lo=0, hi=8):
            i = lo + psrot['i'] % (hi - lo)
            psrot['i'] += 1
            return ps[i], PK[i]

        evrot = {'i': 0}

        def evac_eng():
            evrot['i'] += 1
            return 'dve' if evrot['i'] % 2 else 'act'

        def copy_op(eng, out, in_, reads, writes, scale=None):
            if scale is not None:
                if eng == 'act':
                    S.op('act', lambda e: e.activation(out=out, in_=in_, func=AF.Copy, scale=scale), reads=reads, writes=writes)
                else:
                    S.op(eng, lambda e: e.tensor_scalar(out=out, in0=in_, scalar1=scale, scalar2=None, op0=ALU.mult),
                         reads=reads, writes=writes)
            elif eng == 'act':
                S.op('act', lambda e: e.activation(out=out, in_=in_, func=AF.Copy), reads=reads, writes=writes)
            else:
                S.op(eng, lambda e: e.tensor_copy(out=out, in_=in_), reads=reads, writes=writes)

        def proj_T(wb, wkey, ncol, actT, akey, kc, tok0, pst, pkey):
            for c in range(kc):
                S.op('pe', lambda e, c=c: e.matmul(pst[:, 0:ncol], lhsT=actT[:, c, tok0:tok0 + 128], rhs=wb[:, c, 0:ncol],
                                                   start=(c == 0), stop=(c == kc - 1)),
                     reads=[wkey, akey], writes=[pkey])

        def proj_F(wb, wkey, m0, mcol, actT, akey, kc, tok0, ntok, pst, pkey):
            for c in range(kc):
                S.op('pe', lambda e, c=c: e.matmul(pst[0:mcol, 0:ntok], lhsT=wb[:, c, m0:m0 + mcol],
                                                   rhs=actT[:, c, tok0:tok0 + ntok], start=(c == 0), stop=(c == kc - 1)),
                     reads=[wkey, akey], writes=[pkey])

        def ntiles(total):
            out, t = [], 0
            while t < total:
                n = min(512, total - t)
                out.append((t, n))
                t += n
            return out

        def sigmoid_from(eng_in, out, in_, reads, writes):
            S.op('act', lambda e: e.activation(out=out, in_=in_, func=AF.Exp, scale=-1.0), reads=reads, writes=writes)
            S.op('act', lambda e: e.activation(out=out, in_=out, func=AF.Ln, bias=1.0), reads=writes, writes=writes)
            S.op('act', lambda e: e.activation(out=out, in_=out, func=AF.Exp, scale=-1.0), reads=writes, writes=writes)

        def load_xT(x_ap, rows, xT, xkey):
            with contextlib.ExitStack() as st2:
                xt = [sb(st2, "xt%d" % i, [128, D], BF16) for i in range(3)]
                for t, r0 in enumerate(rows):
                    b = xt[t % 3]
                    bk = "xt%d" % (t % 3)
                    S.dma('pool', b[:], x_ap[r0:r0 + 128, :], writes=[bk])
                    for g in range(0, KC, 4):
                        n = min(4, KC - g)
                        pt, pk = nextps()
                        for j in range(n):
                            S.op('pe', lambda e, j=j: e.matmul(pt[:, j * 128:(j + 1) * 128],
                                                               lhsT=b[:, (g + j) * 128:(g + j + 1) * 128], rhs=identb[:],
                                                               start=True, stop=True), reads=[bk, 'identb'], writes=[pk])
                        copy_op(evac_eng(), xT[:, g:g + n, t * 128:(t + 1) * 128],
                                pt[:, 0:n * 128].rearrange("p (a b) -> p a b", b=128), [pk], [xkey])
                S.barrier()

        Cst = [sb(es, "Cst%d" % h, [128, 2, ML_DV]) for h in range(ML_H)]
        nst = sb(es, "nst", [128, ML_H, 2])
        mst = sb(es, "mst", [128, ML_H])
        Sst = [sb(es, "Sst%d" % h, [128, HG_DV]) for h in range(HG_H)]
        for h in range(ML_H):
            P(lambda e, h=h: e.memset(Cst[h][:], 0.0), ['Cst%d' % h])
        P(lambda e: e.memset(nst[:], 0.0), ['nst'])
        P(lambda e: e.memset(mst[:], 0.0), ['mst'])
        for h in range(HG_H):
            P(lambda e, h=h: e.memset(Sst[h][:], 0.0), ['Sst%d' % h])

        chk_holder = [lambda tag: None]
        def mixer_pass(xT, xkey, ntile, full, brA, brB, samp, R, final, rmask=None):
            rmask = rst if rmask is None else rmask
            tiles = list(range(ntile)) + ([ntile] if samp else [])
            ntl = len(tiles)
            Mtok = ntl * 128
            with contextlib.ExitStack() as st:
                ift = sb(st, "ift", [128, ntl, 2 * ML_H])
                wb, wk = wget('w_in', w_in, 0, KC, off['mi'], 2 * ML_H)
                for ti in range(ntl):
                    pt, pk = nextps()
                    proj_T(wb, wk, 2 * ML_H, xT, xkey, KC, ti * 128, pt, pk)
                    copy_op('dve', ift[:, ti, :], pt[:, 0:2 * ML_H], [pk], ['ift'])
                chk_holder[0]('ift')
                ro = [0]

                def rview(n, inner):
                    v = R[:, ro[0]:ro[0] + n].rearrange("p (a b) -> p a b", b=inner)
                    ro[0] += n
                    return v
                PBS = []
                for i in range(2):
                    PB = {'k_tok': rview(ntl * ML_DK, ML_DK), 'v_tok': rview(ntl * ML_DV, ML_DV)}
                    if full:
                        PB['qT'] = rview(2 * Mtok, Mtok)
                        PB['kT'] = rview(2 * Mtok, Mtok)
                        PB['og'] = rview(ntl * ML_DV, ML_DV)
                    PBS.append(PB)
                if samp:
                    NCB = 3
                    Cf = [sb(st, "Cf%d" % i, [128, 2, ML_DV]) for i in range(NCB)]
                    Cfb = [sb(st, "Cfb%d" % i, [128, 2, ML_DV], BF16) for i in range(NCB)]
                    qTq = [sb(st, "qTq%d" % i, [128, 2, 128], BF16) for i in range(2)]
                    kwq = [sb(st, "kwq%d" % i, [128, ML_DK], BF16) for i in range(2)]
                    nall = sb(st, "nall", [128, NSQ * ML_H * 2])
                    nallb = sb(st, "nallb", [128, NSQ * ML_H * 2], BF16)
                    nrow = sb(st, "nrow", [128, 128])
                    Dbc = sb(st, "Dbc", [128, NSQ])
                    dsel = sb(st, "dsel", [128, NSQ])
                    msamp = sb(st, "msamp", [NSQ, ML_H])
                    mtok = sb(st, "mtok", [128, ML_H])
                    mnew_all = sb(st, "mnew_all", [128, ML_H])
                    mout = sb(st, "mout", [NSQ, ML_H])
                    S.dma('sp', nrow[:], st_n, writes=['nrow'])
                    chk_holder[0]('sA0')
                    S.op('pe', lambda e: e.matmul(ps[4][:, 0:128], lhsT=nrow[:], rhs=ident[:], start=True, stop=True),
                         reads=['nrow', 'ident'], writes=[PK[4]])
                    copy_op('dve', nall[:], ps[4][:, 0:128], [PK[4]], ['nall'])
                    chk_holder[0]('sA05')
                    copy_op('act', nallb[:], ps[4][:, 0:128], [PK[4]], ['nallb'])
                    chk_holder[0]('sA1')
                    S.dma('sp', msamp[:], st_m, writes=['msamp'])
                    S.op('pe', lambda e: e.matmul(ps[4][:, 0:ML_H], lhsT=indT[:], rhs=msamp[:], start=True, stop=True),
                         reads=['indT', 'msamp'], writes=[PK[4]])
                    copy_op('dve', mtok[:], ps[4][:, 0:ML_H], [PK[4]], ['mtok'])
                    chk_holder[0]('sA')

                def ml_rec(h, X, tiles):
                    sfx = X['sfx']
                    K = lambda nm: nm + sfx
                    Cbf, nbf, sc, bm, mprev = X['Cbf'], X['nbf'], X['sc'], X['bm'], X['mprev']
                    diagc, logd, dmat, sdm, sdT, kw = X['diagc'], X['logd'], X['dmat'], X['sdm'], X['sdT'], X['kw']
                    hbuf, h2, bst, bmv = X['hbuf'], X['h2'], X['bst'], X['bmv']
                    PB = X['PB']
                    k_tok, v_tok = PB['k_tok'], PB['v_tok']
                    qT, kT, og = PB.get('qT'), PB.get('kT'), PB.get('og')
                    bG, bQ, bA, bB = X['bG'], X['bQ'], X['bA'], X['bB']
                    kG, kQ, kA, kB = PK[bG], PK[bQ], PK[bA], PK[bB]
                    cB0, cC0, cS0, cQ0, cT0, cN0 = X['cols']
                    urot = [0]

                    def ubank():
                        bnk = X['ub'][urot[0] % len(X['ub'])]
                        urot[0] += 1
                        return ps[bnk], PK[bnk]
                    Ck, nk, mk_ = 'Cst%d' % h, 'nst%d' % h, 'mst%d' % h
                    S.op('act', lambda e: e.activation(out=Cbf[:], in_=Cst[h][:], func=AF.Copy), reads=[Ck], writes=[K('Cbf')])
                    S.op('dve', lambda e: e.tensor_copy(out=nbf[:], in_=nst[:, h, :]), reads=[nk], writes=[K('nbf')])
                    S.op('dve', lambda e: e.tensor_copy(out=mprev[:], in_=mst[:, h:h + 1]), reads=[mk_], writes=[K('mprev')])
                    for ti in tiles:
                        is_s = samp and ti == ntl - 1
                        mstm = mst_b if is_s else mst_c
                        mstk = 'mst_b' if is_s else 'mst_c'
                        negm_ = neg_b if is_s else neg_c
                        negk = 'neg_b' if is_s else 'neg_c'
                        selm = sel_b if is_s else sel_c
                        selk = 'sel_b' if is_s else 'sel_c'
                        tok = slice(ti * 128, (ti + 1) * 128)
                        col = lambda j: sc[:, j:j + 1]
                        if is_s:
                            S.op('dve', lambda e: e.tensor_copy(out=mprev[:], in_=mtok[:, h:h + 1]), reads=['mtok'],
                                 writes=[K('mprev')])
                        S.op('dve', lambda e: e.tensor_scalar(out=col(0), in0=ift[:, ti, h:h + 1], scalar1=bigb[:, h:h + 1],
                                                              scalar2=None, op0=ALU.add), reads=['ift', 'bigb'], writes=[K('sc0')])
                        S.op('act', lambda e: e.activation(out=col(1), in_=ift[:, ti, ML_H + h:ML_H + h + 1], func=AF.Exp,
                                                           scale=-1.0, bias=nbfg[:, h:h + 1]), reads=['ift', 'nbfg'],
                             writes=[K('sc1')])
                        S.op('act', lambda e: e.activation(out=col(1), in_=col(1), func=AF.Ln, bias=1.0), reads=[K('sc1')],
                             writes=[K('sc1')])
                        S.op('pe', lambda e: e.matmul(ps[bG][:, cB0:cB0 + 1], lhsT=mstm[:], rhs=col(1), start=True, stop=True),
                             reads=[mstk, K('sc1')], writes=[kG])
                        S.op('dve', lambda e: e.tensor_copy(out=bm[:, 0:1], in_=ps[bG][:, cB0:cB0 + 1]), reads=[kG], writes=[K('bm0')])
                        S.op('dve', lambda e: e.tensor_tensor(out=col(2), in0=col(0), in1=bm[:, 0:1], op=ALU.add),
                             reads=[K('sc0'), K('bm0')], writes=[K('sc2')])
                        S.op('dve', lambda e: e.tensor_scalar(out=diagc[:], in0=ident[:], scalar1=col(2), scalar2=None,
                                                              op0=ALU.mult), reads=['ident', K('sc2')], writes=[K('diagc')])
                        S.op('pe', lambda e: e.matmul(ps[bG][:, cC0:cC0 + 128], lhsT=ones[:], rhs=diagc[:], start=True, stop=True),
                             reads=['ones', K('diagc')], writes=[kG])
                        S.op('dve', lambda e: e.scalar_tensor_tensor(out=logd[:], in0=ps[bG][:, cC0:cC0 + 128], scalar=bm[:, 0:1],
                                                                     in1=negm_[:], op0=ALU.subtract, op1=ALU.add),
                             reads=[kG, K('bm0'), negk], writes=[K('logd')])
                        S.op('dve', lambda e: e.tensor_reduce(out=col(3), in_=logd[:], axis=AX.X, op=ALU.max),
                             reads=[K('logd')], writes=[K('sc3')])
                        S.op('dve', lambda e: e.tensor_tensor(out=col(4), in0=mprev[:], in1=bm[:, 0:1], op=ALU.subtract),
                             reads=[K('mprev'), K('bm0')], writes=[K('sc4')])
                        S.op('dve', lambda e: e.tensor_tensor(out=bm[:, 1:2], in0=col(4), in1=col(3), op=ALU.max),
                             reads=[K('sc4'), K('sc3')], writes=[K('bm1')])
                        S.op('dve', lambda e: e.tensor_scalar(out=col(5), in0=bm[:, 1:2], scalar1=-1.0, scalar2=None,
                                                              op0=ALU.mult), reads=[K('bm1')], writes=[K('sc5')])
                        S.op('pe', lambda e: e.matmul(ps[bG][:, cS0:cS0 + 2], lhsT=selm[:], rhs=bm[:], start=True, stop=True),
                             reads=[selk, K('bm0'), K('bm1')], writes=[kG])
                        S.op('dve', lambda e: e.tensor_tensor(out=col(8), in0=bm[:, 0:1], in1=ps[bG][:, cS0:cS0 + 1],
                                                              op=ALU.subtract), reads=[K('bm0'), kG], writes=[K('sc8')])
                        S.op('dve', lambda e: e.tensor_tensor(out=col(8), in0=col(8), in1=col(0), op=ALU.add),
                             reads=[K('sc8'), K('sc0')], writes=[K('sc8')])
                        S.op('dve', lambda e: e.tensor_scalar(out=col(9), in0=ps[bG][:, cS0 + 1:cS0 + 2], scalar1=-1.0, scalar2=None,
                                                              op0=ALU.mult), reads=[kG], writes=[K('sc9')])
                        S.op('act', lambda e: e.activation(out=col(10), in_=col(8), func=AF.Exp, bias=col(9)),
                             reads=[K('sc8'), K('sc9')], writes=[K('sc10')])
                        S.op('dve', lambda e: e.tensor_tensor(out=col(11), in0=mprev[:], in1=ps[bG][:, cS0:cS0 + 1],
                                                              op=ALU.subtract), reads=[K('mprev'), kG], writes=[K('sc11')])
                        S.op('act', lambda e: e.activation(out=col(12), in_=col(11), func=AF.Exp, bias=col(9)),
                             reads=[K('sc11'), K('sc9')], writes=[K('sc12')])
                        if full:
                            S.op('act', lambda e: e.activation(out=dmat[:], in_=logd[:], func=AF.Exp, bias=col(5)),
                                 reads=[K('logd'), K('sc5')], writes=[K('dmat')])
                            S.op('act', lambda e: e.activation(out=col(6), in_=col(4), func=AF.Exp, bias=col(5)),
                                 reads=[K('sc4'), K('sc5')], writes=[K('sc6')])
                            S.op('act', lambda e: e.activation(out=col(7), in_=col(5), func=AF.Exp), reads=[K('sc5')],
                                 writes=[K('sc7')])
                            for c in range(2):
                                S.op('pe', lambda e, c=c: e.matmul(ps[bQ][:, cQ0:cQ0 + 128], lhsT=qT[:, c, tok], rhs=kT[:, c, tok],
                                                                   start=(c == 0), stop=(c == 1)), reads=[K('qT'), K('kT')],
                                     writes=[kQ])
                            S.op('dve', lambda e: e.scalar_tensor_tensor(out=sdm[:], in0=ps[bQ][:, cQ0:cQ0 + 128], scalar=1.0,
                                                                         in1=dmat[:], op0=ALU.mult, op1=ALU.mult,
                                                                         accum_out=col(13)),
                                 reads=[kQ, K('dmat')], writes=[K('sdm'), K('sc13')])
                            S.op('pe', lambda e: e.matmul(ps[bQ][:, cT0:cT0 + 128], lhsT=sdm[:], rhs=ident[:], start=True, stop=True),
                                 reads=[K('sdm'), 'ident'], writes=[kQ])
                            copy_op('act', sdT[:], ps[bQ][:, cT0:cT0 + 128], [kQ], [K('sdT')])
                            S.op('pe', lambda e: e.matmul(ps[bA][:, :], lhsT=sdT[:], rhs=v_tok[:, ti, :], start=True, stop=True),
                                 reads=[K('sdT'), K('v_tok')], writes=[kA])
                        if not is_s:
                            if full:
                                for c in range(2):
                                    S.op('pe', lambda e, c=c: e.matmul(ps[bB][:, :], lhsT=qT[:, c, tok], rhs=Cbf[:, c, :],
                                                                       start=(c == 0), stop=(c == 1)), reads=[K('qT'), K('Cbf')],
                                         writes=[kB])
                                for c in range(2):
                                    S.op('pe', lambda e, c=c: e.matmul(ps[bQ][:, cN0:cN0 + 1], lhsT=qT[:, c, tok], rhs=nbf[:, c:c + 1],
                                                                       start=(c == 0), stop=(c == 1)), reads=[K('qT'), K('nbf')],
                                         writes=[kQ])
                            S.op('dve', lambda e: e.tensor_scalar(out=kw[:], in0=k_tok[:, ti, :], scalar1=col(10), scalar2=None,
                                                                  op0=ALU.mult), reads=[K('k_tok'), K('sc10')], writes=[K('kw')])
                            for c in range(2):
                                pt, pk = ubank()
                                S.op('pe', lambda e, c=c, pt=pt: e.matmul(pt[:, :], lhsT=kw[:, c * 128:(c + 1) * 128],
                                                                          rhs=v_tok[:, ti, :], start=True, stop=True),
                                     reads=[K('kw'), K('v_tok')], writes=[pk])
                                S.op('dve', lambda e, c=c, pt=pt: e.scalar_tensor_tensor(
                                    out=Cst[h][:, c, :], in0=Cst[h][:, c, :], scalar=col(12), in1=pt[:, :],
                                    op0=ALU.mult, op1=ALU.add), reads=[Ck, K('sc12'), pk, K('Cbf')], writes=[Ck])
                            pt, pk = ubank()
                            for c in range(2):
                                S.op('pe', lambda e, c=c, pt=pt: e.matmul(pt[:, c:c + 1], lhsT=kw[:, c * 128:(c + 1) * 128],
                                                                          rhs=onesb[:, 0:1], start=True, stop=True),
                                     reads=[K('kw'), 'onesb'], writes=[pk])
                            S.op('dve', lambda e, pt=pt: e.scalar_tensor_tensor(
                                out=nst[:, h, :], in0=nst[:, h, :], scalar=col(12), in1=pt[:, 0:2],
                                op0=ALU.mult, op1=ALU.add), reads=[nk, K('sc12'), pk, K('nbf')], writes=[nk])
                            S.op('act', lambda e: e.activation(out=Cbf[:], in_=Cst[h][:], func=AF.Copy), reads=[Ck],
                                 writes=[K('Cbf')])
                            S.op('dve', lambda e: e.tensor_copy(out=nbf[:], in_=nst[:, h, :]), reads=[nk], writes=[K('nbf')])
                            S.op('dve', lambda e: e.tensor_copy(out=mprev[:], in_=ps[bG][:, cS0 + 1:cS0 + 2]), reads=[kG],
                                 writes=[K('mprev')])
                            S.op('dve', lambda e: e.tensor_copy(out=mst[:, h:h + 1], in_=ps[bG][:, cS0 + 1:cS0 + 2]), reads=[kG],
                                 writes=[mk_])
                        else:
                            S.op('dve', lambda e: e.tensor_copy(out=mnew_all[:, h:h + 1], in_=ps[bG][:, cS0 + 1:cS0 + 2]),
                                 reads=[kG], writes=['mnew_all'])
                            S.op('dve', lambda e: e.tensor_scalar(out=dsel[:], in0=lastind[:], scalar1=col(12), scalar2=None,
                                                                  op0=ALU.mult), reads=['lastind', K('sc12')], writes=['dsel'])
                            pt, pk = ubank()
                            S.op('pe', lambda e, pt=pt: e.matmul(pt[:, 0:NSQ], lhsT=ones[:], rhs=dsel[:], start=True, stop=True),
                                 reads=['ones', 'dsel'], writes=[pk])
                            copy_op('dve', Dbc[:], pt[:, 0:NSQ], [pk], ['Dbc'])
                            S.op('dve', lambda e: e.tensor_scalar(out=kw[:], in0=k_tok[:, ti, :], scalar1=col(10), scalar2=None,
                                                                  op0=ALU.mult), reads=[K('k_tok'), K('sc10')], writes=[K('kw')])
                            def cload(q):
                                S.dma('sp', Cf[q % NCB][:], st_C[q, h].rearrange("(c p) v -> p c v", p=128),
                                      writes=['Cf%d' % (q % NCB)])
                            for q in range(min(NCB - 1, NSQ)):
                                cload(q)
                            for q in range(NSQ):
                                if q + NCB - 1 < NSQ:
                                    cload(q + NCB - 1)
                                cf, cfk = Cf[q % NCB], 'Cf%d' % (q % NCB)
                                cb, cbk = Cfb[q % NCB], 'Cfb%d' % (q % NCB)
                                qq, qqk = qTq[q % 2], 'qTq%d' % (q % 2)
                                kq, kqk = kwq[q % 2], 'kwq%d' % (q % 2)
                                S.op('dve', lambda e, q=q, qq=qq: e.tensor_tensor(
                                    out=qq[:], in0=qT[:, :, tok], in1=blkb[:, q:q + 1, :].to_broadcast([128, 2, 128]),
                                    op=ALU.mult), reads=[K('qT'), 'blkb'], writes=[qqk])
                                S.op('dve', lambda e, q=q, kq=kq: e.tensor_scalar(
                                    out=kq[:], in0=kw[:], scalar1=ind[:, q:q + 1], scalar2=None, op0=ALU.mult),
                                    reads=[K('kw'), 'ind'], writes=[kqk])
                                copy_op('act', cb[:], cf[:], [cfk], [cbk])
                                for c in range(2):
                                    S.op('pe', lambda e, c=c, q=q, cb=cb: e.matmul(
                                        ps[bB][:, :], lhsT=qq[:, c, :], rhs=cb[:, c, :],
                                        start=(q == 0 and c == 0), stop=(q == NSQ - 1 and c == 1)),
                                        reads=[qqk, cbk], writes=[kB])
                                for c in range(2):
                                    j = (q * ML_H + h) * 2 + c
                                    S.op('pe', lambda e, c=c, q=q, j=j: e.matmul(
                                        ps[bQ][:, cN0:cN0 + 1], lhsT=qq[:, c, :], rhs=nallb[:, j:j + 1],
                                        start=(q == 0 and c == 0), stop=(q == NSQ - 1 and c == 1)),
                                        reads=[qqk, 'nallb'], writes=[kQ])
                                for c in range(2):
                                    pt, pk = ubank()
                                    S.op('pe', lambda e, c=c, q=q, pt=pt: e.matmul(
                                        pt[:, :], lhsT=kq[:, c * 128:(c + 1) * 128], rhs=v_tok[:, ti, :],
                                        start=True, stop=True), reads=[kqk, K('v_tok')], writes=[pk])
                                    S.op('dve', lambda e, c=c, q=q, pt=pt, cf=cf: e.scalar_tensor_tensor(
                                        out=cf[:, c, :], in0=cf[:, c, :], scalar=Dbc[:, q:q + 1], in1=pt[:, :],
                                        op0=ALU.mult, op1=ALU.add), reads=[cfk, 'Dbc', pk, cbk], writes=[cfk])
                                S.dma('sp', o_Cs[q, h].rearrange("(c p) v -> p c v", p=128), cf[:], reads=[cfk])
                                pt, pk = ubank()
                                for c in range(2):
                                    S.op('pe', lambda e, c=c, q=q, pt=pt: e.matmul(
                                        pt[:, c:c + 1], lhsT=kq[:, c * 128:(c + 1) * 128], rhs=onesb[:, 0:1],
                                        start=True, stop=True), reads=[kqk, 'onesb'], writes=[pk])
                                j0 = (q * ML_H + h) * 2
                                S.op('dve', lambda e, q=q, pt=pt, j0=j0: e.scalar_tensor_tensor(
                                    out=nall[:, j0:j0 + 2], in0=nall[:, j0:j0 + 2], scalar=Dbc[:, q:q + 1], in1=pt[:, 0:2],
                                    op0=ALU.mult, op1=ALU.add), reads=['nall', 'Dbc', pk], writes=['nall'])
                        if is_s:
                            chk_holder[0]('sC')
                        if full:
                            S.op('dve', lambda e: e.scalar_tensor_tensor(out=col(14), in0=ps[bQ][:, cN0:cN0 + 1], scalar=col(6),
                                                                         in1=col(13), op0=ALU.mult, op1=ALU.add),
                                 reads=[kQ, K('sc6'), K('sc13')], writes=[K('sc14')])
                            S.op('act', lambda e: e.activation(out=col(14), in_=col(14), func=AF.Abs), reads=[K('sc14')],
                                 writes=[K('sc14')])
                            S.op('dve', lambda e: e.tensor_tensor(out=col(14), in0=col(14), in1=col(7), op=ALU.max),
                                 reads=[K('sc14'), K('sc7')], writes=[K('sc14')])
                            S.op('dve', lambda e: e.reciprocal(out=col(15), in_=col(14)), reads=[K('sc14')], writes=[K('sc15')])
                            S.op('dve', lambda e: e.tensor_tensor(out=col(16), in0=col(15), in1=col(6), op=ALU.mult),
                                 reads=[K('sc15'), K('sc6')], writes=[K('sc16')])
                            S.op('act', lambda e: e.activation(out=h2[:], in_=ps[bB][:, :], func=AF.Identity, scale=col(16)),
                                 reads=[kB, K('sc16')], writes=[K('h2')])
                            S.op('dve', lambda e: e.scalar_tensor_tensor(out=hbuf[:], in0=ps[bA][:, :], scalar=col(15),
                                                                         in1=h2[:], op0=ALU.mult, op1=ALU.add),
                                 reads=[kA, K('sc15'), K('h2')], writes=[K('hbuf')])
                            S.op('dve', lambda e: e.bn_stats(out=bst[:], in_=hbuf[:]), reads=[K('hbuf')], writes=[K('bst')])
                            S.op('dve', lambda e: e.bn_aggr(out=bmv[:], in_=bst[:]), reads=[K('bst')], writes=[K('bmv')])
                            S.op('act', lambda e: e.activation(out=col(17), in_=bmv[:, 1:2], func=AF.Ln, bias=LN_EPS),
                                 reads=[K('bmv')], writes=[K('sc17')])
                            S.op('act', lambda e: e.activation(out=col(17), in_=col(17), func=AF.Exp, scale=-0.5),
                                 reads=[K('sc17')], writes=[K('sc17')])
                            S.op('dve', lambda e: e.tensor_scalar(out=col(18), in0=bmv[:, 0:1], scalar1=-1.0, scalar2=col(17),
                                                                  op0=ALU.mult, op1=ALU.mult), reads=[K('bmv'), K('sc17')],
                                 writes=[K('sc18')])
                            S.op('act', lambda e: e.activation(out=h2[:], in_=hbuf[:], func=AF.Identity, scale=col(17),
                                                               bias=col(18)), reads=[K('hbuf'), K('sc17'), K('sc18')], writes=[K('h2')])
                            S.op('dve', lambda e: e.tensor_tensor(out=hbuf[:], in0=h2[:], in1=og[:, ti, :], op=ALU.mult),
                                 reads=[K('h2'), K('og')], writes=[K('hbuf')])
                            for j in range(4):
                                S.op('pe', lambda e, j=j: e.matmul(ps[bA][:, j * 128:(j + 1) * 128],
                                                                   lhsT=hbuf[:, j * 128:(j + 1) * 128], rhs=ident[:],
                                                                   start=True, stop=True), reads=[K('hbuf'), 'ident'],
                                     writes=[kA])
                            for j in range(4):
                                ch = 4 * h + j
                                if j % 2 == 0:
                                    S.op('act', lambda e, j=j, ch=ch: e.activation(
                                        out=brA[:, ch, tok], in_=ps[bA][:, j * 128:(j + 1) * 128], func=AF.Identity,
                                        scale=gcol_a[:, ch:ch + 1]), reads=[kA, 'gcol_a'], writes=[K('brA')])
                                else:
                                    S.op('dve', lambda e, j=j, ch=ch: e.tensor_scalar(
                                        out=brA[:, ch, tok], in0=ps[bA][:, j * 128:(j + 1) * 128], scalar1=gcol_a[:, ch:ch + 1],
                                        scalar2=None, op0=ALU.mult), reads=[kA, 'gcol_a'], writes=[K('brA')])
                def ml_proj(h, PB, psfx):
                    k_tok, v_tok = PB['k_tok'], PB['v_tok']
                    qT, kT, og = PB.get('qT'), PB.get('kT'), PB.get('og')
                    h2 = MX[0]['h2']
                    def tmode(colname, width, dst, dkey, post=None, scale=None):
                        for c0 in range(0, width, BW):
                            wb, wk = wget('w_in', w_in, 0, KC, off[colname] + h * width + c0, BW)
                            for ti in range(ntl):
                                pt, pk = nextps()
                                proj_T(wb, wk, BW, xT, xkey, KC, ti * 128, pt, pk)
                                if post is None:
                                    copy_op(evac_eng(), dst[:, ti, c0:c0 + BW], pt[:, 0:BW], [pk], [dkey], scale=scale)
                                else:
                                    post(dst[:, ti, c0:c0 + BW], pt[:, 0:BW], pk, dkey)

                    def fmode(colname, dst, dkey, scale=None):
                        wb, wk = wget('w_in', w_in, 0, KC, off[colname] + h * ML_DK, BW)
                        for cc in range(2):
                            for (t0, n) in ntiles(Mtok):
                                pt, pk = nextps()
                                proj_F(wb, wk, cc * 128, 128, xT, xkey, KC, t0, n, pt, pk)
                                copy_op(evac_eng(), dst[:, cc, t0:t0 + n], pt[:, 0:n], [pk], [dkey], scale=scale)

                    if full:
                        fmode('mq', qT, 'qT' + psfx)
                        fmode('mk', kT, 'kT' + psfx, scale=ML_DK ** -0.5)
                    tmode('mk', ML_DK, k_tok, 'k_tok' + psfx, scale=ML_DK ** -0.5)
                    tmode('mv', ML_DV, v_tok, 'v_tok' + psfx)
                    if full:
                        osig = sb(st, "osig%d" % h, [128, BW]) if False else None

                        def opost(dst, src, pk, dkey):
                            sigmoid_from('act', h2[:, 0:BW], src, [pk], ['h2_0'])
                            S.op('dve', lambda e: e.tensor_copy(out=dst, in_=h2[:, 0:BW]), reads=['h2_0'], writes=[dkey])
                        tmode('mo', ML_DV, og, 'og' + psfx, post=opost)


                def ml_scratch(i):
                    return {'sfx': '_%d' % i,
                            'Cbf': sb(st, "Cbf", [128, 2, ML_DV], BF16), 'nbf': sb(st, "nbf", [128, 2], BF16),
                            'sc': sb(st, "sc", [128, 24]), 'bm': sb(st, "bm", [128, 2]), 'mprev': sb(st, "mprev", [128, 1]),
                            'diagc': sb(st, "diagc", [128, 128]), 'logd': sb(st, "logd", [128, 128]),
                            'dmat': sb(st, "dmat", [128, 128]), 'sdm': sb(st, "sdm", [128, 128]),
                            'sdT': sb(st, "sdT", [128, 128], BF16), 'kw': sb(st, "kw", [128, ML_DK], BF16),
                            'hbuf': sb(st, "hbuf", [128, ML_DV]), 'h2': sb(st, "h2", [128, ML_DV]),
                            'bst': sb(st, "bst", [128, 6]), 'bmv': sb(st, "bmv", [128, 2])}
                MX = [ml_scratch(0), ml_scratch(1)]
                hbuf = MX[0]['hbuf']
                MERGED = (384, 128, 386, 0, 256, 390)
                MX[0].update(bG=4, bQ=4, bA=5, bB=6, cols=MERGED)
                MX[1].update(bG=0, bQ=0, bA=1, bB=2, cols=MERGED)
                for h0 in range(0, ML_H, 2):
                    for i in range(2):
                        ml_proj(h0 + i, PBS[i], '_%d' % i)
                        MX[i]['PB'] = PBS[i]
                    ptiles = list(range(ntile))
                    MX[0]['ub'] = [7]
                    MX[1]['ub'] = [3]
                    lists = []
                    for i in range(2):
                        S.start_record()
                        ml_rec(h0 + i, MX[i], ptiles)
                        lists.append(S.stop_record())
                    S.replay_interleaved(lists)
                    if samp:
                        MX[0]['ub'] = [7, 0, 1, 2, 3]
                        ml_rec(h0, MX[0], [ntile])
                        MX[1]['ub'] = [3, 4, 5, 6, 7]
                        ml_rec(h0 + 1, MX[1], [ntile])
                if final:
                    for h in range(ML_H):
                        S.dma('sp', o_Cp[h].rearrange("(c p) v -> p c v", p=128), Cst[h][:], reads=['Cst%d' % h])
                    S.op('pe', lambda e: e.matmul(ps[4][0:ML_H * 2, 0:128], lhsT=nst[:].rearrange("p h c -> p (h c)"),
                                                  rhs=ident[:], start=True, stop=True), reads=['nst%d' % hh for hh in range(ML_H)] + ['ident'], writes=[PK[4]])
                    copy_op('dve', hbuf[0:ML_H * 2, 0:128], ps[4][0:ML_H * 2, 0:128], [PK[4]], ['hbuf_0'])
                    S.dma('sp', o_np, hbuf[0:ML_H * 2, 0:128], reads=['hbuf_0'])
                    S.dma('sp', o_mp, mst[0:1, :], reads=['mst%d' % hh for hh in range(ML_H)])
                    if samp:
                        S.op('pe', lambda e: e.matmul(ps[4][:, 0:128], lhsT=nall[:], rhs=ident[:], start=True, stop=True),
                             reads=['nall', 'ident'], writes=[PK[4]])
                        copy_op('dve', nrow[:], ps[4][:, 0:128], [PK[4]], ['nrow'])
                        S.dma('sp', o_ns, nrow[:], reads=['nrow'])
                        S.op('pe', lambda e: e.matmul(ps[4][0:NSQ, 256:256 + ML_H], lhsT=lastind[:], rhs=mnew_all[:],
                                                      start=True, stop=True), reads=['lastind', 'mnew_all'], writes=[PK[4]])
                        copy_op('dve', mout[:], ps[4][0:NSQ, 256:256 + ML_H], [PK[4]], ['mout'])
                        S.dma('sp', o_ms, mout[:], reads=['mout'])
                S.barrier()
            chk_holder[0]('ml')

            with contextlib.ExitStack() as st:
                ro = [0]

                def rview2(n, inner=None):
                    v = R[:, ro[0]:ro[0] + n]
                    if inner is not None:
                        v = v.rearrange("p (a b) -> p a b", b=inner)
                    ro[0] += n
                    return v
                qg = [rview2(Mtok) for i in range(2)]
                kg = [rview2(Mtok) for i in range(2)]
                eG = [sb(st, "eG%d" % i, [128, Mtok]) for i in range(2)]
                i_tok = [rview2(ntl * HG_DV, HG_DV) for i in range(2)]
                if full:
                    sg = [rview2(ntl * HG_DV, HG_DV) for i in range(2)]
                fa = sb(st, "fa", [128, Mtok])
                fb = sb(st, "fb", [128, Mtok])
                qf = sb(st, "qf", [128, Mtok])
                QB, NSG = 4, 2
                hres = []
                for i in range(2):
                    X = {'Sbf': sb(st, "Sbf", [128, HG_DV], BF16), 'aTm': sb(st, "aTm", [128, 128], BF16),
                         'kgt': sb(st, "kgt", [128, 128], BF16), 'obuf': sb(st, "obuf", [128, HG_DV]),
                         'tmpS': sb(st, "tmpS", [128, HG_DV]), 'hc': sb(st, "hc", [128, 4]),
                         'banks': (5, 6, 7, 0) if i == 0 else (1, 2, 3, 4)}
                    if samp:
                        X['Sf'] = [sb(st, "Sf%d" % j, [128, QB, HG_DV]) for j in range(NSG)]
                        X['Sfb'] = [sb(st, "Sfb%d" % j, [128, QB, HG_DV], BF16) for j in range(1)]
                        X['qgq'] = [sb(st, "qgq%d" % j, [128, 128], BF16) for j in range(2)]
                        X['kgq'] = [sb(st, "kgq%d" % j, [128, 128], BF16) for j in range(2)]
                    hres.append(X)
                obuf = hres[0]['obuf']
                for pr in range(HG_H // 2):
                    if full:
                        wb, wk = wget('w_in', w_in, 0, KC, off['hq'] + pr * BW, BW)
                        for i in range(2):
                            for (t0, n) in ntiles(Mtok):
                                pt, pk = nextps()
                                proj_F(wb, wk, i * 128, 128, xT, xkey, KC, t0, n, pt, pk)
                                copy_op(evac_eng(), (qf if i == 0 else fb)[:, t0:t0 + n], pt[:, 0:n], [pk],
                                        ['qf' if i == 0 else 'fb'])
                    wb, wk = wget('w_in', w_in, 0, KC, off['hf'] + pr * BW, BW)
                    for i in range(2):
                        hd = 2 * pr + i
                        for (t0, n) in ntiles(Mtok):
                            pt, pk = nextps()
                            proj_F(wb, wk, i * 128, 128, xT, xkey, KC, t0, n, pt, pk)
                            sigmoid_from('act', fa[:, t0:t0 + n], pt[:, 0:n], [pk], ['fa'])
                        S.op('dve', lambda e, hd=hd: e.tensor_scalar(out=fa[:], in0=fa[:], scalar1=lbc[:, HG_H + hd:HG_H + hd + 1],
                                                                     scalar2=lbc[:, hd:hd + 1], op0=ALU.mult, op1=ALU.add),
                             reads=['fa', 'lbc'], writes=['fa'])
                        S.op('act', lambda e, i=i: e.activation(out=eG[i][:], in_=fa[:], func=AF.Ln), reads=['fa'],
                             writes=['eG%d' % i])
                        rs = rmask[:, 0:Mtok]
                        S.op('dve', lambda e, i=i: e.tensor_tensor_scan(out=eG[i][:], data0=rs, data1=eG[i][:], initial=0.0,
                                                                        op0=ALU.mult, op1=ALU.add),
                             reads=['rst', 'eG%d' % i], writes=['eG%d' % i])
                        S.op('dve', lambda e: e.tensor_scalar(out=fa[:], in0=fa[:], scalar1=-1.0, scalar2=1.0, op0=ALU.mult,
                                                              op1=ALU.add), reads=['fa'], writes=['fa'])
                        tq = sb(st, "tq%d_%d" % (pr, i), [1, 1]) if False else None
                        S.op('act', lambda e, i=i: e.activation(out=kg[i][:], in_=eG[i][:], func=AF.Exp, scale=-1.0),
                             reads=['eG%d' % i], writes=['kg%d' % i])
                        S.op('dve', lambda e, i=i: e.tensor_tensor(out=kg[i][:], in0=kg[i][:], in1=fa[:], op=ALU.mult),
                             reads=['kg%d' % i, 'fa'], writes=['kg%d' % i])
                        S.op('act', lambda e, i=i: e.activation(out=eG[i][:], in_=eG[i][:], func=AF.Exp),
                             reads=['eG%d' % i], writes=['eG%d' % i])
                        if full:
                            qsrc, qk_ = (qf, 'qf') if i == 0 else (fb, 'fb')
                            S.op('dve', lambda e, i=i, qsrc=qsrc: e.tensor_tensor(out=qg[i][:], in0=qsrc[:], in1=eG[i][:],
                                                                                  op=ALU.mult),
                                 reads=[qk_, 'eG%d' % i], writes=['qg%d' % i])
                    for i in range(2):
                        hd = 2 * pr + i
                        wb, wk = wget('w_in', w_in, 0, KC, off['hi'] + hd * HG_DV, BW)
                        for ti in range(ntl):
                            pt, pk = nextps()
                            proj_T(wb, wk, BW, xT, xkey, KC, ti * 128, pt, pk)
                            copy_op(evac_eng(), i_tok[i][:, ti, :], pt[:, 0:BW], [pk], ['i_tok%d' % i])
                    if full:
                        for i in range(2):
                            hd = 2 * pr + i
                            wb, wk = wget('w_in', w_in, 0, KC, off['hg'] + hd * HG_DV, BW)
                            for ti in range(ntl):
                                pt, pk = nextps()
                                proj_T(wb, wk, BW, xT, xkey, KC, ti * 128, pt, pk)
                                sigmoid_from('act', obuf[:], pt[:, 0:BW], [pk], ['obuf'])
                                S.op('dve', lambda e, i=i, ti=ti, pt=pt: e.tensor_tensor(out=sg[i][:, ti, :], in0=pt[:, 0:BW],
                                                                                         in1=obuf[:], op=ALU.mult),
                                     reads=[pk, 'obuf'], writes=['sg%d' % i])
                    def hg_chain(i, hd):
                        X = hres[i]
                        Sbf, aTm, kgt, obuf, tmpS, hc = X['Sbf'], X['aTm'], X['kgt'], X['obuf'], X['tmpS'], X['hc']
                        bA, bO, bT, bS = X['banks']
                        kA, kO, kT, kS = PK[bA], PK[bO], PK[bT], PK[bS]
                        sfx = '_%d' % i
                        Sk = 'Sst%d' % hd
                        qgk, kgk, eGk, itk = 'qg%d' % i, 'kg%d' % i, 'eG%d' % i, 'i_tok%d' % i
                        copy_op('act', Sbf[:], Sst[hd][:], [Sk], ['Sbf' + sfx])
                        for ti in range(ntl):
                            is_s = samp and ti == ntl - 1
                            tok = slice(ti * 128, (ti + 1) * 128)
                            mm = mst_b if is_s else mst_c
                            mmk = 'mst_b' if is_s else 'mst_c'
                            if full:
                                S.op('pe', lambda e: e.matmul(ps[bA][:, 0:128], lhsT=kg[i][:, tok], rhs=qg[i][:, tok],
                                                              start=True, stop=True), reads=[kgk, qgk], writes=[kA])
                                S.op('dve', lambda e: e.tensor_tensor(out=aTm[:], in0=ps[bA][:, 0:128], in1=mm[:], op=ALU.mult),
                                     reads=[kA, mmk], writes=['aTm' + sfx])
                                S.op('pe', lambda e: e.matmul(ps[bO][:, 0:HG_DV], lhsT=aTm[:], rhs=i_tok[i][:, ti, :],
                                                              start=True, stop=False), reads=['aTm' + sfx, itk], writes=[kO])
                            S.op('pe', lambda e: e.matmul(ps[bA][:, 128:256], lhsT=kg[i][:, tok], rhs=identb[:],
                                                          start=True, stop=True), reads=[kgk, 'identb'], writes=[kA])
                            copy_op('act', kgt[:], ps[bA][:, 128:256], [kA], ['kgt' + sfx])
                            if not is_s:
                                if full:
                                    S.op('pe', lambda e: e.matmul(ps[bO][:, 0:HG_DV], lhsT=qg[i][:, tok], rhs=Sbf[:],
                                                                  start=False, stop=True), reads=[qgk, 'Sbf' + sfx], writes=[kO])
                                S.op('pe', lambda e: e.matmul(ps[bS][:, 0:HG_DV], lhsT=kgt[:], rhs=i_tok[i][:, ti, :],
                                                              start=True, stop=True), reads=['kgt' + sfx, itk], writes=[kS])
                                S.op('dve', lambda e: e.tensor_tensor(out=tmpS[:], in0=Sst[hd][:], in1=ps[bS][:, 0:HG_DV],
                                                                      op=ALU.add), reads=[Sk, kS], writes=['tmpS' + sfx])
                                ecol = eG[i][:, ti * 128 + 127:ti * 128 + 128]
                                S.op('act', lambda e: e.activation(out=Sst[hd][:], in_=tmpS[:], func=AF.Identity, scale=ecol),
                                     reads=['tmpS' + sfx, eGk, 'Sbf' + sfx], writes=[Sk])
                                copy_op('dve', Sbf[:], Sst[hd][:], [Sk], ['Sbf' + sfx])
                            else:
                                Sf, Sfb, qgq, kgq = X['Sf'], X['Sfb'], X['qgq'], X['kgq']

                                def sload(g):
                                    S.dma('sp', Sf[g % NSG][:], st_S[g * QB:(g + 1) * QB, hd].rearrange("q k v -> k q v"),
                                          writes=['Sf%d%s' % (g % NSG, sfx)])
                                sload(0)
                                for g in range(NSQ // QB):
                                    if g + 1 < NSQ // QB:
                                        sload(g + 1)
                                    sfg, sfk = Sf[g % NSG], 'Sf%d%s' % (g % NSG, sfx)
                                    sfbg, sfbk = Sfb[0], 'Sfb0' + sfx
                                    if full:
                                        copy_op('act', sfbg[:], sfg[:], [sfk], [sfbk])
                                    for j in range(QB):
                                        q = g * QB + j
                                        sf, sfb = sfg[:, j, :], sfbg[:, j, :]
                                        qq, qqk = qgq[q % 2], 'qgq%d%s' % (q % 2, sfx)
                                        kq, kqk = kgq[q % 2], 'kgq%d%s' % (q % 2, sfx)
                                        if full:
                                            S.op('dve', lambda e: e.tensor_tensor(out=qq[:], in0=qg[i][:, tok], in1=blkb[:, q, :],
                                                                                  op=ALU.mult), reads=[qgk, 'blkb'], writes=[qqk])
                                        S.op('dve', lambda e: e.tensor_scalar(out=kq[:], in0=kgt[:], scalar1=ind[:, q:q + 1],
                                                                              scalar2=None, op0=ALU.mult),
                                             reads=['kgt' + sfx, 'ind'], writes=[kqk])
                                        if full:
                                            S.op('pe', lambda e: e.matmul(ps[bO][:, 0:HG_DV], lhsT=qq[:], rhs=sfb,
                                                                          start=False, stop=(q == NSQ - 1)),
                                                 reads=[qqk, sfbk], writes=[kO])
                                        S.op('pe', lambda e: e.matmul(ps[bS][:, 0:HG_DV], lhsT=kq[:], rhs=i_tok[i][:, ti, :],
                                                                      start=True, stop=True), reads=[kqk, itk], writes=[kS])
                                        S.op('dve', lambda e: e.tensor_tensor(out=tmpS[:], in0=sf, in1=ps[bS][:, 0:HG_DV],
                                                                              op=ALU.add), reads=[sfk, kS], writes=['tmpS' + sfx])
                                        ecol = eG[i][:, ti * 128 + q * SQL + SQL - 1:ti * 128 + q * SQL + SQL]
                                        S.op('act', lambda e: e.activation(out=sf, in_=tmpS[:], func=AF.Identity, scale=ecol),
                                             reads=['tmpS' + sfx, eGk, sfbk], writes=[sfk])
                                    S.dma('sp', o_Ss[g * QB:(g + 1) * QB, hd].rearrange("q k v -> k q v"), sfg[:], reads=[sfk])
                            if full:
                                S.op('act', lambda e: e.activation(out=obuf[:], in_=ps[bO][:, 0:HG_DV], func=AF.Square,
                                                                   accum_out=hc[:, 0:1]), reads=[kO],
                                     writes=['obuf' + sfx, 'hc0' + sfx])
                                S.op('act', lambda e: e.activation(out=hc[:, 1:2], in_=hc[:, 0:1], func=AF.Ln, scale=1.0 / HG_DV,
                                                                   bias=LN_EPS), reads=['hc0' + sfx], writes=['hc1' + sfx])
                                S.op('act', lambda e: e.activation(out=hc[:, 1:2], in_=hc[:, 1:2], func=AF.Exp, scale=-0.5),
                                     reads=['hc1' + sfx], writes=['hc1' + sfx])
                                S.op('dve', lambda e: e.scalar_tensor_tensor(
                                    out=obuf[:], in0=ps[bO][:, 0:HG_DV], scalar=hc[:, 1:2], in1=sg[i][:, ti, :],
                                    op0=ALU.mult, op1=ALU.mult), reads=[kO, 'hc1' + sfx, 'sg%d' % i, 'obuf' + sfx],
                                    writes=['obuf' + sfx])
                                for j in range(2):
                                    S.op('pe', lambda e: e.matmul(ps[bT][:, j * 128:(j + 1) * 128],
                                                                  lhsT=obuf[:, j * 128:(j + 1) * 128], rhs=ident[:],
                                                                  start=True, stop=True), reads=['obuf' + sfx, 'ident'],
                                         writes=[kT])
                                for j in range(2):
                                    ch = 2 * hd + j
                                    if j == 0:
                                        S.op('act', lambda e: e.activation(
                                            out=brB[:, ch, tok], in_=ps[bT][:, j * 128:(j + 1) * 128], func=AF.Identity,
                                            scale=gcol_b[:, ch:ch + 1]), reads=[kT, 'gcol_b'], writes=['brB' + sfx])
                                    else:
                                        S.op('dve', lambda e: e.tensor_scalar(
                                            out=brB[:, ch, tok], in0=ps[bT][:, j * 128:(j + 1) * 128],
                                            scalar1=gcol_b[:, ch:ch + 1], scalar2=None, op0=ALU.mult),
                                            reads=[kT, 'gcol_b'], writes=['brB' + sfx])

                    lists = []
                    for i in range(2):
                        S.start_record()
                        hg_chain(i, 2 * pr + i)
                        lists.append(S.stop_record())
                    S.replay_interleaved(lists)
                if final:
                    for hd in range(HG_H):
                        S.dma('sp', o_Sp[hd], Sst[hd][:], reads=['Sst%d' % hd])
                S.barrier()


        R = sb(es, "R", [128, cfg.RSZ], BF16)
        NMV, NHV = cfg.MLV // 128, cfg.HGV // 128
        DFF, FG, NFG = cfg.DFF, cfg.FG, cfg.NFG

        def layernorm(zt, zk, gt, bt, st, lst, lmv, lc):
            nchk = (D + 511) // 512
            for j in range(nchk):
                a, b_ = j * 512, min(D, (j + 1) * 512)
                S.op('dve', lambda e, j=j, a=a, b_=b_: e.bn_stats(out=lst[:, j, :], in_=zt[:, a:b_]), reads=[zk], writes=['lst'])
            S.op('dve', lambda e: e.bn_aggr(out=lmv[:], in_=lst[:].rearrange("p a b -> p (a b)")), reads=['lst'], writes=['lmv'])
            S.op('act', lambda e: e.activation(out=lc[:, 0:1], in_=lmv[:, 1:2], func=AF.Ln, bias=LN_EPS), reads=['lmv'],
                 writes=['lc0'])
            S.op('act', lambda e: e.activation(out=lc[:, 0:1], in_=lc[:, 0:1], func=AF.Exp, scale=-0.5), reads=['lc0'],
                 writes=['lc0'])
            S.op('dve', lambda e: e.tensor_scalar(out=lc[:, 1:2], in0=lmv[:, 0:1], scalar1=-1.0, scalar2=lc[:, 0:1],
                                                  op0=ALU.mult, op1=ALU.mult), reads=['lmv', 'lc0'], writes=['lc1'])
            S.op('act', lambda e: e.activation(out=zt, in_=zt, func=AF.Identity, scale=lc[:, 0:1], bias=lc[:, 1:2]),
                 reads=[zk, 'lc0', 'lc1'], writes=[zk])
            S.op('dve', lambda e: e.tensor_tensor(out=zt, in0=zt, in1=gt[:], op=ALU.mult), reads=[zk, 'gb'], writes=[zk])
            S.op('pool', lambda e: e.tensor_tensor(out=zt, in0=zt, in1=bt[:], op=ALU.add), reads=[zk, 'bb'], writes=[zk])

        def run_group(x_ap, rows, nprompt, full, samp, final, rmask=None):
            ntl = len(rows)
            Mt = ntl * 128
            with contextlib.ExitStack() as stG:
                xT = sb(stG, "xT", [128, KC, Mt], BF16)
                brA = brB = None
                if full:
                    brA = sb(stG, "brA", [128, NMV, Mt], BF16)
                    brB = sb(stG, "brB", [128, NHV, Mt], BF16)
                load_xT(x_ap, rows, xT, 'xT')
                chk('ldx')
                mixer_pass(xT, 'xT', nprompt, full, brA, brB, samp, R, final, rmask)
                if full:
                    chk('mix')
                if not full:
                    S.barrier()
                    return
                mrg = R[:, 0:KC * Mt].rearrange("p (a b) -> p a b", b=Mt)
                with contextlib.ExitStack() as st:
                    sga = sb(st, "sga", [128, 2, Mt])
                    sgb = sb(st, "sgb", [128, 2, Mt])
                    m1 = sb(st, "m1", [128, 2, Mt])
                    for d0 in range(0, D, BW):
                        nsub = min(BW, D - d0) // 128
                        for (gname, dst, dk) in (('ga', sga, 'sga'), ('gb', sgb, 'sgb')):
                            wb, wk = wget('w_in', w_in, 0, KC, off[gname] + d0, nsub * 128)
                            for i in range(nsub):
                                for (t0, n) in ntiles(Mt):
                                    pt, pk = nextps()
                                    proj_F(wb, wk, i * 128, 128, xT, 'xT', KC, t0, n, pt, pk)
                                    sigmoid_from('act', dst[:, i, t0:t0 + n], pt[:, 0:n], [pk], [dk])
                        wb, wk = wget('w_ba', w_ba, 0, NMV, d0, nsub * 128)
                        for i in range(nsub):
                            for (t0, n) in ntiles(Mt):
                                pt, pk = nextps()
                                proj_F(wb, wk, i * 128, 128, brA, 'brA', NMV, t0, n, pt, pk)
                                S.op('dve', lambda e, i=i, t0=t0, n=n, pt=pt: e.tensor_tensor(
                                    out=m1[:, i, t0:t0 + n], in0=pt[:, 0:n], in1=sga[:, i, t0:t0 + n], op=ALU.mult),
                                    reads=[pk, 'sga'], writes=['m1'])
                        wb, wk = wget('w_bb', w_bb, 0, NHV, d0, nsub * 128)
                        for i in range(nsub):
                            for (t0, n) in ntiles(Mt):
                                pt, pk = nextps()
                                proj_F(wb, wk, i * 128, 128, brB, 'brB', NHV, t0, n, pt, pk)
                                S.op('dve', lambda e, i=i, t0=t0, n=n, pt=pt: e.tensor_tensor(
                                    out=sgb[:, i, t0:t0 + n], in0=pt[:, 0:n], in1=sgb[:, i, t0:t0 + n], op=ALU.mult),
                                    reads=[pk, 'sgb'], writes=['sgb'])
                                S.op('pool', lambda e, i=i, t0=t0, n=n, d0=d0: e.tensor_tensor(
                                    out=mrg[:, d0 // 128 + i, t0:t0 + n], in0=sgb[:, i, t0:t0 + n], in1=m1[:, i, t0:t0 + n],
                                    op=ALU.add), reads=['sgb', 'm1'], writes=['mrg'])
                    S.barrier()
            S.barrier()
            chk('merge')
            with contextlib.ExitStack() as st:
                z = sb(st, "z", [128, ntl, D])
                gt = sb(st, "gt", [128, D])
                bt = sb(st, "bt", [128, D])
                hid = sb(st, "hid", [128, FG, Mt], BF16)
                rtmp = sb(st, "rtmp", [128, 512])
                lst = sb(st, "lst", [128, (D + 511) // 512, 6])
                lmv = sb(st, "lmv", [128, 2])
                lc = sb(st, "lc", [128, 2])
                zk = lambda ti: 'z%d' % ti
                for ti in range(ntl):
                    S.dma('sp', z[:, ti, :], x_ap[rows[ti]:rows[ti] + 128, :], writes=[zk(ti)])
                S.dma('sp', gt[:], ln1_g.partition_broadcast(128), writes=['gb'])
                S.dma('sp', bt[:], ln1_b.partition_broadcast(128), writes=['bb'])
                for d0 in range(0, D, BW):
                    wb, wk = wget('w_out', w_out, 0, KC, d0, BW)
                    for ti in range(ntl):
                        pt, pk = nextps()
                        proj_T(wb, wk, BW, mrg, 'mrg', KC, ti * 128, pt, pk)
                        S.op('dve', lambda e, ti=ti, d0=d0, pt=pt: e.scalar_tensor_tensor(
                            out=z[:, ti, d0:d0 + BW], in0=z[:, ti, d0:d0 + BW], scalar=ALPHA, in1=pt[:, 0:BW],
                            op0=ALU.mult, op1=ALU.add), reads=[zk(ti), pk], writes=[zk(ti)])
                for ti in range(ntl):
                    layernorm(z[:, ti, :], zk(ti), gt, bt, st, lst, lmv, lc)
                S.barrier()
                x1T = mrg
                zb = [sb(st, "zb%d" % i, [128, D], BF16) for i in range(2)]
                for ti in range(ntl):
                    zbt, zbk = zb[ti % 2], 'zb%d' % (ti % 2)
                    copy_op('act' if ti % 2 else 'dve', zbt[:], z[:, ti, :], [zk(ti)], [zbk])
                    for g in range(0, KC, 4):
                        n = min(4, KC - g)
                        pt, pk = nextps()
                        for j in range(n):
                            S.op('pe', lambda e, j=j, g=g, ti=ti, pt=pt: e.matmul(
                                pt[:, j * 128:(j + 1) * 128], lhsT=zbt[:, (g + j) * 128:(g + j + 1) * 128], rhs=identb[:],
                                start=True, stop=True), reads=[zbk, 'identb'], writes=[pk])
                        copy_op(evac_eng(), x1T[:, g:g + n, ti * 128:(ti + 1) * 128],
                                pt[:, 0:n * 128].rearrange("p (a b) -> p a b", b=128), [pk], ['x1T'])
                S.dma('sp', gt[:], ln2_g.partition_broadcast(128), writes=['gb'])
                S.dma('sp', bt[:], ln2_b.partition_broadcast(128), writes=['bb'])
                for fg in range(NFG):
                    for f0 in range(0, FG * 128, BW):
                        wb, wk = wget('w_up', w_up, 0, KC, fg * FG * 128 + f0, BW)
                        for i in range(BW // 128):
                            for (t0, n) in ntiles(Mt):
                                pt, pk = nextps()
                                proj_F(wb, wk, i * 128, 128, x1T, 'x1T', KC, t0, n, pt, pk)
                                S.op('act', lambda e, n=n, pt=pt: e.activation(out=rtmp[:, 0:n], in_=pt[:, 0:n], func=AF.Relu),
                                     reads=[pk], writes=['rtmp'])
                                S.op('dve', lambda e, i=i, f0=f0, t0=t0, n=n: e.tensor_tensor(
                                    out=hid[:, f0 // 128 + i, t0:t0 + n], in0=rtmp[:, 0:n], in1=rtmp[:, 0:n], op=ALU.mult),
                                    reads=['rtmp'], writes=['hid'])
                    for d0 in range(0, D, BW):
                        wb, wk = wget('w_dn', w_dn, fg * FG * 128, FG, d0, BW)
                        for ti in range(ntl):
                            pt, pk = nextps()
                            proj_T(wb, wk, BW, hid, 'hid', FG, ti * 128, pt, pk)
                            if fg == 0:
                                S.op('dve', lambda e, ti=ti, d0=d0, pt=pt: e.scalar_tensor_tensor(
                                    out=z[:, ti, d0:d0 + BW], in0=z[:, ti, d0:d0 + BW], scalar=ALPHA, in1=pt[:, 0:BW],
                                    op0=ALU.mult, op1=ALU.add), reads=[zk(ti), pk], writes=[zk(ti)])
                            else:
                                S.op('dve', lambda e, ti=ti, d0=d0, pt=pt: e.tensor_tensor(
                                    out=z[:, ti, d0:d0 + BW], in0=z[:, ti, d0:d0 + BW], in1=pt[:, 0:BW], op=ALU.add),
                                    reads=[zk(ti), pk], writes=[zk(ti)])
                for ti in range(ntl):
                    layernorm(z[:, ti, :], zk(ti), gt, bt, st, lst, lmv, lc)
                    S.dma('sp', y_main[rows[ti]:rows[ti] + 128, :], z[:, ti, :], reads=[zk(ti)])
                S.barrier()

        chkcnt = {}

        def chk(tag):
            chkcnt[tag] = chkcnt.get(tag, 0) + 1
            if DBG['stop'] == tag or DBG['stop'] == '%s#%d' % (tag, chkcnt[tag]):
                S.dead = True

        chk_holder[0] = chk

        def _drive():
            chk('consts')
            with contextlib.ExitStack() as stp:
                rstp = sb(stp, "rstp", [128, NTP * 128])
                P(lambda e: e.memset(rstp[:], 1.0), ['rstp'])
                P(lambda e: e.memset(rstp[:].rearrange("p (a b) -> p a b", b=128)[:, :, 0:1], 0.0), ['rstp'], ['rstp'])
                S.barrier()
                run_group(x_pre, [t * 128 for t in range(NTP)], NTP, False, False, False, rmask=rstp)
            chk('pre')
            groups = list(range(0, NTP, GT))
            for gi, g0 in enumerate(groups):
                last = gi == len(groups) - 1
                rows = [t * 128 for t in range(g0, min(NTP, g0 + GT))]
                npr = len(rows)
                if last:
                    rows = rows + [NTP * 128]
                run_group(x_main, rows, npr, True, last, last)
                chk('g%d' % gi)
        try:
            _drive()
        except _Stop:
            pass
        S.finish()
    return nc, S


_CACHE = {}


def _get_program(cfg_key):
    if cfg_key not in _CACHE:
        cfg = Cfg(*cfg_key)
        rec = []
        build(cfg, wplan=None, record=rec)
        nc, S = build(cfg, wplan=rec, record=None)
        _CACHE[cfg_key] = (cfg, nc)
    return _CACHE[cfg_key]


def run_module(inputs, D, DFF, SEQ, BATCH, DEC_BATCH, core_ids=None):
    TH = SEQ // 2
    cfg, nc = _get_program((D, DFF, TH))
    ncores = 2 * BATCH
    assert DEC_BATCH == ncores * NSQ
    f = lambda a: np.ascontiguousarray(np.asarray(a, dtype=np.float32))
    xp, xs = f(inputs["x_prompt"]), f(inputs["x_sample"])
    stC, stn = f(inputs["state_mlstm_C"])[0], f(inputs["state_mlstm_n"])[0]
    stm, stS = f(inputs["state_mlstm_m"])[0], f(inputs["state_hgrn_S"])[0]
    shared = {
        "lb_logits": f(inputs["hg_lb_logits"]).reshape(2 * HG_H, 128),
        "w_in": f(inputs["w_in"])[0], "b_ig": f(inputs["b_ig"]).reshape(1, ML_H), "b_fg": f(inputs["b_fg"]).reshape(1, ML_H),
        "ml_g": f(inputs["ml_norm_g"]).reshape(-1, 128), "hg_g": f(inputs["hg_norm_g"]).reshape(-1, 128),
        "w_ba": f(inputs["w_branch_a"])[0], "w_bb": f(inputs["w_branch_b"])[0], "w_out": f(inputs["w_out"])[0],
        "ln1_g": f(inputs["ln1_g"]).reshape(1, D), "ln1_b": f(inputs["ln1_b"]).reshape(1, D),
        "w_up": f(inputs["w_up"])[0], "w_dn": f(inputs["w_down"])[0],
        "ln2_g": f(inputs["ln2_g"]).reshape(1, D), "ln2_b": f(inputs["ln2_b"]).reshape(1, D),
    }
    in_maps = []
    for c in range(ncores):
        b, half = c // 2, c % 2
        sl = slice(c * NSQ, (c + 1) * NSQ)
        m = dict(shared)
        m["x_pre"] = np.ascontiguousarray(xp[b, 0:TH]) if half == 1 else np.zeros((TH, D), np.float32)
        m["x_main"] = np.ascontiguousarray(np.concatenate([xp[b, half * TH:(half + 1) * TH], xs[sl].reshape(NSQ * SQL, D)], 0))
        m["st_C"] = np.ascontiguousarray(stC[sl])
        m["st_n"] = np.ascontiguousarray(stn[sl].reshape(NSQ * ML_H * 2, 128))
        m["st_m"] = np.ascontiguousarray(stm[sl])
        m["st_S"] = np.ascontiguousarray(stS[sl])
        in_maps.append(m)
    res = run_bass_kernel_spmd(nc, in_maps, core_ids=list(range(ncores)) if core_ids is None else core_ids)
    rs = res.results
    y_p = np.zeros((BATCH, SEQ, D), np.float32)
    y_s = np.zeros((DEC_BATCH, SQL, D), np.float32)
    Cp = np.zeros((1, BATCH, ML_H, ML_DK, ML_DV), np.float32)
    n_p = np.zeros((1, BATCH, ML_H, ML_DK), np.float32)
    mp = np.zeros((1, BATCH, ML_H), np.float32)
    Sp = np.zeros((1, BATCH, HG_H, HG_DK, HG_DV), np.float32)
    Cs = np.zeros((1, DEC_BATCH, ML_H, ML_DK, ML_DV), np.float32)
    ns = np.zeros((1, DEC_BATCH, ML_H, ML_DK), np.float32)
    ms = np.zeros((1, DEC_BATCH, ML_H), np.float32)
    Ss = np.zeros((1, DEC_BATCH, HG_H, HG_DK, HG_DV), np.float32)
    for c in range(ncores):
        b, half = c // 2, c % 2
        r = rs[c]
        sl = slice(c * NSQ, (c + 1) * NSQ)
        y_p[b, half * TH:(half + 1) * TH] = r["y_main"][0:TH]
        y_s[sl] = r["y_main"][TH:].reshape(NSQ, SQL, D)
        if half == 1:
            Cp[0, b] = r["o_Cp"]
            n_p[0, b] = r["o_np"].reshape(ML_H, ML_DK)
            mp[0, b] = r["o_mp"].reshape(ML_H)
            Sp[0, b] = r["o_Sp"]
        Cs[0, sl] = r["o_Cs"]
        ns[0, sl] = r["o_ns"].reshape(NSQ, ML_H, ML_DK)
        ms[0, sl] = r["o_ms"]
        Ss[0, sl] = r["o_Ss"]
    return (y_p, y_s, Cp, n_p, mp, Sp, Cs, ns, ms, Ss)


def kernel(**inputs):
    return run_module(inputs, D=2048, DFF=8192, SEQ=2048, BATCH=4, DEC_BATCH=128)
```

```python
import contextlib
import numpy as np
import concourse.bass as bass
import concourse.mybir as mybir
from concourse.alu_op_type import AluOpType as ALU
from concourse.bass_utils import run_bass_kernel_spmd

F32 = mybir.dt.float32
BF16 = mybir.dt.bfloat16
AF = mybir.ActivationFunctionType
AX = mybir.AxisListType

ML_H, ML_DK, ML_DV = 4, 256, 512
HG_H, HG_DK, HG_DV = 8, 128, 256
LN_EPS = 1e-5
ALPHA = 2.0 ** 0.25
NSQ = 16
SQL = 8
BW = 256
NEG = -60000.0


class _Proxy:
    def __init__(self):
        self.call = None

    def __getattr__(self, name):
        def f(*a, **k):
            self.call = (name, a, k)
            return self
        return f


class Sch:
    def __init__(self, nc, ndma=6):
        self.nc = nc
        self.E = {'pe': nc.tensor, 'act': nc.scalar, 'dve': nc.vector, 'pool': nc.gpsimd, 'sp': nc.sync}
        self.semh, self.cnt = {}, {}
        for k in self.E:
            self.semh[k] = nc.alloc_semaphore("c_" + k)
            self.cnt[k] = 0
        self.waited = {k: {} for k in self.E}
        self.dq = {}
        for q in ('sp', 'pool'):
            sems = []
            for j in range(ndma):
                nm = "d_%s%d" % (q, j)
                self.semh[nm] = nc.alloc_semaphore(nm)
                self.cnt[nm] = 0
                sems.append(nm)
            self.dq[q] = [sems, 0]
        self.track = {}
        self.n_inst = 0
        self.dead = False
        self.rec = None

    def _need(self, eng, reads, writes):
        need = {}

        def add(tok):
            if tok is None:
                return
            s, v = tok
            if eng == 'pe' and s == 'pe':
                return
            if self.waited[eng].get(s, 0) < v:
                need[s] = max(need.get(s, 0), v)
        for k in reads:
            t = self.track.get(k)
            if t:
                add(t[0])
        for k in writes:
            t = self.track.get(k)
            if t:
                add(t[0])
                for r in t[1]:
                    add(r)
        for s, v in need.items():
            self.E[eng].wait_ge(self.semh[s], v)
            self.waited[eng][s] = v

    def _upd(self, tok, reads, writes):
        for k in reads:
            t = self.track.setdefault(k, [None, []])
            t[1].append(tok)
            if len(t[1]) > 64:
                t[1] = t[1][-64:] if False else self._compact(t[1])
        for k in writes:
            self.track[k] = [tok, []]

    @staticmethod
    def _compact(lst):
        best = {}
        for s, v in lst:
            best[s] = max(best.get(s, 0), v)
        return list(best.items())

    def start_record(self):
        assert self.rec is None
        self.rec = []

    def stop_record(self):
        r, self.rec = self.rec, None
        return r

    def replay_interleaved(self, lists):
        its = [iter(l) for l in lists]
        live = list(range(len(its)))
        while live:
            for i in list(live):
                item = next(its[i], None)
                if item is None:
                    live.remove(i)
                    continue
                if item[0] == 'op':
                    _, eng, (name, a, k), reads, writes = item
                    self.op(eng, lambda e, name=name, a=a, k=k: getattr(e, name)(*a, **k), reads, writes)
                else:
                    _, q, out, in_, reads, writes, kw = item
                    self.dma(q, out, in_, reads, writes, **kw)

    def op(self, eng, fn, reads=(), writes=()):
        if self.dead:
            return
        if self.rec is not None:
            p = _Proxy()
            fn(p)
            assert p.call is not None
            self.rec.append(('op', eng, p.call, tuple(reads), tuple(writes)))
            return
        pr = [k for k in reads if k.startswith('ps') and k not in writes]
        if pr:
            writes = list(writes) + pr
        self._need(eng, reads, writes)
        inst = fn(self.E[eng])
        self.cnt[eng] += 1
        inst.then_inc(self.semh[eng], 1)
        self._upd((eng, self.cnt[eng]), reads, writes)
        self.n_inst += 1

    def dma(self, q, out, in_, reads=(), writes=(), **kw):
        if self.dead:
            return
        if self.rec is not None:
            self.rec.append(('dma', q, out, in_, tuple(reads), tuple(writes), kw))
            return
        sems, idx = self.dq[q]
        s = sems[idx % len(sems)]
        self.dq[q][1] = idx + 1
        if self.cnt[s] > 0 and self.waited[q].get(s, 0) < self.cnt[s]:
            self.E[q].wait_ge(self.semh[s], self.cnt[s])
            self.waited[q][s] = self.cnt[s]
        self._need(q, reads, writes)
        inst = self.E[q].dma_start(out=out, in_=in_, **kw)
        self.cnt[s] += 16
        inst.then_inc(self.semh[s], 16)
        self._upd((s, self.cnt[s]), reads, writes)
        self.n_inst += 1

    def barrier(self):
        if self.dead:
            return
        assert self.rec is None
        for eng in self.E:
            for s, v in self.cnt.items():
                if v > 0 and s != eng and self.waited[eng].get(s, 0) < v:
                    self.E[eng].wait_ge(self.semh[s], v)
                    self.waited[eng][s] = v
        self.track = {}

    def finish(self):
        self.dead = False
        self.barrier()


class Cfg:
    def __init__(self, D, DFF, TH):
        self.D, self.DFF, self.TH = D, DFF, TH
        self.KC = D // 128
        self.NTP = TH // 128
        self.NT = self.NTP + 1
        self.M = self.NT * 128
        self.GT = max(1, self.NTP // 2)
        self.MG = (self.GT + 1) * 128
        self.RSZ = max(self.KC * self.MG, 2 * (4 * self.MG + 1280 * (self.GT + 1)))
        self.FG = min(16, DFF // 128)
        self.NFG = DFF // (128 * self.FG)
        self.MLQK, self.MLV = ML_H * ML_DK, ML_H * ML_DV
        self.HGK, self.HGV = HG_H * HG_DK, HG_H * HG_DV
        o = 0
        self.off = {}
        for nm, sz in (('mq', self.MLQK), ('mk', self.MLQK), ('mv', self.MLV), ('mi', ML_H), ('mf', ML_H),
                       ('mo', self.MLV), ('hq', self.HGK), ('hf', self.HGK), ('hi', self.HGV), ('hg', self.HGV),
                       ('ga', D), ('gb', D)):
            self.off[nm] = o
            o += sz
        self.DIN = o


DBG = {'stop': None}


class _Stop(Exception):
    pass


def build(cfg, wplan=None, record=None):
    D, KC, TH, NTP, NT, M = cfg.D, cfg.KC, cfg.TH, cfg.NTP, cfg.NT, cfg.M
    off = cfg.off
    nc = bass.Bass("TRN2", target_bir_lowering=False)

    def din(name, shape):
        return nc.dram_tensor(name, list(shape), F32, kind="ExternalInput").ap()

    def dout(name, shape):
        return nc.dram_tensor(name, list(shape), F32, kind="ExternalOutput").ap()

    x_pre = din("x_pre", [TH, D])
    x_main = din("x_main", [M, D])
    st_C = din("st_C", [NSQ, ML_H, ML_DK, ML_DV])
    st_n = din("st_n", [NSQ * ML_H * 2, 128])
    st_m = din("st_m", [NSQ, ML_H])
    st_S = din("st_S", [NSQ, HG_H, HG_DK, HG_DV])
    lb_logits = din("lb_logits", [2 * HG_H, 128])
    w_in = din("w_in", [D, cfg.DIN])
    b_ig = din("b_ig", [1, ML_H])
    b_fg = din("b_fg", [1, ML_H])
    ml_g = din("ml_g", [cfg.MLV // 128, 128])
    hg_g = din("hg_g", [cfg.HGV // 128, 128])
    w_ba = din("w_ba", [cfg.MLV, D])
    w_bb = din("w_bb", [cfg.HGV, D])
    w_out = din("w_out", [D, D])
    ln1_g = din("ln1_g", [1, D])
    ln1_b = din("ln1_b", [1, D])
    w_up = din("w_up", [D, cfg.DFF])
    w_dn = din("w_dn", [cfg.DFF, D])
    ln2_g = din("ln2_g", [1, D])
    ln2_b = din("ln2_b", [1, D])

    y_main = dout("y_main", [M, D])
    o_Cp = dout("o_Cp", [ML_H, ML_DK, ML_DV])
    o_np = dout("o_np", [ML_H * 2, 128])
    o_mp = dout("o_mp", [1, ML_H])
    o_Sp = dout("o_Sp", [HG_H, HG_DK, HG_DV])
    o_Cs = dout("o_Cs", [NSQ, ML_H, ML_DK, ML_DV])
    o_ns = dout("o_ns", [NSQ * ML_H * 2, 128])
    o_ms = dout("o_ms", [NSQ, ML_H])
    o_Ss = dout("o_Ss", [NSQ, HG_H, HG_DK, HG_DV])

    S = Sch(nc)
    es = contextlib.ExitStack()

    uid = [0]

    def sb(stack, name, shape, dt=F32):
        uid[0] += 1
        return stack.enter_context(nc.sbuf_tensor("%s_%d" % (name, uid[0]), list(shape), dt))

    with es:
        ps = [es.enter_context(nc.psum_tensor("ps%d" % i, [128, 512], F32)) for i in range(8)]
        PK = ["ps%d" % i for i in range(8)]

        ident = sb(es, "ident", [128, 128])
        identb = sb(es, "identb", [128, 128], BF16)
        ones = sb(es, "ones", [128, 128])
        onesb = sb(es, "onesb", [128, 2], BF16)
        mst_c = sb(es, "mst_c", [128, 128])
        mst_b = sb(es, "mst_b", [128, 128])
        mts_b = sb(es, "mts_b", [128, 128])
        neg_c = sb(es, "neg_c", [128, 128])
        neg_b = sb(es, "neg_b", [128, 128])
        sel_c = sb(es, "sel_c", [128, 128])
        sel_b = sb(es, "sel_b", [128, 128])
        ind = sb(es, "ind", [128, NSQ])
        lastind = sb(es, "lastind", [128, NSQ])
        indT = sb(es, "indT", [NSQ, 128])
        blkb = sb(es, "blkb", [128, NSQ, 128], BF16)
        GT, MG = cfg.GT, cfg.MG
        rst = sb(es, "rst", [128, MG])
        gcol_a = sb(es, "gcol_a", [128, cfg.MLV // 128])
        gcol_b = sb(es, "gcol_b", [128, cfg.HGV // 128])
        lbc = sb(es, "lbc", [128, 2 * HG_H])
        bigb = sb(es, "bigb", [128, ML_H])
        nbfg = sb(es, "nbfg", [128, ML_H])
        ctmp = sb(es, "ctmp", [128, 128])

        def P(fn, w, r=()):
            S.op('pool', fn, reads=r, writes=w)

        P(lambda e: e.memset(ident[:], 1.0), ['ident'])
        P(lambda e: e.affine_select(out=ident[:], in_=ident[:], pattern=[[-1, 128]], compare_op=ALU.is_equal,
                                    fill=0.0, base=0, channel_multiplier=1), ['ident'], ['ident'])
        P(lambda e: e.tensor_copy(out=identb[:], in_=ident[:]), ['identb'], ['ident'])
        P(lambda e: e.memset(ones[:], 1.0), ['ones'])
        P(lambda e: e.memset(onesb[:], 1.0), ['onesb'])
        P(lambda e: e.memset(mst_c[:], 1.0), ['mst_c'])
        P(lambda e: e.affine_select(out=mst_c[:], in_=mst_c[:], pattern=[[1, 128]], compare_op=ALU.is_ge,
                                    fill=0.0, base=0, channel_multiplier=-1), ['mst_c'], ['mst_c'])
        P(lambda e: e.memset(ctmp[:], 1.0), ['ctmp'])
        P(lambda e: e.affine_select(out=ctmp[:], in_=ctmp[:], pattern=[[-1, 128]], compare_op=ALU.is_ge,
                                    fill=0.0, base=0, channel_multiplier=1), ['ctmp'], ['ctmp'])
        P(lambda e: e.tensor_scalar(out=neg_c[:], in0=ctmp[:], scalar1=-1.0, scalar2=-NEG, op0=ALU.add, op1=ALU.mult),
          ['neg_c'], ['ctmp'])
        P(lambda e: e.memset(sel_c[:], 1.0), ['sel_c'])
        P(lambda e: e.affine_select(out=sel_c[:], in_=sel_c[:], pattern=[[0, 128]], compare_op=ALU.is_equal,
                                    fill=0.0, base=-127, channel_multiplier=1), ['sel_c'], ['sel_c'])
        P(lambda e: e.memset(ind[:], 1.0), ['ind'])
        P(lambda e: e.affine_select(out=ind[:], in_=ind[:], pattern=[[-SQL, NSQ]], compare_op=ALU.is_ge,
                                    fill=0.0, base=0, channel_multiplier=1), ['ind'], ['ind'])
        P(lambda e: e.affine_select(out=ind[:], in_=ind[:], pattern=[[SQL, NSQ]], compare_op=ALU.is_ge,
                                    fill=0.0, base=SQL - 1, channel_multiplier=-1), ['ind'], ['ind'])
        P(lambda e: e.memset(lastind[:], 1.0), ['lastind'])
        P(lambda e: e.affine_select(out=lastind[:], in_=lastind[:], pattern=[[-SQL, NSQ]], compare_op=ALU.is_equal,
                                    fill=0.0, base=-(SQL - 1), channel_multiplier=1), ['lastind'], ['lastind'])
        P(lambda e: e.memset(indT[:], 1.0), ['indT'])
        P(lambda e: e.affine_select(out=indT[:], in_=indT[:], pattern=[[1, 128]], compare_op=ALU.is_ge,
                                    fill=0.0, base=0, channel_multiplier=-SQL), ['indT'], ['indT'])
        P(lambda e: e.affine_select(out=indT[:], in_=indT[:], pattern=[[-1, 128]], compare_op=ALU.is_ge,
                                    fill=0.0, base=SQL - 1, channel_multiplier=SQL), ['indT'], ['indT'])
        P(lambda e: e.memset(blkb[:], 1.0), ['blkb'])
        P(lambda e: e.affine_select(out=blkb[:], in_=blkb[:], pattern=[[-SQL, NSQ], [1, 128]], compare_op=ALU.is_ge,
                                    fill=0.0, base=0, channel_multiplier=0), ['blkb'], ['blkb'])
        P(lambda e: e.affine_select(out=blkb[:], in_=blkb[:], pattern=[[SQL, NSQ], [-1, 128]], compare_op=ALU.is_ge,
                                    fill=0.0, base=SQL - 1, channel_multiplier=0), ['blkb'], ['blkb'])
        S.op('pe', lambda e: e.matmul(ps[0][:, 0:128], lhsT=indT[:], rhs=indT[:], start=True, stop=True),
             reads=['indT'], writes=[PK[0]])
        S.op('dve', lambda e: e.tensor_tensor(out=mst_b[:], in0=ps[0][:, 0:128], in1=mst_c[:], op=ALU.mult),
             reads=[PK[0], 'mst_c'], writes=['mst_b'])
        S.op('dve', lambda e: e.tensor_tensor(out=mts_b[:], in0=ps[0][:, 0:128], in1=ctmp[:], op=ALU.mult),
             reads=[PK[0], 'ctmp'], writes=['mts_b'])
        S.op('dve', lambda e: e.tensor_scalar(out=neg_b[:], in0=mts_b[:], scalar1=-1.0, scalar2=-NEG, op0=ALU.add,
                                              op1=ALU.mult), reads=['mts_b'], writes=['neg_b'])
        S.op('dve', lambda e: e.tensor_copy(out=sel_b[:].rearrange("p (q j) -> p q j", j=SQL),
                                            in_=lastind[:].unsqueeze(2).to_broadcast([128, NSQ, SQL])),
             reads=['lastind'], writes=['sel_b'])
        P(lambda e: e.memset(rst[:], 1.0), ['rst'])
        P(lambda e: e.memset(rst[:, 0:GT * 128].rearrange("p (a b) -> p a b", b=128)[:, :, 0:1], 0.0), ['rst'], ['rst'])
        P(lambda e: e.memset(rst[:, GT * 128:MG].rearrange("p (a b) -> p a b", b=SQL)[:, :, 0:1], 0.0), ['rst'], ['rst'])

        rowt = sb(es, "rowt", [128, 128])

        def load_cols(src_rows, nrows, dst, dkey):
            S.dma('sp', rowt[0:nrows, :], src_rows, writes=['rowt'])
            S.op('pe', lambda e: e.matmul(ps[1][:, 0:nrows], lhsT=rowt[0:nrows, :], rhs=ident[0:nrows, 0:nrows],
                                          start=True, stop=True), reads=['rowt', 'ident'], writes=[PK[1]])
            S.op('dve', lambda e: e.tensor_copy(out=dst, in_=ps[1][:, 0:nrows]), reads=[PK[1]], writes=[dkey])

        load_cols(ml_g, cfg.MLV // 128, gcol_a[:], 'gcol_a')
        load_cols(hg_g, cfg.HGV // 128, gcol_b[:], 'gcol_b')
        load_cols(lb_logits, 2 * HG_H, lbc[:], 'lbc')
        S.op('dve', lambda e: e.tensor_tensor(out=lbc[:, 0:HG_H], in0=lbc[:, 0:HG_H], in1=lbc[:, HG_H:2 * HG_H],
                                              op=ALU.subtract), reads=['lbc'], writes=['lbc'])
        S.op('act', lambda e: e.activation(out=lbc[:, 0:HG_H], in_=lbc[:, 0:HG_H], func=AF.Exp, scale=-1.0),
             reads=['lbc'], writes=['lbc'])
        S.op('act', lambda e: e.activation(out=lbc[:, 0:HG_H], in_=lbc[:, 0:HG_H], func=AF.Ln, bias=1.0),
             reads=['lbc'], writes=['lbc'])
        S.op('act', lambda e: e.activation(out=lbc[:, 0:HG_H], in_=lbc[:, 0:HG_H], func=AF.Exp, scale=-1.0),
             reads=['lbc'], writes=['lbc'])
        S.op('dve', lambda e: e.tensor_scalar(out=lbc[:, HG_H:2 * HG_H], in0=lbc[:, 0:HG_H], scalar1=-1.0, scalar2=1.0,
                                              op0=ALU.mult, op1=ALU.add), reads=['lbc'], writes=['lbc'])
        S.dma('sp', bigb[:], b_ig.partition_broadcast(128), writes=['bigb'])
        S.dma('sp', nbfg[:], b_fg.partition_broadcast(128), writes=['nbfg'])
        S.op('dve', lambda e: e.tensor_scalar(out=nbfg[:], in0=nbfg[:], scalar1=-1.0, scalar2=None, op0=ALU.mult),
             reads=['nbfg'], writes=['nbfg'])

        NWB = 4
        wbuf = [sb(es, "wbuf%d" % i, [128, 16, BW], BF16) for i in range(NWB)]
        wstate = {'issued': 0, 'req': 0}

        def _issue(spec):
            wap, r0, kc, c0, ncol = spec
            i = wstate['issued']
            wstate['issued'] += 1
            buf = wbuf[i % NWB]
            key = 'wbuf%d' % (i % NWB)
            src = wap[r0:r0 + kc * 128, c0:c0 + ncol].rearrange("(c p) n -> p c n", p=128)
            S.dma('pool', buf[:, 0:kc, 0:ncol], src, writes=[key])

        def wget(wname, wap, r0, kc, c0, ncol):
            if record is not None:
                record.append((wname, r0, kc, c0, ncol))
            idx = wstate['req']
            wstate['req'] += 1
            if wstate['issued'] <= idx:
                assert wstate['issued'] == idx
                _issue((wap, r0, kc, c0, ncol))
            if wplan is not None:
                assert wplan[idx] == (wname, r0, kc, c0, ncol), (idx, wplan[idx], (wname, r0, kc, c0, ncol))
                while wstate['issued'] < min(len(wplan), idx + NWB):
                    nm, a, b, c, d = wplan[wstate['issued']]
                    _issue((WMAP[nm], a, b, c, d))
            return wbuf[idx % NWB], 'wbuf%d' % (idx % NWB)

        WMAP = {'w_in': w_in, 'w_ba': w_ba, 'w_bb': w_bb, 'w_out': w_out, 'w_up': w_up, 'w_dn': w_dn}
        psrot = {'i': 0}

        def nextps(lo=0, hi=8):
            i = lo + psrot['i'] % (hi - lo)
            psrot['i'] += 1
            return ps[i], PK[i]

        evrot = {'i': 0}

        def evac_eng():
            evrot['i'] += 1
            return 'dve' if evrot['i'] % 2 else 'act'

        def copy_op(eng, out, in_, reads, writes, scale=None):
            if scale is not None:
                if eng == 'act':
                    S.op('act', lambda e: e.activation(out=out, in_=in_, func=AF.Copy, scale=scale), reads=reads, writes=writes)
                else:
                    S.op(eng, lambda e: e.tensor_scalar(out=out, in0=in_, scalar1=scale, scalar2=None, op0=ALU.mult),
                         reads=reads, writes=writes)
            elif eng == 'act':
                S.op('act', lambda e: e.activation(out=out, in_=in_, func=AF.Copy), reads=reads, writes=writes)
            else:
                S.op(eng, lambda e: e.tensor_copy(out=out, in_=in_), reads=reads, writes=writes)

        def proj_T(wb, wkey, ncol, actT, akey, kc, tok0, pst, pkey):
            for c in range(kc):
                S.op('pe', lambda e, c=c: e.matmul(pst[:, 0:ncol], lhsT=actT[:, c, tok0:tok0 + 128], rhs=wb[:, c, 0:ncol],
                                                   start=(c == 0), stop=(c == kc - 1)),
                     reads=[wkey, akey], writes=[pkey])

        def proj_F(wb, wkey, m0, mcol, actT, akey, kc, tok0, ntok, pst, pkey):
            for c in range(kc):
                S.op('pe', lambda e, c=c: e.matmul(pst[0:mcol, 0:ntok], lhsT=wb[:, c, m0:m0 + mcol],
                                                   rhs=actT[:, c, tok0:tok0 + ntok], start=(c == 0), stop=(c == kc - 1)),
                     reads=[wkey, akey], writes=[pkey])

        def ntiles(total):
            out, t = [], 0
            while t < total:
                n = min(512, total - t)
                out.append((t, n))
                t += n
            return out

        def sigmoid_from(eng_in, out, in_, reads, writes):
            S.op('act', lambda e: e.activation(out=out, in_=in_, func=AF.Exp, scale=-1.0), reads=reads, writes=writes)
            S.op('act', lambda e: e.activation(out=out, in_=out, func=AF.Ln, bias=1.0), reads=writes, writes=writes)
            S.op('act', lambda e: e.activation(out=out, in_=out, func=AF.Exp, scale=-1.0), reads=writes, writes=writes)

        def load_xT(x_ap, rows, xT, xkey):
            with contextlib.ExitStack() as st2:
                xt = [sb(st2, "xt%d" % i, [128, D], BF16) for i in range(3)]
                for t, r0 in enumerate(rows):
                    b = xt[t % 3]
                    bk = "xt%d" % (t % 3)
                    S.dma('pool', b[:], x_ap[r0:r0 + 128, :], writes=[bk])
                    for g in range(0, KC, 4):
                        n = min(4, KC - g)
                        pt, pk = nextps()
                        for j in range(n):
                            S.op('pe', lambda e, j=j: e.matmul(pt[:, j * 128:(j + 1) * 128],
                                                               lhsT=b[:, (g + j) * 128:(g + j + 1) * 128], rhs=identb[:],
                                                               start=True, stop=True), reads=[bk, 'identb'], writes=[pk])
                        copy_op(evac_eng(), xT[:, g:g + n, t * 128:(t + 1) * 128],
                                pt[:, 0:n * 128].rearrange("p (a b) -> p a b", b=128), [pk], [xkey])
                S.barrier()

        Cst = [sb(es, "Cst%d" % h, [128, 2, ML_DV]) for h in range(ML_H)]
        nst = sb(es, "nst", [128, ML_H, 2])
        mst = sb(es, "mst", [128, ML_H])
        Sst = [sb(es, "Sst%d" % h, [128, HG_DV]) for h in range(HG_H)]
        for h in range(ML_H):
            P(lambda e, h=h: e.memset(Cst[h][:], 0.0), ['Cst%d' % h])
        P(lambda e: e.memset(nst[:], 0.0), ['nst'])
        P(lambda e: e.memset(mst[:], 0.0), ['mst'])
        for h in range(HG_H):
            P(lambda e, h=h: e.memset(Sst[h][:], 0.0), ['Sst%d' % h])

        chk_holder = [lambda tag: None]
        def mixer_pass(xT, xkey, ntile, full, brA, brB, samp, R, final, rmask=None):
            rmask = rst if rmask is None else rmask
            tiles = list(range(ntile)) + ([ntile] if samp else [])
            ntl = len(tiles)
            Mtok = ntl * 128
            with contextlib.ExitStack() as st:
                ift = sb(st, "ift", [128, ntl, 2 * ML_H])
                wb, wk = wget('w_in', w_in, 0, KC, off['mi'], 2 * ML_H)
                for ti in range(ntl):
                    pt, pk = nextps()
                    proj_T(wb, wk, 2 * ML_H, xT, xkey, KC, ti * 128, pt, pk)
                    copy_op('dve', ift[:, ti, :], pt[:, 0:2 * ML_H], [pk], ['ift'])
                chk_holder[0]('ift')
                ro = [0]

                def rview(n, inner):
                    v = R[:, ro[0]:ro[0] + n].rearrange("p (a b) -> p a b", b=inner)
                    ro[0] += n
                    return v
                PBS = []
                for i in range(2):
                    PB = {'k_tok': rview(ntl * ML_DK, ML_DK), 'v_tok': rview(ntl * ML_DV, ML_DV)}
                    if full:
                        PB['qT'] = rview(2 * Mtok, Mtok)
                        PB['kT'] = rview(2 * Mtok, Mtok)
                        PB['og'] = rview(ntl * ML_DV, ML_DV)
                    PBS.append(PB)
                if samp:
                    NCB = 3
                    Cf = [sb(st, "Cf%d" % i, [128, 2, ML_DV]) for i in range(NCB)]
                    Cfb = [sb(st, "Cfb%d" % i, [128, 2, ML_DV], BF16) for i in range(NCB)]
                    qTq = [sb(st, "qTq%d" % i, [128, 2, 128], BF16) for i in range(2)]
                    kwq = [sb(st, "kwq%d" % i, [128, ML_DK], BF16) for i in range(2)]
                    nall = sb(st, "nall", [128, NSQ * ML_H * 2])
                    nallb = sb(st, "nallb", [128, NSQ * ML_H * 2], BF16)
                    nrow = sb(st, "nrow", [128, 128])
                    Dbc = sb(st, "Dbc", [128, NSQ])
                    dsel = sb(st, "dsel", [128, NSQ])
                    msamp = sb(st, "msamp", [NSQ, ML_H])
                    mtok = sb(st, "mtok", [128, ML_H])
                    mnew_all = sb(st, "mnew_all", [128, ML_H])
                    mout = sb(st, "mout", [NSQ, ML_H])
                    S.dma('sp', nrow[:], st_n, writes=['nrow'])
                    chk_holder[0]('sA0')
                    S.op('pe', lambda e: e.matmul(ps[4][:, 0:128], lhsT=nrow[:], rhs=ident[:], start=True, stop=True),
                         reads=['nrow', 'ident'], writes=[PK[4]])
                    copy_op('dve', nall[:], ps[4][:, 0:128], [PK[4]], ['nall'])
                    chk_holder[0]('sA05')
                    copy_op('act', nallb[:], ps[4][:, 0:128], [PK[4]], ['nallb'])
                    chk_holder[0]('sA1')
                    S.dma('sp', msamp[:], st_m, writes=['msamp'])
                    S.op('pe', lambda e: e.matmul(ps[4][:, 0:ML_H], lhsT=indT[:], rhs=msamp[:], start=True, stop=True),
                         reads=['indT', 'msamp'], writes=[PK[4]])
                    copy_op('dve', mtok[:], ps[4][:, 0:ML_H], [PK[4]], ['mtok'])
                    chk_holder[0]('sA')

                def ml_rec(h, X, tiles):
                    sfx = X['sfx']
                    K = lambda nm: nm + sfx
                    Cbf, nbf, sc, bm, mprev = X['Cbf'], X['nbf'], X['sc'], X['bm'], X['mprev']
                    diagc, logd, dmat, sdm, sdT, kw = X['diagc'], X['logd'], X['dmat'], X['sdm'], X['sdT'], X['kw']
                    hbuf, h2, bst, bmv = X['hbuf'], X['h2'], X['bst'], X['bmv']
                    hb16 = X['hb16']
                    PB = X['PB']
                    k_tok, v_tok = PB['k_tok'], PB['v_tok']
                    qT, kT, og = PB.get('qT'), PB.get('kT'), PB.get('og')
                    bG, bQ, bA, bB = X['bG'], X['bQ'], X['bA'], X['bB']
                    kG, kQ, kA, kB = PK[bG], PK[bQ], PK[bA], PK[bB]
                    cB0, cC0, cS0, cQ0, cT0, cN0 = X['cols']
                    urot = [0]

                    def ubank():
                        bnk = X['ub'][urot[0] % len(X['ub'])]
                        urot[0] += 1
                        return ps[bnk], PK[bnk]
                    Ck, nk, mk_ = 'Cst%d' % h, 'nst%d' % h, 'mst%d' % h
                    S.op('act', lambda e: e.activation(out=Cbf[:], in_=Cst[h][:], func=AF.Copy), reads=[Ck], writes=[K('Cbf')])
                    S.op('dve', lambda e: e.tensor_copy(out=nbf[:], in_=nst[:, h, :]), reads=[nk], writes=[K('nbf')])
                    S.op('dve', lambda e: e.tensor_copy(out=mprev[:], in_=mst[:, h:h + 1]), reads=[mk_], writes=[K('mprev')])
                    for ti in tiles:
                        is_s = samp and ti == ntl - 1
                        mstm = mst_b if is_s else mst_c
                        mstk = 'mst_b' if is_s else 'mst_c'
                        negm_ = neg_b if is_s else neg_c
                        negk = 'neg_b' if is_s else 'neg_c'
                        selm = sel_b if is_s else sel_c
                        selk = 'sel_b' if is_s else 'sel_c'
                        tok = slice(ti * 128, (ti + 1) * 128)
                        col = lambda j: sc[:, j:j + 1]
                        if is_s:
                            S.op('dve', lambda e: e.tensor_copy(out=mprev[:], in_=mtok[:, h:h + 1]), reads=['mtok'],
                                 writes=[K('mprev')])
                        S.op('dve', lambda e: e.tensor_scalar(out=col(0), in0=ift[:, ti, h:h + 1], scalar1=bigb[:, h:h + 1],
                                                              scalar2=None, op0=ALU.add), reads=['ift', 'bigb'], writes=[K('sc0')])
                        S.op('act', lambda e: e.activation(out=col(1), in_=ift[:, ti, ML_H + h:ML_H + h + 1], func=AF.Exp,
                                                           scale=-1.0, bias=nbfg[:, h:h + 1]), reads=['ift', 'nbfg'],
                             writes=[K('sc1')])
                        S.op('act', lambda e: e.activation(out=col(1), in_=col(1), func=AF.Ln, bias=1.0), reads=[K('sc1')],
                             writes=[K('sc1')])
                        S.op('pe', lambda e: e.matmul(ps[bG][:, cB0:cB0 + 1], lhsT=mstm[:], rhs=col(1), start=True, stop=True),
                             reads=[mstk, K('sc1')], writes=[kG])
                        S.op('dve', lambda e: e.tensor_copy(out=bm[:, 0:1], in_=ps[bG][:, cB0:cB0 + 1]), reads=[kG], writes=[K('bm0')])
                        S.op('dve', lambda e: e.tensor_tensor(out=col(2), in0=col(0), in1=bm[:, 0:1], op=ALU.add),
                             reads=[K('sc0'), K('bm0')], writes=[K('sc2')])
                        S.op('dve', lambda e: e.tensor_scalar(out=diagc[:], in0=ident[:], scalar1=col(2), scalar2=None,
                                                              op0=ALU.mult), reads=['ident', K('sc2')], writes=[K('diagc')])
                        S.op('pe', lambda e: e.matmul(ps[bG][:, cC0:cC0 + 128], lhsT=ones[:], rhs=diagc[:], start=True, stop=True),
                             reads=['ones', K('diagc')], writes=[kG])
                        S.op('dve', lambda e: e.scalar_tensor_tensor(out=logd[:], in0=ps[bG][:, cC0:cC0 + 128], scalar=bm[:, 0:1],
                                                                     in1=negm_[:], op0=ALU.subtract, op1=ALU.add),
                             reads=[kG, K('bm0'), negk], writes=[K('logd')])
                        S.op('dve', lambda e: e.tensor_reduce(out=col(3), in_=logd[:], axis=AX.X, op=ALU.max),
                             reads=[K('logd')], writes=[K('sc3')])
                        S.op('dve', lambda e: e.tensor_tensor(out=col(4), in0=mprev[:], in1=bm[:, 0:1], op=ALU.subtract),
                             reads=[K('mprev'), K('bm0')], writes=[K('sc4')])
                        S.op('dve', lambda e: e.tensor_tensor(out=bm[:, 1:2], in0=col(4), in1=col(3), op=ALU.max),
                             reads=[K('sc4'), K('sc3')], writes=[K('bm1')])
                        S.op('dve', lambda e: e.tensor_scalar(out=col(5), in0=bm[:, 1:2], scalar1=-1.0, scalar2=None,
                                                              op0=ALU.mult), reads=[K('bm1')], writes=[K('sc5')])
                        S.op('pe', lambda e: e.matmul(ps[bG][:, cS0:cS0 + 2], lhsT=selm[:], rhs=bm[:], start=True, stop=True),
                             reads=[selk, K('bm0'), K('bm1')], writes=[kG])
                        S.op('dve', lambda e: e.tensor_tensor(out=col(8), in0=bm[:, 0:1], in1=ps[bG][:, cS0:cS0 + 1],
                                                              op=ALU.subtract), reads=[K('bm0'), kG], writes=[K('sc8')])
                        S.op('dve', lambda e: e.tensor_tensor(out=col(8), in0=col(8), in1=col(0), op=ALU.add),
                             reads=[K('sc8'), K('sc0')], writes=[K('sc8')])
                        S.op('dve', lambda e: e.tensor_scalar(out=col(9), in0=ps[bG][:, cS0 + 1:cS0 + 2], scalar1=-1.0, scalar2=None,
                                                              op0=ALU.mult), reads=[kG], writes=[K('sc9')])
                        S.op('act', lambda e: e.activation(out=col(10), in_=col(8), func=AF.Exp, bias=col(9)),
                             reads=[K('sc8'), K('sc9')], writes=[K('sc10')])
                        S.op('dve', lambda e: e.tensor_tensor(out=col(11), in0=mprev[:], in1=ps[bG][:, cS0:cS0 + 1],
                                                              op=ALU.subtract), reads=[K('mprev'), kG], writes=[K('sc11')])
                        S.op('act', lambda e: e.activation(out=col(12), in_=col(11), func=AF.Exp, bias=col(9)),
                             reads=[K('sc11'), K('sc9')], writes=[K('sc12')])
                        if full:
                            S.op('act', lambda e: e.activation(out=dmat[:], in_=logd[:], func=AF.Exp, bias=col(5)),
                                 reads=[K('logd'), K('sc5')], writes=[K('dmat')])
                            S.op('act', lambda e: e.activation(out=col(6), in_=col(4), func=AF.Exp, bias=col(5)),
                                 reads=[K('sc4'), K('sc5')], writes=[K('sc6')])
                            S.op('act', lambda e: e.activation(out=col(7), in_=col(5), func=AF.Exp), reads=[K('sc5')],
                                 writes=[K('sc7')])
                            for c in range(2):
                                S.op('pe', lambda e, c=c: e.matmul(ps[bQ][:, cQ0:cQ0 + 128], lhsT=qT[:, c, tok], rhs=kT[:, c, tok],
                                                                   start=(c == 0), stop=(c == 1)), reads=[K('qT'), K('kT')],
                                     writes=[kQ])
                            S.op('dve', lambda e: e.scalar_tensor_tensor(out=sdm[:], in0=ps[bQ][:, cQ0:cQ0 + 128], scalar=1.0,
                                                                         in1=dmat[:], op0=ALU.mult, op1=ALU.mult,
                                                                         accum_out=col(13)),
                                 reads=[kQ, K('dmat')], writes=[K('sdm'), K('sc13')])
                            S.op('pe', lambda e: e.matmul(ps[bQ][:, cT0:cT0 + 128], lhsT=sdm[:], rhs=ident[:], start=True, stop=True),
                                 reads=[K('sdm'), 'ident'], writes=[kQ])
                            copy_op('act', sdT[:], ps[bQ][:, cT0:cT0 + 128], [kQ], [K('sdT')])
                            S.op('pe', lambda e: e.matmul(ps[bA][:, :], lhsT=sdT[:], rhs=v_tok[:, ti, :], start=True, stop=True),
                                 reads=[K('sdT'), K('v_tok')], writes=[kA])
                        if not is_s:
                            if full:
                                for c in range(2):
                                    S.op('pe', lambda e, c=c: e.matmul(ps[bB][:, :], lhsT=qT[:, c, tok], rhs=Cbf[:, c, :],
                                                                       start=(c == 0), stop=(c == 1)), reads=[K('qT'), K('Cbf')],
                                         writes=[kB])
                                for c in range(2):
                                    S.op('pe', lambda e, c=c: e.matmul(ps[bQ][:, cN0:cN0 + 1], lhsT=qT[:, c, tok], rhs=nbf[:, c:c + 1],
                                                                       start=(c == 0), stop=(c == 1)), reads=[K('qT'), K('nbf')],
                                         writes=[kQ])
                            S.op('dve', lambda e: e.tensor_scalar(out=kw[:], in0=k_tok[:, ti, :], scalar1=col(10), scalar2=None,
                                                                  op0=ALU.mult), reads=[K('k_tok'), K('sc10')], writes=[K('kw')])
                            for c in range(2):
                                pt, pk = ubank()
                                S.op('pe', lambda e, c=c, pt=pt: e.matmul(pt[:, :], lhsT=kw[:, c * 128:(c + 1) * 128],
                                                                          rhs=v_tok[:, ti, :], start=True, stop=True),
                                     reads=[K('kw'), K('v_tok')], writes=[pk])
                                S.op('dve', lambda e, c=c, pt=pt: e.scalar_tensor_tensor(
                                    out=Cst[h][:, c, :], in0=Cst[h][:, c, :], scalar=col(12), in1=pt[:, :],
                                    op0=ALU.mult, op1=ALU.add), reads=[Ck, K('sc12'), pk, K('Cbf')], writes=[Ck])
                            pt, pk = ubank()
                            for c in range(2):
                                S.op('pe', lambda e, c=c, pt=pt: e.matmul(pt[:, c:c + 1], lhsT=kw[:, c * 128:(c + 1) * 128],
                                                                          rhs=onesb[:, 0:1], start=True, stop=True),
                                     reads=[K('kw'), 'onesb'], writes=[pk])
                            S.op('dve', lambda e, pt=pt: e.scalar_tensor_tensor(
                                out=nst[:, h, :], in0=nst[:, h, :], scalar=col(12), in1=pt[:, 0:2],
                                op0=ALU.mult, op1=ALU.add), reads=[nk, K('sc12'), pk, K('nbf')], writes=[nk])
                            S.op('act', lambda e: e.activation(out=Cbf[:], in_=Cst[h][:], func=AF.Copy), reads=[Ck],
                                 writes=[K('Cbf')])
                            S.op('dve', lambda e: e.tensor_copy(out=nbf[:], in_=nst[:, h, :]), reads=[nk], writes=[K('nbf')])
                            S.op('dve', lambda e: e.tensor_copy(out=mprev[:], in_=ps[bG][:, cS0 + 1:cS0 + 2]), reads=[kG],
                                 writes=[K('mprev')])
                            S.op('dve', lambda e: e.tensor_copy(out=mst[:, h:h + 1], in_=ps[bG][:, cS0 + 1:cS0 + 2]), reads=[kG],
                                 writes=[mk_])
                        else:
                            S.op('dve', lambda e: e.tensor_copy(out=mnew_all[:, h:h + 1], in_=ps[bG][:, cS0 + 1:cS0 + 2]),
                                 reads=[kG], writes=['mnew_all'])
                            S.op('dve', lambda e: e.tensor_scalar(out=dsel[:], in0=lastind[:], scalar1=col(12), scalar2=None,
                                                                  op0=ALU.mult), reads=['lastind', K('sc12')], writes=['dsel'])
                            pt, pk = ubank()
                            S.op('pe', lambda e, pt=pt: e.matmul(pt[:, 0:NSQ], lhsT=ones[:], rhs=dsel[:], start=True, stop=True),
                                 reads=['ones', 'dsel'], writes=[pk])
                            copy_op('dve', Dbc[:], pt[:, 0:NSQ], [pk], ['Dbc'])
                            S.op('dve', lambda e: e.tensor_scalar(out=kw[:], in0=k_tok[:, ti, :], scalar1=col(10), scalar2=None,
                                                                  op0=ALU.mult), reads=[K('k_tok'), K('sc10')], writes=[K('kw')])
                            def cload(q):
                                S.dma('sp', Cf[q % NCB][:], st_C[q, h].rearrange("(c p) v -> p c v", p=128),
                                      writes=['Cf%d' % (q % NCB)])
                            for q in range(min(NCB - 1, NSQ)):
                                cload(q)
                            for q in range(NSQ):
                                if q + NCB - 1 < NSQ:
                                    cload(q + NCB - 1)
                                cf, cfk = Cf[q % NCB], 'Cf%d' % (q % NCB)
                                cb, cbk = Cfb[q % NCB], 'Cfb%d' % (q % NCB)
                                qq, qqk = qTq[q % 2], 'qTq%d' % (q % 2)
                                kq, kqk = kwq[q % 2], 'kwq%d' % (q % 2)
                                S.op('dve', lambda e, q=q, qq=qq: e.tensor_tensor(
                                    out=qq[:], in0=qT[:, :, tok], in1=blkb[:, q:q + 1, :].to_broadcast([128, 2, 128]),
                                    op=ALU.mult), reads=[K('qT'), 'blkb'], writes=[qqk])
                                S.op('dve', lambda e, q=q, kq=kq: e.tensor_scalar(
                                    out=kq[:], in0=kw[:], scalar1=ind[:, q:q + 1], scalar2=None, op0=ALU.mult),
                                    reads=[K('kw'), 'ind'], writes=[kqk])
                                copy_op('act', cb[:], cf[:], [cfk], [cbk])
                                for c in range(2):
                                    S.op('pe', lambda e, c=c, q=q, cb=cb: e.matmul(
                                        ps[bB][:, :], lhsT=qq[:, c, :], rhs=cb[:, c, :],
                                        start=(q == 0 and c == 0), stop=(q == NSQ - 1 and c == 1)),
                                        reads=[qqk, cbk], writes=[kB])
                                for c in range(2):
                                    j = (q * ML_H + h) * 2 + c
                                    S.op('pe', lambda e, c=c, q=q, j=j: e.matmul(
                                        ps[bQ][:, cN0:cN0 + 1], lhsT=qq[:, c, :], rhs=nallb[:, j:j + 1],
                                        start=(q == 0 and c == 0), stop=(q == NSQ - 1 and c == 1)),
                                        reads=[qqk, 'nallb'], writes=[kQ])
                                for c in range(2):
                                    pt, pk = ubank()
                                    S.op('pe', lambda e, c=c, q=q, pt=pt: e.matmul(
                                        pt[:, :], lhsT=kq[:, c * 128:(c + 1) * 128], rhs=v_tok[:, ti, :],
                                        start=True, stop=True), reads=[kqk, K('v_tok')], writes=[pk])
                                    S.op('dve', lambda e, c=c, q=q, pt=pt, cf=cf: e.scalar_tensor_tensor(
                                        out=cf[:, c, :], in0=cf[:, c, :], scalar=Dbc[:, q:q + 1], in1=pt[:, :],
                                        op0=ALU.mult, op1=ALU.add), reads=[cfk, 'Dbc', pk, cbk], writes=[cfk])
                                S.dma('pool', o_Cs[q, h].rearrange("(c p) v -> p c v", p=128), cf[:], reads=[cfk])
                                pt, pk = ubank()
                                for c in range(2):
                                    S.op('pe', lambda e, c=c, q=q, pt=pt: e.matmul(
                                        pt[:, c:c + 1], lhsT=kq[:, c * 128:(c + 1) * 128], rhs=onesb[:, 0:1],
                                        start=True, stop=True), reads=[kqk, 'onesb'], writes=[pk])
                                j0 = (q * ML_H + h) * 2
                                S.op('dve', lambda e, q=q, pt=pt, j0=j0: e.scalar_tensor_tensor(
                                    out=nall[:, j0:j0 + 2], in0=nall[:, j0:j0 + 2], scalar=Dbc[:, q:q + 1], in1=pt[:, 0:2],
                                    op0=ALU.mult, op1=ALU.add), reads=['nall', 'Dbc', pk], writes=['nall'])
                        if is_s:
                            chk_holder[0]('sC')
                        if full:
                            S.op('dve', lambda e: e.scalar_tensor_tensor(out=col(14), in0=ps[bQ][:, cN0:cN0 + 1], scalar=col(6),
                                                                         in1=col(13), op0=ALU.mult, op1=ALU.add),
                                 reads=[kQ, K('sc6'), K('sc13')], writes=[K('sc14')])
                            S.op('act', lambda e: e.activation(out=col(14), in_=col(14), func=AF.Abs), reads=[K('sc14')],
                                 writes=[K('sc14')])
                            S.op('dve', lambda e: e.tensor_tensor(out=col(14), in0=col(14), in1=col(7), op=ALU.max),
                                 reads=[K('sc14'), K('sc7')], writes=[K('sc14')])
                            S.op('dve', lambda e: e.reciprocal(out=col(15), in_=col(14)), reads=[K('sc14')], writes=[K('sc15')])
                            S.op('dve', lambda e: e.tensor_tensor(out=col(16), in0=col(15), in1=col(6), op=ALU.mult),
                                 reads=[K('sc15'), K('sc6')], writes=[K('sc16')])
                            S.op('act', lambda e: e.activation(out=h2[:], in_=ps[bB][:, :], func=AF.Identity, scale=col(16)),
                                 reads=[kB, K('sc16')], writes=[K('h2')])
                            S.op('dve', lambda e: e.scalar_tensor_tensor(out=hbuf[:], in0=ps[bA][:, :], scalar=col(15),
                                                                         in1=h2[:], op0=ALU.mult, op1=ALU.add),
                                 reads=[kA, K('sc15'), K('h2')], writes=[K('hbuf')])
                            S.op('dve', lambda e: e.bn_stats(out=bst[:], in_=hbuf[:]), reads=[K('hbuf')], writes=[K('bst')])
                            S.op('dve', lambda e: e.bn_aggr(out=bmv[:], in_=bst[:]), reads=[K('bst')], writes=[K('bmv')])
                            S.op('act', lambda e: e.activation(out=col(17), in_=bmv[:, 1:2], func=AF.Ln, bias=LN_EPS),
                                 reads=[K('bmv')], writes=[K('sc17')])
                            S.op('act', lambda e: e.activation(out=col(17), in_=col(17), func=AF.Exp, scale=-0.5),
                                 reads=[K('sc17')], writes=[K('sc17')])
                            S.op('dve', lambda e: e.tensor_scalar(out=col(18), in0=bmv[:, 0:1], scalar1=-1.0, scalar2=col(17),
                                                                  op0=ALU.mult, op1=ALU.mult), reads=[K('bmv'), K('sc17')],
                                 writes=[K('sc18')])
                            S.op('act', lambda e: e.activation(out=h2[:], in_=hbuf[:], func=AF.Identity, scale=col(17),
                                                               bias=col(18)), reads=[K('hbuf'), K('sc17'), K('sc18')], writes=[K('h2')])
                            S.op('dve', lambda e: e.tensor_tensor(out=hb16[:], in0=h2[:], in1=og[:, ti, :], op=ALU.mult),
                                 reads=[K('h2'), K('og')], writes=[K('hb16')])
                            for j in range(4):
                                S.op('pe', lambda e, j=j: e.matmul(ps[bA][:, j * 128:(j + 1) * 128],
                                                                   lhsT=hb16[:, j * 128:(j + 1) * 128], rhs=identb[:],
                                                                   start=True, stop=True), reads=[K('hb16'), 'identb'],
                                     writes=[kA])
                            for j in range(4):
                                ch = 4 * h + j
                                if j % 2 == 0:
                                    S.op('act', lambda e, j=j, ch=ch: e.activation(
                                        out=brA[:, ch, tok], in_=ps[bA][:, j * 128:(j + 1) * 128], func=AF.Identity,
                                        scale=gcol_a[:, ch:ch + 1]), reads=[kA, 'gcol_a'], writes=[K('brA')])
                                else:
                                    S.op('dve', lambda e, j=j, ch=ch: e.tensor_scalar(
                                        out=brA[:, ch, tok], in0=ps[bA][:, j * 128:(j + 1) * 128], scalar1=gcol_a[:, ch:ch + 1],
                                        scalar2=None, op0=ALU.mult), reads=[kA, 'gcol_a'], writes=[K('brA')])
                def ml_proj(h, PB, psfx):
                    k_tok, v_tok = PB['k_tok'], PB['v_tok']
                    qT, kT, og = PB.get('qT'), PB.get('kT'), PB.get('og')
                    h2 = MX[0]['h2']
                    def tmode(colname, width, dst, dkey, post=None, scale=None):
                        for c0 in range(0, width, BW):
                            wb, wk = wget('w_in', w_in, 0, KC, off[colname] + h * width + c0, BW)
                            for ti in range(ntl):
                                pt, pk = nextps()
                                proj_T(wb, wk, BW, xT, xkey, KC, ti * 128, pt, pk)
                                if post is None:
                                    copy_op(evac_eng(), dst[:, ti, c0:c0 + BW], pt[:, 0:BW], [pk], [dkey], scale=scale)
                                else:
                                    post(dst[:, ti, c0:c0 + BW], pt[:, 0:BW], pk, dkey)

                    def fmode(colname, dst, dkey, scale=None):
                        wb, wk = wget('w_in', w_in, 0, KC, off[colname] + h * ML_DK, BW)
                        for cc in range(2):
                            for (t0, n) in ntiles(Mtok):
                                pt, pk = nextps()
                                proj_F(wb, wk, cc * 128, 128, xT, xkey, KC, t0, n, pt, pk)
                                copy_op(evac_eng(), dst[:, cc, t0:t0 + n], pt[:, 0:n], [pk], [dkey], scale=scale)

                    if full:
                        fmode('mq', qT, 'qT' + psfx)
                        fmode('mk', kT, 'kT' + psfx, scale=ML_DK ** -0.5)
                    tmode('mk', ML_DK, k_tok, 'k_tok' + psfx, scale=ML_DK ** -0.5)
                    tmode('mv', ML_DV, v_tok, 'v_tok' + psfx)
                    if full:
                        osig = sb(st, "osig%d" % h, [128, BW]) if False else None

                        def opost(dst, src, pk, dkey):
                            sigmoid_from('act', h2[:, 0:BW], src, [pk], ['h2_0'])
                            S.op('dve', lambda e: e.tensor_copy(out=dst, in_=h2[:, 0:BW]), reads=['h2_0'], writes=[dkey])
                        tmode('mo', ML_DV, og, 'og' + psfx, post=opost)


                def ml_scratch(i):
                    return {'sfx': '_%d' % i,
                            'Cbf': sb(st, "Cbf", [128, 2, ML_DV], BF16), 'nbf': sb(st, "nbf", [128, 2], BF16),
                            'sc': sb(st, "sc", [128, 24]), 'bm': sb(st, "bm", [128, 2]), 'mprev': sb(st, "mprev", [128, 1]),
                            'diagc': sb(st, "diagc", [128, 128]), 'logd': sb(st, "logd", [128, 128]),
                            'dmat': sb(st, "dmat", [128, 128]), 'sdm': sb(st, "sdm", [128, 128]),
                            'sdT': sb(st, "sdT", [128, 128], BF16), 'kw': sb(st, "kw", [128, ML_DK], BF16),
                            'hbuf': sb(st, "hbuf", [128, ML_DV]), 'h2': sb(st, "h2", [128, ML_DV]),
                            'bst': sb(st, "bst", [128, 6]), 'bmv': sb(st, "bmv", [128, 2]),
                            'hb16': sb(st, "hb16", [128, ML_DV], BF16)}
                MX = [ml_scratch(0), ml_scratch(1)]
                hbuf = MX[0]['hbuf']
                MERGED = (384, 128, 386, 0, 256, 390)
                MX[0].update(bG=4, bQ=4, bA=5, bB=6, cols=MERGED)
                MX[1].update(bG=0, bQ=0, bA=1, bB=2, cols=MERGED)
                for h0 in range(0, ML_H, 2):
                    for i in range(2):
                        ml_proj(h0 + i, PBS[i], '_%d' % i)
                        MX[i]['PB'] = PBS[i]
                    ptiles = list(range(ntile))
                    MX[0]['ub'] = [7]
                    MX[1]['ub'] = [3]
                    lists = []
                    for i in range(2):
                        S.start_record()
                        ml_rec(h0 + i, MX[i], ptiles)
                        lists.append(S.stop_record())
                    S.replay_interleaved(lists)
                    if samp:
                        MX[0]['ub'] = [7, 0, 1, 2, 3]
                        ml_rec(h0, MX[0], [ntile])
                        MX[1]['ub'] = [3, 4, 5, 6, 7]
                        ml_rec(h0 + 1, MX[1], [ntile])
                if final:
                    for h in range(ML_H):
                        S.dma('sp', o_Cp[h].rearrange("(c p) v -> p c v", p=128), Cst[h][:], reads=['Cst%d' % h])
                    S.op('pe', lambda e: e.matmul(ps[4][0:ML_H * 2, 0:128], lhsT=nst[:].rearrange("p h c -> p (h c)"),
                                                  rhs=ident[:], start=True, stop=True), reads=['nst%d' % hh for hh in range(ML_H)] + ['ident'], writes=[PK[4]])
                    copy_op('dve', hbuf[0:ML_H * 2, 0:128], ps[4][0:ML_H * 2, 0:128], [PK[4]], ['hbuf_0'])
                    S.dma('sp', o_np, hbuf[0:ML_H * 2, 0:128], reads=['hbuf_0'])
                    S.dma('sp', o_mp, mst[0:1, :], reads=['mst%d' % hh for hh in range(ML_H)])
                    if samp:
                        S.op('pe', lambda e: e.matmul(ps[4][:, 0:128], lhsT=nall[:], rhs=ident[:], start=True, stop=True),
                             reads=['nall', 'ident'], writes=[PK[4]])
                        copy_op('dve', nrow[:], ps[4][:, 0:128], [PK[4]], ['nrow'])
                        S.dma('sp', o_ns, nrow[:], reads=['nrow'])
                        S.op('pe', lambda e: e.matmul(ps[4][0:NSQ, 256:256 + ML_H], lhsT=lastind[:], rhs=mnew_all[:],
                                                      start=True, stop=True), reads=['lastind', 'mnew_all'], writes=[PK[4]])
                        copy_op('dve', mout[:], ps[4][0:NSQ, 256:256 + ML_H], [PK[4]], ['mout'])
                        S.dma('sp', o_ms, mout[:], reads=['mout'])
                S.barrier()
            chk_holder[0]('ml')

            with contextlib.ExitStack() as st:
                ro = [0]

                def rview2(n, inner=None):
                    v = R[:, ro[0]:ro[0] + n]
                    if inner is not None:
                        v = v.rearrange("p (a b) -> p a b", b=inner)
                    ro[0] += n
                    return v
                qg = [rview2(Mtok) for i in range(2)]
                kg = [rview2(Mtok) for i in range(2)]
                eG = [sb(st, "eG%d" % i, [128, Mtok]) for i in range(2)]
                i_tok = [rview2(ntl * HG_DV, HG_DV) for i in range(2)]
                if full:
                    sg = [rview2(ntl * HG_DV, HG_DV) for i in range(2)]
                fa = sb(st, "fa", [128, Mtok])
                fb = sb(st, "fb", [128, Mtok])
                qf = sb(st, "qf", [128, Mtok])
                NSB = 4
                hres = []
                for i in range(2):
                    X = {'Sbf': sb(st, "Sbf", [128, HG_DV], BF16), 'aTm': sb(st, "aTm", [128, 128], BF16),
                         'kgt': sb(st, "kgt", [128, 128], BF16), 'obuf': sb(st, "obuf", [128, HG_DV]),
                         'tmpS': sb(st, "tmpS", [128, HG_DV]), 'hc': sb(st, "hc", [128, 4]),
                         'ob16': sb(st, "ob16", [128, HG_DV], BF16),
                         'banks': (5, 6, 7, 0) if i == 0 else (1, 2, 3, 4)}
                    if samp:
                        X['Sf'] = [sb(st, "Sf%d" % j, [128, HG_DV]) for j in range(NSB)]
                        X['Sfb'] = [sb(st, "Sfb%d" % j, [128, HG_DV], BF16) for j in range(NSB)]
                        X['qgq'] = [sb(st, "qgq%d" % j, [128, 128], BF16) for j in range(2)]
                        X['kgq'] = [sb(st, "kgq%d" % j, [128, 128], BF16) for j in range(2)]
                    hres.append(X)
                obuf = hres[0]['obuf']
                for pr in range(HG_H // 2):
                    if full:
                        wb, wk = wget('w_in', w_in, 0, KC, off['hq'] + pr * BW, BW)
                        for i in range(2):
                            for (t0, n) in ntiles(Mtok):
                                pt, pk = nextps()
                                proj_F(wb, wk, i * 128, 128, xT, xkey, KC, t0, n, pt, pk)
                                copy_op(evac_eng(), (qf if i == 0 else fb)[:, t0:t0 + n], pt[:, 0:n], [pk],
                                        ['qf' if i == 0 else 'fb'])
                    wb, wk = wget('w_in', w_in, 0, KC, off['hf'] + pr * BW, BW)
                    for i in range(2):
                        hd = 2 * pr + i
                        for (t0, n) in ntiles(Mtok):
                            pt, pk = nextps()
                            proj_F(wb, wk, i * 128, 128, xT, xkey, KC, t0, n, pt, pk)
                            sigmoid_from('act', fa[:, t0:t0 + n], pt[:, 0:n], [pk], ['fa'])
                        S.op('dve', lambda e, hd=hd: e.tensor_scalar(out=fa[:], in0=fa[:], scalar1=lbc[:, HG_H + hd:HG_H + hd + 1],
                                                                     scalar2=lbc[:, hd:hd + 1], op0=ALU.mult, op1=ALU.add),
                             reads=['fa', 'lbc'], writes=['fa'])
                        S.op('act', lambda e, i=i: e.activation(out=eG[i][:], in_=fa[:], func=AF.Ln), reads=['fa'],
                             writes=['eG%d' % i])
                        rs = rmask[:, 0:Mtok]
                        S.op('dve', lambda e, i=i: e.tensor_tensor_scan(out=eG[i][:], data0=rs, data1=eG[i][:], initial=0.0,
                                                                        op0=ALU.mult, op1=ALU.add),
                             reads=['rst', 'eG%d' % i], writes=['eG%d' % i])
                        S.op('dve', lambda e: e.tensor_scalar(out=fa[:], in0=fa[:], scalar1=-1.0, scalar2=1.0, op0=ALU.mult,
                                                              op1=ALU.add), reads=['fa'], writes=['fa'])
                        tq = sb(st, "tq%d_%d" % (pr, i), [1, 1]) if False else None
                        S.op('act', lambda e, i=i: e.activation(out=kg[i][:], in_=eG[i][:], func=AF.Exp, scale=-1.0),
                             reads=['eG%d' % i], writes=['kg%d' % i])
                        S.op('dve', lambda e, i=i: e.tensor_tensor(out=kg[i][:], in0=kg[i][:], in1=fa[:], op=ALU.mult),
                             reads=['kg%d' % i, 'fa'], writes=['kg%d' % i])
                        S.op('act', lambda e, i=i: e.activation(out=eG[i][:], in_=eG[i][:], func=AF.Exp),
                             reads=['eG%d' % i], writes=['eG%d' % i])
                        if full:
                            qsrc, qk_ = (qf, 'qf') if i == 0 else (fb, 'fb')
                            S.op('dve', lambda e, i=i, qsrc=qsrc: e.tensor_tensor(out=qg[i][:], in0=qsrc[:], in1=eG[i][:],
                                                                                  op=ALU.mult),
                                 reads=[qk_, 'eG%d' % i], writes=['qg%d' % i])
                    for i in range(2):
                        hd = 2 * pr + i
                        wb, wk = wget('w_in', w_in, 0, KC, off['hi'] + hd * HG_DV, BW)
                        for ti in range(ntl):
                            pt, pk = nextps()
                            proj_T(wb, wk, BW, xT, xkey, KC, ti * 128, pt, pk)
                            copy_op(evac_eng(), i_tok[i][:, ti, :], pt[:, 0:BW], [pk], ['i_tok%d' % i])
                    if full:
                        for i in range(2):
                            hd = 2 * pr + i
                            wb, wk = wget('w_in', w_in, 0, KC, off['hg'] + hd * HG_DV, BW)
                            for ti in range(ntl):
                                pt, pk = nextps()
                                proj_T(wb, wk, BW, xT, xkey, KC, ti * 128, pt, pk)
                                sigmoid_from('act', obuf[:], pt[:, 0:BW], [pk], ['obuf'])
                                S.op('dve', lambda e, i=i, ti=ti, pt=pt: e.tensor_tensor(out=sg[i][:, ti, :], in0=pt[:, 0:BW],
                                                                                         in1=obuf[:], op=ALU.mult),
                                     reads=[pk, 'obuf'], writes=['sg%d' % i])
                    def hg_chain(i, hd):
                        X = hres[i]
                        Sbf, aTm, kgt, obuf, tmpS, hc = X['Sbf'], X['aTm'], X['kgt'], X['obuf'], X['tmpS'], X['hc']
                        ob16 = X['ob16']
                        bA, bO, bT, bS = X['banks']
                        kA, kO, kT, kS = PK[bA], PK[bO], PK[bT], PK[bS]
                        sfx = '_%d' % i
                        Sk = 'Sst%d' % hd
                        qgk, kgk, eGk, itk = 'qg%d' % i, 'kg%d' % i, 'eG%d' % i, 'i_tok%d' % i
                        copy_op('act', Sbf[:], Sst[hd][:], [Sk], ['Sbf' + sfx])
                        for ti in range(ntl):
                            is_s = samp and ti == ntl - 1
                            tok = slice(ti * 128, (ti + 1) * 128)
                            mm = mst_b if is_s else mst_c
                            mmk = 'mst_b' if is_s else 'mst_c'
                            if full:
                                S.op('pe', lambda e: e.matmul(ps[bA][:, 0:128], lhsT=kg[i][:, tok], rhs=qg[i][:, tok],
                                                              start=True, stop=True), reads=[kgk, qgk], writes=[kA])
                                S.op('dve', lambda e: e.tensor_tensor(out=aTm[:], in0=ps[bA][:, 0:128], in1=mm[:], op=ALU.mult),
                                     reads=[kA, mmk], writes=['aTm' + sfx])
                                S.op('pe', lambda e: e.matmul(ps[bO][:, 0:HG_DV], lhsT=aTm[:], rhs=i_tok[i][:, ti, :],
                                                              start=True, stop=False), reads=['aTm' + sfx, itk], writes=[kO])
                            S.op('pe', lambda e: e.matmul(ps[bA][:, 128:256], lhsT=kg[i][:, tok], rhs=identb[:],
                                                          start=True, stop=True), reads=[kgk, 'identb'], writes=[kA])
                            copy_op('act', kgt[:], ps[bA][:, 128:256], [kA], ['kgt' + sfx])
                            if not is_s:
                                if full:
                                    S.op('pe', lambda e: e.matmul(ps[bO][:, 0:HG_DV], lhsT=qg[i][:, tok], rhs=Sbf[:],
                                                                  start=False, stop=True), reads=[qgk, 'Sbf' + sfx], writes=[kO])
                                S.op('pe', lambda e: e.matmul(ps[bS][:, 0:HG_DV], lhsT=kgt[:], rhs=i_tok[i][:, ti, :],
                                                              start=True, stop=True), reads=['kgt' + sfx, itk], writes=[kS])
                                S.op('dve', lambda e: e.tensor_tensor(out=tmpS[:], in0=Sst[hd][:], in1=ps[bS][:, 0:HG_DV],
                                                                      op=ALU.add), reads=[Sk, kS], writes=['tmpS' + sfx])
                                ecol = eG[i][:, ti * 128 + 127:ti * 128 + 128]
                                S.op('act', lambda e: e.activation(out=Sst[hd][:], in_=tmpS[:], func=AF.Identity, scale=ecol),
                                     reads=['tmpS' + sfx, eGk, 'Sbf' + sfx], writes=[Sk])
                                copy_op('dve', Sbf[:], Sst[hd][:], [Sk], ['Sbf' + sfx])
                            else:
                                Sf, Sfb, qgq, kgq = X['Sf'], X['Sfb'], X['qgq'], X['kgq']

                                def sload(q):
                                    S.dma('sp', Sf[q % NSB][:], st_S[q, hd], writes=['Sf%d%s' % (q % NSB, sfx)])
                                for q in range(min(NSB - 1, NSQ)):
                                    sload(q)
                                for q in range(NSQ):
                                    if q + NSB - 1 < NSQ:
                                        sload(q + NSB - 1)
                                    sf, sfk = Sf[q % NSB], 'Sf%d%s' % (q % NSB, sfx)
                                    sfb, sfbk = Sfb[q % NSB], 'Sfb%d%s' % (q % NSB, sfx)
                                    qq, qqk = qgq[q % 2], 'qgq%d%s' % (q % 2, sfx)
                                    kq, kqk = kgq[q % 2], 'kgq%d%s' % (q % 2, sfx)
                                    if full:
                                        S.op('dve', lambda e: e.tensor_tensor(out=qq[:], in0=qg[i][:, tok], in1=blkb[:, q, :],
                                                                              op=ALU.mult), reads=[qgk, 'blkb'], writes=[qqk])
                                    S.op('dve', lambda e: e.tensor_scalar(out=kq[:], in0=kgt[:], scalar1=ind[:, q:q + 1],
                                                                          scalar2=None, op0=ALU.mult),
                                         reads=['kgt' + sfx, 'ind'], writes=[kqk])
                                    if full:
                                        copy_op('act', sfb[:], sf[:], [sfk], [sfbk])
                                        S.op('pe', lambda e: e.matmul(ps[bO][:, 0:HG_DV], lhsT=qq[:], rhs=sfb[:],
                                                                      start=False, stop=(q == NSQ - 1)),
                                             reads=[qqk, sfbk], writes=[kO])
                                    S.op('pe', lambda e: e.matmul(ps[bS][:, 0:HG_DV], lhsT=kq[:], rhs=i_tok[i][:, ti, :],
                                                                  start=True, stop=True), reads=[kqk, itk], writes=[kS])
                                    S.op('dve', lambda e: e.tensor_tensor(out=tmpS[:], in0=sf[:], in1=ps[bS][:, 0:HG_DV],
                                                                          op=ALU.add), reads=[sfk, kS], writes=['tmpS' + sfx])
                                    ecol = eG[i][:, ti * 128 + q * SQL + SQL - 1:ti * 128 + q * SQL + SQL]
                                    S.op('act', lambda e: e.activation(out=sf[:], in_=tmpS[:], func=AF.Identity, scale=ecol),
                                         reads=['tmpS' + sfx, eGk, sfbk], writes=[sfk])
                                    S.dma('pool', o_Ss[q, hd], sf[:], reads=[sfk])
                            if full:
                                S.op('act', lambda e: e.activation(out=obuf[:], in_=ps[bO][:, 0:HG_DV], func=AF.Square,
                                                                   accum_out=hc[:, 0:1]), reads=[kO],
                                     writes=['obuf' + sfx, 'hc0' + sfx])
                                S.op('act', lambda e: e.activation(out=hc[:, 1:2], in_=hc[:, 0:1], func=AF.Ln, scale=1.0 / HG_DV,
                                                                   bias=LN_EPS), reads=['hc0' + sfx], writes=['hc1' + sfx])
                                S.op('act', lambda e: e.activation(out=hc[:, 1:2], in_=hc[:, 1:2], func=AF.Exp, scale=-0.5),
                                     reads=['hc1' + sfx], writes=['hc1' + sfx])
                                S.op('dve', lambda e: e.scalar_tensor_tensor(
                                    out=ob16[:], in0=ps[bO][:, 0:HG_DV], scalar=hc[:, 1:2], in1=sg[i][:, ti, :],
                                    op0=ALU.mult, op1=ALU.mult), reads=[kO, 'hc1' + sfx, 'sg%d' % i, 'obuf' + sfx],
                                    writes=['ob16' + sfx])
                                for j in range(2):
                                    S.op('pe', lambda e: e.matmul(ps[bT][:, j * 128:(j + 1) * 128],
                                                                  lhsT=ob16[:, j * 128:(j + 1) * 128], rhs=identb[:],
                                                                  start=True, stop=True), reads=['ob16' + sfx, 'identb'],
                                         writes=[kT])
                                for j in range(2):
                                    ch = 2 * hd + j
                                    if j == 0:
                                        S.op('act', lambda e: e.activation(
                                            out=brB[:, ch, tok], in_=ps[bT][:, j * 128:(j + 1) * 128], func=AF.Identity,
                                            scale=gcol_b[:, ch:ch + 1]), reads=[kT, 'gcol_b'], writes=['brB' + sfx])
                                    else:
                                        S.op('dve', lambda e: e.tensor_scalar(
                                            out=brB[:, ch, tok], in0=ps[bT][:, j * 128:(j + 1) * 128],
                                            scalar1=gcol_b[:, ch:ch + 1], scalar2=None, op0=ALU.mult),
                                            reads=[kT, 'gcol_b'], writes=['brB' + sfx])

                    lists = []
                    for i in range(2):
                        S.start_record()
                        hg_chain(i, 2 * pr + i)
                        lists.append(S.stop_record())
                    S.replay_interleaved(lists)
                if final:
                    for hd in range(HG_H):
                        S.dma('sp', o_Sp[hd], Sst[hd][:], reads=['Sst%d' % hd])
                S.barrier()


        R = sb(es, "R", [128, cfg.RSZ], BF16)
        NMV, NHV = cfg.MLV // 128, cfg.HGV // 128
        DFF, FG, NFG = cfg.DFF, cfg.FG, cfg.NFG

        def layernorm(zt, zk, gt, bt, st, lst, lmv, lc):
            nchk = (D + 511) // 512
            for j in range(nchk):
                a, b_ = j * 512, min(D, (j + 1) * 512)
                S.op('dve', lambda e, j=j, a=a, b_=b_: e.bn_stats(out=lst[:, j, :], in_=zt[:, a:b_]), reads=[zk], writes=['lst'])
            S.op('dve', lambda e: e.bn_aggr(out=lmv[:], in_=lst[:].rearrange("p a b -> p (a b)")), reads=['lst'], writes=['lmv'])
            S.op('act', lambda e: e.activation(out=lc[:, 0:1], in_=lmv[:, 1:2], func=AF.Ln, bias=LN_EPS), reads=['lmv'],
                 writes=['lc0'])
            S.op('act', lambda e: e.activation(out=lc[:, 0:1], in_=lc[:, 0:1], func=AF.Exp, scale=-0.5), reads=['lc0'],
                 writes=['lc0'])
            S.op('dve', lambda e: e.tensor_scalar(out=lc[:, 1:2], in0=lmv[:, 0:1], scalar1=-1.0, scalar2=lc[:, 0:1],
                                                  op0=ALU.mult, op1=ALU.mult), reads=['lmv', 'lc0'], writes=['lc1'])
            S.op('act', lambda e: e.activation(out=zt, in_=zt, func=AF.Identity, scale=lc[:, 0:1], bias=lc[:, 1:2]),
                 reads=[zk, 'lc0', 'lc1'], writes=[zk])
            S.op('dve', lambda e: e.tensor_tensor(out=zt, in0=zt, in1=gt[:], op=ALU.mult), reads=[zk, 'gb'], writes=[zk])
            S.op('pool', lambda e: e.tensor_tensor(out=zt, in0=zt, in1=bt[:], op=ALU.add), reads=[zk, 'bb'], writes=[zk])

        def run_group(x_ap, rows, nprompt, full, samp, final, rmask=None):
            ntl = len(rows)
            Mt = ntl * 128
            with contextlib.ExitStack() as stG:
                xT = sb(stG, "xT", [128, KC, Mt], BF16)
                brA = brB = None
                if full:
                    brA = sb(stG, "brA", [128, NMV, Mt], BF16)
                    brB = sb(stG, "brB", [128, NHV, Mt], BF16)
                load_xT(x_ap, rows, xT, 'xT')
                chk('ldx')
                mixer_pass(xT, 'xT', nprompt, full, brA, brB, samp, R, final, rmask)
                if full:
                    chk('mix')
                if not full:
                    S.barrier()
                    return
                mrg = R[:, 0:KC * Mt].rearrange("p (a b) -> p a b", b=Mt)
                with contextlib.ExitStack() as st:
                    sga = sb(st, "sga", [128, 2, Mt])
                    sgb = sb(st, "sgb", [128, 2, Mt])
                    m1 = sb(st, "m1", [128, 2, Mt])
                    for d0 in range(0, D, BW):
                        nsub = min(BW, D - d0) // 128
                        for (gname, dst, dk) in (('ga', sga, 'sga'), ('gb', sgb, 'sgb')):
                            wb, wk = wget('w_in', w_in, 0, KC, off[gname] + d0, nsub * 128)
                            for i in range(nsub):
                                for (t0, n) in ntiles(Mt):
                                    pt, pk = nextps()
                                    proj_F(wb, wk, i * 128, 128, xT, 'xT', KC, t0, n, pt, pk)
                                    sigmoid_from('act', dst[:, i, t0:t0 + n], pt[:, 0:n], [pk], [dk])
                        wb, wk = wget('w_ba', w_ba, 0, NMV, d0, nsub * 128)
                        for i in range(nsub):
                            for (t0, n) in ntiles(Mt):
                                pt, pk = nextps()
                                proj_F(wb, wk, i * 128, 128, brA, 'brA', NMV, t0, n, pt, pk)
                                S.op('dve', lambda e, i=i, t0=t0, n=n, pt=pt: e.tensor_tensor(
                                    out=m1[:, i, t0:t0 + n], in0=pt[:, 0:n], in1=sga[:, i, t0:t0 + n], op=ALU.mult),
                                    reads=[pk, 'sga'], writes=['m1'])
                        wb, wk = wget('w_bb', w_bb, 0, NHV, d0, nsub * 128)
                        for i in range(nsub):
                            for (t0, n) in ntiles(Mt):
                                pt, pk = nextps()
                                proj_F(wb, wk, i * 128, 128, brB, 'brB', NHV, t0, n, pt, pk)
                                S.op('dve', lambda e, i=i, t0=t0, n=n, pt=pt: e.tensor_tensor(
                                    out=sgb[:, i, t0:t0 + n], in0=pt[:, 0:n], in1=sgb[:, i, t0:t0 + n], op=ALU.mult),
                                    reads=[pk, 'sgb'], writes=['sgb'])
                                S.op('pool', lambda e, i=i, t0=t0, n=n, d0=d0: e.tensor_tensor(
                                    out=mrg[:, d0 // 128 + i, t0:t0 + n], in0=sgb[:, i, t0:t0 + n], in1=m1[:, i, t0:t0 + n],
                                    op=ALU.add), reads=['sgb', 'm1'], writes=['mrg'])
                    S.barrier()
            S.barrier()
            chk('merge')
            with contextlib.ExitStack() as st:
                z = sb(st, "z", [128, ntl, D])
                gt = sb(st, "gt", [128, D])
                bt = sb(st, "bt", [128, D])
                hid = sb(st, "hid", [128, FG, Mt], BF16)
                rtmp = sb(st, "rtmp", [128, 512])
                lst = sb(st, "lst", [128, (D + 511) // 512, 6])
                lmv = sb(st, "lmv", [128, 2])
                lc = sb(st, "lc", [128, 2])
                zk = lambda ti: 'z%d' % ti
                for ti in range(ntl):
                    S.dma('sp', z[:, ti, :], x_ap[rows[ti]:rows[ti] + 128, :], writes=[zk(ti)])
                S.dma('sp', gt[:], ln1_g.partition_broadcast(128), writes=['gb'])
                S.dma('sp', bt[:], ln1_b.partition_broadcast(128), writes=['bb'])
                for d0 in range(0, D, BW):
                    wb, wk = wget('w_out', w_out, 0, KC, d0, BW)
                    for ti in range(ntl):
                        pt, pk = nextps()
                        proj_T(wb, wk, BW, mrg, 'mrg', KC, ti * 128, pt, pk)
                        S.op('dve', lambda e, ti=ti, d0=d0, pt=pt: e.scalar_tensor_tensor(
                            out=z[:, ti, d0:d0 + BW], in0=z[:, ti, d0:d0 + BW], scalar=ALPHA, in1=pt[:, 0:BW],
                            op0=ALU.mult, op1=ALU.add), reads=[zk(ti), pk], writes=[zk(ti)])
                for ti in range(ntl):
                    layernorm(z[:, ti, :], zk(ti), gt, bt, st, lst, lmv, lc)
                S.barrier()
                x1T = mrg
                zb = [sb(st, "zb%d" % i, [128, D], BF16) for i in range(2)]
                for ti in range(ntl):
                    zbt, zbk = zb[ti % 2], 'zb%d' % (ti % 2)
                    copy_op('act' if ti % 2 else 'dve', zbt[:], z[:, ti, :], [zk(ti)], [zbk])
                    for g in range(0, KC, 4):
                        n = min(4, KC - g)
                        pt, pk = nextps()
                        for j in range(n):
                            S.op('pe', lambda e, j=j, g=g, ti=ti, pt=pt: e.matmul(
                                pt[:, j * 128:(j + 1) * 128], lhsT=zbt[:, (g + j) * 128:(g + j + 1) * 128], rhs=identb[:],
                                start=True, stop=True), reads=[zbk, 'identb'], writes=[pk])
                        copy_op(evac_eng(), x1T[:, g:g + n, ti * 128:(ti + 1) * 128],
                                pt[:, 0:n * 128].rearrange("p (a b) -> p a b", b=128), [pk], ['x1T'])
                S.dma('sp', gt[:], ln2_g.partition_broadcast(128), writes=['gb'])
                S.dma('sp', bt[:], ln2_b.partition_broadcast(128), writes=['bb'])
                for fg in range(NFG):
                    for f0 in range(0, FG * 128, BW):
                        wb, wk = wget('w_up', w_up, 0, KC, fg * FG * 128 + f0, BW)
                        for i in range(BW // 128):
                            for (t0, n) in ntiles(Mt):
                                pt, pk = nextps()
                                proj_F(wb, wk, i * 128, 128, x1T, 'x1T', KC, t0, n, pt, pk)
                                S.op('act', lambda e, n=n, pt=pt: e.activation(out=rtmp[:, 0:n], in_=pt[:, 0:n], func=AF.Relu),
                                     reads=[pk], writes=['rtmp'])
                                S.op('dve', lambda e, i=i, f0=f0, t0=t0, n=n: e.tensor_tensor(
                                    out=hid[:, f0 // 128 + i, t0:t0 + n], in0=rtmp[:, 0:n], in1=rtmp[:, 0:n], op=ALU.mult),
                                    reads=['rtmp'], writes=['hid'])
                    for d0 in range(0, D, BW):
                        wb, wk = wget('w_dn', w_dn, fg * FG * 128, FG, d0, BW)
                        for ti in range(ntl):
                            pt, pk = nextps()
                            proj_T(wb, wk, BW, hid, 'hid', FG, ti * 128, pt, pk)
                            if fg == 0:
                                S.op('dve', lambda e, ti=ti, d0=d0, pt=pt: e.scalar_tensor_tensor(
                                    out=z[:, ti, d0:d0 + BW], in0=z[:, ti, d0:d0 + BW], scalar=ALPHA, in1=pt[:, 0:BW],
                                    op0=ALU.mult, op1=ALU.add), reads=[zk(ti), pk], writes=[zk(ti)])
                            else:
                                S.op('dve', lambda e, ti=ti, d0=d0, pt=pt: e.tensor_tensor(
                                    out=z[:, ti, d0:d0 + BW], in0=z[:, ti, d0:d0 + BW], in1=pt[:, 0:BW], op=ALU.add),
                                    reads=[zk(ti), pk], writes=[zk(ti)])
                for ti in range(ntl):
                    layernorm(z[:, ti, :], zk(ti), gt, bt, st, lst, lmv, lc)
                    S.dma('sp', y_main[rows[ti]:rows[ti] + 128, :], z[:, ti, :], reads=[zk(ti)])
                S.barrier()

        chkcnt = {}

        def chk(tag):
            chkcnt[tag] = chkcnt.get(tag, 0) + 1
            if DBG['stop'] == tag or DBG['stop'] == '%s#%d' % (tag, chkcnt[tag]):
                S.dead = True

        chk_holder[0] = chk

        def _drive():
            chk('consts')
            with contextlib.ExitStack() as stp:
                rstp = sb(stp, "rstp", [128, NTP * 128])
                P(lambda e: e.memset(rstp[:], 1.0), ['rstp'])
                P(lambda e: e.memset(rstp[:].rearrange("p (a b) -> p a b", b=128)[:, :, 0:1], 0.0), ['rstp'], ['rstp'])
                S.barrier()
                run_group(x_pre, [t * 128 for t in range(NTP)], NTP, False, False, False, rmask=rstp)
            chk('pre')
            groups = list(range(0, NTP, GT))
            for gi, g0 in enumerate(groups):
                last = gi == len(groups) - 1
                rows = [t * 128 for t in range(g0, min(NTP, g0 + GT))]
                npr = len(rows)
                if last:
                    rows = rows + [NTP * 128]
                run_group(x_main, rows, npr, True, last, last)
                chk('g%d' % gi)
        try:
            _drive()
        except _Stop:
            pass
        S.finish()
    return nc, S


_CACHE = {}


def _get_program(cfg_key):
    if cfg_key not in _CACHE:
        cfg = Cfg(*cfg_key)
        rec = []
        build(cfg, wplan=None, record=rec)
        nc, S = build(cfg, wplan=rec, record=None)
        _CACHE[cfg_key] = (cfg, nc)
    return _CACHE[cfg_key]


def run_module(inputs, D, DFF, SEQ, BATCH, DEC_BATCH, core_ids=None):
    TH = SEQ // 2
    cfg, nc = _get_program((D, DFF, TH))
    ncores = 2 * BATCH
    assert DEC_BATCH == ncores * NSQ
    f = lambda a: np.ascontiguousarray(np.asarray(a, dtype=np.float32))
    xp, xs = f(inputs["x_prompt"]), f(inputs["x_sample"])
    stC, stn = f(inputs["state_mlstm_C"])[0], f(inputs["state_mlstm_n"])[0]
    stm, stS = f(inputs["state_mlstm_m"])[0], f(inputs["state_hgrn_S"])[0]
    shared = {
        "lb_logits": f(inputs["hg_lb_logits"]).reshape(2 * HG_H, 128),
        "w_in": f(inputs["w_in"])[0], "b_ig": f(inputs["b_ig"]).reshape(1, ML_H), "b_fg": f(inputs["b_fg"]).reshape(1, ML_H),
        "ml_g": f(inputs["ml_norm_g"]).reshape(-1, 128), "hg_g": f(inputs["hg_norm_g"]).reshape(-1, 128),
        "w_ba": f(inputs["w_branch_a"])[0], "w_bb": f(inputs["w_branch_b"])[0], "w_out": f(inputs["w_out"])[0],
        "ln1_g": f(inputs["ln1_g"]).reshape(1, D), "ln1_b": f(inputs["ln1_b"]).reshape(1, D),
        "w_up": f(inputs["w_up"])[0], "w_dn": f(inputs["w_down"])[0],
        "ln2_g": f(inputs["ln2_g"]).reshape(1, D), "ln2_b": f(inputs["ln2_b"]).reshape(1, D),
    }
    in_maps = []
    for c in range(ncores):
        b, half = c // 2, c % 2
        sl = slice(c * NSQ, (c + 1) * NSQ)
        m = dict(shared)
        m["x_pre"] = np.ascontiguousarray(xp[b, 0:TH]) if half == 1 else np.zeros((TH, D), np.float32)
        m["x_main"] = np.ascontiguousarray(np.concatenate([xp[b, half * TH:(half + 1) * TH], xs[sl].reshape(NSQ * SQL, D)], 0))
        m["st_C"] = np.ascontiguousarray(stC[sl])
        m["st_n"] = np.ascontiguousarray(stn[sl].reshape(NSQ * ML_H * 2, 128))
        m["st_m"] = np.ascontiguousarray(stm[sl])
        m["st_S"] = np.ascontiguousarray(stS[sl])
        in_maps.append(m)
    res = run_bass_kernel_spmd(nc, in_maps, core_ids=list(range(ncores)) if core_ids is None else core_ids)
    rs = res.results
    y_p = np.zeros((BATCH, SEQ, D), np.float32)
    y_s = np.zeros((DEC_BATCH, SQL, D), np.float32)
    Cp = np.zeros((1, BATCH, ML_H, ML_DK, ML_DV), np.float32)
    n_p = np.zeros((1, BATCH, ML_H, ML_DK), np.float32)
    mp = np.zeros((1, BATCH, ML_H), np.float32)
    Sp = np.zeros((1, BATCH, HG_H, HG_DK, HG_DV), np.float32)
    Cs = np.zeros((1, DEC_BATCH, ML_H, ML_DK, ML_DV), np.float32)
    ns = np.zeros((1, DEC_BATCH, ML_H, ML_DK), np.float32)
    ms = np.zeros((1, DEC_BATCH, ML_H), np.float32)
    Ss = np.zeros((1, DEC_BATCH, HG_H, HG_DK, HG_DV), np.float32)
    for c in range(ncores):
        b, half = c // 2, c % 2
        r = rs[c]
        sl = slice(c * NSQ, (c + 1) * NSQ)
        y_p[b, half * TH:(half + 1) * TH] = r["y_main"][0:TH]
        y_s[sl] = r["y_main"][TH:].reshape(NSQ, SQL, D)
        if half == 1:
            Cp[0, b] = r["o_Cp"]
            n_p[0, b] = r["o_np"].reshape(ML_H, ML_DK)
            mp[0, b] = r["o_mp"].reshape(ML_H)
            Sp[0, b] = r["o_Sp"]
        Cs[0, sl] = r["o_Cs"]
        ns[0, sl] = r["o_ns"].reshape(NSQ, ML_H, ML_DK)
        ms[0, sl] = r["o_ms"]
        Ss[0, sl] = r["o_Ss"]
    return (y_p, y_s, Cp, n_p, mp, Sp, Cs, ns, ms, Ss)


def kernel(**inputs):
    return run_module(inputs, D=2048, DFF=8192, SEQ=2048, BATCH=4, DEC_BATCH=128)
```

```python
import contextlib
import numpy as np
import concourse.bass as bass
import concourse.mybir as mybir
from concourse.alu_op_type import AluOpType as ALU
from concourse.bass_utils import run_bass_kernel_spmd

F32 = mybir.dt.float32
BF16 = mybir.dt.bfloat16
AF = mybir.ActivationFunctionType
AX = mybir.AxisListType

ML_H, ML_DK, ML_DV = 4, 256, 512
HG_H, HG_DK, HG_DV = 8, 128, 256
LN_EPS = 1e-5
ALPHA = 2.0 ** 0.25
NSQ = 16
SQL = 8
BW = 256
NEG = -60000.0


class _Proxy:
    def __init__(self):
        self.call = None

    def __getattr__(self, name):
        def f(*a, **k):
            self.call = (name, a, k)
            return self
        return f


class Sch:
    def __init__(self, nc, ndma=6):
        self.nc = nc
        self.E = {'pe': nc.tensor, 'act': nc.scalar, 'dve': nc.vector, 'pool': nc.gpsimd, 'sp': nc.sync}
        self.semh, self.cnt = {}, {}
        for k in self.E:
            self.semh[k] = nc.alloc_semaphore("c_" + k)
            self.cnt[k] = 0
        self.waited = {k: {} for k in self.E}
        self.dq = {}
        for q in ('sp', 'pool'):
            sems = []
            for j in range(ndma):
                nm = "d_%s%d" % (q, j)
                self.semh[nm] = nc.alloc_semaphore(nm)
                self.cnt[nm] = 0
                sems.append(nm)
            self.dq[q] = [sems, 0]
        self.track = {}
        self.n_inst = 0
        self.dead = False
        self.rec = None

    def _need(self, eng, reads, writes):
        need = {}

        def add(tok):
            if tok is None:
                return
            s, v = tok
            if eng == 'pe' and s == 'pe':
                return
            if self.waited[eng].get(s, 0) < v:
                need[s] = max(need.get(s, 0), v)
        for k in reads:
            t = self.track.get(k)
            if t:
                add(t[0])
        for k in writes:
            t = self.track.get(k)
            if t:
                add(t[0])
                for r in t[1]:
                    add(r)
        for s, v in need.items():
            self.E[eng].wait_ge(self.semh[s], v)
            self.waited[eng][s] = v

    def _upd(self, tok, reads, writes):
        for k in reads:
            t = self.track.setdefault(k, [None, []])
            t[1].append(tok)
            if len(t[1]) > 64:
                t[1] = t[1][-64:] if False else self._compact(t[1])
        for k in writes:
            self.track[k] = [tok, []]

    @staticmethod
    def _compact(lst):
        best = {}
        for s, v in lst:
            best[s] = max(best.get(s, 0), v)
        return list(best.items())

    def start_record(self):
        assert self.rec is None
        self.rec = []

    def stop_record(self):
        r, self.rec = self.rec, None
        return r

    def replay_interleaved(self, lists):
        its = [iter(l) for l in lists]
        live = list(range(len(its)))
        while live:
            for i in list(live):
                item = next(its[i], None)
                if item is None:
                    live.remove(i)
                    continue
                if item[0] == 'op':
                    _, eng, (name, a, k), reads, writes = item
                    self.op(eng, lambda e, name=name, a=a, k=k: getattr(e, name)(*a, **k), reads, writes)
                else:
                    _, q, out, in_, reads, writes, kw = item
                    self.dma(q, out, in_, reads, writes, **kw)

    def op(self, eng, fn, reads=(), writes=()):
        if self.dead:
            return
        if self.rec is not None:
            p = _Proxy()
            fn(p)
            assert p.call is not None
            self.rec.append(('op', eng, p.call, tuple(reads), tuple(writes)))
            return
        pr = [k for k in reads if k.startswith('ps') and k not in writes]
        if pr:
            writes = list(writes) + pr
        self._need(eng, reads, writes)
        inst = fn(self.E[eng])
        self.cnt[eng] += 1
        inst.then_inc(self.semh[eng], 1)
        self._upd((eng, self.cnt[eng]), reads, writes)
        self.n_inst += 1

    def dma(self, q, out, in_, reads=(), writes=(), **kw):
        if self.dead:
            return
        if self.rec is not None:
            self.rec.append(('dma', q, out, in_, tuple(reads), tuple(writes), kw))
            return
        sems, idx = self.dq[q]
        s = sems[idx % len(sems)]
        self.dq[q][1] = idx + 1
        if self.cnt[s] > 0 and self.waited[q].get(s, 0) < self.cnt[s]:
            self.E[q].wait_ge(self.semh[s], self.cnt[s])
            self.waited[q][s] = self.cnt[s]
        self._need(q, reads, writes)
        inst = self.E[q].dma_start(out=out, in_=in_, **kw)
        self.cnt[s] += 16
        inst.then_inc(self.semh[s], 16)
        self._upd((s, self.cnt[s]), reads, writes)
        self.n_inst += 1

    def barrier(self):
        if self.dead:
            return
        assert self.rec is None
        for eng in self.E:
            for s, v in self.cnt.items():
                if v > 0 and s != eng and self.waited[eng].get(s, 0) < v:
                    self.E[eng].wait_ge(self.semh[s], v)
                    self.waited[eng][s] = v
        self.track = {}

    def finish(self):
        self.dead = False
        self.barrier()


class Cfg:
    def __init__(self, D, DFF, TH):
        self.D, self.DFF, self.TH = D, DFF, TH
        self.KC = D // 128
        self.NTP = TH // 128
        self.NT = self.NTP + 1
        self.M = self.NT * 128
        self.GT = max(1, self.NTP // 2)
        self.MG = (self.GT + 1) * 128
        self.RSZ = max(self.KC * self.MG, 2 * (4 * self.MG + 1280 * (self.GT + 1)))
        self.FG = min(16, DFF // 128)
        self.NFG = DFF // (128 * self.FG)
        self.MLQK, self.MLV = ML_H * ML_DK, ML_H * ML_DV
        self.HGK, self.HGV = HG_H * HG_DK, HG_H * HG_DV
        o = 0
        self.off = {}
        for nm, sz in (('mq', self.MLQK), ('mk', self.MLQK), ('mv', self.MLV), ('mi', ML_H), ('mf', ML_H),
                       ('mo', self.MLV), ('hq', self.HGK), ('hf', self.HGK), ('hi', self.HGV), ('hg', self.HGV),
                       ('ga', D), ('gb', D)):
            self.off[nm] = o
            o += sz
        self.DIN = o


DBG = {'stop': None}


class _Stop(Exception):
    pass


def build(cfg, wplan=None, record=None):
    D, KC, TH, NTP, NT, M = cfg.D, cfg.KC, cfg.TH, cfg.NTP, cfg.NT, cfg.M
    off = cfg.off
    nc = bass.Bass("TRN2", target_bir_lowering=False)

    def din(name, shape):
        return nc.dram_tensor(name, list(shape), F32, kind="ExternalInput").ap()

    def dout(name, shape):
        return nc.dram_tensor(name, list(shape), F32, kind="ExternalOutput").ap()

    x_pre = din("x_pre", [TH, D])
    x_main = din("x_main", [M, D])
    st_C = din("st_C", [NSQ, ML_H, ML_DK, ML_DV])
    st_n = din("st_n", [NSQ * ML_H * 2, 128])
    st_m = din("st_m", [NSQ, ML_H])
    st_S = din("st_S", [NSQ, HG_H, HG_DK, HG_DV])
    lb_logits = din("lb_logits", [2 * HG_H, 128])
    w_in = din("w_in", [D, cfg.DIN])
    b_ig = din("b_ig", [1, ML_H])
    b_fg = din("b_fg", [1, ML_H])
    ml_g = din("ml_g", [cfg.MLV // 128, 128])
    hg_g = din("hg_g", [cfg.HGV // 128, 128])
    w_ba = din("w_ba", [cfg.MLV, D])
    w_bb = din("w_bb", [cfg.HGV, D])
    w_out = din("w_out", [D, D])
    ln1_g = din("ln1_g", [1, D])
    ln1_b = din("ln1_b", [1, D])
    w_up = din("w_up", [D, cfg.DFF])
    w_dn = din("w_dn", [cfg.DFF, D])
    ln2_g = din("ln2_g", [1, D])
    ln2_b = din("ln2_b", [1, D])

    y_main = dout("y_main", [M, D])
    o_Cp = dout("o_Cp", [ML_H, ML_DK, ML_DV])
    o_np = dout("o_np", [ML_H * 2, 128])
    o_mp = dout("o_mp", [1, ML_H])
    o_Sp = dout("o_Sp", [HG_H, HG_DK, HG_DV])
    o_Cs = dout("o_Cs", [NSQ, ML_H, ML_DK, ML_DV])
    o_ns = dout("o_ns", [NSQ * ML_H * 2, 128])
    o_ms = dout("o_ms", [NSQ, ML_H])
    o_Ss = dout("o_Ss", [NSQ, HG_H, HG_DK, HG_DV])

    S = Sch(nc)
    es = contextlib.ExitStack()

    uid = [0]

    def sb(stack, name, shape, dt=F32):
        uid[0] += 1
        return stack.enter_context(nc.sbuf_tensor("%s_%d" % (name, uid[0]), list(shape), dt))

    with es:
        ps = [es.enter_context(nc.psum_tensor("ps%d" % i, [128, 512], F32)) for i in range(8)]
        PK = ["ps%d" % i for i in range(8)]

        ident = sb(es, "ident", [128, 128])
        identb = sb(es, "identb", [128, 128], BF16)
        ones = sb(es, "ones", [128, 128])
        onesb = sb(es, "onesb", [128, 2], BF16)
        mst_c = sb(es, "mst_c", [128, 128])
        mst_b = sb(es, "mst_b", [128, 128])
        mts_b = sb(es, "mts_b", [128, 128])
        neg_c = sb(es, "neg_c", [128, 128])
        neg_b = sb(es, "neg_b", [128, 128])
        sel_c = sb(es, "sel_c", [128, 128])
        sel_b = sb(es, "sel_b", [128, 128])
        ind = sb(es, "ind", [128, NSQ])
        lastind = sb(es, "lastind", [128, NSQ])
        indT = sb(es, "indT", [NSQ, 128])
        blkb = sb(es, "blkb", [128, NSQ, 128], BF16)
        GT, MG = cfg.GT, cfg.MG
        rst = sb(es, "rst", [128, MG])
        gcol_a = sb(es, "gcol_a", [128, cfg.MLV // 128])
        gcol_b = sb(es, "gcol_b", [128, cfg.HGV // 128])
        lbc = sb(es, "lbc", [128, 2 * HG_H])
        bigb = sb(es, "bigb", [128, ML_H])
        nbfg = sb(es, "nbfg", [128, ML_H])
        ctmp = sb(es, "ctmp", [128, 128])

        def P(fn, w, r=()):
            S.op('pool', fn, reads=r, writes=w)

        P(lambda e: e.memset(ident[:], 1.0), ['ident'])
        P(lambda e: e.affine_select(out=ident[:], in_=ident[:], pattern=[[-1, 128]], compare_op=ALU.is_equal,
                                    fill=0.0, base=0, channel_multiplier=1), ['ident'], ['ident'])
        P(lambda e: e.tensor_copy(out=identb[:], in_=ident[:]), ['identb'], ['ident'])
        P(lambda e: e.memset(ones[:], 1.0), ['ones'])
        P(lambda e: e.memset(onesb[:], 1.0), ['onesb'])
        P(lambda e: e.memset(mst_c[:], 1.0), ['mst_c'])
        P(lambda e: e.affine_select(out=mst_c[:], in_=mst_c[:], pattern=[[1, 128]], compare_op=ALU.is_ge,
                                    fill=0.0, base=0, channel_multiplier=-1), ['mst_c'], ['mst_c'])
        P(lambda e: e.memset(ctmp[:], 1.0), ['ctmp'])
        P(lambda e: e.affine_select(out=ctmp[:], in_=ctmp[:], pattern=[[-1, 128]], compare_op=ALU.is_ge,
                                    fill=0.0, base=0, channel_multiplier=1), ['ctmp'], ['ctmp'])
        P(lambda e: e.tensor_scalar(out=neg_c[:], in0=ctmp[:], scalar1=-1.0, scalar2=-NEG, op0=ALU.add, op1=ALU.mult),
          ['neg_c'], ['ctmp'])
        P(lambda e: e.memset(sel_c[:], 1.0), ['sel_c'])
        P(lambda e: e.affine_select(out=sel_c[:], in_=sel_c[:], pattern=[[0, 128]], compare_op=ALU.is_equal,
                                    fill=0.0, base=-127, channel_multiplier=1), ['sel_c'], ['sel_c'])
        P(lambda e: e.memset(ind[:], 1.0), ['ind'])
        P(lambda e: e.affine_select(out=ind[:], in_=ind[:], pattern=[[-SQL, NSQ]], compare_op=ALU.is_ge,
                                    fill=0.0, base=0, channel_multiplier=1), ['ind'], ['ind'])
        P(lambda e: e.affine_select(out=ind[:], in_=ind[:], pattern=[[SQL, NSQ]], compare_op=ALU.is_ge,
                                    fill=0.0, base=SQL - 1, channel_multiplier=-1), ['ind'], ['ind'])
        P(lambda e: e.memset(lastind[:], 1.0), ['lastind'])
        P(lambda e: e.affine_select(out=lastind[:], in_=lastind[:], pattern=[[-SQL, NSQ]], compare_op=ALU.is_equal,
                                    fill=0.0, base=-(SQL - 1), channel_multiplier=1), ['lastind'], ['lastind'])
        P(lambda e: e.memset(indT[:], 1.0), ['indT'])
        P(lambda e: e.affine_select(out=indT[:], in_=indT[:], pattern=[[1, 128]], compare_op=ALU.is_ge,
                                    fill=0.0, base=0, channel_multiplier=-SQL), ['indT'], ['indT'])
        P(lambda e: e.affine_select(out=indT[:], in_=indT[:], pattern=[[-1, 128]], compare_op=ALU.is_ge,
                                    fill=0.0, base=SQL - 1, channel_multiplier=SQL), ['indT'], ['indT'])
        P(lambda e: e.memset(blkb[:], 1.0), ['blkb'])
        P(lambda e: e.affine_select(out=blkb[:], in_=blkb[:], pattern=[[-SQL, NSQ], [1, 128]], compare_op=ALU.is_ge,
                                    fill=0.0, base=0, channel_multiplier=0), ['blkb'], ['blkb'])
        P(lambda e: e.affine_select(out=blkb[:], in_=blkb[:], pattern=[[SQL, NSQ], [-1, 128]], compare_op=ALU.is_ge,
                                    fill=0.0, base=SQL - 1, channel_multiplier=0), ['blkb'], ['blkb'])
        S.op('pe', lambda e: e.matmul(ps[0][:, 0:128], lhsT=indT[:], rhs=indT[:], start=True, stop=True),
             reads=['indT'], writes=[PK[0]])
        S.op('dve', lambda e: e.tensor_tensor(out=mst_b[:], in0=ps[0][:, 0:128], in1=mst_c[:], op=ALU.mult),
             reads=[PK[0], 'mst_c'], writes=['mst_b'])
        S.op('dve', lambda e: e.tensor_tensor(out=mts_b[:], in0=ps[0][:, 0:128], in1=ctmp[:], op=ALU.mult),
             reads=[PK[0], 'ctmp'], writes=['mts_b'])
        S.op('dve', lambda e: e.tensor_scalar(out=neg_b[:], in0=mts_b[:], scalar1=-1.0, scalar2=-NEG, op0=ALU.add,
                                              op1=ALU.mult), reads=['mts_b'], writes=['neg_b'])
        S.op('dve', lambda e: e.tensor_copy(out=sel_b[:].rearrange("p (q j) -> p q j", j=SQL),
                                            in_=lastind[:].unsqueeze(2).to_broadcast([128, NSQ, SQL])),
             reads=['lastind'], writes=['sel_b'])
        P(lambda e: e.memset(rst[:], 1.0), ['rst'])
        P(lambda e: e.memset(rst[:, 0:GT * 128].rearrange("p (a b) -> p a b", b=128)[:, :, 0:1], 0.0), ['rst'], ['rst'])
        P(lambda e: e.memset(rst[:, GT * 128:MG].rearrange("p (a b) -> p a b", b=SQL)[:, :, 0:1], 0.0), ['rst'], ['rst'])

        rowt = sb(es, "rowt", [128, 128])

        def load_cols(src_rows, nrows, dst, dkey):
            S.dma('sp', rowt[0:nrows, :], src_rows, writes=['rowt'])
            S.op('pe', lambda e: e.matmul(ps[1][:, 0:nrows], lhsT=rowt[0:nrows, :], rhs=ident[0:nrows, 0:nrows],
                                          start=True, stop=True), reads=['rowt', 'ident'], writes=[PK[1]])
            S.op('dve', lambda e: e.tensor_copy(out=dst, in_=ps[1][:, 0:nrows]), reads=[PK[1]], writes=[dkey])

        load_cols(ml_g, cfg.MLV // 128, gcol_a[:], 'gcol_a')
        load_cols(hg_g, cfg.HGV // 128, gcol_b[:], 'gcol_b')
        load_cols(lb_logits, 2 * HG_H, lbc[:], 'lbc')
        S.op('dve', lambda e: e.tensor_tensor(out=lbc[:, 0:HG_H], in0=lbc[:, 0:HG_H], in1=lbc[:, HG_H:2 * HG_H],
                                              op=ALU.subtract), reads=['lbc'], writes=['lbc'])
        S.op('act', lambda e: e.activation(out=lbc[:, 0:HG_H], in_=lbc[:, 0:HG_H], func=AF.Exp, scale=-1.0),
             reads=['lbc'], writes=['lbc'])
        S.op('act', lambda e: e.activation(out=lbc[:, 0:HG_H], in_=lbc[:, 0:HG_H], func=AF.Ln, bias=1.0),
             reads=['lbc'], writes=['lbc'])
        S.op('act', lambda e: e.activation(out=lbc[:, 0:HG_H], in_=lbc[:, 0:HG_H], func=AF.Exp, scale=-1.0),
             reads=['lbc'], writes=['lbc'])
        S.op('dve', lambda e: e.tensor_scalar(out=lbc[:, HG_H:2 * HG_H], in0=lbc[:, 0:HG_H], scalar1=-1.0, scalar2=1.0,
                                              op0=ALU.mult, op1=ALU.add), reads=['lbc'], writes=['lbc'])
        S.dma('sp', bigb[:], b_ig.partition_broadcast(128), writes=['bigb'])
        S.dma('sp', nbfg[:], b_fg.partition_broadcast(128), writes=['nbfg'])
        S.op('dve', lambda e: e.tensor_scalar(out=nbfg[:], in0=nbfg[:], scalar1=-1.0, scalar2=None, op0=ALU.mult),
             reads=['nbfg'], writes=['nbfg'])

        NWB = 4
        wbuf = [sb(es, "wbuf%d" % i, [128, 16, BW], BF16) for i in range(NWB)]
        wstate = {'issued': 0, 'req': 0}

        def _issue(spec):
            wap, r0, kc, c0, ncol = spec
            i = wstate['issued']
            wstate['issued'] += 1
            buf = wbuf[i % NWB]
            key = 'wbuf%d' % (i % NWB)
            src = wap[r0:r0 + kc * 128, c0:c0 + ncol].rearrange("(c p) n -> p c n", p=128)
            S.dma('pool', buf[:, 0:kc, 0:ncol], src, writes=[key])

        def wget(wname, wap, r0, kc, c0, ncol):
            if record is not None:
                record.append((wname, r0, kc, c0, ncol))
            idx = wstate['req']
            wstate['req'] += 1
            if wstate['issued'] <= idx:
                assert wstate['issued'] == idx
                _issue((wap, r0, kc, c0, ncol))
            if wplan is not None:
                assert wplan[idx] == (wname, r0, kc, c0, ncol), (idx, wplan[idx], (wname, r0, kc, c0, ncol))
                while wstate['issued'] < min(len(wplan), idx + NWB):
                    nm, a, b, c, d = wplan[wstate['issued']]
                    _issue((WMAP[nm], a, b, c, d))
            return wbuf[idx % NWB], 'wbuf%d' % (idx % NWB)

        WMAP = {'w_in': w_in, 'w_ba': w_ba, 'w_bb': w_bb, 'w_out': w_out, 'w_up': w_up, 'w_dn': w_dn}
        psrot = {'i': 0}

        def nextps(lo=0, hi=8):
            i = lo + psrot['i'] % (hi - lo)
            psrot['i'] += 1
            return ps[i], PK[i]

        evrot = {'i': 0}

        def evac_eng():
            evrot['i'] += 1
            return 'dve' if evrot['i'] % 2 else 'act'

        def copy_op(eng, out, in_, reads, writes, scale=None):
            if scale is not None:
                if eng == 'act':
                    S.op('act', lambda e: e.activation(out=out, in_=in_, func=AF.Copy, scale=scale), reads=reads, writes=writes)
                else:
                    S.op(eng, lambda e: e.tensor_scalar(out=out, in0=in_, scalar1=scale, scalar2=None, op0=ALU.mult),
                         reads=reads, writes=writes)
            elif eng == 'act':
                S.op('act', lambda e: e.activation(out=out, in_=in_, func=AF.Copy), reads=reads, writes=writes)
            else:
                S.op(eng, lambda e: e.tensor_copy(out=out, in_=in_), reads=reads, writes=writes)

        def proj_T(wb, wkey, ncol, actT, akey, kc, tok0, pst, pkey):
            for c in range(kc):
                S.op('pe', lambda e, c=c: e.matmul(pst[:, 0:ncol], lhsT=actT[:, c, tok0:tok0 + 128], rhs=wb[:, c, 0:ncol],
                                                   start=(c == 0), stop=(c == kc - 1)),
                     reads=[wkey, akey], writes=[pkey])

        def proj_F(wb, wkey, m0, mcol, actT, akey, kc, tok0, ntok, pst, pkey):
            for c in range(kc):
                S.op('pe', lambda e, c=c: e.matmul(pst[0:mcol, 0:ntok], lhsT=wb[:, c, m0:m0 + mcol],
                                                   rhs=actT[:, c, tok0:tok0 + ntok], start=(c == 0), stop=(c == kc - 1)),
                     reads=[wkey, akey], writes=[pkey])

        def ntiles(total):
            out, t = [], 0
            while t < total:
                n = min(512, total - t)
                out.append((t, n))
                t += n
            return out

        def sigmoid_from(eng_in, out, in_, reads, writes):
            S.op('act', lambda e: e.activation(out=out, in_=in_, func=AF.Exp, scale=-1.0), reads=reads, writes=writes)
            S.op('act', lambda e: e.activation(out=out, in_=out, func=AF.Ln, bias=1.0), reads=writes, writes=writes)
            S.op('act', lambda e: e.activation(out=out, in_=out, func=AF.Exp, scale=-1.0), reads=writes, writes=writes)

        def load_xT(x_ap, rows, xT, xkey):
            with contextlib.ExitStack() as st2:
                xt = [sb(st2, "xt%d" % i, [128, D], BF16) for i in range(3)]
                for t, r0 in enumerate(rows):
                    b = xt[t % 3]
                    bk = "xt%d" % (t % 3)
                    S.dma('pool', b[:], x_ap[r0:r0 + 128, :], writes=[bk])
                    for g in range(0, KC, 4):
                        n = min(4, KC - g)
                        pt, pk = nextps()
                        for j in range(n):
                            S.op('pe', lambda e, j=j: e.matmul(pt[:, j * 128:(j + 1) * 128],
                                                               lhsT=b[:, (g + j) * 128:(g + j + 1) * 128], rhs=identb[:],
                                                               start=True, stop=True), reads=[bk, 'identb'], writes=[pk])
                        copy_op(evac_eng(), xT[:, g:g + n, t * 128:(t + 1) * 128],
                                pt[:, 0:n * 128].rearrange("p (a b) -> p a b", b=128), [pk], [xkey])
                S.barrier()

        Cst = [sb(es, "Cst%d" % h, [128, 2, ML_DV]) for h in range(ML_H)]
        nst = sb(es, "nst", [128, ML_H, 2])
        mst = sb(es, "mst", [128, ML_H])
        Sst = [sb(es, "Sst%d" % h, [128, HG_DV]) for h in range(HG_H)]
        for h in range(ML_H):
            P(lambda e, h=h: e.memset(Cst[h][:], 0.0), ['Cst%d' % h])
        P(lambda e: e.memset(nst[:], 0.0), ['nst'])
        P(lambda e: e.memset(mst[:], 0.0), ['mst'])
        for h in range(HG_H):
            P(lambda e, h=h: e.memset(Sst[h][:], 0.0), ['Sst%d' % h])

        chk_holder = [lambda tag: None]
        def mixer_pass(xT, xkey, ntile, full, brA, brB, samp, R, final, rmask=None):
            rmask = rst if rmask is None else rmask
            tiles = list(range(ntile)) + ([ntile] if samp else [])
            ntl = len(tiles)
            Mtok = ntl * 128
            with contextlib.ExitStack() as st:
                ift = sb(st, "ift", [128, ntl, 2 * ML_H])
                wb, wk = wget('w_in', w_in, 0, KC, off['mi'], 2 * ML_H)
                for ti in range(ntl):
                    pt, pk = nextps()
                    proj_T(wb, wk, 2 * ML_H, xT, xkey, KC, ti * 128, pt, pk)
                    copy_op('dve', ift[:, ti, :], pt[:, 0:2 * ML_H], [pk], ['ift'])
                chk_holder[0]('ift')
                ro = [0]

                def rview(n, inner):
                    v = R[:, ro[0]:ro[0] + n].rearrange("p (a b) -> p a b", b=inner)
                    ro[0] += n
                    return v
                PBS = []
                for i in range(2):
                    PB = {'k_tok': rview(ntl * ML_DK, ML_DK), 'v_tok': rview(ntl * ML_DV, ML_DV)}
                    if full:
                        PB['qT'] = rview(2 * Mtok, Mtok)
                        PB['kT'] = rview(2 * Mtok, Mtok)
                        PB['og'] = rview(ntl * ML_DV, ML_DV)
                    PBS.append(PB)
                if samp:
                    NCB = 3
                    Cf = [sb(st, "Cf%d" % i, [128, 2, ML_DV]) for i in range(NCB)]
                    Cfb = [sb(st, "Cfb%d" % i, [128, 2, ML_DV], BF16) for i in range(NCB)]
                    qTq = [sb(st, "qTq%d" % i, [128, 2, 128], BF16) for i in range(2)]
                    kwq = [sb(st, "kwq%d" % i, [128, ML_DK], BF16) for i in range(2)]
                    nall = sb(st, "nall", [128, NSQ * ML_H * 2])
                    nallb = sb(st, "nallb", [128, NSQ * ML_H * 2], BF16)
                    nrow = sb(st, "nrow", [128, 128])
                    Dbc = sb(st, "Dbc", [128, NSQ])
                    dsel = sb(st, "dsel", [128, NSQ])
                    msamp = sb(st, "msamp", [NSQ, ML_H])
                    mtok = sb(st, "mtok", [128, ML_H])
                    mnew_all = sb(st, "mnew_all", [128, ML_H])
                    mout = sb(st, "mout", [NSQ, ML_H])
                    S.dma('sp', nrow[:], st_n, writes=['nrow'])
                    chk_holder[0]('sA0')
                    S.op('pe', lambda e: e.matmul(ps[4][:, 0:128], lhsT=nrow[:], rhs=ident[:], start=True, stop=True),
                         reads=['nrow', 'ident'], writes=[PK[4]])
                    copy_op('dve', nall[:], ps[4][:, 0:128], [PK[4]], ['nall'])
                    chk_holder[0]('sA05')
                    copy_op('act', nallb[:], ps[4][:, 0:128], [PK[4]], ['nallb'])
                    chk_holder[0]('sA1')
                    S.dma('sp', msamp[:], st_m, writes=['msamp'])
                    S.op('pe', lambda e: e.matmul(ps[4][:, 0:ML_H], lhsT=indT[:], rhs=msamp[:], start=True, stop=True),
                         reads=['indT', 'msamp'], writes=[PK[4]])
                    copy_op('dve', mtok[:], ps[4][:, 0:ML_H], [PK[4]], ['mtok'])
                    chk_holder[0]('sA')

                def ml_rec(h, X, tiles):
                    sfx = X['sfx']
                    K = lambda nm: nm + sfx
                    Cbf, nbf, sc, bm, mprev = X['Cbf'], X['nbf'], X['sc'], X['bm'], X['mprev']
                    diagc, logd, dmat, sdm, sdT, kw = X['diagc'], X['logd'], X['dmat'], X['sdm'], X['sdT'], X['kw']
                    hbuf, h2, bst, bmv = X['hbuf'], X['h2'], X['bst'], X['bmv']
                    PB = X['PB']
                    k_tok, v_tok = PB['k_tok'], PB['v_tok']
                    qT, kT, og = PB.get('qT'), PB.get('kT'), PB.get('og')
                    bG, bQ, bA, bB = X['bG'], X['bQ'], X['bA'], X['bB']
                    kG, kQ, kA, kB = PK[bG], PK[bQ], PK[bA], PK[bB]
                    cB0, cC0, cS0, cQ0, cT0, cN0 = X['cols']
                    urot = [0]

                    def ubank():
                        bnk = X['ub'][urot[0] % len(X['ub'])]
                        urot[0] += 1
                        return ps[bnk], PK[bnk]
                    Ck, nk, mk_ = 'Cst%d' % h, 'nst%d' % h, 'mst%d' % h
                    S.op('act', lambda e: e.activation(out=Cbf[:], in_=Cst[h][:], func=AF.Copy), reads=[Ck], writes=[K('Cbf')])
                    S.op('dve', lambda e: e.tensor_copy(out=nbf[:], in_=nst[:, h, :]), reads=[nk], writes=[K('nbf')])
                    S.op('dve', lambda e: e.tensor_copy(out=mprev[:], in_=mst[:, h:h + 1]), reads=[mk_], writes=[K('mprev')])
                    for ti in tiles:
                        is_s = samp and ti == ntl - 1
                        mstm = mst_b if is_s else mst_c
                        mstk = 'mst_b' if is_s else 'mst_c'
                        negm_ = neg_b if is_s else neg_c
                        negk = 'neg_b' if is_s else 'neg_c'
                        selm = sel_b if is_s else sel_c
                        selk = 'sel_b' if is_s else 'sel_c'
                        tok = slice(ti * 128, (ti + 1) * 128)
                        col = lambda j: sc[:, j:j + 1]
                        if is_s:
                            S.op('dve', lambda e: e.tensor_copy(out=mprev[:], in_=mtok[:, h:h + 1]), reads=['mtok'],
                                 writes=[K('mprev')])
                        S.op('dve', lambda e: e.tensor_scalar(out=col(0), in0=ift[:, ti, h:h + 1], scalar1=bigb[:, h:h + 1],
                                                              scalar2=None, op0=ALU.add), reads=['ift', 'bigb'], writes=[K('sc0')])
                        S.op('act', lambda e: e.activation(out=col(1), in_=ift[:, ti, ML_H + h:ML_H + h + 1], func=AF.Exp,
                                                           scale=-1.0, bias=nbfg[:, h:h + 1]), reads=['ift', 'nbfg'],
                             writes=[K('sc1')])
                        S.op('act', lambda e: e.activation(out=col(1), in_=col(1), func=AF.Ln, bias=1.0), reads=[K('sc1')],
                             writes=[K('sc1')])
                        S.op('pe', lambda e: e.matmul(ps[bG][:, cB0:cB0 + 1], lhsT=mstm[:], rhs=col(1), start=True, stop=True),
                             reads=[mstk, K('sc1')], writes=[kG])
                        S.op('dve', lambda e: e.tensor_copy(out=bm[:, 0:1], in_=ps[bG][:, cB0:cB0 + 1]), reads=[kG], writes=[K('bm0')])
                        S.op('dve', lambda e: e.tensor_tensor(out=col(2), in0=col(0), in1=bm[:, 0:1], op=ALU.add),
                             reads=[K('sc0'), K('bm0')], writes=[K('sc2')])
                        S.op('dve', lambda e: e.tensor_scalar(out=diagc[:], in0=ident[:], scalar1=col(2), scalar2=None,
                                                              op0=ALU.mult), reads=['ident', K('sc2')], writes=[K('diagc')])
                        S.op('pe', lambda e: e.matmul(ps[bG][:, cC0:cC0 + 128], lhsT=ones[:], rhs=diagc[:], start=True, stop=True),
                             reads=['ones', K('diagc')], writes=[kG])
                        S.op('dve', lambda e: e.scalar_tensor_tensor(out=logd[:], in0=ps[bG][:, cC0:cC0 + 128], scalar=bm[:, 0:1],
                                                                     in1=negm_[:], op0=ALU.subtract, op1=ALU.add),
                             reads=[kG, K('bm0'), negk], writes=[K('logd')])
                        S.op('dve', lambda e: e.tensor_reduce(out=col(3), in_=logd[:], axis=AX.X, op=ALU.max),
                             reads=[K('logd')], writes=[K('sc3')])
                        S.op('dve', lambda e: e.tensor_tensor(out=col(4), in0=mprev[:], in1=bm[:, 0:1], op=ALU.subtract),
                             reads=[K('mprev'), K('bm0')], writes=[K('sc4')])
                        S.op('dve', lambda e: e.tensor_tensor(out=bm[:, 1:2], in0=col(4), in1=col(3), op=ALU.max),
                             reads=[K('sc4'), K('sc3')], writes=[K('bm1')])
                        S.op('dve', lambda e: e.tensor_scalar(out=col(5), in0=bm[:, 1:2], scalar1=-1.0, scalar2=None,
                                                              op0=ALU.mult), reads=[K('bm1')], writes=[K('sc5')])
                        S.op('pe', lambda e: e.matmul(ps[bG][:, cS0:cS0 + 2], lhsT=selm[:], rhs=bm[:], start=True, stop=True),
                             reads=[selk, K('bm0'), K('bm1')], writes=[kG])
                        S.op('dve', lambda e: e.tensor_tensor(out=col(8), in0=bm[:, 0:1], in1=ps[bG][:, cS0:cS0 + 1],
                                                              op=ALU.subtract), reads=[K('bm0'), kG], writes=[K('sc8')])
                        S.op('dve', lambda e: e.tensor_tensor(out=col(8), in0=col(8), in1=col(0), op=ALU.add),
                             reads=[K('sc8'), K('sc0')], writes=[K('sc8')])
                        S.op('dve', lambda e: e.tensor_scalar(out=col(9), in0=ps[bG][:, cS0 + 1:cS0 + 2], scalar1=-1.0, scalar2=None,
                                                              op0=ALU.mult), reads=[kG], writes=[K('sc9')])
                        S.op('act', lambda e: e.activation(out=col(10), in_=col(8), func=AF.Exp, bias=col(9)),
                             reads=[K('sc8'), K('sc9')], writes=[K('sc10')])
                        S.op('dve', lambda e: e.tensor_tensor(out=col(11), in0=mprev[:], in1=ps[bG][:, cS0:cS0 + 1],
                                                              op=ALU.subtract), reads=[K('mprev'), kG], writes=[K('sc11')])
                        S.op('act', lambda e: e.activation(out=col(12), in_=col(11), func=AF.Exp, bias=col(9)),
                             reads=[K('sc11'), K('sc9')], writes=[K('sc12')])
                        if full:
                            S.op('act', lambda e: e.activation(out=dmat[:], in_=logd[:], func=AF.Exp, bias=col(5)),
                                 reads=[K('logd'), K('sc5')], writes=[K('dmat')])
                            S.op('act', lambda e: e.activation(out=col(6), in_=col(4), func=AF.Exp, bias=col(5)),
                                 reads=[K('sc4'), K('sc5')], writes=[K('sc6')])
                            S.op('act', lambda e: e.activation(out=col(7), in_=col(5), func=AF.Exp), reads=[K('sc5')],
                                 writes=[K('sc7')])
                            for c in range(2):
                                S.op('pe', lambda e, c=c: e.matmul(ps[bQ][:, cQ0:cQ0 + 128], lhsT=qT[:, c, tok], rhs=kT[:, c, tok],
                                                                   start=(c == 0), stop=(c == 1)), reads=[K('qT'), K('kT')],
                                     writes=[kQ])
                            S.op('dve', lambda e: e.scalar_tensor_tensor(out=sdm[:], in0=ps[bQ][:, cQ0:cQ0 + 128], scalar=1.0,
                                                                         in1=dmat[:], op0=ALU.mult, op1=ALU.mult,
                                                                         accum_out=col(13)),
                                 reads=[kQ, K('dmat')], writes=[K('sdm'), K('sc13')])
                            S.op('pe', lambda e: e.matmul(ps[bQ][:, cT0:cT0 + 128], lhsT=sdm[:], rhs=ident[:], start=True, stop=True),
                                 reads=[K('sdm'), 'ident'], writes=[kQ])
                            copy_op('act', sdT[:], ps[bQ][:, cT0:cT0 + 128], [kQ], [K('sdT')])
                            S.op('pe', lambda e: e.matmul(ps[bA][:, :], lhsT=sdT[:], rhs=v_tok[:, ti, :], start=True, stop=True),
                                 reads=[K('sdT'), K('v_tok')], writes=[kA])
                        if not is_s:
                            if full:
                                for c in range(2):
                                    S.op('pe', lambda e, c=c: e.matmul(ps[bB][:, :], lhsT=qT[:, c, tok], rhs=Cbf[:, c, :],
                                                                       start=(c == 0), stop=(c == 1)), reads=[K('qT'), K('Cbf')],
                                         writes=[kB])
                                for c in range(2):
                                    S.op('pe', lambda e, c=c: e.matmul(ps[bQ][:, cN0:cN0 + 1], lhsT=qT[:, c, tok], rhs=nbf[:, c:c + 1],
                                                                       start=(c == 0), stop=(c == 1)), reads=[K('qT'), K('nbf')],
                                         writes=[kQ])
                            S.op('dve', lambda e: e.tensor_scalar(out=kw[:], in0=k_tok[:, ti, :], scalar1=col(10), scalar2=None,
                                                                  op0=ALU.mult), reads=[K('k_tok'), K('sc10')], writes=[K('kw')])
                            for c in range(2):
                                pt, pk = ubank()
                                S.op('pe', lambda e, c=c, pt=pt: e.matmul(pt[:, :], lhsT=kw[:, c * 128:(c + 1) * 128],
                                                                          rhs=v_tok[:, ti, :], start=True, stop=True),
                                     reads=[K('kw'), K('v_tok')], writes=[pk])
                                S.op('dve', lambda e, c=c, pt=pt: e.scalar_tensor_tensor(
                                    out=Cst[h][:, c, :], in0=Cst[h][:, c, :], scalar=col(12), in1=pt[:, :],
                                    op0=ALU.mult, op1=ALU.add), reads=[Ck, K('sc12'), pk, K('Cbf')], writes=[Ck])
                            pt, pk = ubank()
                            for c in range(2):
                                S.op('pe', lambda e, c=c, pt=pt: e.matmul(pt[:, c:c + 1], lhsT=kw[:, c * 128:(c + 1) * 128],
                                                                          rhs=onesb[:, 0:1], start=True, stop=True),
                                     reads=[K('kw'), 'onesb'], writes=[pk])
                            S.op('dve', lambda e, pt=pt: e.scalar_tensor_tensor(
                                out=nst[:, h, :], in0=nst[:, h, :], scalar=col(12), in1=pt[:, 0:2],
                                op0=ALU.mult, op1=ALU.add), reads=[nk, K('sc12'), pk, K('nbf')], writes=[nk])
                            S.op('act', lambda e: e.activation(out=Cbf[:], in_=Cst[h][:], func=AF.Copy), reads=[Ck],
                                 writes=[K('Cbf')])
                            S.op('dve', lambda e: e.tensor_copy(out=nbf[:], in_=nst[:, h, :]), reads=[nk], writes=[K('nbf')])
                            S.op('dve', lambda e: e.tensor_copy(out=mprev[:], in_=ps[bG][:, cS0 + 1:cS0 + 2]), reads=[kG],
                                 writes=[K('mprev')])
                            S.op('dve', lambda e: e.tensor_copy(out=mst[:, h:h + 1], in_=ps[bG][:, cS0 + 1:cS0 + 2]), reads=[kG],
                                 writes=[mk_])
                        else:
                            S.op('dve', lambda e: e.tensor_copy(out=mnew_all[:, h:h + 1], in_=ps[bG][:, cS0 + 1:cS0 + 2]),
                                 reads=[kG], writes=['mnew_all'])
                            S.op('dve', lambda e: e.tensor_scalar(out=dsel[:], in0=lastind[:], scalar1=col(12), scalar2=None,
                                                                  op0=ALU.mult), reads=['lastind', K('sc12')], writes=['dsel'])
                            pt, pk = ubank()
                            S.op('pe', lambda e, pt=pt: e.matmul(pt[:, 0:NSQ], lhsT=ones[:], rhs=dsel[:], start=True, stop=True),
                                 reads=['ones', 'dsel'], writes=[pk])
                            copy_op('dve', Dbc[:], pt[:, 0:NSQ], [pk], ['Dbc'])
                            S.op('dve', lambda e: e.tensor_scalar(out=kw[:], in0=k_tok[:, ti, :], scalar1=col(10), scalar2=None,
                                                                  op0=ALU.mult), reads=[K('k_tok'), K('sc10')], writes=[K('kw')])
                            def cload(q):
                                S.dma('sp', Cf[q % NCB][:], st_C[q, h].rearrange("(c p) v -> p c v", p=128),
                                      writes=['Cf%d' % (q % NCB)])
                            for q in range(min(NCB - 1, NSQ)):
                                cload(q)
                            for q in range(NSQ):
                                if q + NCB - 1 < NSQ:
                                    cload(q + NCB - 1)
                                cf, cfk = Cf[q % NCB], 'Cf%d' % (q % NCB)
                                cb, cbk = Cfb[q % NCB], 'Cfb%d' % (q % NCB)
                                qq, qqk = qTq[q % 2], 'qTq%d' % (q % 2)
                                kq, kqk = kwq[q % 2], 'kwq%d' % (q % 2)
                                S.op('dve', lambda e, q=q, qq=qq: e.tensor_tensor(
                                    out=qq[:], in0=qT[:, :, tok], in1=blkb[:, q:q + 1, :].to_broadcast([128, 2, 128]),
                                    op=ALU.mult), reads=[K('qT'), 'blkb'], writes=[qqk])
                                S.op('dve', lambda e, q=q, kq=kq: e.tensor_scalar(
                                    out=kq[:], in0=kw[:], scalar1=ind[:, q:q + 1], scalar2=None, op0=ALU.mult),
                                    reads=[K('kw'), 'ind'], writes=[kqk])
                                copy_op('act', cb[:], cf[:], [cfk], [cbk])
                                for c in range(2):
                                    S.op('pe', lambda e, c=c, q=q, cb=cb: e.matmul(
                                        ps[bB][:, :], lhsT=qq[:, c, :], rhs=cb[:, c, :],
                                        start=(q == 0 and c == 0), stop=(q == NSQ - 1 and c == 1)),
                                        reads=[qqk, cbk], writes=[kB])
                                for c in range(2):
                                    j = (q * ML_H + h) * 2 + c
                                    S.op('pe', lambda e, c=c, q=q, j=j: e.matmul(
                                        ps[bQ][:, cN0:cN0 + 1], lhsT=qq[:, c, :], rhs=nallb[:, j:j + 1],
                                        start=(q == 0 and c == 0), stop=(q == NSQ - 1 and c == 1)),
                                        reads=[qqk, 'nallb'], writes=[kQ])
                                for c in range(2):
                                    pt, pk = ubank()
                                    S.op('pe', lambda e, c=c, q=q, pt=pt: e.matmul(
                                        pt[:, :], lhsT=kq[:, c * 128:(c + 1) * 128], rhs=v_tok[:, ti, :],
                                        start=True, stop=True), reads=[kqk, K('v_tok')], writes=[pk])
                                    S.op('dve', lambda e, c=c, q=q, pt=pt, cf=cf: e.scalar_tensor_tensor(
                                        out=cf[:, c, :], in0=cf[:, c, :], scalar=Dbc[:, q:q + 1], in1=pt[:, :],
                                        op0=ALU.mult, op1=ALU.add), reads=[cfk, 'Dbc', pk, cbk], writes=[cfk])
                                S.dma('pool', o_Cs[q, h].rearrange("(c p) v -> p c v", p=128), cf[:], reads=[cfk])
                                pt, pk = ubank()
                                for c in range(2):
                                    S.op('pe', lambda e, c=c, q=q, pt=pt: e.matmul(
                                        pt[:, c:c + 1], lhsT=kq[:, c * 128:(c + 1) * 128], rhs=onesb[:, 0:1],
                                        start=True, stop=True), reads=[kqk, 'onesb'], writes=[pk])
                                j0 = (q * ML_H + h) * 2
                                S.op('dve', lambda e, q=q, pt=pt, j0=j0: e.scalar_tensor_tensor(
                                    out=nall[:, j0:j0 + 2], in0=nall[:, j0:j0 + 2], scalar=Dbc[:, q:q + 1], in1=pt[:, 0:2],
                                    op0=ALU.mult, op1=ALU.add), reads=['nall', 'Dbc', pk], writes=['nall'])
                        if is_s:
                            chk_holder[0]('sC')
                        if full:
                            S.op('dve', lambda e: e.scalar_tensor_tensor(out=col(14), in0=ps[bQ][:, cN0:cN0 + 1], scalar=col(6),
                                                                         in1=col(13), op0=ALU.mult, op1=ALU.add),
                                 reads=[kQ, K('sc6'), K('sc13')], writes=[K('sc14')])
                            S.op('act', lambda e: e.activation(out=col(14), in_=col(14), func=AF.Abs), reads=[K('sc14')],
                                 writes=[K('sc14')])
                            S.op('dve', lambda e: e.tensor_tensor(out=col(14), in0=col(14), in1=col(7), op=ALU.max),
                                 reads=[K('sc14'), K('sc7')], writes=[K('sc14')])
                            S.op('dve', lambda e: e.reciprocal(out=col(15), in_=col(14)), reads=[K('sc14')], writes=[K('sc15')])
                            S.op('dve', lambda e: e.tensor_tensor(out=col(16), in0=col(15), in1=col(6), op=ALU.mult),
                                 reads=[K('sc15'), K('sc6')], writes=[K('sc16')])
                            S.op('act', lambda e: e.activation(out=h2[:], in_=ps[bB][:, :], func=AF.Identity, scale=col(16)),
                                 reads=[kB, K('sc16')], writes=[K('h2')])
                            S.op('dve', lambda e: e.scalar_tensor_tensor(out=hbuf[:], in0=ps[bA][:, :], scalar=col(15),
                                                                         in1=h2[:], op0=ALU.mult, op1=ALU.add),
                                 reads=[kA, K('sc15'), K('h2')], writes=[K('hbuf')])
                            S.op('dve', lambda e: e.bn_stats(out=bst[:], in_=hbuf[:]), reads=[K('hbuf')], writes=[K('bst')])
                            S.op('dve', lambda e: e.bn_aggr(out=bmv[:], in_=bst[:]), reads=[K('bst')], writes=[K('bmv')])
                            S.op('act', lambda e: e.activation(out=col(17), in_=bmv[:, 1:2], func=AF.Ln, bias=LN_EPS),
                                 reads=[K('bmv')], writes=[K('sc17')])
                            S.op('act', lambda e: e.activation(out=col(17), in_=col(17), func=AF.Exp, scale=-0.5),
                                 reads=[K('sc17')], writes=[K('sc17')])
                            S.op('dve', lambda e: e.tensor_scalar(out=col(18), in0=bmv[:, 0:1], scalar1=-1.0, scalar2=col(17),
                                                                  op0=ALU.mult, op1=ALU.mult), reads=[K('bmv'), K('sc17')],
                                 writes=[K('sc18')])
                            S.op('act', lambda e: e.activation(out=h2[:], in_=hbuf[:], func=AF.Identity, scale=col(17),
                                                               bias=col(18)), reads=[K('hbuf'), K('sc17'), K('sc18')], writes=[K('h2')])
                            S.op('dve', lambda e: e.tensor_tensor(out=hbuf[:], in0=h2[:], in1=og[:, ti, :], op=ALU.mult),
                                 reads=[K('h2'), K('og')], writes=[K('hbuf')])
                            for j in range(4):
                                S.op('pe', lambda e, j=j: e.matmul(ps[bA][:, j * 128:(j + 1) * 128],
                                                                   lhsT=hbuf[:, j * 128:(j + 1) * 128], rhs=ident[:],
                                                                   start=True, stop=True), reads=[K('hbuf'), 'ident'],
                                     writes=[kA])
                            for j in range(4):
                                ch = 4 * h + j
                                if j % 2 == 0:
                                    S.op('act', lambda e, j=j, ch=ch: e.activation(
                                        out=brA[:, ch, tok], in_=ps[bA][:, j * 128:(j + 1) * 128], func=AF.Identity,
                                        scale=gcol_a[:, ch:ch + 1]), reads=[kA, 'gcol_a'], writes=[K('brA')])
                                else:
                                    S.op('dve', lambda e, j=j, ch=ch: e.tensor_scalar(
                                        out=brA[:, ch, tok], in0=ps[bA][:, j * 128:(j + 1) * 128], scalar1=gcol_a[:, ch:ch + 1],
                                        scalar2=None, op0=ALU.mult), reads=[kA, 'gcol_a'], writes=[K('brA')])
                def ml_proj(h, PB, psfx):
                    k_tok, v_tok = PB['k_tok'], PB['v_tok']
                    qT, kT, og = PB.get('qT'), PB.get('kT'), PB.get('og')
                    h2 = MX[0]['h2']
                    def tmode(colname, width, dst, dkey, post=None, scale=None):
                        for c0 in range(0, width, BW):
                            wb, wk = wget('w_in', w_in, 0, KC, off[colname] + h * width + c0, BW)
                            for ti in range(ntl):
                                pt, pk = nextps()
                                proj_T(wb, wk, BW, xT, xkey, KC, ti * 128, pt, pk)
                                if post is None:
                                    copy_op(evac_eng(), dst[:, ti, c0:c0 + BW], pt[:, 0:BW], [pk], [dkey], scale=scale)
                                else:
                                    post(dst[:, ti, c0:c0 + BW], pt[:, 0:BW], pk, dkey)

                    def fmode(colname, dst, dkey, scale=None):
                        wb, wk = wget('w_in', w_in, 0, KC, off[colname] + h * ML_DK, BW)
                        for cc in range(2):
                            for (t0, n) in ntiles(Mtok):
                                pt, pk = nextps()
                                proj_F(wb, wk, cc * 128, 128, xT, xkey, KC, t0, n, pt, pk)
                                copy_op(evac_eng(), dst[:, cc, t0:t0 + n], pt[:, 0:n], [pk], [dkey], scale=scale)

                    if full:
                        fmode('mq', qT, 'qT' + psfx)
                        fmode('mk', kT, 'kT' + psfx, scale=ML_DK ** -0.5)
                    tmode('mk', ML_DK, k_tok, 'k_tok' + psfx, scale=ML_DK ** -0.5)
                    tmode('mv', ML_DV, v_tok, 'v_tok' + psfx)
                    if full:
                        osig = sb(st, "osig%d" % h, [128, BW]) if False else None

                        oc = [0]

                        def opost(dst, src, pk, dkey):
                            i2 = oc[0] % 2
                            oc[0] += 1
                            hh, hk = MX[i2]['h2'], 'h2_%d' % i2
                            sigmoid_from('act', hh[:, 0:BW], src, [pk], [hk])
                            S.op('dve', lambda e: e.tensor_copy(out=dst, in_=hh[:, 0:BW]), reads=[hk], writes=[dkey])
                        tmode('mo', ML_DV, og, 'og' + psfx, post=opost)


                def ml_scratch(i):
                    return {'sfx': '_%d' % i,
                            'Cbf': sb(st, "Cbf", [128, 2, ML_DV], BF16), 'nbf': sb(st, "nbf", [128, 2], BF16),
                            'sc': sb(st, "sc", [128, 24]), 'bm': sb(st, "bm", [128, 2]), 'mprev': sb(st, "mprev", [128, 1]),
                            'diagc': sb(st, "diagc", [128, 128]), 'logd': sb(st, "logd", [128, 128]),
                            'dmat': sb(st, "dmat", [128, 128]), 'sdm': sb(st, "sdm", [128, 128]),
                            'sdT': sb(st, "sdT", [128, 128], BF16), 'kw': sb(st, "kw", [128, ML_DK], BF16),
                            'hbuf': sb(st, "hbuf", [128, ML_DV]), 'h2': sb(st, "h2", [128, ML_DV]),
                            'bst': sb(st, "bst", [128, 6]), 'bmv': sb(st, "bmv", [128, 2])}
                MX = [ml_scratch(0), ml_scratch(1)]
                hbuf = MX[0]['hbuf']
                MERGED = (384, 128, 386, 0, 256, 390)
                MX[0].update(bG=4, bQ=4, bA=5, bB=6, cols=MERGED)
                MX[1].update(bG=0, bQ=0, bA=1, bB=2, cols=MERGED)
                for h0 in range(0, ML_H, 2):
                    for i in range(2):
                        ml_proj(h0 + i, PBS[i], '_%d' % i)
                        MX[i]['PB'] = PBS[i]
                    ptiles = list(range(ntile))
                    MX[0]['ub'] = [7]
                    MX[1]['ub'] = [3]
                    lists = []
                    for i in range(2):
                        S.start_record()
                        ml_rec(h0 + i, MX[i], ptiles)
                        lists.append(S.stop_record())
                    S.replay_interleaved(lists)
                    if samp:
                        MX[0]['ub'] = [7, 0, 1, 2, 3]
                        ml_rec(h0, MX[0], [ntile])
                        MX[1]['ub'] = [3, 4, 5, 6, 7]
                        ml_rec(h0 + 1, MX[1], [ntile])
                if final:
                    for h in range(ML_H):
                        S.dma('sp', o_Cp[h].rearrange("(c p) v -> p c v", p=128), Cst[h][:], reads=['Cst%d' % h])
                    S.op('pe', lambda e: e.matmul(ps[4][0:ML_H * 2, 0:128], lhsT=nst[:].rearrange("p h c -> p (h c)"),
                                                  rhs=ident[:], start=True, stop=True), reads=['nst%d' % hh for hh in range(ML_H)] + ['ident'], writes=[PK[4]])
                    copy_op('dve', hbuf[0:ML_H * 2, 0:128], ps[4][0:ML_H * 2, 0:128], [PK[4]], ['hbuf_0'])
                    S.dma('sp', o_np, hbuf[0:ML_H * 2, 0:128], reads=['hbuf_0'])
                    S.dma('sp', o_mp, mst[0:1, :], reads=['mst%d' % hh for hh in range(ML_H)])
                    if samp:
                        S.op('pe', lambda e: e.matmul(ps[4][:, 0:128], lhsT=nall[:], rhs=ident[:], start=True, stop=True),
                             reads=['nall', 'ident'], writes=[PK[4]])
                        copy_op('dve', nrow[:], ps[4][:, 0:128], [PK[4]], ['nrow'])
                        S.dma('sp', o_ns, nrow[:], reads=['nrow'])
                        S.op('pe', lambda e: e.matmul(ps[4][0:NSQ, 256:256 + ML_H], lhsT=lastind[:], rhs=mnew_all[:],
                                                      start=True, stop=True), reads=['lastind', 'mnew_all'], writes=[PK[4]])
                        copy_op('dve', mout[:], ps[4][0:NSQ, 256:256 + ML_H], [PK[4]], ['mout'])
                        S.dma('sp', o_ms, mout[:], reads=['mout'])
                S.barrier()
            chk_holder[0]('ml')

            with contextlib.ExitStack() as st:
                ro = [0]

                def rview2(n, inner=None):
                    v = R[:, ro[0]:ro[0] + n]
                    if inner is not None:
                        v = v.rearrange("p (a b) -> p a b", b=inner)
                    ro[0] += n
                    return v
                qg = [rview2(Mtok) for i in range(2)]
                kg = [rview2(Mtok) for i in range(2)]
                eG = [sb(st, "eG%d" % i, [128, Mtok]) for i in range(2)]
                i_tok = [rview2(ntl * HG_DV, HG_DV) for i in range(2)]
                if full:
                    sg = [rview2(ntl * HG_DV, HG_DV) for i in range(2)]
                fa = sb(st, "fa", [128, Mtok])
                fb = sb(st, "fb", [128, Mtok])
                qf = sb(st, "qf", [128, Mtok])
                NSB = 4
                hres = []
                for i in range(2):
                    X = {'Sbf': sb(st, "Sbf", [128, HG_DV], BF16), 'aTm': sb(st, "aTm", [128, 128], BF16),
                         'kgt': sb(st, "kgt", [128, 128], BF16), 'obuf': sb(st, "obuf", [128, HG_DV]),
                         'tmpS': sb(st, "tmpS", [128, HG_DV]), 'hc': sb(st, "hc", [128, 4]),
                         'banks': (5, 6, 7, 0) if i == 0 else (1, 2, 3, 4)}
                    if samp:
                        X['Sf'] = [sb(st, "Sf%d" % j, [128, HG_DV]) for j in range(NSB)]
                        X['Sfb'] = [sb(st, "Sfb%d" % j, [128, HG_DV], BF16) for j in range(NSB)]
                        X['qgq'] = [sb(st, "qgq%d" % j, [128, 128], BF16) for j in range(2)]
                        X['kgq'] = [sb(st, "kgq%d" % j, [128, 128], BF16) for j in range(2)]
                    hres.append(X)
                obuf = hres[0]['obuf']
                for pr in range(HG_H // 2):
                    if full:
                        wb, wk = wget('w_in', w_in, 0, KC, off['hq'] + pr * BW, BW)
                        for i in range(2):
                            for (t0, n) in ntiles(Mtok):
                                pt, pk = nextps()
                                proj_F(wb, wk, i * 128, 128, xT, xkey, KC, t0, n, pt, pk)
                                copy_op(evac_eng(), (qf if i == 0 else fb)[:, t0:t0 + n], pt[:, 0:n], [pk],
                                        ['qf' if i == 0 else 'fb'])
                    wb, wk = wget('w_in', w_in, 0, KC, off['hf'] + pr * BW, BW)
                    for i in range(2):
                        hd = 2 * pr + i
                        for (t0, n) in ntiles(Mtok):
                            pt, pk = nextps()
                            proj_F(wb, wk, i * 128, 128, xT, xkey, KC, t0, n, pt, pk)
                            sigmoid_from('act', fa[:, t0:t0 + n], pt[:, 0:n], [pk], ['fa'])
                        S.op('dve', lambda e, hd=hd: e.tensor_scalar(out=fa[:], in0=fa[:], scalar1=lbc[:, HG_H + hd:HG_H + hd + 1],
                                                                     scalar2=lbc[:, hd:hd + 1], op0=ALU.mult, op1=ALU.add),
                             reads=['fa', 'lbc'], writes=['fa'])
                        S.op('act', lambda e, i=i: e.activation(out=eG[i][:], in_=fa[:], func=AF.Ln), reads=['fa'],
                             writes=['eG%d' % i])
                        rs = rmask[:, 0:Mtok]
                        S.op('dve', lambda e, i=i: e.tensor_tensor_scan(out=eG[i][:], data0=rs, data1=eG[i][:], initial=0.0,
                                                                        op0=ALU.mult, op1=ALU.add),
                             reads=['rst', 'eG%d' % i], writes=['eG%d' % i])
                        S.op('dve', lambda e: e.tensor_scalar(out=fa[:], in0=fa[:], scalar1=-1.0, scalar2=1.0, op0=ALU.mult,
                                                              op1=ALU.add), reads=['fa'], writes=['fa'])
                        tq = sb(st, "tq%d_%d" % (pr, i), [1, 1]) if False else None
                        S.op('act', lambda e, i=i: e.activation(out=kg[i][:], in_=eG[i][:], func=AF.Exp, scale=-1.0),
                             reads=['eG%d' % i], writes=['kg%d' % i])
                        S.op('dve', lambda e, i=i: e.tensor_tensor(out=kg[i][:], in0=kg[i][:], in1=fa[:], op=ALU.mult),
                             reads=['kg%d' % i, 'fa'], writes=['kg%d' % i])
                        S.op('act', lambda e, i=i: e.activation(out=eG[i][:], in_=eG[i][:], func=AF.Exp),
                             reads=['eG%d' % i], writes=['eG%d' % i])
                        if full:
                            qsrc, qk_ = (qf, 'qf') if i == 0 else (fb, 'fb')
                            S.op('dve', lambda e, i=i, qsrc=qsrc: e.tensor_tensor(out=qg[i][:], in0=qsrc[:], in1=eG[i][:],
                                                                                  op=ALU.mult),
                                 reads=[qk_, 'eG%d' % i], writes=['qg%d' % i])
                    for i in range(2):
                        hd = 2 * pr + i
                        wb, wk = wget('w_in', w_in, 0, KC, off['hi'] + hd * HG_DV, BW)
                        for ti in range(ntl):
                            pt, pk = nextps()
                            proj_T(wb, wk, BW, xT, xkey, KC, ti * 128, pt, pk)
                            copy_op(evac_eng(), i_tok[i][:, ti, :], pt[:, 0:BW], [pk], ['i_tok%d' % i])
                    if full:
                        for i in range(2):
                            hd = 2 * pr + i
                            wb, wk = wget('w_in', w_in, 0, KC, off['hg'] + hd * HG_DV, BW)
                            for ti in range(ntl):
                                pt, pk = nextps()
                                proj_T(wb, wk, BW, xT, xkey, KC, ti * 128, pt, pk)
                                ob, obk = hres[ti % 2]['obuf'], 'obuf_%d' % (ti % 2)
                                sigmoid_from('act', ob[:], pt[:, 0:BW], [pk], [obk])
                                S.op('dve', lambda e, i=i, ti=ti, pt=pt, ob=ob: e.tensor_tensor(
                                    out=sg[i][:, ti, :], in0=pt[:, 0:BW], in1=ob[:], op=ALU.mult),
                                     reads=[pk, obk], writes=['sg%d' % i])
                    def hg_chain(i, hd):
                        X = hres[i]
                        Sbf, aTm, kgt, obuf, tmpS, hc = X['Sbf'], X['aTm'], X['kgt'], X['obuf'], X['tmpS'], X['hc']
                        bA, bO, bT, bS = X['banks']
                        kA, kO, kT, kS = PK[bA], PK[bO], PK[bT], PK[bS]
                        sfx = '_%d' % i
                        Sk = 'Sst%d' % hd
                        qgk, kgk, eGk, itk = 'qg%d' % i, 'kg%d' % i, 'eG%d' % i, 'i_tok%d' % i
                        copy_op('act', Sbf[:], Sst[hd][:], [Sk], ['Sbf' + sfx])
                        for ti in range(ntl):
                            is_s = samp and ti == ntl - 1
                            tok = slice(ti * 128, (ti + 1) * 128)
                            mm = mst_b if is_s else mst_c
                            mmk = 'mst_b' if is_s else 'mst_c'
                            if full:
                                S.op('pe', lambda e: e.matmul(ps[bA][:, 0:128], lhsT=kg[i][:, tok], rhs=qg[i][:, tok],
                                                              start=True, stop=True), reads=[kgk, qgk], writes=[kA])
                                S.op('dve', lambda e: e.tensor_tensor(out=aTm[:], in0=ps[bA][:, 0:128], in1=mm[:], op=ALU.mult),
                                     reads=[kA, mmk], writes=['aTm' + sfx])
                                S.op('pe', lambda e: e.matmul(ps[bO][:, 0:HG_DV], lhsT=aTm[:], rhs=i_tok[i][:, ti, :],
                                                              start=True, stop=False), reads=['aTm' + sfx, itk], writes=[kO])
                            S.op('pe', lambda e: e.matmul(ps[bA][:, 128:256], lhsT=kg[i][:, tok], rhs=identb[:],
                                                          start=True, stop=True), reads=[kgk, 'identb'], writes=[kA])
                            copy_op('act', kgt[:], ps[bA][:, 128:256], [kA], ['kgt' + sfx])
                            if not is_s:
                                if full:
                                    S.op('pe', lambda e: e.matmul(ps[bO][:, 0:HG_DV], lhsT=qg[i][:, tok], rhs=Sbf[:],
                                                                  start=False, stop=True), reads=[qgk, 'Sbf' + sfx], writes=[kO])
                                S.op('pe', lambda e: e.matmul(ps[bS][:, 0:HG_DV], lhsT=kgt[:], rhs=i_tok[i][:, ti, :],
                                                              start=True, stop=True), reads=['kgt' + sfx, itk], writes=[kS])
                                S.op('dve', lambda e: e.tensor_tensor(out=tmpS[:], in0=Sst[hd][:], in1=ps[bS][:, 0:HG_DV],
                                                                      op=ALU.add), reads=[Sk, kS], writes=['tmpS' + sfx])
                                ecol = eG[i][:, ti * 128 + 127:ti * 128 + 128]
                                S.op('act', lambda e: e.activation(out=Sst[hd][:], in_=tmpS[:], func=AF.Identity, scale=ecol),
                                     reads=['tmpS' + sfx, eGk, 'Sbf' + sfx], writes=[Sk])
                                copy_op('dve', Sbf[:], Sst[hd][:], [Sk], ['Sbf' + sfx])
                            else:
                                Sf, Sfb, qgq, kgq = X['Sf'], X['Sfb'], X['qgq'], X['kgq']

                                def sload(q):
                                    S.dma('sp', Sf[q % NSB][:], st_S[q, hd], writes=['Sf%d%s' % (q % NSB, sfx)])
                                for q in range(min(NSB - 1, NSQ)):
                                    sload(q)
                                for q in range(NSQ):
                                    if q + NSB - 1 < NSQ:
                                        sload(q + NSB - 1)
                                    sf, sfk = Sf[q % NSB], 'Sf%d%s' % (q % NSB, sfx)
                                    sfb, sfbk = Sfb[q % NSB], 'Sfb%d%s' % (q % NSB, sfx)
                                    qq, qqk = qgq[q % 2], 'qgq%d%s' % (q % 2, sfx)
                                    kq, kqk = kgq[q % 2], 'kgq%d%s' % (q % 2, sfx)
                                    if full:
                                        S.op('dve', lambda e: e.tensor_tensor(out=qq[:], in0=qg[i][:, tok], in1=blkb[:, q, :],
                                                                              op=ALU.mult), reads=[qgk, 'blkb'], writes=[qqk])
                                    S.op('dve', lambda e: e.tensor_scalar(out=kq[:], in0=kgt[:], scalar1=ind[:, q:q + 1],
                                                                          scalar2=None, op0=ALU.mult),
                                         reads=['kgt' + sfx, 'ind'], writes=[kqk])
                                    if full:
                                        copy_op('act', sfb[:], sf[:], [sfk], [sfbk])
                                        S.op('pe', lambda e: e.matmul(ps[bO][:, 0:HG_DV], lhsT=qq[:], rhs=sfb[:],
                                                                      start=False, stop=(q == NSQ - 1)),
                                             reads=[qqk, sfbk], writes=[kO])
                                    S.op('pe', lambda e: e.matmul(ps[bS][:, 0:HG_DV], lhsT=kq[:], rhs=i_tok[i][:, ti, :],
                                                                  start=True, stop=True), reads=[kqk, itk], writes=[kS])
                                    S.op('dve', lambda e: e.tensor_tensor(out=tmpS[:], in0=sf[:], in1=ps[bS][:, 0:HG_DV],
                                                                          op=ALU.add), reads=[sfk, kS], writes=['tmpS' + sfx])
                                    ecol = eG[i][:, ti * 128 + q * SQL + SQL - 1:ti * 128 + q * SQL + SQL]
                                    S.op('act', lambda e: e.activation(out=sf[:], in_=tmpS[:], func=AF.Identity, scale=ecol),
                                         reads=['tmpS' + sfx, eGk, sfbk], writes=[sfk])
                                    S.dma('pool', o_Ss[q, hd], sf[:], reads=[sfk])
                            if full:
                                S.op('act', lambda e: e.activation(out=obuf[:], in_=ps[bO][:, 0:HG_DV], func=AF.Square,
                                                                   accum_out=hc[:, 0:1]), reads=[kO],
                                     writes=['obuf' + sfx, 'hc0' + sfx])
                                S.op('act', lambda e: e.activation(out=hc[:, 1:2], in_=hc[:, 0:1], func=AF.Ln, scale=1.0 / HG_DV,
                                                                   bias=LN_EPS), reads=['hc0' + sfx], writes=['hc1' + sfx])
                                S.op('act', lambda e: e.activation(out=hc[:, 1:2], in_=hc[:, 1:2], func=AF.Exp, scale=-0.5),
                                     reads=['hc1' + sfx], writes=['hc1' + sfx])
                                S.op('dve', lambda e: e.scalar_tensor_tensor(
                                    out=obuf[:], in0=ps[bO][:, 0:HG_DV], scalar=hc[:, 1:2], in1=sg[i][:, ti, :],
                                    op0=ALU.mult, op1=ALU.mult), reads=[kO, 'hc1' + sfx, 'sg%d' % i, 'obuf' + sfx],
                                    writes=['obuf' + sfx])
                                for j in range(2):
                                    S.op('pe', lambda e: e.matmul(ps[bT][:, j * 128:(j + 1) * 128],
                                                                  lhsT=obuf[:, j * 128:(j + 1) * 128], rhs=ident[:],
                                                                  start=True, stop=True), reads=['obuf' + sfx, 'ident'],
                                         writes=[kT])
                                for j in range(2):
                                    ch = 2 * hd + j
                                    if j == 0:
                                        S.op('act', lambda e: e.activation(
                                            out=brB[:, ch, tok], in_=ps[bT][:, j * 128:(j + 1) * 128], func=AF.Identity,
                                            scale=gcol_b[:, ch:ch + 1]), reads=[kT, 'gcol_b'], writes=['brB' + sfx])
                                    else:
                                        S.op('dve', lambda e: e.tensor_scalar(
                                            out=brB[:, ch, tok], in0=ps[bT][:, j * 128:(j + 1) * 128],
                                            scalar1=gcol_b[:, ch:ch + 1], scalar2=None, op0=ALU.mult),
                                            reads=[kT, 'gcol_b'], writes=['brB' + sfx])

                    lists = []
                    for i in range(2):
                        S.start_record()
                        hg_chain(i, 2 * pr + i)
                        lists.append(S.stop_record())
                    S.replay_interleaved(lists)
                if final:
                    for hd in range(HG_H):
                        S.dma('sp', o_Sp[hd], Sst[hd][:], reads=['Sst%d' % hd])
                S.barrier()


        R = sb(es, "R", [128, cfg.RSZ], BF16)
        NMV, NHV = cfg.MLV // 128, cfg.HGV // 128
        DFF, FG, NFG = cfg.DFF, cfg.FG, cfg.NFG

        def layernorm(zt, zk, gt, bt, st, lst, lmv, lc):
            nchk = (D + 511) // 512
            for j in range(nchk):
                a, b_ = j * 512, min(D, (j + 1) * 512)
                S.op('dve', lambda e, j=j, a=a, b_=b_: e.bn_stats(out=lst[:, j, :], in_=zt[:, a:b_]), reads=[zk], writes=['lst'])
            S.op('dve', lambda e: e.bn_aggr(out=lmv[:], in_=lst[:].rearrange("p a b -> p (a b)")), reads=['lst'], writes=['lmv'])
            S.op('act', lambda e: e.activation(out=lc[:, 0:1], in_=lmv[:, 1:2], func=AF.Ln, bias=LN_EPS), reads=['lmv'],
                 writes=['lc0'])
            S.op('act', lambda e: e.activation(out=lc[:, 0:1], in_=lc[:, 0:1], func=AF.Exp, scale=-0.5), reads=['lc0'],
                 writes=['lc0'])
            S.op('dve', lambda e: e.tensor_scalar(out=lc[:, 1:2], in0=lmv[:, 0:1], scalar1=-1.0, scalar2=lc[:, 0:1],
                                                  op0=ALU.mult, op1=ALU.mult), reads=['lmv', 'lc0'], writes=['lc1'])
            S.op('act', lambda e: e.activation(out=zt, in_=zt, func=AF.Identity, scale=lc[:, 0:1], bias=lc[:, 1:2]),
                 reads=[zk, 'lc0', 'lc1'], writes=[zk])
            S.op('dve', lambda e: e.tensor_tensor(out=zt, in0=zt, in1=gt[:], op=ALU.mult), reads=[zk, 'gb'], writes=[zk])
            S.op('pool', lambda e: e.tensor_tensor(out=zt, in0=zt, in1=bt[:], op=ALU.add), reads=[zk, 'bb'], writes=[zk])

        def run_group(x_ap, rows, nprompt, full, samp, final, rmask=None):
            ntl = len(rows)
            Mt = ntl * 128
            with contextlib.ExitStack() as stG:
                xT = sb(stG, "xT", [128, KC, Mt], BF16)
                brA = brB = None
                if full:
                    brA = sb(stG, "brA", [128, NMV, Mt], BF16)
                    brB = sb(stG, "brB", [128, NHV, Mt], BF16)
                load_xT(x_ap, rows, xT, 'xT')
                chk('ldx')
                mixer_pass(xT, 'xT', nprompt, full, brA, brB, samp, R, final, rmask)
                if full:
                    chk('mix')
                if not full:
                    S.barrier()
                    return
                mrg = R[:, 0:KC * Mt].rearrange("p (a b) -> p a b", b=Mt)
                with contextlib.ExitStack() as st:
                    sga = sb(st, "sga", [128, 2, Mt])
                    sgb = sb(st, "sgb", [128, 2, Mt])
                    m1 = sb(st, "m1", [128, 2, Mt])
                    for d0 in range(0, D, BW):
                        nsub = min(BW, D - d0) // 128
                        for (gname, dst, dk) in (('ga', sga, 'sga'), ('gb', sgb, 'sgb')):
                            wb, wk = wget('w_in', w_in, 0, KC, off[gname] + d0, nsub * 128)
                            for i in range(nsub):
                                for (t0, n) in ntiles(Mt):
                                    pt, pk = nextps()
                                    proj_F(wb, wk, i * 128, 128, xT, 'xT', KC, t0, n, pt, pk)
                                    sigmoid_from('act', dst[:, i, t0:t0 + n], pt[:, 0:n], [pk], [dk])
                        wb, wk = wget('w_ba', w_ba, 0, NMV, d0, nsub * 128)
                        for i in range(nsub):
                            for (t0, n) in ntiles(Mt):
                                pt, pk = nextps()
                                proj_F(wb, wk, i * 128, 128, brA, 'brA', NMV, t0, n, pt, pk)
                                S.op('dve', lambda e, i=i, t0=t0, n=n, pt=pt: e.tensor_tensor(
                                    out=m1[:, i, t0:t0 + n], in0=pt[:, 0:n], in1=sga[:, i, t0:t0 + n], op=ALU.mult),
                                    reads=[pk, 'sga'], writes=['m1'])
                        wb, wk = wget('w_bb', w_bb, 0, NHV, d0, nsub * 128)
                        for i in range(nsub):
                            for (t0, n) in ntiles(Mt):
                                pt, pk = nextps()
                                proj_F(wb, wk, i * 128, 128, brB, 'brB', NHV, t0, n, pt, pk)
                                S.op('dve', lambda e, i=i, t0=t0, n=n, pt=pt: e.tensor_tensor(
                                    out=sgb[:, i, t0:t0 + n], in0=pt[:, 0:n], in1=sgb[:, i, t0:t0 + n], op=ALU.mult),
                                    reads=[pk, 'sgb'], writes=['sgb'])
                                S.op('pool', lambda e, i=i, t0=t0, n=n, d0=d0: e.tensor_tensor(
                                    out=mrg[:, d0 // 128 + i, t0:t0 + n], in0=sgb[:, i, t0:t0 + n], in1=m1[:, i, t0:t0 + n],
                                    op=ALU.add), reads=['sgb', 'm1'], writes=['mrg'])
                    S.barrier()
            S.barrier()
            chk('merge')
            with contextlib.ExitStack() as st:
                z = sb(st, "z", [128, ntl, D])
                gt = sb(st, "gt", [128, D])
                bt = sb(st, "bt", [128, D])
                hid = sb(st, "hid", [128, FG, Mt], BF16)
                rtmp = sb(st, "rtmp", [128, 512])
                lst = sb(st, "lst", [128, (D + 511) // 512, 6])
                lmv = sb(st, "lmv", [128, 2])
                lc = sb(st, "lc", [128, 2])
                zk = lambda ti: 'z%d' % ti
                for ti in range(ntl):
                    S.dma('sp', z[:, ti, :], x_ap[rows[ti]:rows[ti] + 128, :], writes=[zk(ti)])
                S.dma('sp', gt[:], ln1_g.partition_broadcast(128), writes=['gb'])
                S.dma('sp', bt[:], ln1_b.partition_broadcast(128), writes=['bb'])
                for d0 in range(0, D, BW):
                    wb, wk = wget('w_out', w_out, 0, KC, d0, BW)
                    for ti in range(ntl):
                        pt, pk = nextps()
                        proj_T(wb, wk, BW, mrg, 'mrg', KC, ti * 128, pt, pk)
                        S.op('dve', lambda e, ti=ti, d0=d0, pt=pt: e.scalar_tensor_tensor(
                            out=z[:, ti, d0:d0 + BW], in0=z[:, ti, d0:d0 + BW], scalar=ALPHA, in1=pt[:, 0:BW],
                            op0=ALU.mult, op1=ALU.add), reads=[zk(ti), pk], writes=[zk(ti)])
                for ti in range(ntl):
                    layernorm(z[:, ti, :], zk(ti), gt, bt, st, lst, lmv, lc)
                S.barrier()
                x1T = mrg
                zb = [sb(st, "zb%d" % i, [128, D], BF16) for i in range(2)]
                for ti in range(ntl):
                    zbt, zbk = zb[ti % 2], 'zb%d' % (ti % 2)
                    copy_op('act' if ti % 2 else 'dve', zbt[:], z[:, ti, :], [zk(ti)], [zbk])
                    for g in range(0, KC, 4):
                        n = min(4, KC - g)
                        pt, pk = nextps()
                        for j in range(n):
                            S.op('pe', lambda e, j=j, g=g, ti=ti, pt=pt: e.matmul(
                                pt[:, j * 128:(j + 1) * 128], lhsT=zbt[:, (g + j) * 128:(g + j + 1) * 128], rhs=identb[:],
                                start=True, stop=True), reads=[zbk, 'identb'], writes=[pk])
                        copy_op(evac_eng(), x1T[:, g:g + n, ti * 128:(ti + 1) * 128],
                                pt[:, 0:n * 128].rearrange("p (a b) -> p a b", b=128), [pk], ['x1T'])
                S.dma('sp', gt[:], ln2_g.partition_broadcast(128), writes=['gb'])
                S.dma('sp', bt[:], ln2_b.partition_broadcast(128), writes=['bb'])
                for fg in range(NFG):
                    for f0 in range(0, FG * 128, BW):
                        wb, wk = wget('w_up', w_up, 0, KC, fg * FG * 128 + f0, BW)
                        for i in range(BW // 128):
                            for (t0, n) in ntiles(Mt):
                                pt, pk = nextps()
                                proj_F(wb, wk, i * 128, 128, x1T, 'x1T', KC, t0, n, pt, pk)
                                S.op('act', lambda e, n=n, pt=pt: e.activation(out=rtmp[:, 0:n], in_=pt[:, 0:n], func=AF.Relu),
                                     reads=[pk], writes=['rtmp'])
                                S.op('dve', lambda e, i=i, f0=f0, t0=t0, n=n: e.tensor_tensor(
                                    out=hid[:, f0 // 128 + i, t0:t0 + n], in0=rtmp[:, 0:n], in1=rtmp[:, 0:n], op=ALU.mult),
                                    reads=['rtmp'], writes=['hid'])
                    for d0 in range(0, D, BW):
                        wb, wk = wget('w_dn', w_dn, fg * FG * 128, FG, d0, BW)
                        for ti in range(ntl):
                            pt, pk = nextps()
                            proj_T(wb, wk, BW, hid, 'hid', FG, ti * 128, pt, pk)
                            if fg == 0:
                                S.op('dve', lambda e, ti=ti, d0=d0, pt=pt: e.scalar_tensor_tensor(
                                    out=z[:, ti, d0:d0 + BW], in0=z[:, ti, d0:d0 + BW], scalar=ALPHA, in1=pt[:, 0:BW],
                                    op0=ALU.mult, op1=ALU.add), reads=[zk(ti), pk], writes=[zk(ti)])
                            else:
                                S.op('dve', lambda e, ti=ti, d0=d0, pt=pt: e.tensor_tensor(
                                    out=z[:, ti, d0:d0 + BW], in0=z[:, ti, d0:d0 + BW], in1=pt[:, 0:BW], op=ALU.add),
                                    reads=[zk(ti), pk], writes=[zk(ti)])
                for ti in range(ntl):
                    layernorm(z[:, ti, :], zk(ti), gt, bt, st, lst, lmv, lc)
                    S.dma('sp', y_main[rows[ti]:rows[ti] + 128, :], z[:, ti, :], reads=[zk(ti)])
                S.barrier()

        chkcnt = {}

        def chk(tag):
            chkcnt[tag] = chkcnt.get(tag, 0) + 1
            if DBG['stop'] == tag or DBG['stop'] == '%s#%d' % (tag, chkcnt[tag]):
                S.dead = True

        chk_holder[0] = chk

        def _drive():
            chk('consts')
            with contextlib.ExitStack() as stp:
                rstp = sb(stp, "rstp", [128, NTP * 128])
                P(lambda e: e.memset(rstp[:], 1.0), ['rstp'])
                P(lambda e: e.memset(rstp[:].rearrange("p (a b) -> p a b", b=128)[:, :, 0:1], 0.0), ['rstp'], ['rstp'])
                S.barrier()
                run_group(x_pre, [t * 128 for t in range(NTP)], NTP, False, False, False, rmask=rstp)
            chk('pre')
            groups = list(range(0, NTP, GT))
            for gi, g0 in enumerate(groups):
                last = gi == len(groups) - 1
                rows = [t * 128 for t in range(g0, min(NTP, g0 + GT))]
                npr = len(rows)
                if last:
                    rows = rows + [NTP * 128]
                run_group(x_main, rows, npr, True, last, last)
                chk('g%d' % gi)
        try:
            _drive()
        except _Stop:
            pass
        S.finish()
    return nc, S


_CACHE = {}


def _get_program(cfg_key):
    if cfg_key not in _CACHE:
        cfg = Cfg(*cfg_key)
        rec = []
        build(cfg, wplan=None, record=rec)
        nc, S = build(cfg, wplan=rec, record=None)
        _CACHE[cfg_key] = (cfg, nc)
    return _CACHE[cfg_key]


def run_module(inputs, D, DFF, SEQ, BATCH, DEC_BATCH, core_ids=None):
    TH = SEQ // 2
    cfg, nc = _get_program((D, DFF, TH))
    ncores = 2 * BATCH
    assert DEC_BATCH == ncores * NSQ
    f = lambda a: np.ascontiguousarray(np.asarray(a, dtype=np.float32))
    xp, xs = f(inputs["x_prompt"]), f(inputs["x_sample"])
    stC, stn = f(inputs["state_mlstm_C"])[0], f(inputs["state_mlstm_n"])[0]
    stm, stS = f(inputs["state_mlstm_m"])[0], f(inputs["state_hgrn_S"])[0]
    shared = {
        "lb_logits": f(inputs["hg_lb_logits"]).reshape(2 * HG_H, 128),
        "w_in": f(inputs["w_in"])[0], "b_ig": f(inputs["b_ig"]).reshape(1, ML_H), "b_fg": f(inputs["b_fg"]).reshape(1, ML_H),
        "ml_g": f(inputs["ml_norm_g"]).reshape(-1, 128), "hg_g": f(inputs["hg_norm_g"]).reshape(-1, 128),
        "w_ba": f(inputs["w_branch_a"])[0], "w_bb": f(inputs["w_branch_b"])[0], "w_out": f(inputs["w_out"])[0],
        "ln1_g": f(inputs["ln1_g"]).reshape(1, D), "ln1_b": f(inputs["ln1_b"]).reshape(1, D),
        "w_up": f(inputs["w_up"])[0], "w_dn": f(inputs["w_down"])[0],
        "ln2_g": f(inputs["ln2_g"]).reshape(1, D), "ln2_b": f(inputs["ln2_b"]).reshape(1, D),
    }
    in_maps = []
    for c in range(ncores):
        b, half = c // 2, c % 2
        sl = slice(c * NSQ, (c + 1) * NSQ)
        m = dict(shared)
        m["x_pre"] = np.ascontiguousarray(xp[b, 0:TH]) if half == 1 else np.zeros((TH, D), np.float32)
        m["x_main"] = np.ascontiguousarray(np.concatenate([xp[b, half * TH:(half + 1) * TH], xs[sl].reshape(NSQ * SQL, D)], 0))
        m["st_C"] = np.ascontiguousarray(stC[sl])
        m["st_n"] = np.ascontiguousarray(stn[sl].reshape(NSQ * ML_H * 2, 128))
        m["st_m"] = np.ascontiguousarray(stm[sl])
        m["st_S"] = np.ascontiguousarray(stS[sl])
        in_maps.append(m)
    res = run_bass_kernel_spmd(nc, in_maps, core_ids=list(range(ncores)) if core_ids is None else core_ids)
    rs = res.results
    y_p = np.zeros((BATCH, SEQ, D), np.float32)
    y_s = np.zeros((DEC_BATCH, SQL, D), np.float32)
    Cp = np.zeros((1, BATCH, ML_H, ML_DK, ML_DV), np.float32)
    n_p = np.zeros((1, BATCH, ML_H, ML_DK), np.float32)
    mp = np.zeros((1, BATCH, ML_H), np.float32)
    Sp = np.zeros((1, BATCH, HG_H, HG_DK, HG_DV), np.float32)
    Cs = np.zeros((1, DEC_BATCH, ML_H, ML_DK, ML_DV), np.float32)
    ns = np.zeros((1, DEC_BATCH, ML_H, ML_DK), np.float32)
    ms = np.zeros((1, DEC_BATCH, ML_H), np.float32)
    Ss = np.zeros((1, DEC_BATCH, HG_H, HG_DK, HG_DV), np.float32)
    for c in range(ncores):
        b, half = c // 2, c % 2
        r = rs[c]
        sl = slice(c * NSQ, (c + 1) * NSQ)
        y_p[b, half * TH:(half + 1) * TH] = r["y_main"][0:TH]
        y_s[sl] = r["y_main"][TH:].reshape(NSQ, SQL, D)
        if half == 1:
            Cp[0, b] = r["o_Cp"]
            n_p[0, b] = r["o_np"].reshape(ML_H, ML_DK)
            mp[0, b] = r["o_mp"].reshape(ML_H)
            Sp[0, b] = r["o_Sp"]
        Cs[0, sl] = r["o_Cs"]
        ns[0, sl] = r["o_ns"].reshape(NSQ, ML_H, ML_DK)
        ms[0, sl] = r["o_ms"]
        Ss[0, sl] = r["o_Ss"]
    return (y_p, y_s, Cp, n_p, mp, Sp, Cs, ns, ms, Ss)


def kernel(**inputs):
    return run_module(inputs, D=2048, DFF=8192, SEQ=2048, BATCH=4, DEC_BATCH=128)
```

```python
import contextlib
import numpy as np
import concourse.bass as bass
import concourse.mybir as mybir
from concourse.alu_op_type import AluOpType as ALU
from concourse.bass_utils import run_bass_kernel_spmd

F32 = mybir.dt.float32
BF16 = mybir.dt.bfloat16
AF = mybir.ActivationFunctionType
AX = mybir.AxisListType

ML_H, ML_DK, ML_DV = 4, 256, 512
HG_H, HG_DK, HG_DV = 8, 128, 256
LN_EPS = 1e-5
ALPHA = 2.0 ** 0.25
NSQ = 16
SQL = 8
BW = 256
NEG = -60000.0


class _Proxy:
    def __init__(self):
        self.call = None

    def __getattr__(self, name):
        def f(*a, **k):
            self.call = (name, a, k)
            return self
        return f


class Sch:
    def __init__(self, nc, ndma=6):
        self.nc = nc
        self.E = {'pe': nc.tensor, 'act': nc.scalar, 'dve': nc.vector, 'pool': nc.gpsimd, 'sp': nc.sync}
        self.semh, self.cnt = {}, {}
        for k in self.E:
            self.semh[k] = nc.alloc_semaphore("c_" + k)
            self.cnt[k] = 0
        self.waited = {k: {} for k in self.E}
        self.dq = {}
        for q in ('sp', 'pool'):
            sems = []
            for j in range(ndma):
                nm = "d_%s%d" % (q, j)
                self.semh[nm] = nc.alloc_semaphore(nm)
                self.cnt[nm] = 0
                sems.append(nm)
            self.dq[q] = [sems, 0]
        self.track = {}
        self.n_inst = 0
        self.dead = False
        self.rec = None

    def _need(self, eng, reads, writes):
        need = {}

        def add(tok):
            if tok is None:
                return
            s, v = tok
            if eng == 'pe' and s == 'pe':
                return
            if self.waited[eng].get(s, 0) < v:
                need[s] = max(need.get(s, 0), v)
        for k in reads:
            t = self.track.get(k)
            if t:
                add(t[0])
        for k in writes:
            t = self.track.get(k)
            if t:
                add(t[0])
                for r in t[1]:
                    add(r)
        for s, v in need.items():
            self.E[eng].wait_ge(self.semh[s], v)
            self.waited[eng][s] = v

    def _upd(self, tok, reads, writes):
        for k in reads:
            t = self.track.setdefault(k, [None, []])
            t[1].append(tok)
            if len(t[1]) > 64:
                t[1] = t[1][-64:] if False else self._compact(t[1])
        for k in writes:
            self.track[k] = [tok, []]

    @staticmethod
    def _compact(lst):
        best = {}
        for s, v in lst:
            best[s] = max(best.get(s, 0), v)
        return list(best.items())

    def start_record(self):
        assert self.rec is None
        self.rec = []

    def stop_record(self):
        r, self.rec = self.rec, None
        return r

    def replay_interleaved(self, lists):
        its = [iter(l) for l in lists]
        live = list(range(len(its)))
        while live:
            for i in list(live):
                item = next(its[i], None)
                if item is None:
                    live.remove(i)
                    continue
                if item[0] == 'op':
                    _, eng, (name, a, k), reads, writes = item
                    self.op(eng, lambda e, name=name, a=a, k=k: getattr(e, name)(*a, **k), reads, writes)
                else:
                    _, q, out, in_, reads, writes, kw = item
                    self.dma(q, out, in_, reads, writes, **kw)

    def op(self, eng, fn, reads=(), writes=()):
        if self.dead:
            return
        if self.rec is not None:
            p = _Proxy()
            fn(p)
            assert p.call is not None
            self.rec.append(('op', eng, p.call, tuple(reads), tuple(writes)))
            return
        pr = [k for k in reads if k.startswith('ps') and k not in writes]
        if pr:
            writes = list(writes) + pr
        self._need(eng, reads, writes)
        inst = fn(self.E[eng])
        self.cnt[eng] += 1
        inst.then_inc(self.semh[eng], 1)
        self._upd((eng, self.cnt[eng]), reads, writes)
        self.n_inst += 1

    def dma(self, q, out, in_, reads=(), writes=(), **kw):
        if self.dead:
            return
        if self.rec is not None:
            self.rec.append(('dma', q, out, in_, tuple(reads), tuple(writes), kw))
            return
        sems, idx = self.dq[q]
        s = sems[idx % len(sems)]
        self.dq[q][1] = idx + 1
        if self.cnt[s] > 0 and self.waited[q].get(s, 0) < self.cnt[s]:
            self.E[q].wait_ge(self.semh[s], self.cnt[s])
            self.waited[q][s] = self.cnt[s]
        self._need(q, reads, writes)
        inst = self.E[q].dma_start(out=out, in_=in_, **kw)
        self.cnt[s] += 16
        inst.then_inc(self.semh[s], 16)
        self._upd((s, self.cnt[s]), reads, writes)
        self.n_inst += 1

    def barrier(self):
        if self.dead:
            return
        assert self.rec is None
        for eng in self.E:
            for s, v in self.cnt.items():
                if v > 0 and s != eng and self.waited[eng].get(s, 0) < v:
                    self.E[eng].wait_ge(self.semh[s], v)
                    self.waited[eng][s] = v
        self.track = {}

    def finish(self):
        self.dead = False
        self.barrier()


class Cfg:
    def __init__(self, D, DFF, TH):
        self.D, self.DFF, self.TH = D, DFF, TH
        self.KC = D // 128
        self.NTP = TH // 128
        self.NT = self.NTP + 1
        self.M = self.NT * 128
        self.GT = max(1, self.NTP // 2)
        self.MG = (self.GT + 1) * 128
        self.RSZ = max(self.KC * self.MG, 2 * (4 * self.MG + 1280 * (self.GT + 1)))
        self.FG = min(16, DFF // 128)
        self.NFG = DFF // (128 * self.FG)
        self.MLQK, self.MLV = ML_H * ML_DK, ML_H * ML_DV
        self.HGK, self.HGV = HG_H * HG_DK, HG_H * HG_DV
        o = 0
        self.off = {}
        for nm, sz in (('mq', self.MLQK), ('mk', self.MLQK), ('mv', self.MLV), ('mi', ML_H), ('mf', ML_H),
                       ('mo', self.MLV), ('hq', self.HGK), ('hf', self.HGK), ('hi', self.HGV), ('hg', self.HGV),
                       ('ga', D), ('gb', D)):
            self.off[nm] = o
            o += sz
        self.DIN = o


DBG = {'stop': None}


class _Stop(Exception):
    pass


def build(cfg, wplan=None, record=None):
    D, KC, TH, NTP, NT, M = cfg.D, cfg.KC, cfg.TH, cfg.NTP, cfg.NT, cfg.M
    off = cfg.off
    nc = bass.Bass("TRN2", target_bir_lowering=False)

    def din(name, shape):
        return nc.dram_tensor(name, list(shape), F32, kind="ExternalInput").ap()

    def dout(name, shape):
        return nc.dram_tensor(name, list(shape), F32, kind="ExternalOutput").ap()

    x_pre = din("x_pre", [TH, D])
    x_main = din("x_main", [M, D])
    st_C = din("st_C", [NSQ, ML_H, ML_DK, ML_DV])
    st_n = din("st_n", [NSQ * ML_H * 2, 128])
    st_m = din("st_m", [NSQ, ML_H])
    st_S = din("st_S", [NSQ, HG_H, HG_DK, HG_DV])
    lb_logits = din("lb_logits", [2 * HG_H, 128])
    w_in = din("w_in", [D, cfg.DIN])
    b_ig = din("b_ig", [1, ML_H])
    b_fg = din("b_fg", [1, ML_H])
    ml_g = din("ml_g", [cfg.MLV // 128, 128])
    hg_g = din("hg_g", [cfg.HGV // 128, 128])
    w_ba = din("w_ba", [cfg.MLV, D])
    w_bb = din("w_bb", [cfg.HGV, D])
    w_out = din("w_out", [D, D])
    ln1_g = din("ln1_g", [1, D])
    ln1_b = din("ln1_b", [1, D])
    w_up = din("w_up", [D, cfg.DFF])
    w_dn = din("w_dn", [cfg.DFF, D])
    ln2_g = din("ln2_g", [1, D])
    ln2_b = din("ln2_b", [1, D])

    y_main = dout("y_main", [M, D])
    o_Cp = dout("o_Cp", [ML_H, ML_DK, ML_DV])
    o_np = dout("o_np", [ML_H * 2, 128])
    o_mp = dout("o_mp", [1, ML_H])
    o_Sp = dout("o_Sp", [HG_H, HG_DK, HG_DV])
    o_Cs = dout("o_Cs", [NSQ, ML_H, ML_DK, ML_DV])
    o_ns = dout("o_ns", [NSQ * ML_H * 2, 128])
    o_ms = dout("o_ms", [NSQ, ML_H])
    o_Ss = dout("o_Ss", [NSQ, HG_H, HG_DK, HG_DV])

    S = Sch(nc)
    es = contextlib.ExitStack()

    uid = [0]

    def sb(stack, name, shape, dt=F32):
        uid[0] += 1
        return stack.enter_context(nc.sbuf_tensor("%s_%d" % (name, uid[0]), list(shape), dt))

    with es:
        ps = [es.enter_context(nc.psum_tensor("ps%d" % i, [128, 512], F32)) for i in range(8)]
        PK = ["ps%d" % i for i in range(8)]

        ident = sb(es, "ident", [128, 128])
        identb = sb(es, "identb", [128, 128], BF16)
        ones = sb(es, "ones", [128, 128])
        onesb = sb(es, "onesb", [128, 2], BF16)
        mst_c = sb(es, "mst_c", [128, 128])
        mst_b = sb(es, "mst_b", [128, 128])
        mts_b = sb(es, "mts_b", [128, 128])
        neg_c = sb(es, "neg_c", [128, 128])
        neg_b = sb(es, "neg_b", [128, 128])
        sel_c = sb(es, "sel_c", [128, 128])
        sel_b = sb(es, "sel_b", [128, 128])
        ind = sb(es, "ind", [128, NSQ])
        lastind = sb(es, "lastind", [128, NSQ])
        indT = sb(es, "indT", [NSQ, 128])
        blkb = sb(es, "blkb", [128, NSQ, 128], BF16)
        GT, MG = cfg.GT, cfg.MG
        rst = sb(es, "rst", [128, MG])
        gcol_a = sb(es, "gcol_a", [128, cfg.MLV // 128])
        gcol_b = sb(es, "gcol_b", [128, cfg.HGV // 128])
        lbc = sb(es, "lbc", [128, 2 * HG_H])
        bigb = sb(es, "bigb", [128, ML_H])
        nbfg = sb(es, "nbfg", [128, ML_H])
        ctmp = sb(es, "ctmp", [128, 128])

        def P(fn, w, r=()):
            S.op('pool', fn, reads=r, writes=w)

        P(lambda e: e.memset(ident[:], 1.0), ['ident'])
        P(lambda e: e.affine_select(out=ident[:], in_=ident[:], pattern=[[-1, 128]], compare_op=ALU.is_equal,
                                    fill=0.0, base=0, channel_multiplier=1), ['ident'], ['ident'])
        P(lambda e: e.tensor_copy(out=identb[:], in_=ident[:]), ['identb'], ['ident'])
        P(lambda e: e.memset(ones[:], 1.0), ['ones'])
        P(lambda e: e.memset(onesb[:], 1.0), ['onesb'])
        P(lambda e: e.memset(mst_c[:], 1.0), ['mst_c'])
        P(lambda e: e.affine_select(out=mst_c[:], in_=mst_c[:], pattern=[[1, 128]], compare_op=ALU.is_ge,
                                    fill=0.0, base=0, channel_multiplier=-1), ['mst_c'], ['mst_c'])
        P(lambda e: e.memset(ctmp[:], 1.0), ['ctmp'])
        P(lambda e: e.affine_select(out=ctmp[:], in_=ctmp[:], pattern=[[-1, 128]], compare_op=ALU.is_ge,
                                    fill=0.0, base=0, channel_multiplier=1), ['ctmp'], ['ctmp'])
        P(lambda e: e.tensor_scalar(out=neg_c[:], in0=ctmp[:], scalar1=-1.0, scalar2=-NEG, op0=ALU.add, op1=ALU.mult),
          ['neg_c'], ['ctmp'])
        P(lambda e: e.memset(sel_c[:], 1.0), ['sel_c'])
        P(lambda e: e.affine_select(out=sel_c[:], in_=sel_c[:], pattern=[[0, 128]], compare_op=ALU.is_equal,
                                    fill=0.0, base=-127, channel_multiplier=1), ['sel_c'], ['sel_c'])
        P(lambda e: e.memset(ind[:], 1.0), ['ind'])
        P(lambda e: e.affine_select(out=ind[:], in_=ind[:], pattern=[[-SQL, NSQ]], compare_op=ALU.is_ge,
                                    fill=0.0, base=0, channel_multiplier=1), ['ind'], ['ind'])
        P(lambda e: e.affine_select(out=ind[:], in_=ind[:], pattern=[[SQL, NSQ]], compare_op=ALU.is_ge,
                                    fill=0.0, base=SQL - 1, channel_multiplier=-1), ['ind'], ['ind'])
        P(lambda e: e.memset(lastind[:], 1.0), ['lastind'])
        P(lambda e: e.affine_select(out=lastind[:], in_=lastind[:], pattern=[[-SQL, NSQ]], compare_op=ALU.is_equal,
                                    fill=0.0, base=-(SQL - 1), channel_multiplier=1), ['lastind'], ['lastind'])
        P(lambda e: e.memset(indT[:], 1.0), ['indT'])
        P(lambda e: e.affine_select(out=indT[:], in_=indT[:], pattern=[[1, 128]], compare_op=ALU.is_ge,
                                    fill=0.0, base=0, channel_multiplier=-SQL), ['indT'], ['indT'])
        P(lambda e: e.affine_select(out=indT[:], in_=indT[:], pattern=[[-1, 128]], compare_op=ALU.is_ge,
                                    fill=0.0, base=SQL - 1, channel_multiplier=SQL), ['indT'], ['indT'])
        P(lambda e: e.memset(blkb[:], 1.0), ['blkb'])
        P(lambda e: e.affine_select(out=blkb[:], in_=blkb[:], pattern=[[-SQL, NSQ], [1, 128]], compare_op=ALU.is_ge,
                                    fill=0.0, base=0, channel_multiplier=0), ['blkb'], ['blkb'])
        P(lambda e: e.affine_select(out=blkb[:], in_=blkb[:], pattern=[[SQL, NSQ], [-1, 128]], compare_op=ALU.is_ge,
                                    fill=0.0, base=SQL - 1, channel_multiplier=0), ['blkb'], ['blkb'])
        S.op('pe', lambda e: e.matmul(ps[0][:, 0:128], lhsT=indT[:], rhs=indT[:], start=True, stop=True),
             reads=['indT'], writes=[PK[0]])
        S.op('dve', lambda e: e.tensor_tensor(out=mst_b[:], in0=ps[0][:, 0:128], in1=mst_c[:], op=ALU.mult),
             reads=[PK[0], 'mst_c'], writes=['mst_b'])
        S.op('dve', lambda e: e.tensor_tensor(out=mts_b[:], in0=ps[0][:, 0:128], in1=ctmp[:], op=ALU.mult),
             reads=[PK[0], 'ctmp'], writes=['mts_b'])
        S.op('dve', lambda e: e.tensor_scalar(out=neg_b[:], in0=mts_b[:], scalar1=-1.0, scalar2=-NEG, op0=ALU.add,
                                              op1=ALU.mult), reads=['mts_b'], writes=['neg_b'])
        S.op('dve', lambda e: e.tensor_copy(out=sel_b[:].rearrange("p (q j) -> p q j", j=SQL),
                                            in_=lastind[:].unsqueeze(2).to_broadcast([128, NSQ, SQL])),
             reads=['lastind'], writes=['sel_b'])
        P(lambda e: e.memset(rst[:], 1.0), ['rst'])
        P(lambda e: e.memset(rst[:, 0:GT * 128].rearrange("p (a b) -> p a b", b=128)[:, :, 0:1], 0.0), ['rst'], ['rst'])
        P(lambda e: e.memset(rst[:, GT * 128:MG].rearrange("p (a b) -> p a b", b=SQL)[:, :, 0:1], 0.0), ['rst'], ['rst'])

        rowt = sb(es, "rowt", [128, 128])

        def load_cols(src_rows, nrows, dst, dkey):
            S.dma('sp', rowt[0:nrows, :], src_rows, writes=['rowt'])
            S.op('pe', lambda e: e.matmul(ps[1][:, 0:nrows], lhsT=rowt[0:nrows, :], rhs=ident[0:nrows, 0:nrows],
                                          start=True, stop=True), reads=['rowt', 'ident'], writes=[PK[1]])
            S.op('dve', lambda e: e.tensor_copy(out=dst, in_=ps[1][:, 0:nrows]), reads=[PK[1]], writes=[dkey])

        load_cols(ml_g, cfg.MLV // 128, gcol_a[:], 'gcol_a')
        load_cols(hg_g, cfg.HGV // 128, gcol_b[:], 'gcol_b')
        load_cols(lb_logits, 2 * HG_H, lbc[:], 'lbc')
        S.op('dve', lambda e: e.tensor_tensor(out=lbc[:, 0:HG_H], in0=lbc[:, 0:HG_H], in1=lbc[:, HG_H:2 * HG_H],
                                              op=ALU.subtract), reads=['lbc'], writes=['lbc'])
        S.op('act', lambda e: e.activation(out=lbc[:, 0:HG_H], in_=lbc[:, 0:HG_H], func=AF.Exp, scale=-1.0),
             reads=['lbc'], writes=['lbc'])
        S.op('act', lambda e: e.activation(out=lbc[:, 0:HG_H], in_=lbc[:, 0:HG_H], func=AF.Ln, bias=1.0),
             reads=['lbc'], writes=['lbc'])
        S.op('act', lambda e: e.activation(out=lbc[:, 0:HG_H], in_=lbc[:, 0:HG_H], func=AF.Exp, scale=-1.0),
             reads=['lbc'], writes=['lbc'])
        S.op('dve', lambda e: e.tensor_scalar(out=lbc[:, HG_H:2 * HG_H], in0=lbc[:, 0:HG_H], scalar1=-1.0, scalar2=1.0,
                                              op0=ALU.mult, op1=ALU.add), reads=['lbc'], writes=['lbc'])
        S.dma('sp', bigb[:], b_ig.partition_broadcast(128), writes=['bigb'])
        S.dma('sp', nbfg[:], b_fg.partition_broadcast(128), writes=['nbfg'])
        S.op('dve', lambda e: e.tensor_scalar(out=nbfg[:], in0=nbfg[:], scalar1=-1.0, scalar2=None, op0=ALU.mult),
             reads=['nbfg'], writes=['nbfg'])

        NWB = 4
        wbuf = [sb(es, "wbuf%d" % i, [128, 16, BW], BF16) for i in range(NWB)]
        wstate = {'issued': 0, 'req': 0}

        def _issue(spec):
            wap, r0, kc, c0, ncol = spec
            i = wstate['issued']
            wstate['issued'] += 1
            buf = wbuf[i % NWB]
            key = 'wbuf%d' % (i % NWB)
            src = wap[r0:r0 + kc * 128, c0:c0 + ncol].rearrange("(c p) n -> p c n", p=128)
            S.dma('pool', buf[:, 0:kc, 0:ncol], src, writes=[key])

        def wget(wname, wap, r0, kc, c0, ncol):
            if record is not None:
                record.append((wname, r0, kc, c0, ncol))
            idx = wstate['req']
            wstate['req'] += 1
            if wstate['issued'] <= idx:
                assert wstate['issued'] == idx
                _issue((wap, r0, kc, c0, ncol))
            if wplan is not None:
                assert wplan[idx] == (wname, r0, kc, c0, ncol), (idx, wplan[idx], (wname, r0, kc, c0, ncol))
                while wstate['issued'] < min(len(wplan), idx + NWB):
                    nm, a, b, c, d = wplan[wstate['issued']]
                    _issue((WMAP[nm], a, b, c, d))
            return wbuf[idx % NWB], 'wbuf%d' % (idx % NWB)

        WMAP = {'w_in': w_in, 'w_ba': w_ba, 'w_bb': w_bb, 'w_out': w_out, 'w_up': w_up, 'w_dn': w_dn}
        psrot = {'i': 0}

        def nextps(lo=0, hi=8):
            i = lo + psrot['i'] % (hi - lo)
            psrot['i'] += 1
            return ps[i], PK[i]

        evrot = {'i': 0}

        def evac_eng():
            evrot['i'] += 1
            return 'dve' if evrot['i'] % 2 else 'act'

        def copy_op(eng, out, in_, reads, writes, scale=None):
            if scale is not None:
                if eng == 'act':
                    S.op('act', lambda e: e.activation(out=out, in_=in_, func=AF.Copy, scale=scale), reads=reads, writes=writes)
                else:
                    S.op(eng, lambda e: e.tensor_scalar(out=out, in0=in_, scalar1=scale, scalar2=None, op0=ALU.mult),
                         reads=reads, writes=writes)
            elif eng == 'act':
                S.op('act', lambda e: e.activation(out=out, in_=in_, func=AF.Copy), reads=reads, writes=writes)
            else:
                S.op(eng, lambda e: e.tensor_copy(out=out, in_=in_), reads=reads, writes=writes)

        def proj_T(wb, wkey, ncol, actT, akey, kc, tok0, pst, pkey):
            for c in range(kc):
                S.op('pe', lambda e, c=c: e.matmul(pst[:, 0:ncol], lhsT=actT[:, c, tok0:tok0 + 128], rhs=wb[:, c, 0:ncol],
                                                   start=(c == 0), stop=(c == kc - 1)),
                     reads=[wkey, akey], writes=[pkey])

        def proj_F(wb, wkey, m0, mcol, actT, akey, kc, tok0, ntok, pst, pkey):
            for c in range(kc):
                S.op('pe', lambda e, c=c: e.matmul(pst[0:mcol, 0:ntok], lhsT=wb[:, c, m0:m0 + mcol],
                                                   rhs=actT[:, c, tok0:tok0 + ntok], start=(c == 0), stop=(c == kc - 1)),
                     reads=[wkey, akey], writes=[pkey])

        def ntiles(total):
            out, t = [], 0
            while t < total:
                n = min(512, total - t)
                out.append((t, n))
                t += n
            return out

        def sigmoid_from(eng_in, out, in_, reads, writes):
            S.op('act', lambda e: e.activation(out=out, in_=in_, func=AF.Exp, scale=-1.0), reads=reads, writes=writes)
            S.op('act', lambda e: e.activation(out=out, in_=out, func=AF.Ln, bias=1.0), reads=writes, writes=writes)
            S.op('act', lambda e: e.activation(out=out, in_=out, func=AF.Exp, scale=-1.0), reads=writes, writes=writes)

        def load_xT(x_ap, rows, xT, xkey):
            with contextlib.ExitStack() as st2:
                xt = [sb(st2, "xt%d" % i, [128, D], BF16) for i in range(3)]
                for t, r0 in enumerate(rows):
                    b = xt[t % 3]
                    bk = "xt%d" % (t % 3)
                    S.dma('pool', b[:], x_ap[r0:r0 + 128, :], writes=[bk])
                    for g in range(0, KC, 4):
                        n = min(4, KC - g)
                        pt, pk = nextps()
                        for j in range(n):
                            S.op('pe', lambda e, j=j: e.matmul(pt[:, j * 128:(j + 1) * 128],
                                                               lhsT=b[:, (g + j) * 128:(g + j + 1) * 128], rhs=identb[:],
                                                               start=True, stop=True), reads=[bk, 'identb'], writes=[pk])
                        copy_op(evac_eng(), xT[:, g:g + n, t * 128:(t + 1) * 128],
                                pt[:, 0:n * 128].rearrange("p (a b) -> p a b", b=128), [pk], [xkey])
                S.barrier()

        Cst = [sb(es, "Cst%d" % h, [128, 2, ML_DV]) for h in range(ML_H)]
        nst = sb(es, "nst", [128, ML_H, 2])
        mst = sb(es, "mst", [128, ML_H])
        Sst = [sb(es, "Sst%d" % h, [128, HG_DV]) for h in range(HG_H)]
        for h in range(ML_H):
            P(lambda e, h=h: e.memset(Cst[h][:], 0.0), ['Cst%d' % h])
        P(lambda e: e.memset(nst[:], 0.0), ['nst'])
        P(lambda e: e.memset(mst[:], 0.0), ['mst'])
        for h in range(HG_H):
            P(lambda e, h=h: e.memset(Sst[h][:], 0.0), ['Sst%d' % h])

        chk_holder = [lambda tag: None]
        def mixer_pass(xT, xkey, ntile, full, brA, brB, samp, R, final, rmask=None):
            rmask = rst if rmask is None else rmask
            tiles = list(range(ntile)) + ([ntile] if samp else [])
            ntl = len(tiles)
            Mtok = ntl * 128
            with contextlib.ExitStack() as st:
                ift = sb(st, "ift", [128, ntl, 2 * ML_H])
                wb, wk = wget('w_in', w_in, 0, KC, off['mi'], 2 * ML_H)
                for ti in range(ntl):
                    pt, pk = nextps()
                    proj_T(wb, wk, 2 * ML_H, xT, xkey, KC, ti * 128, pt, pk)
                    copy_op('dve', ift[:, ti, :], pt[:, 0:2 * ML_H], [pk], ['ift'])
                chk_holder[0]('ift')
                ro = [0]

                def rview(n, inner):
                    v = R[:, ro[0]:ro[0] + n].rearrange("p (a b) -> p a b", b=inner)
                    ro[0] += n
                    return v
                PBS = []
                for i in range(2):
                    PB = {'k_tok': rview(ntl * ML_DK, ML_DK), 'v_tok': rview(ntl * ML_DV, ML_DV)}
                    if full:
                        PB['qT'] = rview(2 * Mtok, Mtok)
                        PB['kT'] = rview(2 * Mtok, Mtok)
                        PB['og'] = rview(ntl * ML_DV, ML_DV)
                    PBS.append(PB)
                if samp:
                    NCB = 3
                    Cf = [sb(st, "Cf%d" % i, [128, 2, ML_DV]) for i in range(NCB)]
                    Cfb = [sb(st, "Cfb%d" % i, [128, 2, ML_DV], BF16) for i in range(NCB)]
                    qTq = [sb(st, "qTq%d" % i, [128, 2, 128], BF16) for i in range(2)]
                    kwq = [sb(st, "kwq%d" % i, [128, ML_DK], BF16) for i in range(2)]
                    nall = sb(st, "nall", [128, NSQ * ML_H * 2])
                    nallb = sb(st, "nallb", [128, NSQ * ML_H * 2], BF16)
                    nrow = sb(st, "nrow", [128, 128])
                    Dbc = sb(st, "Dbc", [128, NSQ])
                    dsel = sb(st, "dsel", [128, NSQ])
                    msamp = sb(st, "msamp", [NSQ, ML_H])
                    mtok = sb(st, "mtok", [128, ML_H])
                    mnew_all = sb(st, "mnew_all", [128, ML_H])
                    mout = sb(st, "mout", [NSQ, ML_H])
                    S.dma('sp', nrow[:], st_n, writes=['nrow'])
                    chk_holder[0]('sA0')
                    S.op('pe', lambda e: e.matmul(ps[4][:, 0:128], lhsT=nrow[:], rhs=ident[:], start=True, stop=True),
                         reads=['nrow', 'ident'], writes=[PK[4]])
                    copy_op('dve', nall[:], ps[4][:, 0:128], [PK[4]], ['nall'])
                    chk_holder[0]('sA05')
                    copy_op('act', nallb[:], ps[4][:, 0:128], [PK[4]], ['nallb'])
                    chk_holder[0]('sA1')
                    S.dma('sp', msamp[:], st_m, writes=['msamp'])
                    S.op('pe', lambda e: e.matmul(ps[4][:, 0:ML_H], lhsT=indT[:], rhs=msamp[:], start=True, stop=True),
                         reads=['indT', 'msamp'], writes=[PK[4]])
                    copy_op('dve', mtok[:], ps[4][:, 0:ML_H], [PK[4]], ['mtok'])
                    chk_holder[0]('sA')

                def ml_rec(h, X, tiles):
                    sfx = X['sfx']
                    K = lambda nm: nm + sfx
                    Cbf, nbf, sc, bm, mprev = X['Cbf'], X['nbf'], X['sc'], X['bm'], X['mprev']
                    diagc, logd, dmat, sdm, sdT, kw = X['diagc'], X['logd'], X['dmat'], X['sdm'], X['sdT'], X['kw']
                    hbuf, h2, bst, bmv = X['hbuf'], X['h2'], X['bst'], X['bmv']
                    PB = X['PB']
                    k_tok, v_tok = PB['k_tok'], PB['v_tok']
                    qT, kT, og = PB.get('qT'), PB.get('kT'), PB.get('og')
                    bG, bQ, bA, bB = X['bG'], X['bQ'], X['bA'], X['bB']
                    kG, kQ, kA, kB = PK[bG], PK[bQ], PK[bA], PK[bB]
                    cB0, cC0, cS0, cQ0, cT0, cN0 = X['cols']
                    urot = [0]

                    def ubank():
                        bnk = X['ub'][urot[0] % len(X['ub'])]
                        urot[0] += 1
                        return ps[bnk], PK[bnk]
                    Ck, nk, mk_ = 'Cst%d' % h, 'nst%d' % h, 'mst%d' % h
                    S.op('act', lambda e: e.activation(out=Cbf[:], in_=Cst[h][:], func=AF.Copy), reads=[Ck], writes=[K('Cbf')])
                    S.op('dve', lambda e: e.tensor_copy(out=nbf[:], in_=nst[:, h, :]), reads=[nk], writes=[K('nbf')])
                    S.op('dve', lambda e: e.tensor_copy(out=mprev[:], in_=mst[:, h:h + 1]), reads=[mk_], writes=[K('mprev')])
                    for ti in tiles:
                        is_s = samp and ti == ntl - 1
                        mstm = mst_b if is_s else mst_c
                        mstk = 'mst_b' if is_s else 'mst_c'
                        negm_ = neg_b if is_s else neg_c
                        negk = 'neg_b' if is_s else 'neg_c'
                        selm = sel_b if is_s else sel_c
                        selk = 'sel_b' if is_s else 'sel_c'
                        tok = slice(ti * 128, (ti + 1) * 128)
                        col = lambda j: sc[:, j:j + 1]
                        if is_s:
                            S.op('dve', lambda e: e.tensor_copy(out=mprev[:], in_=mtok[:, h:h + 1]), reads=['mtok'],
                                 writes=[K('mprev')])
                        S.op('dve', lambda e: e.tensor_scalar(out=col(0), in0=ift[:, ti, h:h + 1], scalar1=bigb[:, h:h + 1],
                                                              scalar2=None, op0=ALU.add), reads=['ift', 'bigb'], writes=[K('sc0')])
                        S.op('act', lambda e: e.activation(out=col(1), in_=ift[:, ti, ML_H + h:ML_H + h + 1], func=AF.Exp,
                                                           scale=-1.0, bias=nbfg[:, h:h + 1]), reads=['ift', 'nbfg'],
                             writes=[K('sc1')])
                        S.op('act', lambda e: e.activation(out=col(1), in_=col(1), func=AF.Ln, bias=1.0), reads=[K('sc1')],
                             writes=[K('sc1')])
                        S.op('pe', lambda e: e.matmul(ps[bG][:, cB0:cB0 + 1], lhsT=mstm[:], rhs=col(1), start=True, stop=True),
                             reads=[mstk, K('sc1')], writes=[kG])
                        S.op('dve', lambda e: e.tensor_copy(out=bm[:, 0:1], in_=ps[bG][:, cB0:cB0 + 1]), reads=[kG], writes=[K('bm0')])
                        S.op('dve', lambda e: e.tensor_tensor(out=col(2), in0=col(0), in1=bm[:, 0:1], op=ALU.add),
                             reads=[K('sc0'), K('bm0')], writes=[K('sc2')])
                        S.op('dve', lambda e: e.tensor_scalar(out=diagc[:], in0=ident[:], scalar1=col(2), scalar2=None,
                                                              op0=ALU.mult), reads=['ident', K('sc2')], writes=[K('diagc')])
                        S.op('pe', lambda e: e.matmul(ps[bG][:, cC0:cC0 + 128], lhsT=ones[:], rhs=diagc[:], start=True, stop=True),
                             reads=['ones', K('diagc')], writes=[kG])
                        S.op('dve', lambda e: e.scalar_tensor_tensor(out=logd[:], in0=ps[bG][:, cC0:cC0 + 128], scalar=bm[:, 0:1],
                                                                     in1=negm_[:], op0=ALU.subtract, op1=ALU.add),
                             reads=[kG, K('bm0'), negk], writes=[K('logd')])
                        S.op('dve', lambda e: e.tensor_reduce(out=col(3), in_=logd[:], axis=AX.X, op=ALU.max),
                             reads=[K('logd')], writes=[K('sc3')])
                        S.op('dve', lambda e: e.tensor_tensor(out=col(4), in0=mprev[:], in1=bm[:, 0:1], op=ALU.subtract),
                             reads=[K('mprev'), K('bm0')], writes=[K('sc4')])
                        S.op('dve', lambda e: e.tensor_tensor(out=bm[:, 1:2], in0=col(4), in1=col(3), op=ALU.max),
                             reads=[K('sc4'), K('sc3')], writes=[K('bm1')])
                        S.op('dve', lambda e: e.tensor_scalar(out=col(5), in0=bm[:, 1:2], scalar1=-1.0, scalar2=None,
                                                              op0=ALU.mult), reads=[K('bm1')], writes=[K('sc5')])
                        S.op('pe', lambda e: e.matmul(ps[bG][:, cS0:cS0 + 2], lhsT=selm[:], rhs=bm[:], start=True, stop=True),
                             reads=[selk, K('bm0'), K('bm1')], writes=[kG])
                        S.op('dve', lambda e: e.tensor_tensor(out=col(8), in0=bm[:, 0:1], in1=ps[bG][:, cS0:cS0 + 1],
                                                              op=ALU.subtract), reads=[K('bm0'), kG], writes=[K('sc8')])
                        S.op('dve', lambda e: e.tensor_tensor(out=col(8), in0=col(8), in1=col(0), op=ALU.add),
                             reads=[K('sc8'), K('sc0')], writes=[K('sc8')])
                        S.op('dve', lambda e: e.tensor_scalar(out=col(9), in0=ps[bG][:, cS0 + 1:cS0 + 2], scalar1=-1.0, scalar2=None,
                                                              op0=ALU.mult), reads=[kG], writes=[K('sc9')])
                        S.op('act', lambda e: e.activation(out=col(10), in_=col(8), func=AF.Exp, bias=col(9)),
                             reads=[K('sc8'), K('sc9')], writes=[K('sc10')])
                        S.op('dve', lambda e: e.tensor_tensor(out=col(11), in0=mprev[:], in1=ps[bG][:, cS0:cS0 + 1],
                                                              op=ALU.subtract), reads=[K('mprev'), kG], writes=[K('sc11')])
                        S.op('act', lambda e: e.activation(out=col(12), in_=col(11), func=AF.Exp, bias=col(9)),
                             reads=[K('sc11'), K('sc9')], writes=[K('sc12')])
                        if full:
                            S.op('act', lambda e: e.activation(out=dmat[:], in_=logd[:], func=AF.Exp, bias=col(5)),
                                 reads=[K('logd'), K('sc5')], writes=[K('dmat')])
                            S.op('act', lambda e: e.activation(out=col(6), in_=col(4), func=AF.Exp, bias=col(5)),
                                 reads=[K('sc4'), K('sc5')], writes=[K('sc6')])
                            S.op('act', lambda e: e.activation(out=col(7), in_=col(5), func=AF.Exp), reads=[K('sc5')],
                                 writes=[K('sc7')])
                            for c in range(2):
                                S.op('pe', lambda e, c=c: e.matmul(ps[bQ][:, cQ0:cQ0 + 128], lhsT=qT[:, c, tok], rhs=kT[:, c, tok],
                                                                   start=(c == 0), stop=(c == 1)), reads=[K('qT'), K('kT')],
                                     writes=[kQ])
                            S.op('dve', lambda e: e.scalar_tensor_tensor(out=sdm[:], in0=ps[bQ][:, cQ0:cQ0 + 128], scalar=1.0,
                                                                         in1=dmat[:], op0=ALU.mult, op1=ALU.mult,
                                                                         accum_out=col(13)),
                                 reads=[kQ, K('dmat')], writes=[K('sdm'), K('sc13')])
                            S.op('pe', lambda e: e.matmul(ps[bQ][:, cT0:cT0 + 128], lhsT=sdm[:], rhs=ident[:], start=True, stop=True),
                                 reads=[K('sdm'), 'ident'], writes=[kQ])
                            copy_op('act', sdT[:], ps[bQ][:, cT0:cT0 + 128], [kQ], [K('sdT')])
                            S.op('pe', lambda e: e.matmul(ps[bA][:, :], lhsT=sdT[:], rhs=v_tok[:, ti, :], start=True, stop=True),
                                 reads=[K('sdT'), K('v_tok')], writes=[kA])
                        if not is_s:
                            if full:
                                for c in range(2):
                                    S.op('pe', lambda e, c=c: e.matmul(ps[bB][:, :], lhsT=qT[:, c, tok], rhs=Cbf[:, c, :],
                                                                       start=(c == 0), stop=(c == 1)), reads=[K('qT'), K('Cbf')],
                                         writes=[kB])
                                for c in range(2):
                                    S.op('pe', lambda e, c=c: e.matmul(ps[bQ][:, cN0:cN0 + 1], lhsT=qT[:, c, tok], rhs=nbf[:, c:c + 1],
                                                                       start=(c == 0), stop=(c == 1)), reads=[K('qT'), K('nbf')],
                                         writes=[kQ])
                            S.op('dve', lambda e: e.tensor_scalar(out=kw[:], in0=k_tok[:, ti, :], scalar1=col(10), scalar2=None,
                                                                  op0=ALU.mult), reads=[K('k_tok'), K('sc10')], writes=[K('kw')])
                            for c in range(2):
                                pt, pk = ubank()
                                S.op('pe', lambda e, c=c, pt=pt: e.matmul(pt[:, :], lhsT=kw[:, c * 128:(c + 1) * 128],
                                                                          rhs=v_tok[:, ti, :], start=True, stop=True),
                                     reads=[K('kw'), K('v_tok')], writes=[pk])
                                S.op('dve', lambda e, c=c, pt=pt: e.scalar_tensor_tensor(
                                    out=Cst[h][:, c, :], in0=Cst[h][:, c, :], scalar=col(12), in1=pt[:, :],
                                    op0=ALU.mult, op1=ALU.add), reads=[Ck, K('sc12'), pk, K('Cbf')], writes=[Ck])
                            pt, pk = ubank()
                            for c in range(2):
                                S.op('pe', lambda e, c=c, pt=pt: e.matmul(pt[:, c:c + 1], lhsT=kw[:, c * 128:(c + 1) * 128],
                                                                          rhs=onesb[:, 0:1], start=True, stop=True),
                                     reads=[K('kw'), 'onesb'], writes=[pk])
                            S.op('dve', lambda e, pt=pt: e.scalar_tensor_tensor(
                                out=nst[:, h, :], in0=nst[:, h, :], scalar=col(12), in1=pt[:, 0:2],
                                op0=ALU.mult, op1=ALU.add), reads=[nk, K('sc12'), pk, K('nbf')], writes=[nk])
                            S.op('act', lambda e: e.activation(out=Cbf[:], in_=Cst[h][:], func=AF.Copy), reads=[Ck],
                                 writes=[K('Cbf')])
                            S.op('dve', lambda e: e.tensor_copy(out=nbf[:], in_=nst[:, h, :]), reads=[nk], writes=[K('nbf')])
                            S.op('dve', lambda e: e.tensor_copy(out=mprev[:], in_=ps[bG][:, cS0 + 1:cS0 + 2]), reads=[kG],
                                 writes=[K('mprev')])
                            S.op('dve', lambda e: e.tensor_copy(out=mst[:, h:h + 1], in_=ps[bG][:, cS0 + 1:cS0 + 2]), reads=[kG],
                                 writes=[mk_])
                        else:
                            S.op('dve', lambda e: e.tensor_copy(out=mnew_all[:, h:h + 1], in_=ps[bG][:, cS0 + 1:cS0 + 2]),
                                 reads=[kG], writes=['mnew_all'])
                            S.op('dve', lambda e: e.tensor_scalar(out=dsel[:], in0=lastind[:], scalar1=col(12), scalar2=None,
                                                                  op0=ALU.mult), reads=['lastind', K('sc12')], writes=['dsel'])
                            pt, pk = ubank()
                            S.op('pe', lambda e, pt=pt: e.matmul(pt[:, 0:NSQ], lhsT=ones[:], rhs=dsel[:], start=True, stop=True),
                                 reads=['ones', 'dsel'], writes=[pk])
                            copy_op('dve', Dbc[:], pt[:, 0:NSQ], [pk], ['Dbc'])
                            S.op('dve', lambda e: e.tensor_scalar(out=kw[:], in0=k_tok[:, ti, :], scalar1=col(10), scalar2=None,
                                                                  op0=ALU.mult), reads=[K('k_tok'), K('sc10')], writes=[K('kw')])
                            def cload(q):
                                S.dma('sp', Cf[q % NCB][:], st_C[q, h].rearrange("(c p) v -> p c v", p=128),
                                      writes=['Cf%d' % (q % NCB)])
                            for q in range(min(NCB - 1, NSQ)):
                                cload(q)
                            for q in range(NSQ):
                                if q + NCB - 1 < NSQ:
                                    cload(q + NCB - 1)
                                cf, cfk = Cf[q % NCB], 'Cf%d' % (q % NCB)
                                cb, cbk = Cfb[q % NCB], 'Cfb%d' % (q % NCB)
                                qq, qqk = qTq[q % 2], 'qTq%d' % (q % 2)
                                kq, kqk = kwq[q % 2], 'kwq%d' % (q % 2)
                                S.op('dve', lambda e, q=q, qq=qq: e.tensor_tensor(
                                    out=qq[:], in0=qT[:, :, tok], in1=blkb[:, q:q + 1, :].to_broadcast([128, 2, 128]),
                                    op=ALU.mult), reads=[K('qT'), 'blkb'], writes=[qqk])
                                S.op('dve', lambda e, q=q, kq=kq: e.tensor_scalar(
                                    out=kq[:], in0=kw[:], scalar1=ind[:, q:q + 1], scalar2=None, op0=ALU.mult),
                                    reads=[K('kw'), 'ind'], writes=[kqk])
                                copy_op('act', cb[:], cf[:], [cfk], [cbk])
                                for c in range(2):
                                    S.op('pe', lambda e, c=c, q=q, cb=cb: e.matmul(
                                        ps[bB][:, :], lhsT=qq[:, c, :], rhs=cb[:, c, :],
                                        start=(q == 0 and c == 0), stop=(q == NSQ - 1 and c == 1)),
                                        reads=[qqk, cbk], writes=[kB])
                                for c in range(2):
                                    j = (q * ML_H + h) * 2 + c
                                    S.op('pe', lambda e, c=c, q=q, j=j: e.matmul(
                                        ps[bQ][:, cN0:cN0 + 1], lhsT=qq[:, c, :], rhs=nallb[:, j:j + 1],
                                        start=(q == 0 and c == 0), stop=(q == NSQ - 1 and c == 1)),
                                        reads=[qqk, 'nallb'], writes=[kQ])
                                for c in range(2):
                                    pt, pk = ubank()
                                    S.op('pe', lambda e, c=c, q=q, pt=pt: e.matmul(
                                        pt[:, :], lhsT=kq[:, c * 128:(c + 1) * 128], rhs=v_tok[:, ti, :],
                                        start=True, stop=True), reads=[kqk, K('v_tok')], writes=[pk])
                                    S.op('dve', lambda e, c=c, q=q, pt=pt, cf=cf: e.scalar_tensor_tensor(
                                        out=cf[:, c, :], in0=cf[:, c, :], scalar=Dbc[:, q:q + 1], in1=pt[:, :],
                                        op0=ALU.mult, op1=ALU.add), reads=[cfk, 'Dbc', pk, cbk], writes=[cfk])
                                S.dma('pool', o_Cs[q, h].rearrange("(c p) v -> p c v", p=128), cf[:], reads=[cfk])
                                pt, pk = ubank()
                                for c in range(2):
                                    S.op('pe', lambda e, c=c, q=q, pt=pt: e.matmul(
                                        pt[:, c:c + 1], lhsT=kq[:, c * 128:(c + 1) * 128], rhs=onesb[:, 0:1],
                                        start=True, stop=True), reads=[kqk, 'onesb'], writes=[pk])
                                j0 = (q * ML_H + h) * 2
                                S.op('dve', lambda e, q=q, pt=pt, j0=j0: e.scalar_tensor_tensor(
                                    out=nall[:, j0:j0 + 2], in0=nall[:, j0:j0 + 2], scalar=Dbc[:, q:q + 1], in1=pt[:, 0:2],
                                    op0=ALU.mult, op1=ALU.add), reads=['nall', 'Dbc', pk], writes=['nall'])
                        if is_s:
                            chk_holder[0]('sC')
                        if full:
                            S.op('dve', lambda e: e.scalar_tensor_tensor(out=col(14), in0=ps[bQ][:, cN0:cN0 + 1], scalar=col(6),
                                                                         in1=col(13), op0=ALU.mult, op1=ALU.add),
                                 reads=[kQ, K('sc6'), K('sc13')], writes=[K('sc14')])
                            S.op('act', lambda e: e.activation(out=col(14), in_=col(14), func=AF.Abs), reads=[K('sc14')],
                                 writes=[K('sc14')])
                            S.op('dve', lambda e: e.tensor_tensor(out=col(14), in0=col(14), in1=col(7), op=ALU.max),
                                 reads=[K('sc14'), K('sc7')], writes=[K('sc14')])
                            S.op('dve', lambda e: e.reciprocal(out=col(15), in_=col(14)), reads=[K('sc14')], writes=[K('sc15')])
                            S.op('dve', lambda e: e.tensor_tensor(out=col(16), in0=col(15), in1=col(6), op=ALU.mult),
                                 reads=[K('sc15'), K('sc6')], writes=[K('sc16')])
                            S.op('act', lambda e: e.activation(out=h2[:], in_=ps[bB][:, :], func=AF.Identity, scale=col(16)),
                                 reads=[kB, K('sc16')], writes=[K('h2')])
                            S.op('dve', lambda e: e.scalar_tensor_tensor(out=hbuf[:], in0=ps[bA][:, :], scalar=col(15),
                                                                         in1=h2[:], op0=ALU.mult, op1=ALU.add),
                                 reads=[kA, K('sc15'), K('h2')], writes=[K('hbuf')])
                            S.op('dve', lambda e: e.bn_stats(out=bst[:], in_=hbuf[:]), reads=[K('hbuf')], writes=[K('bst')])
                            S.op('dve', lambda e: e.bn_aggr(out=bmv[:], in_=bst[:]), reads=[K('bst')], writes=[K('bmv')])
                            S.op('act', lambda e: e.activation(out=col(17), in_=bmv[:, 1:2], func=AF.Ln, bias=LN_EPS),
                                 reads=[K('bmv')], writes=[K('sc17')])
                            S.op('act', lambda e: e.activation(out=col(17), in_=col(17), func=AF.Exp, scale=-0.5),
                                 reads=[K('sc17')], writes=[K('sc17')])
                            S.op('dve', lambda e: e.tensor_scalar(out=col(18), in0=bmv[:, 0:1], scalar1=-1.0, scalar2=col(17),
                                                                  op0=ALU.mult, op1=ALU.mult), reads=[K('bmv'), K('sc17')],
                                 writes=[K('sc18')])
                            S.op('act', lambda e: e.activation(out=h2[:], in_=hbuf[:], func=AF.Identity, scale=col(17),
                                                               bias=col(18)), reads=[K('hbuf'), K('sc17'), K('sc18')], writes=[K('h2')])
                            S.op('dve', lambda e: e.tensor_tensor(out=hbuf[:], in0=h2[:], in1=og[:, ti, :], op=ALU.mult),
                                 reads=[K('h2'), K('og')], writes=[K('hbuf')])
                            for j in range(4):
                                S.op('pe', lambda e, j=j: e.matmul(ps[bA][:, j * 128:(j + 1) * 128],
                                                                   lhsT=hbuf[:, j * 128:(j + 1) * 128], rhs=ident[:],
                                                                   start=True, stop=True), reads=[K('hbuf'), 'ident'],
                                     writes=[kA])
                            for j in range(4):
                                ch = 4 * h + j
                                if j % 2 == 0:
                                    S.op('act', lambda e, j=j, ch=ch: e.activation(
                                        out=brA[:, ch, tok], in_=ps[bA][:, j * 128:(j + 1) * 128], func=AF.Identity,
                                        scale=gcol_a[:, ch:ch + 1]), reads=[kA, 'gcol_a'], writes=[K('brA')])
                                else:
                                    S.op('dve', lambda e, j=j, ch=ch: e.tensor_scalar(
                                        out=brA[:, ch, tok], in0=ps[bA][:, j * 128:(j + 1) * 128], scalar1=gcol_a[:, ch:ch + 1],
                                        scalar2=None, op0=ALU.mult), reads=[kA, 'gcol_a'], writes=[K('brA')])
                def ml_proj(h, PB, psfx):
                    k_tok, v_tok = PB['k_tok'], PB['v_tok']
                    qT, kT, og = PB.get('qT'), PB.get('kT'), PB.get('og')
                    h2 = MX[0]['h2']
                    def tmode(colname, width, dst, dkey, post=None, scale=None):
                        for c0 in range(0, width, BW):
                            wb, wk = wget('w_in', w_in, 0, KC, off[colname] + h * width + c0, BW)
                            for ti in range(ntl):
                                pt, pk = nextps()
                                proj_T(wb, wk, BW, xT, xkey, KC, ti * 128, pt, pk)
                                if post is None:
                                    copy_op(evac_eng(), dst[:, ti, c0:c0 + BW], pt[:, 0:BW], [pk], [dkey], scale=scale)
                                else:
                                    post(dst[:, ti, c0:c0 + BW], pt[:, 0:BW], pk, dkey)

                    def fmode(colname, dst, dkey, scale=None):
                        wb, wk = wget('w_in', w_in, 0, KC, off[colname] + h * ML_DK, BW)
                        for cc in range(2):
                            for (t0, n) in ntiles(Mtok):
                                pt, pk = nextps()
                                proj_F(wb, wk, cc * 128, 128, xT, xkey, KC, t0, n, pt, pk)
                                copy_op(evac_eng(), dst[:, cc, t0:t0 + n], pt[:, 0:n], [pk], [dkey], scale=scale)

                    if full:
                        fmode('mq', qT, 'qT' + psfx)
                        fmode('mk', kT, 'kT' + psfx, scale=ML_DK ** -0.5)
                    tmode('mk', ML_DK, k_tok, 'k_tok' + psfx, scale=ML_DK ** -0.5)
                    tmode('mv', ML_DV, v_tok, 'v_tok' + psfx)
                    if full:
                        osig = sb(st, "osig%d" % h, [128, BW]) if False else None

                        oc = [0]

                        def opost(dst, src, pk, dkey):
                            i2 = oc[0] % 2
                            oc[0] += 1
                            hh, hk = MX[i2]['h2'], 'h2_%d' % i2
                            sigmoid_from('act', hh[:, 0:BW], src, [pk], [hk])
                            S.op('dve', lambda e: e.tensor_copy(out=dst, in_=hh[:, 0:BW]), reads=[hk], writes=[dkey])
                        tmode('mo', ML_DV, og, 'og' + psfx, post=opost)


                def ml_scratch(i):
                    return {'sfx': '_%d' % i,
                            'Cbf': sb(st, "Cbf", [128, 2, ML_DV], BF16), 'nbf': sb(st, "nbf", [128, 2], BF16),
                            'sc': sb(st, "sc", [128, 24]), 'bm': sb(st, "bm", [128, 2]), 'mprev': sb(st, "mprev", [128, 1]),
                            'diagc': sb(st, "diagc", [128, 128]), 'logd': sb(st, "logd", [128, 128]),
                            'dmat': sb(st, "dmat", [128, 128]), 'sdm': sb(st, "sdm", [128, 128]),
                            'sdT': sb(st, "sdT", [128, 128], BF16), 'kw': sb(st, "kw", [128, ML_DK], BF16),
                            'hbuf': sb(st, "hbuf", [128, ML_DV]), 'h2': sb(st, "h2", [128, ML_DV]),
                            'bst': sb(st, "bst", [128, 6]), 'bmv': sb(st, "bmv", [128, 2])}
                MX = [ml_scratch(0), ml_scratch(1)]
                hbuf = MX[0]['hbuf']
                MERGED = (384, 128, 386, 0, 256, 390)
                MX[0].update(bG=4, bQ=4, bA=5, bB=6, cols=MERGED)
                MX[1].update(bG=0, bQ=0, bA=1, bB=2, cols=MERGED)
                for h0 in range(0, ML_H, 2):
                    for i in range(2):
                        ml_proj(h0 + i, PBS[i], '_%d' % i)
                        MX[i]['PB'] = PBS[i]
                    ptiles = list(range(ntile))
                    MX[0]['ub'] = [7]
                    MX[1]['ub'] = [3]
                    lists = []
                    for i in range(2):
                        S.start_record()
                        ml_rec(h0 + i, MX[i], ptiles)
                        lists.append(S.stop_record())
                    S.replay_interleaved(lists)
                    if samp:
                        MX[0]['ub'] = [7, 0, 1, 2, 3]
                        ml_rec(h0, MX[0], [ntile])
                        MX[1]['ub'] = [3, 4, 5, 6, 7]
                        ml_rec(h0 + 1, MX[1], [ntile])
                if final:
                    for h in range(ML_H):
                        S.dma('sp', o_Cp[h].rearrange("(c p) v -> p c v", p=128), Cst[h][:], reads=['Cst%d' % h])
                    S.op('pe', lambda e: e.matmul(ps[4][0:ML_H * 2, 0:128], lhsT=nst[:].rearrange("p h c -> p (h c)"),
                                                  rhs=ident[:], start=True, stop=True), reads=['nst%d' % hh for hh in range(ML_H)] + ['ident'], writes=[PK[4]])
                    copy_op('dve', hbuf[0:ML_H * 2, 0:128], ps[4][0:ML_H * 2, 0:128], [PK[4]], ['hbuf_0'])
                    S.dma('sp', o_np, hbuf[0:ML_H * 2, 0:128], reads=['hbuf_0'])
                    S.dma('sp', o_mp, mst[0:1, :], reads=['mst%d' % hh for hh in range(ML_H)])
                    if samp:
                        S.op('pe', lambda e: e.matmul(ps[4][:, 0:128], lhsT=nall[:], rhs=ident[:], start=True, stop=True),
                             reads=['nall', 'ident'], writes=[PK[4]])
                        copy_op('dve', nrow[:], ps[4][:, 0:128], [PK[4]], ['nrow'])
                        S.dma('sp', o_ns, nrow[:], reads=['nrow'])
                        S.op('pe', lambda e: e.matmul(ps[4][0:NSQ, 256:256 + ML_H], lhsT=lastind[:], rhs=mnew_all[:],
                                                      start=True, stop=True), reads=['lastind', 'mnew_all'], writes=[PK[4]])
                        copy_op('dve', mout[:], ps[4][0:NSQ, 256:256 + ML_H], [PK[4]], ['mout'])
                        S.dma('sp', o_ms, mout[:], reads=['mout'])
                S.barrier()
            chk_holder[0]('ml')

            with contextlib.ExitStack() as st:
                ro = [0]

                def rview2(n, inner=None):
                    v = R[:, ro[0]:ro[0] + n]
                    if inner is not None:
                        v = v.rearrange("p (a b) -> p a b", b=inner)
                    ro[0] += n
                    return v
                qg = [rview2(Mtok) for i in range(2)]
                kg = [rview2(Mtok) for i in range(2)]
                eG = [sb(st, "eG%d" % i, [128, Mtok]) for i in range(2)]
                i_tok = [rview2(ntl * HG_DV, HG_DV) for i in range(2)]
                if full:
                    sg = [rview2(ntl * HG_DV, HG_DV) for i in range(2)]
                fa = sb(st, "fa", [128, Mtok])
                fb = sb(st, "fb", [128, Mtok])
                qf = sb(st, "qf", [128, Mtok])
                NSB = 4
                hres = []
                for i in range(2):
                    X = {'Sbf': sb(st, "Sbf", [128, HG_DV], BF16), 'aTm': sb(st, "aTm", [128, 128], BF16),
                         'kgt': sb(st, "kgt", [128, 128], BF16), 'obuf': sb(st, "obuf", [128, HG_DV]),
                         'tmpS': sb(st, "tmpS", [128, HG_DV]), 'hc': sb(st, "hc", [128, 4]),
                         'banks': (5, 6, 7, 0) if i == 0 else (1, 2, 3, 4)}
                    if samp:
                        X['Sf'] = [sb(st, "Sf%d" % j, [128, HG_DV]) for j in range(NSB)]
                        X['Sfb'] = [sb(st, "Sfb%d" % j, [128, HG_DV], BF16) for j in range(NSB)]
                        X['qgq'] = [sb(st, "qgq%d" % j, [128, 128], BF16) for j in range(2)]
                        X['kgq'] = [sb(st, "kgq%d" % j, [128, 128], BF16) for j in range(2)]
                    hres.append(X)
                obuf = hres[0]['obuf']
                for pr in range(HG_H // 2):
                    if full:
                        wb, wk = wget('w_in', w_in, 0, KC, off['hq'] + pr * BW, BW)
                        for i in range(2):
                            for (t0, n) in ntiles(Mtok):
                                pt, pk = nextps()
                                proj_F(wb, wk, i * 128, 128, xT, xkey, KC, t0, n, pt, pk)
                                copy_op(evac_eng(), (qf if i == 0 else fb)[:, t0:t0 + n], pt[:, 0:n], [pk],
                                        ['qf' if i == 0 else 'fb'])
                    wb, wk = wget('w_in', w_in, 0, KC, off['hf'] + pr * BW, BW)
                    for i in range(2):
                        hd = 2 * pr + i
                        for (t0, n) in ntiles(Mtok):
                            pt, pk = nextps()
                            proj_F(wb, wk, i * 128, 128, xT, xkey, KC, t0, n, pt, pk)
                            sigmoid_from('act', fa[:, t0:t0 + n], pt[:, 0:n], [pk], ['fa'])
                        S.op('dve', lambda e, hd=hd: e.tensor_scalar(out=fa[:], in0=fa[:], scalar1=lbc[:, HG_H + hd:HG_H + hd + 1],
                                                                     scalar2=lbc[:, hd:hd + 1], op0=ALU.mult, op1=ALU.add),
                             reads=['fa', 'lbc'], writes=['fa'])
                        S.op('act', lambda e, i=i: e.activation(out=eG[i][:], in_=fa[:], func=AF.Ln), reads=['fa'],
                             writes=['eG%d' % i])
                        rs = rmask[:, 0:Mtok]
                        S.op('dve', lambda e, i=i: e.tensor_tensor_scan(out=eG[i][:], data0=rs, data1=eG[i][:], initial=0.0,
                                                                        op0=ALU.mult, op1=ALU.add),
                             reads=['rst', 'eG%d' % i], writes=['eG%d' % i])
                        S.op('dve', lambda e: e.tensor_scalar(out=fa[:], in0=fa[:], scalar1=-1.0, scalar2=1.0, op0=ALU.mult,
                                                              op1=ALU.add), reads=['fa'], writes=['fa'])
                        tq = sb(st, "tq%d_%d" % (pr, i), [1, 1]) if False else None
                        S.op('act', lambda e, i=i: e.activation(out=kg[i][:], in_=eG[i][:], func=AF.Exp, scale=-1.0),
                             reads=['eG%d' % i], writes=['kg%d' % i])
                        S.op('dve', lambda e, i=i: e.tensor_tensor(out=kg[i][:], in0=kg[i][:], in1=fa[:], op=ALU.mult),
                             reads=['kg%d' % i, 'fa'], writes=['kg%d' % i])
                        S.op('act', lambda e, i=i: e.activation(out=eG[i][:], in_=eG[i][:], func=AF.Exp),
                             reads=['eG%d' % i], writes=['eG%d' % i])
                        if full:
                            qsrc, qk_ = (qf, 'qf') if i == 0 else (fb, 'fb')
                            S.op('dve', lambda e, i=i, qsrc=qsrc: e.tensor_tensor(out=qg[i][:], in0=qsrc[:], in1=eG[i][:],
                                                                                  op=ALU.mult),
                                 reads=[qk_, 'eG%d' % i], writes=['qg%d' % i])
                    for i in range(2):
                        hd = 2 * pr + i
                        wb, wk = wget('w_in', w_in, 0, KC, off['hi'] + hd * HG_DV, BW)
                        for ti in range(ntl):
                            pt, pk = nextps()
                            proj_T(wb, wk, BW, xT, xkey, KC, ti * 128, pt, pk)
                            copy_op(evac_eng(), i_tok[i][:, ti, :], pt[:, 0:BW], [pk], ['i_tok%d' % i])
                    if full:
                        for i in range(2):
                            hd = 2 * pr + i
                            wb, wk = wget('w_in', w_in, 0, KC, off['hg'] + hd * HG_DV, BW)
                            for ti in range(ntl):
                                pt, pk = nextps()
                                proj_T(wb, wk, BW, xT, xkey, KC, ti * 128, pt, pk)
                                ob, obk = hres[ti % 2]['obuf'], 'obuf_%d' % (ti % 2)
                                sigmoid_from('act', ob[:], pt[:, 0:BW], [pk], [obk])
                                S.op('dve', lambda e, i=i, ti=ti, pt=pt, ob=ob: e.tensor_tensor(
                                    out=sg[i][:, ti, :], in0=pt[:, 0:BW], in1=ob[:], op=ALU.mult),
                                     reads=[pk, obk], writes=['sg%d' % i])
                    def hg_chain(i, hd):
                        X = hres[i]
                        Sbf, aTm, kgt, obuf, tmpS, hc = X['Sbf'], X['aTm'], X['kgt'], X['obuf'], X['tmpS'], X['hc']
                        bA, bO, bT, bS = X['banks']
                        kA, kO, kT, kS = PK[bA], PK[bO], PK[bT], PK[bS]
                        sfx = '_%d' % i
                        Sk = 'Sst%d' % hd
                        qgk, kgk, eGk, itk = 'qg%d' % i, 'kg%d' % i, 'eG%d' % i, 'i_tok%d' % i
                        copy_op('act', Sbf[:], Sst[hd][:], [Sk], ['Sbf' + sfx])
                        for ti in range(ntl):
                            is_s = samp and ti == ntl - 1
                            tok = slice(ti * 128, (ti + 1) * 128)
                            mm = mst_b if is_s else mst_c
                            mmk = 'mst_b' if is_s else 'mst_c'
                            if full:
                                S.op('pe', lambda e: e.matmul(ps[bA][:, 0:128], lhsT=kg[i][:, tok], rhs=qg[i][:, tok],
                                                              start=True, stop=True), reads=[kgk, qgk], writes=[kA])
                                S.op('dve', lambda e: e.tensor_tensor(out=aTm[:], in0=ps[bA][:, 0:128], in1=mm[:], op=ALU.mult),
                                     reads=[kA, mmk], writes=['aTm' + sfx])
                                S.op('pe', lambda e: e.matmul(ps[bO][:, 0:HG_DV], lhsT=aTm[:], rhs=i_tok[i][:, ti, :],
                                                              start=True, stop=False), reads=['aTm' + sfx, itk], writes=[kO])
                            S.op('pe', lambda e: e.matmul(ps[bA][:, 128:256], lhsT=kg[i][:, tok], rhs=identb[:],
                                                          start=True, stop=True), reads=[kgk, 'identb'], writes=[kA])
                            copy_op('act', kgt[:], ps[bA][:, 128:256], [kA], ['kgt' + sfx])
                            if not is_s:
                                if full:
                                    S.op('pe', lambda e: e.matmul(ps[bO][:, 0:HG_DV], lhsT=qg[i][:, tok], rhs=Sbf[:],
                                                                  start=False, stop=True), reads=[qgk, 'Sbf' + sfx], writes=[kO])
                                S.op('pe', lambda e: e.matmul(ps[bS][:, 0:HG_DV], lhsT=kgt[:], rhs=i_tok[i][:, ti, :],
                                                              start=True, stop=True), reads=['kgt' + sfx, itk], writes=[kS])
                                S.op('dve', lambda e: e.tensor_tensor(out=tmpS[:], in0=Sst[hd][:], in1=ps[bS][:, 0:HG_DV],
                                                                      op=ALU.add), reads=[Sk, kS], writes=['tmpS' + sfx])
                                ecol = eG[i][:, ti * 128 + 127:ti * 128 + 128]
                                S.op('act', lambda e: e.activation(out=Sst[hd][:], in_=tmpS[:], func=AF.Identity, scale=ecol),
                                     reads=['tmpS' + sfx, eGk, 'Sbf' + sfx], writes=[Sk])
                                copy_op('dve', Sbf[:], Sst[hd][:], [Sk], ['Sbf' + sfx])
                            else:
                                Sf, Sfb, qgq, kgq = X['Sf'], X['Sfb'], X['qgq'], X['kgq']

                                def sload(q):
                                    S.dma('sp', Sf[q % NSB][:], st_S[q, hd], writes=['Sf%d%s' % (q % NSB, sfx)])
                                for q in range(min(NSB - 1, NSQ)):
                                    sload(q)
                                for q in range(NSQ):
                                    if q + NSB - 1 < NSQ:
                                        sload(q + NSB - 1)
                                    sf, sfk = Sf[q % NSB], 'Sf%d%s' % (q % NSB, sfx)
                                    sfb, sfbk = Sfb[q % NSB], 'Sfb%d%s' % (q % NSB, sfx)
                                    qq, qqk = qgq[q % 2], 'qgq%d%s' % (q % 2, sfx)
                                    kq, kqk = kgq[q % 2], 'kgq%d%s' % (q % 2, sfx)
                                    if full:
                                        S.op('dve', lambda e: e.tensor_tensor(out=qq[:], in0=qg[i][:, tok], in1=blkb[:, q, :],
                                                                              op=ALU.mult), reads=[qgk, 'blkb'], writes=[qqk])
                                    S.op('dve', lambda e: e.tensor_scalar(out=kq[:], in0=kgt[:], scalar1=ind[:, q:q + 1],
                                                                          scalar2=None, op0=ALU.mult),
                                         reads=['kgt' + sfx, 'ind'], writes=[kqk])
                                    if full:
                                        copy_op('act', sfb[:], sf[:], [sfk], [sfbk])
                                        S.op('pe', lambda e: e.matmul(ps[bO][:, 0:HG_DV], lhsT=qq[:], rhs=sfb[:],
                                                                      start=False, stop=(q == NSQ - 1)),
                                             reads=[qqk, sfbk], writes=[kO])
                                    S.op('pe', lambda e: e.matmul(ps[bS][:, 0:HG_DV], lhsT=kq[:], rhs=i_tok[i][:, ti, :],
                                                                  start=True, stop=True), reads=[kqk, itk], writes=[kS])
                                    S.op('dve', lambda e: e.tensor_tensor(out=tmpS[:], in0=sf[:], in1=ps[bS][:, 0:HG_DV],
                                                                          op=ALU.add), reads=[sfk, kS], writes=['tmpS' + sfx])
                                    ecol = eG[i][:, ti * 128 + q * SQL + SQL - 1:ti * 128 + q * SQL + SQL]
                                    S.op('act', lambda e: e.activation(out=sf[:], in_=tmpS[:], func=AF.Identity, scale=ecol),
                                         reads=['tmpS' + sfx, eGk, sfbk], writes=[sfk])
                                    S.dma('pool', o_Ss[q, hd], sf[:], reads=[sfk])
                            if full:
                                S.op('act', lambda e: e.activation(out=obuf[:], in_=ps[bO][:, 0:HG_DV], func=AF.Square,
                                                                   accum_out=hc[:, 0:1]), reads=[kO],
                                     writes=['obuf' + sfx, 'hc0' + sfx])
                                S.op('act', lambda e: e.activation(out=hc[:, 1:2], in_=hc[:, 0:1], func=AF.Ln, scale=1.0 / HG_DV,
                                                                   bias=LN_EPS), reads=['hc0' + sfx], writes=['hc1' + sfx])
                                S.op('act', lambda e: e.activation(out=hc[:, 1:2], in_=hc[:, 1:2], func=AF.Exp, scale=-0.5),
                                     reads=['hc1' + sfx], writes=['hc1' + sfx])
                                S.op('dve', lambda e: e.scalar_tensor_tensor(
                                    out=obuf[:], in0=ps[bO][:, 0:HG_DV], scalar=hc[:, 1:2], in1=sg[i][:, ti, :],
                                    op0=ALU.mult, op1=ALU.mult), reads=[kO, 'hc1' + sfx, 'sg%d' % i, 'obuf' + sfx],
                                    writes=['obuf' + sfx])
                                for j in range(2):
                                    S.op('pe', lambda e: e.matmul(ps[bT][:, j * 128:(j + 1) * 128],
                                                                  lhsT=obuf[:, j * 128:(j + 1) * 128], rhs=ident[:],
                                                                  start=True, stop=True), reads=['obuf' + sfx, 'ident'],
                                         writes=[kT])
                                for j in range(2):
                                    ch = 2 * hd + j
                                    if j == 0:
                                        S.op('act', lambda e: e.activation(
                                            out=brB[:, ch, tok], in_=ps[bT][:, j * 128:(j + 1) * 128], func=AF.Identity,
                                            scale=gcol_b[:, ch:ch + 1]), reads=[kT, 'gcol_b'], writes=['brB' + sfx])
                                    else:
                                        S.op('dve', lambda e: e.tensor_scalar(
                                            out=brB[:, ch, tok], in0=ps[bT][:, j * 128:(j + 1) * 128],
                                            scalar1=gcol_b[:, ch:ch + 1], scalar2=None, op0=ALU.mult),
                                            reads=[kT, 'gcol_b'], writes=['brB' + sfx])

                    lists = []
                    for i in range(2):
                        S.start_record()
                        hg_chain(i, 2 * pr + i)
                        lists.append(S.stop_record())
                    S.replay_interleaved(lists)
                if final:
                    for hd in range(HG_H):
                        S.dma('sp', o_Sp[hd], Sst[hd][:], reads=['Sst%d' % hd])
                S.barrier()


        R = sb(es, "R", [128, cfg.RSZ], BF16)
        NMV, NHV = cfg.MLV // 128, cfg.HGV // 128
        DFF, FG, NFG = cfg.DFF, cfg.FG, cfg.NFG

        def layernorm(zt, zk, gt, bt, st, lst, lmv, lc):
            nchk = (D + 511) // 512
            for j in range(nchk):
                a, b_ = j * 512, min(D, (j + 1) * 512)
                S.op('dve', lambda e, j=j, a=a, b_=b_: e.bn_stats(out=lst[:, j, :], in_=zt[:, a:b_]), reads=[zk], writes=['lst'])
            S.op('dve', lambda e: e.bn_aggr(out=lmv[:], in_=lst[:].rearrange("p a b -> p (a b)")), reads=['lst'], writes=['lmv'])
            S.op('act', lambda e: e.activation(out=lc[:, 0:1], in_=lmv[:, 1:2], func=AF.Ln, bias=LN_EPS), reads=['lmv'],
                 writes=['lc0'])
            S.op('act', lambda e: e.activation(out=lc[:, 0:1], in_=lc[:, 0:1], func=AF.Exp, scale=-0.5), reads=['lc0'],
                 writes=['lc0'])
            S.op('dve', lambda e: e.tensor_scalar(out=lc[:, 1:2], in0=lmv[:, 0:1], scalar1=-1.0, scalar2=lc[:, 0:1],
                                                  op0=ALU.mult, op1=ALU.mult), reads=['lmv', 'lc0'], writes=['lc1'])
            S.op('act', lambda e: e.activation(out=zt, in_=zt, func=AF.Identity, scale=lc[:, 0:1], bias=lc[:, 1:2]),
                 reads=[zk, 'lc0', 'lc1'], writes=[zk])
            S.op('dve', lambda e: e.tensor_tensor(out=zt, in0=zt, in1=gt[:], op=ALU.mult), reads=[zk, 'gb'], writes=[zk])
            S.op('pool', lambda e: e.tensor_tensor(out=zt, in0=zt, in1=bt[:], op=ALU.add), reads=[zk, 'bb'], writes=[zk])

        def run_group(x_ap, rows, nprompt, full, samp, final, rmask=None):
            ntl = len(rows)
            Mt = ntl * 128
            with contextlib.ExitStack() as stG:
                xT = sb(stG, "xT", [128, KC, Mt], BF16)
                brA = brB = None
                if full:
                    brA = sb(stG, "brA", [128, NMV, Mt], BF16)
                    brB = sb(stG, "brB", [128, NHV, Mt], BF16)
                load_xT(x_ap, rows, xT, 'xT')
                chk('ldx')
                mixer_pass(xT, 'xT', nprompt, full, brA, brB, samp, R, final, rmask)
                if full:
                    chk('mix')
                if not full:
                    S.barrier()
                    return
                mrg = R[:, 0:KC * Mt].rearrange("p (a b) -> p a b", b=Mt)
                with contextlib.ExitStack() as st:
                    sga = sb(st, "sga", [128, 2, Mt])
                    sgb = sb(st, "sgb", [128, 2, Mt])
                    m1 = sb(st, "m1", [128, 2, Mt])
                    for d0 in range(0, D, BW):
                        nsub = min(BW, D - d0) // 128
                        for (gname, dst, dk) in (('ga', sga, 'sga'), ('gb', sgb, 'sgb')):
                            wb, wk = wget('w_in', w_in, 0, KC, off[gname] + d0, nsub * 128)
                            for i in range(nsub):
                                for (t0, n) in ntiles(Mt):
                                    pt, pk = nextps()
                                    proj_F(wb, wk, i * 128, 128, xT, 'xT', KC, t0, n, pt, pk)
                                    sigmoid_from('act', dst[:, i, t0:t0 + n], pt[:, 0:n], [pk], [dk])
                        wb, wk = wget('w_ba', w_ba, 0, NMV, d0, nsub * 128)
                        for i in range(nsub):
                            for (t0, n) in ntiles(Mt):
                                pt, pk = nextps()
                                proj_F(wb, wk, i * 128, 128, brA, 'brA', NMV, t0, n, pt, pk)
                                S.op('dve', lambda e, i=i, t0=t0, n=n, pt=pt: e.tensor_tensor(
                                    out=m1[:, i, t0:t0 + n], in0=pt[:, 0:n], in1=sga[:, i, t0:t0 + n], op=ALU.mult),
                                    reads=[pk, 'sga'], writes=['m1'])
                        wb, wk = wget('w_bb', w_bb, 0, NHV, d0, nsub * 128)
                        for i in range(nsub):
                            for (t0, n) in ntiles(Mt):
                                pt, pk = nextps()
                                proj_F(wb, wk, i * 128, 128, brB, 'brB', NHV, t0, n, pt, pk)
                                S.op('dve', lambda e, i=i, t0=t0, n=n, pt=pt: e.tensor_tensor(
                                    out=sgb[:, i, t0:t0 + n], in0=pt[:, 0:n], in1=sgb[:, i, t0:t0 + n], op=ALU.mult),
                                    reads=[pk, 'sgb'], writes=['sgb'])
                                S.op('pool', lambda e, i=i, t0=t0, n=n, d0=d0: e.tensor_tensor(
                                    out=mrg[:, d0 // 128 + i, t0:t0 + n], in0=sgb[:, i, t0:t0 + n], in1=m1[:, i, t0:t0 + n],
                                    op=ALU.add), reads=['sgb', 'm1'], writes=['mrg'])
                    S.barrier()
            S.barrier()
            chk('merge')
            with contextlib.ExitStack() as st:
                z = sb(st, "z", [128, ntl, D])
                gt = sb(st, "gt", [128, D])
                bt = sb(st, "bt", [128, D])
                hid = sb(st, "hid", [128, FG, Mt], BF16)
                rtmp = sb(st, "rtmp", [128, 512])
                lst = sb(st, "lst", [128, (D + 511) // 512, 6])
                lmv = sb(st, "lmv", [128, 2])
                lc = sb(st, "lc", [128, 2])
                zk = lambda ti: 'z%d' % ti
                for ti in range(ntl):
                    S.dma('sp', z[:, ti, :], x_ap[rows[ti]:rows[ti] + 128, :], writes=[zk(ti)])
                S.dma('sp', gt[:], ln1_g.partition_broadcast(128), writes=['gb'])
                S.dma('sp', bt[:], ln1_b.partition_broadcast(128), writes=['bb'])
                for d0 in range(0, D, BW):
                    wb, wk = wget('w_out', w_out, 0, KC, d0, BW)
                    for ti in range(ntl):
                        pt, pk = nextps()
                        proj_T(wb, wk, BW, mrg, 'mrg', KC, ti * 128, pt, pk)
                        S.op('dve', lambda e, ti=ti, d0=d0, pt=pt: e.scalar_tensor_tensor(
                            out=z[:, ti, d0:d0 + BW], in0=z[:, ti, d0:d0 + BW], scalar=ALPHA, in1=pt[:, 0:BW],
                            op0=ALU.mult, op1=ALU.add), reads=[zk(ti), pk], writes=[zk(ti)])
                for ti in range(ntl):
                    layernorm(z[:, ti, :], zk(ti), gt, bt, st, lst, lmv, lc)
                S.barrier()
                x1T = mrg
                zb = [sb(st, "zb%d" % i, [128, D], BF16) for i in range(2)]
                for ti in range(ntl):
                    zbt, zbk = zb[ti % 2], 'zb%d' % (ti % 2)
                    copy_op('act' if ti % 2 else 'dve', zbt[:], z[:, ti, :], [zk(ti)], [zbk])
                    for g in range(0, KC, 4):
                        n = min(4, KC - g)
                        pt, pk = nextps()
                        for j in range(n):
                            S.op('pe', lambda e, j=j, g=g, ti=ti, pt=pt: e.matmul(
                                pt[:, j * 128:(j + 1) * 128], lhsT=zbt[:, (g + j) * 128:(g + j + 1) * 128], rhs=identb[:],
                                start=True, stop=True), reads=[zbk, 'identb'], writes=[pk])
                        copy_op(evac_eng(), x1T[:, g:g + n, ti * 128:(ti + 1) * 128],
                                pt[:, 0:n * 128].rearrange("p (a b) -> p a b", b=128), [pk], ['x1T'])
                S.dma('sp', gt[:], ln2_g.partition_broadcast(128), writes=['gb'])
                S.dma('sp', bt[:], ln2_b.partition_broadcast(128), writes=['bb'])
                for fg in range(NFG):
                    for f0 in range(0, FG * 128, BW):
                        wb, wk = wget('w_up', w_up, 0, KC, fg * FG * 128 + f0, BW)
                        for i in range(BW // 128):
                            for (t0, n) in ntiles(Mt):
                                pt, pk = nextps()
                                proj_F(wb, wk, i * 128, 128, x1T, 'x1T', KC, t0, n, pt, pk)
                                S.op('act', lambda e, n=n, pt=pt: e.activation(out=rtmp[:, 0:n], in_=pt[:, 0:n], func=AF.Relu),
                                     reads=[pk], writes=['rtmp'])
                                S.op('dve', lambda e, i=i, f0=f0, t0=t0, n=n: e.tensor_tensor(
                                    out=hid[:, f0 // 128 + i, t0:t0 + n], in0=rtmp[:, 0:n], in1=rtmp[:, 0:n], op=ALU.mult),
                                    reads=['rtmp'], writes=['hid'])
                    for d0 in range(0, D, BW):
                        wb, wk = wget('w_dn', w_dn, fg * FG * 128, FG, d0, BW)
                        for ti in range(ntl):
                            pt, pk = nextps()
                            proj_T(wb, wk, BW, hid, 'hid', FG, ti * 128, pt, pk)
                            if fg == 0:
                                S.op('dve', lambda e, ti=ti, d0=d0, pt=pt: e.scalar_tensor_tensor(
                                    out=z[:, ti, d0:d0 + BW], in0=z[:, ti, d0:d0 + BW], scalar=ALPHA, in1=pt[:, 0:BW],
                                    op0=ALU.mult, op1=ALU.add), reads=[zk(ti), pk], writes=[zk(ti)])
                            else:
                                S.op('dve', lambda e, ti=ti, d0=d0, pt=pt: e.tensor_tensor(
                                    out=z[:, ti, d0:d0 + BW], in0=z[:, ti, d0:d0 + BW], in1=pt[:, 0:BW], op=ALU.add),
                                    reads=[zk(ti), pk], writes=[zk(ti)])
                            if fg == NFG - 1 and d0 + BW >= D:
                                layernorm(z[:, ti, :], zk(ti), gt, bt, st, lst, lmv, lc)
                                S.dma('sp', y_main[rows[ti]:rows[ti] + 128, :], z[:, ti, :], reads=[zk(ti)])
                S.barrier()

        chkcnt = {}

        def chk(tag):
            chkcnt[tag] = chkcnt.get(tag, 0) + 1
            if DBG['stop'] == tag or DBG['stop'] == '%s#%d' % (tag, chkcnt[tag]):
                S.dead = True

        chk_holder[0] = chk

        def _drive():
            chk('consts')
            with contextlib.ExitStack() as stp:
                rstp = sb(stp, "rstp", [128, NTP * 128])
                P(lambda e: e.memset(rstp[:], 1.0), ['rstp'])
                P(lambda e: e.memset(rstp[:].rearrange("p (a b) -> p a b", b=128)[:, :, 0:1], 0.0), ['rstp'], ['rstp'])
                S.barrier()
                run_group(x_pre, [t * 128 for t in range(NTP)], NTP, False, False, False, rmask=rstp)
            chk('pre')
            groups = list(range(0, NTP, GT))
            for gi, g0 in enumerate(groups):
                last = gi == len(groups) - 1
                rows = [t * 128 for t in range(g0, min(NTP, g0 + GT))]
                npr = len(rows)
                if last:
                    rows = rows + [NTP * 128]
                run_group(x_main, rows, npr, True, last, last)
                chk('g%d' % gi)
        try:
            _drive()
        except _Stop:
            pass
        S.finish()
    return nc, S


_CACHE = {}


def _get_program(cfg_key):
    if cfg_key not in _CACHE:
        cfg = Cfg(*cfg_key)
        rec = []
        build(cfg, wplan=None, record=rec)
        nc, S = build(cfg, wplan=rec, record=None)
        _CACHE[cfg_key] = (cfg, nc)
    return _CACHE[cfg_key]


def run_module(inputs, D, DFF, SEQ, BATCH, DEC_BATCH, core_ids=None):
    TH = SEQ // 2
    cfg, nc = _get_program((D, DFF, TH))
    ncores = 2 * BATCH
    assert DEC_BATCH == ncores * NSQ
    f = lambda a: np.ascontiguousarray(np.asarray(a, dtype=np.float32))
    xp, xs = f(inputs["x_prompt"]), f(inputs["x_sample"])
    stC, stn = f(inputs["state_mlstm_C"])[0], f(inputs["state_mlstm_n"])[0]
    stm, stS = f(inputs["state_mlstm_m"])[0], f(inputs["state_hgrn_S"])[0]
    shared = {
        "lb_logits": f(inputs["hg_lb_logits"]).reshape(2 * HG_H, 128),
        "w_in": f(inputs["w_in"])[0], "b_ig": f(inputs["b_ig"]).reshape(1, ML_H), "b_fg": f(inputs["b_fg"]).reshape(1, ML_H),
        "ml_g": f(inputs["ml_norm_g"]).reshape(-1, 128), "hg_g": f(inputs["hg_norm_g"]).reshape(-1, 128),
        "w_ba": f(inputs["w_branch_a"])[0], "w_bb": f(inputs["w_branch_b"])[0], "w_out": f(inputs["w_out"])[0],
        "ln1_g": f(inputs["ln1_g"]).reshape(1, D), "ln1_b": f(inputs["ln1_b"]).reshape(1, D),
        "w_up": f(inputs["w_up"])[0], "w_dn": f(inputs["w_down"])[0],
        "ln2_g": f(inputs["ln2_g"]).reshape(1, D), "ln2_b": f(inputs["ln2_b"]).reshape(1, D),
    }
    in_maps = []
    for c in range(ncores):
        b, half = c // 2, c % 2
        sl = slice(c * NSQ, (c + 1) * NSQ)
        m = dict(shared)
        m["x_pre"] = np.ascontiguousarray(xp[b, 0:TH]) if half == 1 else np.zeros((TH, D), np.float32)
        m["x_main"] = np.ascontiguousarray(np.concatenate([xp[b, half * TH:(half + 1) * TH], xs[sl].reshape(NSQ * SQL, D)], 0))
        m["st_C"] = np.ascontiguousarray(stC[sl])
        m["st_n"] = np.ascontiguousarray(stn[sl].reshape(NSQ * ML_H * 2, 128))
        m["st_m"] = np.ascontiguousarray(stm[sl])
        m["st_S"] = np.ascontiguousarray(stS[sl])
        in_maps.append(m)
    res = run_bass_kernel_spmd(nc, in_maps, core_ids=list(range(ncores)) if core_ids is None else core_ids)
    rs = res.results
    y_p = np.zeros((BATCH, SEQ, D), np.float32)
    y_s = np.zeros((DEC_BATCH, SQL, D), np.float32)
    Cp = np.zeros((1, BATCH, ML_H, ML_DK, ML_DV), np.float32)
    n_p = np.zeros((1, BATCH, ML_H, ML_DK), np.float32)
    mp = np.zeros((1, BATCH, ML_H), np.float32)
    Sp = np.zeros((1, BATCH, HG_H, HG_DK, HG_DV), np.float32)
    Cs = np.zeros((1, DEC_BATCH, ML_H, ML_DK, ML_DV), np.float32)
    ns = np.zeros((1, DEC_BATCH, ML_H, ML_DK), np.float32)
    ms = np.zeros((1, DEC_BATCH, ML_H), np.float32)
    Ss = np.zeros((1, DEC_BATCH, HG_H, HG_DK, HG_DV), np.float32)
    for c in range(ncores):
        b, half = c // 2, c % 2
        r = rs[c]
        sl = slice(c * NSQ, (c + 1) * NSQ)
        y_p[b, half * TH:(half + 1) * TH] = r["y_main"][0:TH]
        y_s[sl] = r["y_main"][TH:].reshape(NSQ, SQL, D)
        if half == 1:
            Cp[0, b] = r["o_Cp"]
            n_p[0, b] = r["o_np"].reshape(ML_H, ML_DK)
            mp[0, b] = r["o_mp"].reshape(ML_H)
            Sp[0, b] = r["o_Sp"]
        Cs[0, sl] = r["o_Cs"]
        ns[0, sl] = r["o_ns"].reshape(NSQ, ML_H, ML_DK)
        ms[0, sl] = r["o_ms"]
        Ss[0, sl] = r["o_Ss"]
    return (y_p, y_s, Cp, n_p, mp, Sp, Cs, ns, ms, Ss)


def kernel(**inputs):
    return run_module(inputs, D=2048, DFF=8192, SEQ=2048, BATCH=4, DEC_BATCH=128)
```
